# Optimizing a Trainium2 kernel written in Bass

```python
import math
import jax, jax.numpy as jnp
from jax import lax
import numpy as np


D_MODEL = 1024
BATCH = 4
SEQ = 8192
DEPTH = 2

PLE_DIM = 256
DA_HEADS = 4
DA_QK_DIM = 64
DA_V_DIM = 128
DA_WIDTH = DA_HEADS * DA_V_DIM
RET_HEADS = 4
RET_HEAD_DIM = 64
RET_WIDTH = RET_HEADS * RET_HEAD_DIM
RET_CHUNK = 128
S5_WIDTH = 256
S5_GROUP = 16
S5_GROUPS = S5_WIDTH // S5_GROUP
S5_STATE = 64
MIX_WIDTH = DA_WIDTH + RET_WIDTH + S5_WIDTH
D_FF = 2816
CONV_WIDTH = 3
ROPE_THETA = 10000.0
Q_BLOCK = 128
LN_EPS = 1e-5
ALPHA = (2 * DEPTH) ** 0.25
BETA = (8 * DEPTH) ** -0.25

COL_DA_Q = 0
COL_DA_K = COL_DA_Q + DA_HEADS * 2 * DA_QK_DIM
COL_DA_V = COL_DA_K + DA_HEADS * 2 * DA_QK_DIM
COL_RET_Q = COL_DA_V + DA_WIDTH
COL_RET_K = COL_RET_Q + RET_WIDTH
COL_RET_V = COL_RET_K + RET_WIDTH
COL_RET_G = COL_RET_V + RET_WIDTH
COL_S5_U = COL_RET_G + RET_WIDTH
IN_COLS = COL_S5_U + S5_WIDTH

kernel_name = 'hybrid_diffattn_s5_retention_encoder'

F32 = jnp.float32


def layer_norm(x, g, b):
    xf = x.astype(F32)
    mu = jnp.mean(xf, axis=-1, keepdims=True)
    var = jnp.mean(jnp.square(xf - mu), axis=-1, keepdims=True)
    y = (xf - mu) * lax.rsqrt(var + LN_EPS)
    return (y * g.astype(F32) + b.astype(F32)).astype(x.dtype)


def rms_norm(x, g, eps=1e-6):
    xf = x.astype(F32)
    y = xf * lax.rsqrt(jnp.mean(jnp.square(xf), axis=-1, keepdims=True) + eps)
    return (y * g.astype(F32)).astype(x.dtype)


def rope_tables(positions, dim):
    inv_freq = ROPE_THETA ** (-jnp.arange(0, dim, 2, dtype=F32) / dim)
    ang = positions.astype(F32)[..., None] * inv_freq
    return jnp.cos(ang), jnp.sin(ang)


def apply_rope(x, cos, sin):
    extra = x.ndim - 3
    shp = cos.shape[:2] + (1,) * extra + cos.shape[-1:]
    c = cos.reshape(shp).astype(x.dtype)
    s = sin.reshape(shp).astype(x.dtype)
    x1, x2 = jnp.split(x, 2, axis=-1)
    return jnp.concatenate([x1 * c - x2 * s, x2 * c + x1 * s], axis=-1)


def diff_attention(q, k, v, cos, sin, lam, subln_g, lambda_init):
    Bsz, L, H, _, d = q.shape
    dv = v.shape[-1]
    nb = L // Q_BLOCK
    q = apply_rope(q, cos, sin) * (d ** -0.5)
    k = apply_rope(k, cos, sin)
    q_blocks = q.reshape(Bsz, nb, Q_BLOCK, H, 2, d).transpose(1, 0, 3, 4, 2, 5)
    k_t = k.transpose(0, 2, 3, 1, 4)
    v_t = v.transpose(0, 2, 1, 3)

    def attend(qb):
        s = jnp.einsum('bhmqd,bhmkd->bhmqk', qb, k_t).astype(F32)
        a = jax.nn.softmax(s, axis=-1)
        w = (a[:, :, 0] - lam * a[:, :, 1]).astype(v.dtype)
        return jnp.einsum('bhqk,bhkv->bhqv', w, v_t)

    o = lax.map(attend, q_blocks)
    o = o.transpose(1, 0, 3, 2, 4).reshape(Bsz, L, H, dv)
    o = rms_norm(o, subln_g) * (1.0 - lambda_init)
    return o.reshape(Bsz, L, H * dv)


def _retention_causal(q, k, v, log_gamma, include_diag):
    Bsz, L, H, d = q.shape
    dv = v.shape[-1]
    C = RET_CHUNK
    nc = L // C

    def chunks(t):
        return t.reshape(Bsz, nc, C, H, t.shape[-1]).transpose(1, 0, 3, 2, 4)

    qc, kc, vc = chunks(q), chunks(k), chunks(v)
    idx = jnp.arange(C, dtype=F32)
    diff = idx[:, None] - idx[None, :]
    mask = (diff >= 0) if include_diag else (diff > 0)
    lg = log_gamma[:, None, None]
    decay_in = jnp.where(mask[None], jnp.exp(lg * jnp.maximum(diff, 0.0)[None]), 0.0)
    s = jnp.einsum('nbhqd,nbhkd->nbhqk', qc, kc) * decay_in.astype(q.dtype)
    inner = jnp.einsum('nbhqk,nbhkv->nbhqv', s, vc)
    k_decay = jnp.exp(log_gamma[:, None] * (C - 1 - idx)[None]).astype(q.dtype)
    q_decay = jnp.exp(log_gamma[:, None] * (idx + 1)[None]).astype(q.dtype)
    chunk_decay = jnp.exp(log_gamma * C).astype(q.dtype)
    kv = jnp.einsum('nbhkd,nbhkv->nbhdv', kc * k_decay[:, :, None], vc)

    def step(state, kv_c):
        return chunk_decay[:, None, None] * state + kv_c, state

    _, state_before = lax.scan(step, jnp.zeros_like(kv[0]), kv)
    cross = jnp.einsum('nbhqd,nbhdv->nbhqv', qc * q_decay[:, :, None], state_before)
    return (inner + cross).transpose(1, 0, 3, 2, 4).reshape(Bsz, L, H, dv)


def retention_block(q, k, v, g, cos, sin, log_gamma, gn_g, gn_b):
    Bsz, L, H, d = q.shape
    q = apply_rope(q, cos, sin)
    k = apply_rope(k, cos, sin) * (d ** -0.5)
    fwd = _retention_causal(q, k, v, log_gamma, True)
    bwd = jnp.flip(_retention_causal(jnp.flip(q, 1), jnp.flip(k, 1), jnp.flip(v, 1), log_gamma, False), 1)
    o = layer_norm(fwd + bwd, gn_g, gn_b)
    return jax.nn.silu(g) * o.reshape(Bsz, L, H * v.shape[-1])


def _complex_linear_combine(left, right):
    ar1, ai1, br1, bi1 = left
    ar2, ai2, br2, bi2 = right
    ar = ar2 * ar1 - ai2 * ai1
    ai = ar2 * ai1 + ai2 * ar1
    br = ar2 * br1 - ai2 * bi1 + br2
    bi = ar2 * bi1 + ai2 * br1 + bi2
    return ar, ai, br, bi


def s5_block(u, A_re, A_im, log_dt, B_re, B_im, C_re, C_im, D, glu_w, glu_b):
    Bsz, L, _ = u.shape
    dt_ = u.dtype
    ug = u.reshape(Bsz, L, S5_GROUPS, S5_GROUP)
    y = D.reshape(S5_GROUPS, S5_GROUP).astype(dt_) * ug
    for direction in range(2):
        a_re = A_re[direction].astype(F32)
        a_im = A_im[direction].astype(F32)
        step = jnp.exp(log_dt[direction].astype(F32))[:, None]
        e = jnp.exp(step * a_re)
        abar_re = e * jnp.cos(step * a_im)
        abar_im = e * jnp.sin(step * a_im)
        den = a_re * a_re + a_im * a_im
        nr = abar_re - 1.0
        ni = abar_im
        coef_re = (nr * a_re + ni * a_im) / den
        coef_im = (ni * a_re - nr * a_im) / den
        b_re = B_re[direction].astype(F32)
        b_im = B_im[direction].astype(F32)
        bb_re = (coef_re[..., None] * b_re - coef_im[..., None] * b_im).astype(dt_)
        bb_im = (coef_re[..., None] * b_im + coef_im[..., None] * b_re).astype(dt_)
        bu_re = jnp.einsum('blgc,gpc->blgp', ug, bb_re)
        bu_im = jnp.einsum('blgc,gpc->blgp', ug, bb_im)
        a_seq_re = jnp.broadcast_to(abar_re.astype(dt_)[None, None], (1, L, S5_GROUPS, S5_STATE))
        a_seq_im = jnp.broadcast_to(abar_im.astype(dt_)[None, None], (1, L, S5_GROUPS, S5_STATE))
        _, _, x_re, x_im = lax.associative_scan(
            _complex_linear_combine, (a_seq_re, a_seq_im, bu_re, bu_im),
            reverse=(direction == 1), axis=1)
        y = y + jnp.einsum('blgp,gcp->blgc', x_re, C_re[direction].astype(dt_)) \
              - jnp.einsum('blgp,gcp->blgc', x_im, C_im[direction].astype(dt_))
    y = jax.nn.gelu(y.reshape(Bsz, L, S5_WIDTH))
    return y * jax.nn.sigmoid(y @ glu_w + glu_b)


def conv_ffn(x, w_up, conv_w, conv_b, w_down):
    h = x @ w_up
    gate, val = jnp.split(h, 2, axis=-1)
    pad = CONV_WIDTH // 2
    gate = lax.conv_general_dilated(
        gate, conv_w[:, None, :].astype(gate.dtype), window_strides=(1,),
        padding=((pad, pad),), dimension_numbers=('NWC', 'WIO', 'NWC'),
        feature_group_count=D_FF) + conv_b
    return (jax.nn.gelu(gate) * val) @ w_down


def setup_inputs(seed: int = 0) -> dict:
    key = jax.random.key(seed)
    ks = jax.random.split(key, 40)
    nrm = lambda k, shape, s: jax.random.normal(k, shape, F32) * s
    x = jax.random.normal(ks[0], (BATCH, SEQ, D_MODEL), F32)
    p = jax.random.normal(ks[1], (DEPTH, BATCH, SEQ, PLE_DIM), F32)
    positions = jnp.broadcast_to(jnp.arange(SEQ, dtype=jnp.int32)[None], (BATCH, SEQ))
    col_scale = jnp.ones((IN_COLS,), F32)
    col_scale = col_scale.at[COL_DA_V:COL_RET_Q].set(BETA).at[COL_RET_V:COL_RET_G].set(BETA)
    w_in = nrm(ks[2], (DEPTH, D_MODEL, IN_COLS), D_MODEL ** -0.5) * col_scale
    da_lambda_q1 = nrm(ks[3], (DEPTH, DA_QK_DIM), 0.1)
    da_lambda_k1 = nrm(ks[4], (DEPTH, DA_QK_DIM), 0.1)
    da_lambda_q2 = nrm(ks[5], (DEPTH, DA_QK_DIM), 0.1)
    da_lambda_k2 = nrm(ks[6], (DEPTH, DA_QK_DIM), 0.1)
    da_subln_g = 1.0 + nrm(ks[7], (DEPTH, DA_V_DIM), 0.02)
    ret_gn_g = 1.0 + nrm(ks[8], (DEPTH, RET_HEAD_DIM), 0.02)
    ret_gn_b = nrm(ks[9], (DEPTH, RET_HEAD_DIM), 0.02)
    n_idx = jnp.arange(S5_STATE, dtype=F32)
    s5_A_re = -0.5 + nrm(ks[10], (DEPTH, 2, S5_GROUPS, S5_STATE), 0.01)
    s5_A_im = math.pi * n_idx + nrm(ks[11], (DEPTH, 2, S5_GROUPS, S5_STATE), 0.01)
    s5_log_dt = jax.random.uniform(ks[12], (DEPTH, 2, S5_GROUPS), F32, math.log(0.001), math.log(0.1))
    s5_B_re = nrm(ks[13], (DEPTH, 2, S5_GROUPS, S5_STATE, S5_GROUP), (2 * S5_GROUP) ** -0.5)
    s5_B_im = nrm(ks[14], (DEPTH, 2, S5_GROUPS, S5_STATE, S5_GROUP), (2 * S5_GROUP) ** -0.5)
    s5_C_re = nrm(ks[15], (DEPTH, 2, S5_GROUPS, S5_GROUP, S5_STATE), (2 * S5_STATE) ** -0.5)
    s5_C_im = nrm(ks[16], (DEPTH, 2, S5_GROUPS, S5_GROUP, S5_STATE), (2 * S5_STATE) ** -0.5)
    s5_D = nrm(ks[17], (DEPTH, S5_WIDTH), 1.0)
    s5_glu_w = nrm(ks[18], (DEPTH, S5_WIDTH, S5_WIDTH), S5_WIDTH ** -0.5)
    s5_glu_b = nrm(ks[19], (DEPTH, S5_WIDTH), 0.02)
    w_out = nrm(ks[20], (DEPTH, MIX_WIDTH, D_MODEL), MIX_WIDTH ** -0.5 * BETA)
    ln1_g = 1.0 + nrm(ks[21], (DEPTH, D_MODEL), 0.02)
    ln1_b = nrm(ks[22], (DEPTH, D_MODEL), 0.02)
    ffn_w_up = nrm(ks[23], (DEPTH, D_MODEL, 2 * D_FF), D_MODEL ** -0.5 * BETA)
    ffn_conv_w = nrm(ks[24], (DEPTH, CONV_WIDTH, D_FF), CONV_WIDTH ** -0.5)
    ffn_conv_b = nrm(ks[25], (DEPTH, D_FF), 0.02)
    ffn_w_down = nrm(ks[26], (DEPTH, D_FF, D_MODEL), D_FF ** -0.5 * BETA)
    ple_w = nrm(ks[27], (DEPTH, PLE_DIM, D_MODEL), PLE_DIM ** -0.5 * BETA)
    ple_gate_w = nrm(ks[28], (DEPTH, D_MODEL, D_MODEL), D_MODEL ** -0.5)
    ln2_g = 1.0 + nrm(ks[29], (DEPTH, D_MODEL), 0.02)
    ln2_b = nrm(ks[30], (DEPTH, D_MODEL), 0.02)
    return {'x': x, 'p': p, 'positions': positions, 'w_in': w_in,
            'da_lambda_q1': da_lambda_q1, 'da_lambda_k1': da_lambda_k1,
            'da_lambda_q2': da_lambda_q2, 'da_lambda_k2': da_lambda_k2,
            'da_subln_g': da_subln_g, 'ret_gn_g': ret_gn_g, 'ret_gn_b': ret_gn_b,
            's5_A_re': s5_A_re, 's5_A_im': s5_A_im, 's5_log_dt': s5_log_dt,
            's5_B_re': s5_B_re, 's5_B_im': s5_B_im, 's5_C_re': s5_C_re, 's5_C_im': s5_C_im,
            's5_D': s5_D, 's5_glu_w': s5_glu_w, 's5_glu_b': s5_glu_b,
            'w_out': w_out, 'ln1_g': ln1_g, 'ln1_b': ln1_b,
            'ffn_w_up': ffn_w_up, 'ffn_conv_w': ffn_conv_w, 'ffn_conv_b': ffn_conv_b,
            'ffn_w_down': ffn_w_down, 'ple_w': ple_w, 'ple_gate_w': ple_gate_w,
            'ln2_g': ln2_g, 'ln2_b': ln2_b}


def reference(x, p, positions, w_in, da_lambda_q1, da_lambda_k1, da_lambda_q2, da_lambda_k2,
              da_subln_g, ret_gn_g, ret_gn_b, s5_A_re, s5_A_im, s5_log_dt, s5_B_re, s5_B_im,
              s5_C_re, s5_C_im, s5_D, s5_glu_w, s5_glu_b, w_out, ln1_g, ln1_b,
              ffn_w_up, ffn_conv_w, ffn_conv_b, ffn_w_down, ple_w, ple_gate_w, ln2_g, ln2_b):
    Bsz, L, _ = x.shape
    cos, sin = rope_tables(positions, DA_QK_DIM)
    log_gamma = jnp.log(1.0 - 2.0 ** (-5.0 - jnp.arange(RET_HEADS, dtype=F32)))
    for i in range(DEPTH):
        lambda_init = 0.8 - 0.6 * math.exp(-0.3 * i)
        z = x @ w_in[i]
        dq = z[..., COL_DA_Q:COL_DA_K].reshape(Bsz, L, DA_HEADS, 2, DA_QK_DIM)
        dk = z[..., COL_DA_K:COL_DA_V].reshape(Bsz, L, DA_HEADS, 2, DA_QK_DIM)
        dv = z[..., COL_DA_V:COL_RET_Q].reshape(Bsz, L, DA_HEADS, DA_V_DIM)
        lam = (jnp.exp(jnp.sum(da_lambda_q1[i].astype(F32) * da_lambda_k1[i].astype(F32)))
               - jnp.exp(jnp.sum(da_lambda_q2[i].astype(F32) * da_lambda_k2[i].astype(F32)))
               + lambda_init)
        y_da = diff_attention(dq, dk, dv, cos, sin, lam, da_subln_g[i], lambda_init)
        rq = z[..., COL_RET_Q:COL_RET_K].reshape(Bsz, L, RET_HEADS, RET_HEAD_DIM)
        rk = z[..., COL_RET_K:COL_RET_V].reshape(Bsz, L, RET_HEADS, RET_HEAD_DIM)
        rv = z[..., COL_RET_V:COL_RET_G].reshape(Bsz, L, RET_HEADS, RET_HEAD_DIM)
        rg = z[..., COL_RET_G:COL_S5_U]
        y_ret = retention_block(rq, rk, rv, rg, cos, sin, log_gamma, ret_gn_g[i], ret_gn_b[i])
        u = z[..., COL_S5_U:IN_COLS]
        y_s5 = s5_block(u, s5_A_re[i], s5_A_im[i], s5_log_dt[i], s5_B_re[i], s5_B_im[i],
                        s5_C_re[i], s5_C_im[i], s5_D[i], s5_glu_w[i], s5_glu_b[i])
        mix = jnp.concatenate([y_da, y_ret, y_s5], axis=-1) @ w_out[i]
        x = layer_norm(ALPHA * x + mix, ln1_g[i], ln1_b[i])
        f = conv_ffn(x, ffn_w_up[i], ffn_conv_w[i], ffn_conv_b[i], ffn_w_down[i])
        ple = (p[i] @ ple_w[i]) * jax.nn.sigmoid(x @ ple_gate_w[i])
        x = layer_norm(ALPHA * x + f + ple, ln2_g[i], ln2_b[i])
    return x
```

```python
import math
from contextlib import ExitStack

import numpy as np
import ml_dtypes

import concourse.bass as bass
import concourse.mybir as mybir
from concourse.bass_utils import run_bass_kernel_spmd

F32 = mybir.dt.float32
BF16 = mybir.dt.bfloat16
I32 = mybir.dt.int32
ALU = mybir.AluOpType
AF = mybir.ActivationFunctionType
AX = mybir.AxisListType

D_MODEL = 1024
BATCH = 4
SEQ = 8192
DEPTH = 2
PLE_DIM = 256
D_FF = 2816
LN_EPS = 1e-5
ALPHA = (2 * DEPTH) ** 0.25
COL_DA_Q = 0
COL_DA_K = 512
COL_DA_V = 1024
COL_RET_Q = 1536
COL_RET_K = 1792
COL_RET_V = 2048
COL_RET_G = 2304
COL_S5_U = 2560
NCORES = 8
TWO_PI = 2.0 * math.pi


class Prog:
    ENGS = ("pe", "act", "dve", "pool", "sp")

    def __init__(self, nc):
        self.nc = nc
        self.ops = {e: [] for e in self.ENGS}
        self.cnt = {}
        self.seen = {e: {} for e in self.ENGS}
        self.last_w = {}
        self.readers = {}
        self.n = 0

    def _deps(self, reads, writes):
        raw, other = set(), set()

        def fix(t):
            if isinstance(t[0], tuple):
                t = (t[0], self.cnt[t[0]])
            return t
        for k in reads:
            t = self.last_w.get(k)
            if t is not None:
                raw.add(fix(t))
        for k in writes:
            t = self.last_w.get(k)
            if t is not None:
                other.add(fix(t))
            for t in self.readers.get(k, ()):
                other.add(fix(t))
        return raw, other

    def _commit(self, tok, reads, writes):
        for k in reads:
            self.readers.setdefault(k, []).append(tok)
        for k in writes:
            self.last_w[k] = tok
            self.readers[k] = []

    def _waits(self, eng, deps):
        raw, other = deps
        best = {}
        for src_set, is_raw in ((raw, True), (other, False)):
            for (sk, v) in src_set:
                if sk == eng and eng not in ("act", "dve", "pool"):
                    continue
                if self.seen[eng].get(sk, 0) >= v:
                    continue
                if best.get(sk, 0) < v:
                    best[sk] = v
        for sk, v in best.items():
            self.seen[eng][sk] = v
        return list(best.items())

    def op(self, eng, fn, reads=(), writes=()):
        deps = self._deps(reads, writes)
        waits = self._waits(eng, deps)
        v = self.cnt.get(eng, 0) + 1
        self.cnt[eng] = v
        tok = (eng, v)
        self.ops[eng].append((waits, fn, (eng, 1)))
        self._commit(tok, reads, writes)
        self.n += 1
        return tok

    def dma(self, q, out, in_, chan, reads=(), writes=(), slow=False):
        deps = self._deps(reads, writes)
        waits = self._waits(q, deps)
        sk = ("dma", chan)
        v = self.cnt.get(sk, 0) + 16
        self.cnt[sk] = v
        tok = (sk, v)
        if slow:
            self.ops[q].append((waits, lambda e, o=out, i=in_: e.dma_start(out=o, in_=i, allow_slow_non_contiguous=True), (sk, 16)))
        else:
            self.ops[q].append((waits, lambda e, o=out, i=in_: e.dma_start(out=o, in_=i), (sk, 16)))
        self._commit(tok, reads, writes)
        self.n += 1
        return tok

    def emit(self, es, own_sems=False):
        nc = self.nc
        self.sem_handles = []
        fin = [(sk, v) for sk, v in self.cnt.items() if isinstance(sk, tuple)]
        sems = {}
        for i, sk in enumerate(self.cnt.keys()):
            _UID[0] += 1
            if own_sems:
                sems[sk] = nc.alloc_semaphore(name="s%d_%d" % (i, _UID[0]))
                self.sem_handles.append(sems[sk])
            else:
                sems[sk] = es.enter_context(nc.semaphore("s%d_%d" % (i, _UID[0])))
        block = es.enter_context(nc.Block())

        def replay(name, e):
            for waits, fn, inc in self.ops[name]:
                for sk, v in waits:
                    e.wait_ge(sems[sk], v)
                ins = fn(e)
                ins.then_inc(sems[inc[0]], inc[1])
            if name == "sp":
                for sk, v in fin:
                    e.wait_ge(sems[sk], v)
                for en in ("pe", "act", "dve", "pool"):
                    if self.cnt.get(en, 0):
                        e.wait_ge(sems[en], self.cnt[en])

        @block.sync
        def _(e):
            replay("sp", e)

        @block.tensor
        def _(e):
            replay("pe", e)

        @block.scalar
        def _(e):
            replay("act", e)

        @block.vector
        def _(e):
            replay("dve", e)

        @block.gpsimd
        def _(e):
            replay("pool", e)


_UID = [0]


class Ctx:
    def __init__(self, nc, es):
        self.nc = nc
        self.es = es
        self.P = Prog(nc)
        self._i = 0

    def sb(self, shape, dt, name=None):
        _UID[0] += 1
        return self.es.enter_context(self.nc.sbuf_tensor("%s_%d" % (name or "t", _UID[0]), list(shape), dt))

    def ps(self, shape, dt, name=None):
        _UID[0] += 1
        return self.es.enter_context(self.nc.psum_tensor("%s_%d" % (name or "p", _UID[0]), list(shape), dt))

    def din(self, name, shape, dt):
        return self.nc.dram_tensor(name, list(shape), dt, kind="ExternalInput").ap()

    def dout(self, name, shape, dt):
        return self.nc.dram_tensor(name, list(shape), dt, kind="ExternalOutput").ap()

    def dint(self, name, shape, dt):
        return self.nc.dram_tensor(name, list(shape), dt, kind="Internal").ap()


def bf(a):
    return np.ascontiguousarray(a).astype(ml_dtypes.bfloat16)


def stage_T0(C, x_d, xT_d, ident_bf, ntok, tag="t0"):
    P = C.P
    xin = [C.sb([128, 1024], F32, "xin") for _ in range(2)]
    xbf = [C.sb([128, 1024], BF16, "xbf") for _ in range(2)]
    xT = [C.sb([128, 8, 512], BF16, "xT") for _ in range(2)]
    pst = [C.ps([128, 8, 128], BF16, "pst") for _ in range(2)]
    xT_v = xT_d.rearrange("(c p) t -> p c t", p=128)
    nt = ntok // 128
    for i in range(nt):
        s = i % 2
        g = (i // 4) % 2
        P.dma("sp", xin[s][:], x_d[i * 128:(i + 1) * 128, :], chan=(tag, "xin", s),
              writes=[(tag, "xin", s)])
        P.op("act", lambda e, s=s: e.copy(out=xbf[s][:], in_=xin[s][:]),
             reads=[(tag, "xin", s)], writes=[(tag, "xbf", s)])
        for c in range(8):
            P.op("pe", lambda e, s=s, c=c: e.transpose(out=pst[s][:, c, :], in_=xbf[s][:, c * 128:(c + 1) * 128],
                                                        identity=ident_bf[:]),
                 reads=[(tag, "xbf", s), "ident"], writes=[(tag, "pst", s, c)])
        P.op("dve", lambda e, s=s, g=g, i=i: e.tensor_copy(out=xT[g][:, :, (i % 4) * 128:(i % 4 + 1) * 128], in_=pst[s][:]),
             reads=[(tag, "pst", s, c) for c in range(8)], writes=[(tag, "xT", g, i % 4)])
        if i % 4 == 3:
            t0 = (i // 4) * 512
            P.dma("pool", xT_v[:, :, t0:t0 + 512], xT[g][:], chan=(tag, "xTo", g),
                  reads=[(tag, "xT", g, j) for j in range(4)], writes=[(tag, "xTd", i // 4)])


def build_T0(ntok):
    nc = bass.Bass("TRN2", target_bir_lowering=False)
    with ExitStack() as es:
        C = Ctx(nc, es)
        x_d = C.din("x", [ntok, 1024], F32)
        id_d = C.din("ident", [128, 128], BF16)
        xT_d = C.dout("xT", [1024, ntok], BF16)
        ident = C.sb([128, 128], BF16, "ident")
        C.P.dma("sp", ident[:], id_d[:, :], chan="ident", writes=["ident"])
        stage_T0(C, x_d, xT_d, ident, ntok)
        C.P.emit(es)
    return nc


NFM = 13
ROPED = 6


def a1_columns(h):
    def swap64(cols):
        cols = np.asarray(cols).reshape(-1, 64)
        return np.concatenate([cols[:, 32:], cols[:, :32]], axis=1).reshape(-1)
    tiles = []
    for base in (COL_DA_Q, COL_DA_K):
        for hh in range(2):
            tiles.append(base + (2 * h + hh) * 128 + np.arange(128))
    tiles.append(COL_RET_Q + 2 * h * 64 + np.arange(128))
    tiles.append(COL_RET_K + 2 * h * 64 + np.arange(128))
    sw = [swap64(t) for t in tiles]
    u = COL_S5_U + 128 * h + np.arange(128)
    fm = np.concatenate(tiles + sw + [u])
    tm = np.concatenate([COL_DA_V + 2 * h * 128 + np.arange(256),
                         COL_RET_V + 2 * h * 64 + np.arange(128),
                         COL_RET_G + 2 * h * 64 + np.arange(128)])
    return fm, tm


def rope_consts():
    inv = 10000.0 ** (-np.arange(0, 64, 2, dtype=np.float64) / 64.0)
    p = np.arange(128)
    invf = inv[p % 32].astype(np.float32).reshape(128, 1)
    sgn = np.where((p % 64) < 32, -1.0, 1.0).astype(np.float32).reshape(128, 1)
    return invf, sgn


CW1 = float(np.float32(6.28125))
CW2 = float(np.float32(TWO_PI - 6.28125))


def sincos(P, eng, tag, ang, ang_key, sarg, carg, tmp_i, tmp_f):
    ki, kf, ks, kc = (tag, "rr_i"), (tag, "rr_f"), (tag, "sarg"), (tag, "carg")
    P.op(eng, lambda e: e.tensor_scalar(out=tmp_i[:], in0=ang[:], scalar1=1.0 / TWO_PI, scalar2=None, op0=ALU.mult),
         reads=[ang_key], writes=[ki])
    P.op(eng, lambda e: e.tensor_copy(out=tmp_f[:], in_=tmp_i[:]), reads=[ki], writes=[kf])
    P.op(eng, lambda e: e.scalar_tensor_tensor(out=sarg[:], in0=tmp_f[:], scalar=-CW1, in1=ang[:], op0=ALU.mult, op1=ALU.add),
         reads=[kf, ang_key], writes=[ks])
    P.op(eng, lambda e: e.scalar_tensor_tensor(out=sarg[:], in0=tmp_f[:], scalar=-CW2, in1=sarg[:], op0=ALU.mult, op1=ALU.add),
         reads=[kf, ks], writes=[ks])

    def wrap(t, key):
        P.op(eng, lambda e: e.tensor_scalar(out=tmp_f[:], in0=t[:], scalar1=math.pi, scalar2=-TWO_PI, op0=ALU.is_gt, op1=ALU.mult),
             reads=[key], writes=[kf])
        P.op(eng, lambda e: e.tensor_tensor(out=t[:], in0=t[:], in1=tmp_f[:], op=ALU.add), reads=[key, kf], writes=[key])
        P.op(eng, lambda e: e.tensor_scalar(out=tmp_f[:], in0=t[:], scalar1=-math.pi, scalar2=TWO_PI, op0=ALU.is_lt, op1=ALU.mult),
             reads=[key], writes=[kf])
        P.op(eng, lambda e: e.tensor_tensor(out=t[:], in0=t[:], in1=tmp_f[:], op=ALU.add), reads=[key, kf], writes=[key])
    wrap(sarg, ks)
    P.op(eng, lambda e: e.tensor_scalar(out=carg[:], in0=sarg[:], scalar1=0.5 * math.pi, scalar2=None, op0=ALU.add),
         reads=[ks], writes=[kc])
    wrap(carg, kc)


def stage_A1(C, xT_d, wfm_d, wtm_d, pos_d, rc_d, fmT_d, vr_d, sg_d, L, tag="a1"):
    P = C.P
    NW = NFM * 128
    wfm = C.sb([128, 8, NW], BF16, "wfm")
    wtm = C.sb([128, 8, 512], BF16, "wtm")
    stg = [C.sb([128, NW], F32, "wstg") for _ in range(2)]
    rc = C.sb([128, 4], F32, "rc")
    P.dma("sp", rc[:], rc_d[:, :], chan=(tag, "rc"), writes=[(tag, "rc")])
    wfm_v = wfm_d.rearrange("(c p) n -> p c n", p=128)
    wtm_v = wtm_d.rearrange("(c p) n -> p c n", p=128)
    for c in range(8):
        s = c % 2
        P.dma("sp", stg[s][:], wfm_v[:, c, :], chan=(tag, "wstg", s), writes=[(tag, "wstg", s)])
        P.op("pool", lambda e, s=s, c=c: e.tensor_copy(out=wfm[:, c, :], in_=stg[s][:]),
             reads=[(tag, "wstg", s)], writes=[(tag, "wfm", c)])
    for c in range(8):
        s = c % 2
        P.dma("sp", stg[s][:, 0:512], wtm_v[:, c, :], chan=(tag, "wstg", s), writes=[(tag, "wstg", s)])
        P.op("pool", lambda e, s=s, c=c: e.tensor_copy(out=wtm[:, c, :], in_=stg[s][:, 0:512]),
             reads=[(tag, "wstg", s)], writes=[(tag, "wtm", c)])
    wfm_k = [(tag, "wfm", c) for c in range(8)]
    wtm_k = [(tag, "wtm", c) for c in range(8)]

    xT = [C.sb([128, 8, 512], BF16, "xT") for _ in range(2)]
    posi = [C.sb([128, 512], I32, "posi") for _ in range(2)]
    posf = C.sb([128, 512], F32, "posf")
    a0 = C.sb([128, 512], F32, "a0")
    sarg = C.sb([128, 512], F32, "sarg")
    carg = C.sb([128, 512], F32, "carg")
    rr_i = C.sb([128, 512], I32, "rr_i")
    rr_f = C.sb([128, 512], F32, "rr_f")
    cosT = [C.sb([128, 512], F32, "cosT") for _ in range(2)]
    sinT = [C.sb([128, 512], F32, "sinT") for _ in range(2)]
    t1 = [C.sb([128, 512], F32, "t1") for _ in range(2)]
    t2 = [C.sb([128, 512], F32, "t2") for _ in range(2)]
    fmo = [C.sb([128, 7, 512], BF16, "fmo") for _ in range(2)]
    tmo = [C.sb([128, 4, 384], BF16, "tmo") for _ in range(2)]
    sgo = [C.sb([128, 4, 128], F32, "sgo") for _ in range(2)]
    psA = [C.ps([128, 512], F32, "psA") for _ in range(2)]
    psB = [C.ps([128, 512], F32, "psB") for _ in range(2)]
    psC = [C.ps([128, 512], F32, "psC") for _ in range(3)]
    xT_v = xT_d.rearrange("(c p) t -> p c t", p=128)
    fmT_v = fmT_d.rearrange("r p t -> p r t")
    nt = L // 512
    pc = 0
    cc = 0
    for it in range(nt):
        s = it % 2
        tk0 = it * 512
        P.dma("sp", xT[s][:], xT_v[:, :, tk0:tk0 + 512], chan=(tag, "xT", s), writes=[(tag, "xT", s)])
        P.dma("sp", posi[s][:], pos_d[tk0:tk0 + 512].partition_broadcast(128), chan=(tag, "pos", s),
              writes=[(tag, "pos", s)])
        P.op("dve", lambda e, s=s: e.tensor_copy(out=posf[:], in_=posi[s][:]),
             reads=[(tag, "pos", s)], writes=[(tag, "posf")])
        P.op("dve", lambda e: e.tensor_scalar(out=a0[:], in0=posf[:], scalar1=rc[:, 0:1], scalar2=None, op0=ALU.mult),
             reads=[(tag, "posf"), (tag, "rc")], writes=[(tag, "a0")])
        sincos(P, "dve", tag, a0, (tag, "a0"), sarg, carg, rr_i, rr_f)
        P.op("act", lambda e, s=s: e.activation(out=sinT[s][:], in_=sarg[:], func=AF.Sin, scale=rc[:, 1:2]),
             reads=[(tag, "sarg"), (tag, "rc")], writes=[(tag, "sinT", s)])
        P.op("act", lambda e, s=s: e.activation(out=cosT[s][:], in_=carg[:], func=AF.Sin),
             reads=[(tag, "carg")], writes=[(tag, "cosT", s)])
        for r in range(ROPED):
            b = pc % 2
            pc += 1
            for c in range(8):
                P.op("pe", lambda e, b=b, r=r, c=c, s=s: e.matmul(psA[b][:], lhsT=wfm[:, c, r * 128:(r + 1) * 128],
                                                                 rhs=xT[s][:, c, :], start=(c == 0), stop=(c == 7)),
                     reads=[(tag, "xT", s), (tag, "wfm", c)], writes=[(tag, "psA", b)])
            for c in range(8):
                P.op("pe", lambda e, b=b, r=r, c=c, s=s: e.matmul(psB[b][:], lhsT=wfm[:, c, (ROPED + r) * 128:(ROPED + r + 1) * 128],
                                                                 rhs=xT[s][:, c, :], start=(c == 0), stop=(c == 7)),
                     reads=[(tag, "xT", s), (tag, "wfm", c)], writes=[(tag, "psB", b)])
            P.op("dve", lambda e, b=b, s=s: e.tensor_tensor(out=t1[b][:], in0=psA[b][:], in1=cosT[s][:], op=ALU.mult),
                 reads=[(tag, "psA", b), (tag, "cosT", s)], writes=[(tag, "t1", b)])
            P.op("dve", lambda e, b=b, s=s: e.tensor_tensor(out=t2[b][:], in0=psB[b][:], in1=sinT[s][:], op=ALU.mult),
                 reads=[(tag, "psB", b), (tag, "sinT", s)], writes=[(tag, "t2", b)])
            P.op("pool", lambda e, b=b, s=s, r=r: e.tensor_tensor(out=fmo[s][:, r, :], in0=t1[b][:], in1=t2[b][:], op=ALU.add),
                 reads=[(tag, "t1", b), (tag, "t2", b)], writes=[(tag, "fmo", s, r)])
        b = cc % 3
        cc += 1
        for c in range(8):
            P.op("pe", lambda e, b=b, c=c, s=s: e.matmul(psC[b][:], lhsT=wfm[:, c, 12 * 128:13 * 128],
                                                        rhs=xT[s][:, c, :], start=(c == 0), stop=(c == 7)),
                 reads=[(tag, "xT", s), (tag, "wfm", c)], writes=[(tag, "psC", b)])
        P.op("act", lambda e, b=b, s=s: e.copy(out=fmo[s][:, 6, :], in_=psC[b][:]),
             reads=[(tag, "psC", b)], writes=[(tag, "fmo", s, 6)])
        P.dma("pool", fmT_v[:, :, tk0:tk0 + 512], fmo[s][:], chan=(tag, "fmo", s),
              reads=[(tag, "fmo", s, r) for r in range(7)], writes=[(tag, "fmT", it)])
        for j in range(4):
            b = cc % 3
            cc += 1
            for c in range(8):
                P.op("pe", lambda e, b=b, c=c, s=s, j=j: e.matmul(psC[b][:], lhsT=xT[s][:, c, j * 128:(j + 1) * 128],
                                                                 rhs=wtm[:, c, :], start=(c == 0), stop=(c == 7)),
                     reads=[(tag, "xT", s), (tag, "wtm", c)], writes=[(tag, "psC", b)])
            P.op("act", lambda e, b=b, s=s, j=j: e.copy(out=tmo[s][:, j, :], in_=psC[b][:, 0:384]),
                 reads=[(tag, "psC", b)], writes=[(tag, "tmo", s, j)])
            P.op("act", lambda e, b=b, s=s, j=j: e.activation(out=sgo[s][:, j, :], in_=psC[b][:, 384:512], func=AF.Silu),
                 reads=[(tag, "psC", b)], writes=[(tag, "sgo", s, j)])
        P.dma("pool", vr_d[tk0:tk0 + 512, :].rearrange("(j p) c -> p j c", p=128), tmo[s][:], chan=(tag, "tmo", s),
              reads=[(tag, "tmo", s, j) for j in range(4)], writes=[(tag, "vr", it)])
        P.dma("pool", sg_d[tk0:tk0 + 512, :].rearrange("(j p) c -> p j c", p=128), sgo[s][:], chan=(tag, "sgo", s),
              reads=[(tag, "sgo", s, j) for j in range(4)], writes=[(tag, "sg", it)])


def build_A1(L):
    nc = bass.Bass("TRN2", target_bir_lowering=False)
    with ExitStack() as es:
        C = Ctx(nc, es)
        xT_d = C.din("xT", [1024, L], BF16)
        wfm_d = C.din("wfm", [1024, NFM * 128], F32)
        wtm_d = C.din("wtm", [1024, 512], F32)
        pos_d = C.din("pos", [L], I32)
        rc_d = C.din("rc", [128, 4], F32)
        fmT_d = C.dout("fmT", [7, 128, L], BF16)
        vr_d = C.dout("vr", [L, 384], BF16)
        sg_d = C.dout("sg", [L, 128], F32)
        stage_A1(C, xT_d, wfm_d, wtm_d, pos_d, rc_d, fmT_d, vr_d, sg_d, L)
        C.P.emit(es)
    return nc


def rc_const():
    invf, sgn = rope_consts()
    return np.concatenate([invf, sgn, -math.pi * sgn, np.full((128, 1), -math.pi, np.float32)], axis=1).astype(np.float32)


def stage_A2(C, fmT_d, vr_d, lamp_d, subg_d, lc_d, ydaT_d, ident, L, tag="a2", dbg=None):
    P = C.P
    NKB = L // 128
    NQB = L // 512
    lamp = C.sb([128, 4, 64], F32, "lamp")
    lc = C.sb([128, 2], F32, "lc")
    gcol = C.sb([128, 1], F32, "gcol")
    lprod = C.sb([128, 2, 64], F32, "lprod")
    lsum = C.sb([128, 2], F32, "lsum")
    lexp = C.sb([128, 2], F32, "lexp")
    neglam = C.sb([128, 1], F32, "neglam")
    epsb = C.sb([128, 1], F32, "epsb")
    ones = C.sb([128, 128], F32, "ones")
    P.dma("sp", lamp[:], lamp_d.rearrange("a d -> (a d)").partition_broadcast(128).rearrange("p (a d) -> p a d", a=4),
          chan=(tag, "c0"), writes=[(tag, "lamp")])
    P.dma("sp", lc[:], lc_d[:, :], chan=(tag, "c0"), writes=[(tag, "lc")])
    P.dma("sp", gcol[:], subg_d.rearrange("(p o) -> p o", o=1), chan=(tag, "c0"), writes=[(tag, "gcol")])
    P.op("dve", lambda e: e.tensor_tensor(out=lprod[:, 0, :], in0=lamp[:, 0, :], in1=lamp[:, 1, :], op=ALU.mult),
         reads=[(tag, "lamp")], writes=[(tag, "lprod")])
    P.op("dve", lambda e: e.tensor_tensor(out=lprod[:, 1, :], in0=lamp[:, 2, :], in1=lamp[:, 3, :], op=ALU.mult),
         reads=[(tag, "lamp")], writes=[(tag, "lprod")])
    P.op("dve", lambda e: e.tensor_reduce(out=lsum[:], in_=lprod[:], axis=AX.X, op=ALU.add),
         reads=[(tag, "lprod")], writes=[(tag, "lsum")])
    P.op("act", lambda e: e.activation(out=lexp[:], in_=lsum[:], func=AF.Exp), reads=[(tag, "lsum")], writes=[(tag, "lexp")])
    P.op("dve", lambda e: e.tensor_tensor(out=neglam[:], in0=lexp[:, 1:2], in1=lexp[:, 0:1], op=ALU.subtract),
         reads=[(tag, "lexp")], writes=[(tag, "neglam")])
    P.op("dve", lambda e: e.tensor_tensor(out=neglam[:], in0=neglam[:], in1=lc[:, 0:1], op=ALU.subtract),
         reads=[(tag, "neglam"), (tag, "lc")], writes=[(tag, "neglam")])
    P.op("dve", lambda e: e.tensor_tensor(out=gcol[:], in0=gcol[:], in1=lc[:, 1:2], op=ALU.mult),
         reads=[(tag, "gcol"), (tag, "lc")], writes=[(tag, "gcol")])
    P.op("dve", lambda e: e.memset(epsb[:], 1e-6), writes=[(tag, "epsb")])
    P.op("pool", lambda e: e.memset(ones[:], 1.0), writes=[(tag, "ones")])

    qT = [C.sb([128, L], BF16, "qT") for _ in range(2)]
    kT = [C.sb([128, L], BF16, "kT") for _ in range(2)]
    va = [C.sb([128, NKB, 128], BF16, "va") for _ in range(2)]
    yT = [C.sb([128, L], BF16, "yT") for _ in range(2)]
    NPT = 4
    PT = [C.sb([128, 512], BF16, "PT") for _ in range(NPT)]
    accS = [[C.sb([128, 512], F32, "accS") for _ in range(2)] for _ in range(2)]
    oacc = C.sb([128, 2, 512], F32, "oacc")
    rinv = C.sb([128, 2, 512], F32, "rinv")
    o = C.sb([128, 512], F32, "o")
    t2 = C.sb([128, 512], F32, "t2")
    sq = C.sb([128, 512], F32, "sq")
    rs = C.sb([128, 512], F32, "rs")
    S = [C.ps([128, 512], F32, "S") for _ in range(3)]
    accO = [C.ps([128, 512], F32, "accO") for _ in range(2)]
    Rps = [C.ps([128, 512], F32, "Rps") for _ in range(2)]

    for h in range(2):
        P.dma("sp", qT[h][:], fmT_d[h, :, :], chan=(tag, "ld", h), writes=[(tag, "qT", h)])
        P.dma("sp", kT[h][:], fmT_d[2 + h, :, :], chan=(tag, "ld", h), writes=[(tag, "kT", h)])
        P.dma("sp", va[h][:], vr_d[:, h * 128:(h + 1) * 128].rearrange("(kb p) c -> p kb c", p=128),
              chan=(tag, "ld", h), writes=[(tag, "va", h)])
    step = 0
    blk = 0
    for h in range(2):
        for qb in range(NQB):
            q0 = qb * 512
            nst = NKB * 2
            par = blk % 2
            blk += 1

            def qk(i, h=h, q0=q0, step=step):
                kb, m = i // 2, i % 2
                sl = (step + i) % 3
                P.op("pe", lambda e: e.matmul(S[sl][:], lhsT=kT[h][m * 64:(m + 1) * 64, kb * 128:(kb + 1) * 128],
                                               rhs=qT[h][m * 64:(m + 1) * 64, q0:q0 + 512], start=True, stop=True),
                     reads=[(tag, "kT", h), (tag, "qT", h)], writes=[(tag, "S", sl)])

            def ex(i, step=step):
                sl = (step + i) % 3
                pl = (step + i) % NPT
                P.op("act", lambda e: e.activation(out=PT[pl][:], in_=S[sl][:], func=AF.Exp, scale=0.125),
                     writes=[(tag, "PT", pl), (tag, "S", sl)])

            def av(i, h=h, step=step, par=par):
                kb, m = i // 2, i % 2
                pl = (step + i) % NPT
                P.op("pe", lambda e: e.matmul(accO[m][:], lhsT=va[h][:, kb, :], rhs=PT[pl][:], start=(kb == 0), stop=(kb == NKB - 1)),
                     reads=[(tag, "PT", pl), (tag, "va", h)], writes=[(tag, "accO", m)])
                if kb == 0:
                    P.op("dve", lambda e: e.tensor_copy(out=accS[par][m][:], in_=PT[pl][:]), reads=[(tag, "PT", pl)], writes=[(tag, "accS", par, m)])
                else:
                    P.op("dve", lambda e: e.tensor_tensor(out=accS[par][m][:], in0=accS[par][m][:], in1=PT[pl][:], op=ALU.add),
                         reads=[(tag, "PT", pl), (tag, "accS", par, m)], writes=[(tag, "accS", par, m)])

            qk(0); ex(0)
            qk(1); ex(1)
            for i in range(nst):
                if i + 2 < nst:
                    qk(i + 2); ex(i + 2)
                av(i)
            step += nst

            def post(h=h, q0=q0, par=par):
                for m in range(2):
                    P.op("pe", lambda e, m=m: e.matmul(Rps[m][:], lhsT=ones[:], rhs=accS[par][m][:], start=True, stop=True),
                         reads=[(tag, "ones"), (tag, "accS", par, m)], writes=[(tag, "Rps", m)])
                P.op("act", lambda e: e.copy(out=oacc[:, 0, :], in_=accO[0][:]), writes=[(tag, "oacc", 0), (tag, "accO", 0)])
                P.op("dve", lambda e: e.tensor_copy(out=oacc[:, 1, :], in_=accO[1][:]), writes=[(tag, "oacc", 1), (tag, "accO", 1)])
                for m in range(2):
                    P.op("dve", lambda e, m=m: e.reciprocal(out=rinv[:, m, :], in_=Rps[m][:]), writes=[(tag, "rinv", m), (tag, "Rps", m)])
                P.op("dve", lambda e: e.tensor_tensor(out=o[:], in0=oacc[:, 0, :], in1=rinv[:, 0, :], op=ALU.mult),
                     reads=[(tag, "oacc", 0), (tag, "rinv", 0)], writes=[(tag, "o")])
                P.op("pool", lambda e: e.tensor_tensor(out=t2[:], in0=oacc[:, 1, :], in1=rinv[:, 1, :], op=ALU.mult),
                     reads=[(tag, "oacc", 1), (tag, "rinv", 1)], writes=[(tag, "t2")])
                P.op("dve", lambda e: e.scalar_tensor_tensor(out=o[:], in0=t2[:], scalar=neglam[:, 0:1], in1=o[:], op0=ALU.mult, op1=ALU.add),
                     reads=[(tag, "t2"), (tag, "o"), (tag, "neglam")], writes=[(tag, "o")])
                P.op("act", lambda e: e.activation(out=sq[:], in_=o[:], func=AF.Square), reads=[(tag, "o")], writes=[(tag, "sq")])
                P.op("pe", lambda e: e.matmul(Rps[0][:], lhsT=ones[:], rhs=sq[:], start=True, stop=True),
                     reads=[(tag, "ones"), (tag, "sq")], writes=[(tag, "Rps", 0)])
                P.op("act", lambda e: e.activation(out=rs[:], in_=Rps[0][:], func=AF.Sqrt, bias=epsb[:, 0:1], scale=1.0 / 128.0),
                     reads=[(tag, "epsb")], writes=[(tag, "rs"), (tag, "Rps", 0)])
                P.op("dve", lambda e: e.reciprocal(out=rs[:], in_=rs[:]), reads=[(tag, "rs")], writes=[(tag, "rs")])
                P.op("dve", lambda e: e.scalar_tensor_tensor(out=yT[h][:, q0:q0 + 512], in0=o[:], scalar=gcol[:, 0:1], in1=rs[:],
                                                              op0=ALU.mult, op1=ALU.mult),
                     reads=[(tag, "o"), (tag, "rs"), (tag, "gcol")], writes=[(tag, "yT", h)])
            post()
        P.dma("sp", ydaT_d[h, :, :], yT[h][:], chan=(tag, "yo", h), reads=[(tag, "yT", h)], writes=[(tag, "ydaT", h)])


def build_A2(L, debug=False):
    nc = bass.Bass("TRN2", target_bir_lowering=False)
    with ExitStack() as es:
        C = Ctx(nc, es)
        fmT_d = C.din("fmT", [7, 128, L], BF16)
        vr_d = C.din("vr", [L, 384], BF16)
        lamp_d = C.din("lamp", [4, 64], F32)
        subg_d = C.din("subg", [128], F32)
        lc_d = C.din("lc", [128, 2], F32)
        id_d = C.din("ident", [128, 128], BF16)
        ydaT_d = C.dout("ydaT", [2, 128, L], BF16)
        ident = C.sb([128, 128], BF16, "ident")
        C.P.dma("sp", ident[:], id_d[:, :], chan="ident", writes=["ident"])
        dbg = None
        stage_A2(C, fmT_d, vr_d, lamp_d, subg_d, lc_d, ydaT_d, ident, L, dbg=dbg)
        C.P.emit(es)
    return nc


RT_W = 256 + 128 + 128 + 4 + 1


def ret_consts(h):
    idx = np.arange(128, dtype=np.float64)
    out = np.zeros((128, RT_W), np.float64)
    for hh in range(2):
        lg = math.log(1.0 - 2.0 ** (-5.0 - (2 * h + hh)))
        out[:, hh * 128:(hh + 1) * 128] = np.exp(lg * np.abs(idx[None, :] - idx[:, None])) / 8.0
        out[:, 256 + hh * 64:256 + (hh + 1) * 64] = (np.exp(lg * (127 - idx)) / 8.0)[:, None]
        out[:, 384 + hh * 64:384 + (hh + 1) * 64] = (np.exp(lg * idx) / 8.0)[:, None]
        out[:, 512 + 2 * hh] = np.exp(lg * (idx + 1))
        out[:, 512 + 2 * hh + 1] = np.exp(lg * (128 - idx))
        out[hh * 64:(hh + 1) * 64, 516] = math.exp(lg * 128)
    return out.astype(np.float32)


def stage_A3(C, fmT_d, vr_d, sg_d, rt_d, gng_d, gnb_d, yretT_d, ident, L, tag="a3"):
    P = C.P
    NCH = L // 128
    rt = C.sb([128, RT_W], F32, "rt")
    gng = C.sb([128, 2, 64], F32, "gng")
    gnb = C.sb([128, 2, 64], F32, "gnb")
    epsb = C.sb([128, 1], F32, "epsb")
    P.dma("sp", rt[:], rt_d[:, :], chan=(tag, "c0"), writes=[(tag, "rt")])
    for hh in range(2):
        P.dma("sp", gng[:, hh, :], gng_d.partition_broadcast(128), chan=(tag, "c0"), writes=[(tag, "gng")])
        P.dma("sp", gnb[:, hh, :], gnb_d.partition_broadcast(128), chan=(tag, "c0"), writes=[(tag, "gnb")])
    P.op("dve", lambda e: e.memset(epsb[:], LN_EPS), writes=[(tag, "epsb")])
    Dm = rt[:, 0:256].rearrange("p (a n) -> p a n", a=2)
    decF = rt[:, 256:384]
    decB = rt[:, 384:512]
    gC = rt[:, 516:517]

    rqT = C.sb([128, L], BF16, "rqT")
    rkT = C.sb([128, L], BF16, "rkT")
    rv = C.sb([128, NCH, 128], BF16, "rv")
    sg = C.sb([128, NCH, 128], F32, "sg")
    yT = C.sb([128, L], BF16, "yT")
    Gs = C.sb([128, NCH, 64], BF16, "Gs")
    G = C.sb([128, 64], F32, "G")
    F = C.sb([128, 64], F32, "F")
    Fbf = [C.sb([128, 64], BF16, "Fbf") for _ in range(2)]
    kd = [C.sb([128, 128], BF16, "kd") for _ in range(2)]
    Sm = [C.sb([128, 2, 128], BF16, "Sm") for _ in range(2)]
    t = [C.sb([128, 2, 64], F32, "t") for _ in range(2)]
    st = C.sb([128, 2, 6], F32, "st")
    mv = C.sb([128, 2, 2], F32, "mv")
    rstd = C.sb([128, 2], F32, "rstd")
    ybf = [C.sb([128, 128], BF16, "ybf") for _ in range(2)]
    pk = [C.ps([128, 128], BF16, "pk") for _ in range(2)]
    pkv = [C.ps([128, 128], F32, "pkv") for _ in range(2)]
    pS = [C.ps([128, 128], F32, "pS") for _ in range(2)]
    po = [C.ps([128, 3, 64], F32, "po") for _ in range(2)]
    P.dma("sp", rqT[:], fmT_d[4, :, :], chan=(tag, "ld"), writes=[(tag, "rqT")])
    P.dma("sp", rkT[:], fmT_d[5, :, :], chan=(tag, "ld"), writes=[(tag, "rkT")])
    P.dma("sp", rv[:], vr_d[:, 256:384].rearrange("(c p) n -> p c n", p=128), chan=(tag, "ld"), writes=[(tag, "rv")])
    P.dma("sp", sg[:], sg_d.rearrange("(c p) n -> p c n", p=128), chan=(tag, "ld"), writes=[(tag, "sg")])
    P.op("dve", lambda e: e.memset(G[:], 0.0), writes=[(tag, "G")])
    P.op("dve", lambda e: e.memset(F[:], 0.0), writes=[(tag, "F")])

    def kv_step(c, dec, state, skey, i):
        s = i % 2
        P.op("pe", lambda e: e.transpose(out=pk[s][:], in_=rkT[:, c * 128:(c + 1) * 128], identity=ident[:]),
             reads=[(tag, "rkT"), "ident"], writes=[(tag, "pk", s)])
        P.op("dve", lambda e: e.tensor_tensor(out=kd[s][:], in0=pk[s][:], in1=dec, op=ALU.mult),
             reads=[(tag, "rt")], writes=[(tag, "kd", s), (tag, "pk", s)])
        P.op("pe", lambda e: e.matmul(pkv[s][:], lhsT=kd[s][:], rhs=rv[:, c, :], start=True, stop=True),
             reads=[(tag, "kd", s), (tag, "rv")], writes=[(tag, "pkv", s)])
        for hh in range(2):
            hs = slice(hh * 64, (hh + 1) * 64)
            P.op("dve", lambda e, hs=hs: e.scalar_tensor_tensor(out=state[hs, :], in0=state[hs, :], scalar=gC[hs, :], in1=pkv[s][hs, hs],
                                                                 op0=ALU.mult, op1=ALU.add),
                 reads=[skey, (tag, "rt")], writes=[skey, (tag, "pkv", s)])

    it = 0
    for c in range(NCH - 1, -1, -1):
        P.op("act", lambda e, c=c: e.copy(out=Gs[:, c, :], in_=G[:]), reads=[(tag, "G")], writes=[(tag, "Gs", c)])
        if c > 0:
            kv_step(c, decB, G, (tag, "G"), it)
            it += 1
    def chunk2(c, it):
        s = c % 2
        cs = slice(c * 128, (c + 1) * 128)
        P.op("act", lambda e, s=s: e.copy(out=Fbf[s][:], in_=F[:]), reads=[(tag, "F")], writes=[(tag, "Fbf", s)])
        for hh in range(2):
            hs = slice(hh * 64, (hh + 1) * 64)
            P.op("pe", lambda e, hh=hh, hs=hs: e.matmul(pS[hh][:], lhsT=rkT[hs, cs], rhs=rqT[hs, cs], start=True, stop=True),
                 reads=[(tag, "rkT"), (tag, "rqT")], writes=[(tag, "pS", hh)])
            P.op("dve", lambda e, s=s, hh=hh: e.tensor_tensor(out=Sm[s][:, hh, :], in0=pS[hh][:], in1=Dm[:, hh, :], op=ALU.mult),
                 reads=[(tag, "rt")], writes=[(tag, "Sm", s, hh), (tag, "pS", hh)])
        for hh in range(2):
            hs = slice(hh * 64, (hh + 1) * 64)
            P.op("pe", lambda e, hh=hh, hs=hs: e.matmul(po[hh][:, 0, :], lhsT=Sm[s][:, hh, :], rhs=rv[:, c, hs], start=True, stop=False),
                 reads=[(tag, "Sm", s, hh), (tag, "rv")], writes=[(tag, "po", hh)])
            P.op("pe", lambda e, hh=hh, hs=hs: e.matmul(po[hh][:, 1, :], lhsT=rqT[hs, cs], rhs=Fbf[s][hs, :], start=False, stop=False),
                 reads=[(tag, "rqT"), (tag, "Fbf", s)], writes=[(tag, "po", hh)])
            P.op("pe", lambda e, hh=hh, hs=hs, c=c: e.matmul(po[hh][:, 2, :], lhsT=rqT[hs, cs], rhs=Gs[hs, c, :], start=False, stop=True),
                 reads=[(tag, "rqT"), (tag, "Gs", c)], writes=[(tag, "po", hh)])
        if c < NCH - 1:
            kv_step(c, decF, F, (tag, "F"), it)
        for hh in range(2):
            P.op("dve", lambda e, s=s, hh=hh: e.tensor_copy(out=t[s][:, hh, :], in_=po[hh][:, 0, :]),
                 writes=[(tag, "t", s), (tag, "po", hh)])
            for j in (1, 2):
                P.op("dve", lambda e, s=s, hh=hh, j=j: e.scalar_tensor_tensor(
                    out=t[s][:, hh, :], in0=po[hh][:, j, :], scalar=rt[:, 512 + 2 * hh + j - 1:512 + 2 * hh + j],
                    in1=t[s][:, hh, :], op0=ALU.mult, op1=ALU.add),
                    reads=[(tag, "t", s), (tag, "rt")], writes=[(tag, "t", s), (tag, "po", hh)])
        for hh in range(2):
            P.op("dve", lambda e, s=s, hh=hh: e.bn_stats(out=st[:, hh, :], in_=t[s][:, hh, :]), reads=[(tag, "t", s)], writes=[(tag, "st")])
            P.op("dve", lambda e, hh=hh: e.bn_aggr(out=mv[:, hh, :], in_=st[:, hh, :]), reads=[(tag, "st")], writes=[(tag, "mv")])
        P.op("act", lambda e: e.activation(out=rstd[:], in_=mv[:, :, 1], func=AF.Sqrt, bias=epsb[:, 0:1], scale=1.0),
             reads=[(tag, "mv"), (tag, "epsb")], writes=[(tag, "rstd")])
        P.op("dve", lambda e: e.reciprocal(out=rstd[:], in_=rstd[:]), reads=[(tag, "rstd")], writes=[(tag, "rstd")])
        for hh in range(2):
            P.op("dve", lambda e, s=s, hh=hh: e.tensor_scalar(out=t[s][:, hh, :], in0=t[s][:, hh, :], scalar1=mv[:, hh, 0:1],
                                                              scalar2=rstd[:, hh:hh + 1], op0=ALU.subtract, op1=ALU.mult),
                 reads=[(tag, "t", s), (tag, "mv"), (tag, "rstd")], writes=[(tag, "t", s)])
        P.op("pool", lambda e, s=s: e.tensor_tensor(out=t[s][:], in0=t[s][:], in1=gng[:], op=ALU.mult),
             reads=[(tag, "t", s), (tag, "gng")], writes=[(tag, "t", s)])
        P.op("pool", lambda e, s=s: e.tensor_tensor(out=t[s][:], in0=t[s][:], in1=gnb[:], op=ALU.add),
             reads=[(tag, "t", s), (tag, "gnb")], writes=[(tag, "t", s)])
        P.op("pool", lambda e, s=s, c=c: e.tensor_tensor(out=ybf[s][:], in0=t[s][:].rearrange("p a b -> p (a b)"), in1=sg[:, c, :], op=ALU.mult),
             reads=[(tag, "t", s), (tag, "sg")], writes=[(tag, "ybf", s)])
        P.op("pe", lambda e, s=s: e.transpose(out=pk[s][:], in_=ybf[s][:], identity=ident[:]),
             reads=[(tag, "ybf", s), "ident"], writes=[(tag, "pk", s)])
        P.op("act", lambda e, s=s, cs=cs: e.copy(out=yT[:, cs], in_=pk[s][:]), writes=[(tag, "yT"), (tag, "pk", s)])
    for c in range(NCH):
        chunk2(c, it)
        it += 1
    P.dma("pool", yretT_d[:, :], yT[:], chan=(tag, "yo"), reads=[(tag, "yT")], writes=[(tag, "yretT")])


def build_A3(L):
    nc = bass.Bass("TRN2", target_bir_lowering=False)
    with ExitStack() as es:
        C = Ctx(nc, es)
        fmT_d = C.din("fmT", [7, 128, L], BF16)
        vr_d = C.din("vr", [L, 384], BF16)
        sg_d = C.din("sg", [L, 128], F32)
        rt_d = C.din("rt", [128, RT_W], F32)
        gng_d = C.din("gng", [64], F32)
        gnb_d = C.din("gnb", [64], F32)
        id_d = C.din("ident", [128, 128], BF16)
        yretT_d = C.dout("yretT", [128, L], BF16)
        ident = C.sb([128, 128], BF16, "ident")
        C.P.dma("sp", ident[:], id_d[:, :], chan="ident", writes=["ident"])
        stage_A3(C, fmT_d, vr_d, sg_d, rt_d, gng_d, gnb_d, yretT_d, ident, L)
        C.P.emit(es)
    return nc


S5C_W = 1 + 1 + 1 + 8 + 512 + 512
GELU_K = 2.0 * math.sqrt(2.0 / math.pi)


def s5_consts():
    p = np.arange(128)
    out = np.zeros((128, S5C_W), np.float32)
    out[:, 0] = (p < 64)
    out[:, 1] = (p >= 64)
    out[:, 2] = np.where(p < 64, 1.0, -1.0)
    for g in range(8):
        out[:, 3 + g] = (p // 16 == g)
    out[:, 11:11 + 512] = np.arange(512)[None, :]
    out[:, 11 + 512:11 + 1024] = (511 - np.arange(512))[None, :]
    return out


def s5_layout(inp, i, h):
    gs = slice(8 * h, 8 * h + 8)

    def pg(a):
        a = np.asarray(a[i][:, gs, :]).transpose(2, 0, 1).reshape(64, 16)
        return np.ascontiguousarray(np.concatenate([a, a], axis=0)).astype(np.float32)
    ldt = np.asarray(inp['s5_log_dt'][i][:, gs]).reshape(1, 16)
    ldt = np.ascontiguousarray(np.broadcast_to(ldt, (128, 16))).astype(np.float32)

    def pb(a):
        a = np.asarray(a[i][:, gs]).transpose(2, 0, 1, 3).reshape(64, 16, 16)
        return np.ascontiguousarray(np.concatenate([a, a], axis=0)).astype(np.float32)
    cre = np.asarray(inp['s5_C_re'][i][:, gs])
    cim = np.asarray(inp['s5_C_im'][i][:, gs])
    cc = np.stack([cre, cim], axis=2)
    cc = cc.transpose(1, 3, 0, 2, 4).reshape(128, 2, 128)
    dv = np.asarray(inp['s5_D'][i][128 * h:128 * h + 128]).reshape(128, 1)
    sp = np.concatenate([pg(inp['s5_A_re']), pg(inp['s5_A_im']), ldt, dv.astype(np.float32)], axis=1)
    return {"s5p": np.ascontiguousarray(sp), "s5bre": pb(inp['s5_B_re']), "s5bim": pb(inp['s5_B_im']),
            "s5cc": np.ascontiguousarray(cc).astype(np.float32)}


def emit_gelu(P, tag, y, ykey, tmp, out, okey):
    kt = (tag, "gelu_tmp")
    P.op("dve", lambda e: e.tensor_tensor(out=tmp, in0=y, in1=y, op=ALU.mult), reads=[ykey], writes=[kt])
    P.op("dve", lambda e: e.tensor_scalar(out=tmp, in0=tmp, scalar1=0.044715, scalar2=1.0, op0=ALU.mult, op1=ALU.add),
         reads=[kt], writes=[kt])
    P.op("dve", lambda e: e.tensor_tensor(out=tmp, in0=tmp, in1=y, op=ALU.mult), reads=[kt, ykey], writes=[kt])
    P.op("act", lambda e: e.activation(out=tmp, in_=tmp, func=AF.Sigmoid, scale=GELU_K), reads=[kt], writes=[kt])
    P.op("dve", lambda e: e.tensor_tensor(out=out, in0=tmp, in1=y, op=ALU.mult), reads=[kt, ykey], writes=[okey])


def stage_A4(C, fmT_d, s5p_d, bre_d, bim_d, cc_d, sc_d, identf_d, ys5T_d, L, tag="a4", scan_pool=True):
    P = C.P
    NT = L // 512
    sc = C.sb([128, S5C_W], F32, "sc")
    sp = C.sb([128, 49], F32, "sp")
    bre = C.sb([128, 16, 16], F32, "bre")
    bim = C.sb([128, 16, 16], F32, "bim")
    cc = C.sb([128, 2, 128], F32, "cc")
    idf = C.sb([128, 128], F32, "idf")
    for tl, src_, k in ((sc, sc_d, "sc"), (sp, s5p_d, "sp"), (bre, bre_d, "bre"), (bim, bim_d, "bim"), (cc, cc_d, "cc"), (idf, identf_d, "idf")):
        P.dma("sp", tl[:], src_, chan=(tag, "c0"), writes=[(tag, k)])
    mtop, mbot, sgnC = sc[:, 0:1], sc[:, 1:2], sc[:, 2:3]
    tau = [sc[:, 11:11 + 512], sc[:, 11 + 512:11 + 1024]]
    ar, ai, ldt, dv = sp[:, 0:16], sp[:, 16:32], sp[:, 32:48], sp[:, 48:49]

    sm = {}
    for nm in ("step", "zr", "th", "pp", "em1", "e", "sarg", "carg", "c1", "s1", "sh", "cm1", "nr", "ni", "den", "t0", "t1",
               "cre", "cim", "s1A", "s2A", "s1B", "c512", "s512", "a512", "rrf"):
        sm[nm] = C.sb([128, 16], F32, "s5_" + nm)
    rri = C.sb([128, 16], I32, "s5_rri")

    def S(nm):
        return sm[nm][:]

    def k(nm):
        return (tag, "sm", nm)

    def dv_op(fn, reads, writes):
        P.op("dve", fn, reads=[k(r) if isinstance(r, str) else r for r in reads], writes=[k(w) for w in writes])
    kp = (tag, "sp")
    dv_op(lambda e: e.tensor_copy(out=S("t0"), in_=ldt), [kp], ["t0"])
    P.op("act", lambda e: e.activation(out=S("step"), in_=S("t0"), func=AF.Exp), reads=[k("t0")], writes=[k("step")])
    dv_op(lambda e: e.tensor_tensor(out=S("zr"), in0=S("step"), in1=ar, op=ALU.mult), ["step", kp], ["zr"])
    dv_op(lambda e: e.tensor_tensor(out=S("th"), in0=S("step"), in1=ai, op=ALU.mult), ["step", kp], ["th"])
    dv_op(lambda e: e.tensor_scalar(out=S("pp"), in0=S("zr"), scalar1=1.0 / 6.0, scalar2=1.0, op0=ALU.mult, op1=ALU.add), ["zr"], ["pp"])
    for cdiv in (5.0, 4.0, 3.0, 2.0):
        dv_op(lambda e, cdiv=cdiv: e.scalar_tensor_tensor(out=S("pp"), in0=S("zr"), scalar=1.0 / cdiv, in1=S("pp"), op0=ALU.mult, op1=ALU.mult),
              ["zr", "pp"], ["pp"])
        dv_op(lambda e: e.tensor_scalar(out=S("pp"), in0=S("pp"), scalar1=1.0, scalar2=None, op0=ALU.add), ["pp"], ["pp"])
    dv_op(lambda e: e.tensor_tensor(out=S("em1"), in0=S("zr"), in1=S("pp"), op=ALU.mult), ["zr", "pp"], ["em1"])
    dv_op(lambda e: e.tensor_scalar(out=S("e"), in0=S("em1"), scalar1=1.0, scalar2=None, op0=ALU.add), ["em1"], ["e"])
    sincos(P, "dve", (tag, "sc1"), sm["th"], k("th"), sm["sarg"], sm["carg"], rri, sm["rrf"])
    ks, kc = ((tag, "sc1"), "sarg"), ((tag, "sc1"), "carg")
    P.op("act", lambda e: e.activation(out=S("s1"), in_=S("sarg"), func=AF.Sin), reads=[ks], writes=[k("s1")])
    P.op("act", lambda e: e.activation(out=S("sh"), in_=S("sarg"), func=AF.Sin, scale=0.5), reads=[ks], writes=[k("sh")])
    dv_op(lambda e: e.scalar_tensor_tensor(out=S("cm1"), in0=S("sh"), scalar=-2.0, in1=S("sh"), op0=ALU.mult, op1=ALU.mult), ["sh"], ["cm1"])
    dv_op(lambda e: e.tensor_scalar(out=S("c1"), in0=S("cm1"), scalar1=1.0, scalar2=None, op0=ALU.add), ["cm1"], ["c1"])
    dv_op(lambda e: e.tensor_tensor(out=S("nr"), in0=S("em1"), in1=S("c1"), op=ALU.mult), ["em1", "c1"], ["nr"])
    dv_op(lambda e: e.tensor_tensor(out=S("nr"), in0=S("nr"), in1=S("cm1"), op=ALU.add), ["nr", "cm1"], ["nr"])
    dv_op(lambda e: e.tensor_tensor(out=S("ni"), in0=S("e"), in1=S("s1"), op=ALU.mult), ["e", "s1"], ["ni"])
    dv_op(lambda e: e.tensor_tensor(out=S("den"), in0=ar, in1=ar, op=ALU.mult), [kp], ["den"])
    dv_op(lambda e: e.tensor_tensor(out=S("t0"), in0=ai, in1=ai, op=ALU.mult), [kp], ["t0"])
    dv_op(lambda e: e.tensor_tensor(out=S("den"), in0=S("den"), in1=S("t0"), op=ALU.add), ["den", "t0"], ["den"])
    dv_op(lambda e: e.reciprocal(out=S("den"), in_=S("den")), ["den"], ["den"])
    dv_op(lambda e: e.tensor_tensor(out=S("t0"), in0=S("nr"), in1=ar, op=ALU.mult), ["nr", kp], ["t0"])
    dv_op(lambda e: e.tensor_tensor(out=S("t1"), in0=S("ni"), in1=ai, op=ALU.mult), ["ni", kp], ["t1"])
    dv_op(lambda e: e.tensor_tensor(out=S("t0"), in0=S("t0"), in1=S("t1"), op=ALU.add), ["t0", "t1"], ["t0"])
    dv_op(lambda e: e.tensor_tensor(out=S("cre"), in0=S("t0"), in1=S("den"), op=ALU.mult), ["t0", "den"], ["cre"])
    dv_op(lambda e: e.tensor_tensor(out=S("t0"), in0=S("ni"), in1=ar, op=ALU.mult), ["ni", kp], ["t0"])
    dv_op(lambda e: e.tensor_tensor(out=S("t1"), in0=S("nr"), in1=ai, op=ALU.mult), ["nr", kp], ["t1"])
    dv_op(lambda e: e.tensor_tensor(out=S("t0"), in0=S("t0"), in1=S("t1"), op=ALU.subtract), ["t0", "t1"], ["t0"])
    dv_op(lambda e: e.tensor_tensor(out=S("cim"), in0=S("t0"), in1=S("den"), op=ALU.mult), ["t0", "den"], ["cim"])
    ksc = (tag, "sc")
    dv_op(lambda e: e.tensor_scalar(out=S("s1A"), in0=S("cre"), scalar1=mtop, scalar2=None, op0=ALU.mult), ["cre", ksc], ["s1A"])
    dv_op(lambda e: e.scalar_tensor_tensor(out=S("s1A"), in0=S("cim"), scalar=mbot, in1=S("s1A"), op0=ALU.mult, op1=ALU.add), ["cim", "s1A", ksc], ["s1A"])
    dv_op(lambda e: e.tensor_scalar(out=S("s2A"), in0=S("cre"), scalar1=mbot, scalar2=None, op0=ALU.mult), ["cre", ksc], ["s2A"])
    dv_op(lambda e: e.tensor_scalar(out=S("t0"), in0=S("cim"), scalar1=mtop, scalar2=None, op0=ALU.mult), ["cim", ksc], ["t0"])
    dv_op(lambda e: e.tensor_tensor(out=S("s2A"), in0=S("s2A"), in1=S("t0"), op=ALU.subtract), ["s2A", "t0"], ["s2A"])
    dv_op(lambda e: e.tensor_scalar(out=S("s1B"), in0=S("s1A"), scalar1=sgnC, scalar2=None, op0=ALU.mult), ["s1A", ksc], ["s1B"])
    dv_op(lambda e: e.tensor_scalar(out=S("s1B"), in0=S("cim"), scalar1=mtop, scalar2=None, op0=ALU.mult), ["cim", ksc], ["s1B"])
    dv_op(lambda e: e.tensor_scalar(out=S("t0"), in0=S("cre"), scalar1=mbot, scalar2=None, op0=ALU.mult), ["cre", ksc], ["t0"])
    dv_op(lambda e: e.tensor_tensor(out=S("s1B"), in0=S("s1B"), in1=S("t0"), op=ALU.subtract), ["s1B", "t0"], ["s1B"])
    dv_op(lambda e: e.tensor_scalar(out=S("a512"), in0=S("th"), scalar1=512.0, scalar2=None, op0=ALU.mult), ["th"], ["a512"])
    sincos(P, "dve", (tag, "sc2"), sm["a512"], k("a512"), sm["sarg"], sm["carg"], rri, sm["rrf"])
    ks2, kc2 = ((tag, "sc2"), "sarg"), ((tag, "sc2"), "carg")
    P.op("act", lambda e: e.activation(out=S("s512"), in_=S("sarg"), func=AF.Sin), reads=[ks2], writes=[k("s512")])
    P.op("act", lambda e: e.activation(out=S("c512"), in_=S("carg"), func=AF.Sin), reads=[kc2], writes=[k("c512")])

    BA = C.sb([128, 16, 128], BF16, "BA")
    BB = C.sb([128, 16, 128], BF16, "BB")
    CP = C.sb([128, 16, 128], BF16, "CP")
    Dg = C.sb([128, 128], BF16, "Dg")
    xa = C.sb([128, 8, 16], F32, "xa")
    xb = C.sb([128, 8, 16], F32, "xb")
    cfull = C.sb([128, 128], F32, "cfull")
    pset = C.ps([128, 128], F32, "pset")
    P.op("pool", lambda e: e.memset(CP[:], 0.0), writes=[(tag, "CP")])
    P.op("dve", lambda e: e.tensor_scalar(out=Dg[:], in0=idf[:], scalar1=dv, scalar2=None, op0=ALU.mult),
         reads=[(tag, "idf"), kp], writes=[(tag, "Dg")])
    for d in range(2):
        for var, (sa, sb_), dst in ((0, ("s1A", "s2A"), BA), (1, ("s1B", "s1A"), BB)):
            def bc(nm, d=d):
                return sm[nm][:, d * 8:(d + 1) * 8].unsqueeze(2).to_broadcast([128, 8, 16])
            P.op("dve", lambda e, d=d, sa=sa, bc=bc: e.tensor_tensor(out=xa[:], in0=bre[:, d * 8:(d + 1) * 8, :], in1=bc(sa), op=ALU.mult),
                 reads=[(tag, "bre"), k(sa)], writes=[(tag, "xa")])
            P.op("dve", lambda e, d=d, sb_=sb_, bc=bc: e.tensor_tensor(out=xb[:], in0=bim[:, d * 8:(d + 1) * 8, :], in1=bc(sb_), op=ALU.mult),
                 reads=[(tag, "bim"), k(sb_)], writes=[(tag, "xb")])
            P.op("dve", lambda e: e.tensor_tensor(out=xa[:], in0=xa[:], in1=xb[:], op=ALU.add),
                 reads=[(tag, "xa"), (tag, "xb")], writes=[(tag, "xa")])
            P.op("pe", lambda e: e.transpose(out=pset[:], in_=xa[:].rearrange("p a b -> p (a b)"), identity=idf[:]),
                 reads=[(tag, "xa"), (tag, "idf")], writes=[(tag, "pset")])
            for g in range(8):
                P.op("dve", lambda e, g=g, d=d, dst=dst: e.tensor_scalar(out=dst[:, d * 8 + g, :], in0=pset[:], scalar1=sc[:, 3 + g:4 + g],
                                                                      scalar2=None, op0=ALU.mult),
                     reads=[ksc], writes=[(tag, "Btab"), (tag, "pset")])
        P.op("pe", lambda e, d=d: e.transpose(out=pset[:], in_=cc[:, d, :], identity=idf[:]),
             reads=[(tag, "cc"), (tag, "idf")], writes=[(tag, "pset")])
        P.op("dve", lambda e: e.tensor_copy(out=cfull[:], in_=pset[:]), writes=[(tag, "cfull"), (tag, "pset")])
        for g in range(8):
            P.op("dve", lambda e, g=g, d=d: e.tensor_scalar(out=CP[:, d * 8 + g, 16 * g:16 * g + 16], in0=cfull[:, 16 * g:16 * g + 16],
                                                            scalar1=sgnC, scalar2=None, op0=ALU.mult),
                 reads=[(tag, "cfull"), ksc], writes=[(tag, "CP")])

    cosT = C.sb([128, 16, 512], F32, "cosT")
    sinT = C.sb([128, 16, 512], F32, "sinT")
    ang = C.sb([128, 512], F32, "ang")
    rsa = C.sb([128, 512], F32, "rsa")
    rca = C.sb([128, 512], F32, "rca")
    rr_i = C.sb([128, 512], I32, "rr_i")
    rr_f = C.sb([128, 512], F32, "rr_f")
    for gd in range(16):
        d = gd // 8
        P.op("dve", lambda e, gd=gd, d=d: e.tensor_scalar(out=ang[:], in0=tau[d], scalar1=sm["th"][:, gd:gd + 1], scalar2=None, op0=ALU.mult),
             reads=[ksc, k("th")], writes=[(tag, "ang")])
        sincos(P, "dve", (tag, "rt"), ang, (tag, "ang"), rsa, rca, rr_i, rr_f)
        P.op("act", lambda e, gd=gd: e.activation(out=sinT[:, gd, :], in_=rsa[:], func=AF.Sin), reads=[((tag, "rt"), "sarg")], writes=[(tag, "sinT", gd)])
        P.op("act", lambda e, gd=gd: e.activation(out=cosT[:, gd, :], in_=rca[:], func=AF.Sin), reads=[((tag, "rt"), "carg")], writes=[(tag, "cosT", gd)])

    uT = C.sb([128, L], BF16, "uT")
    yb = C.sb([128, L], F32, "yb")
    P.dma("sp", uT[:], fmT_d[6, :, :], chan=(tag, "ld"), writes=[(tag, "uT")])
    cA = C.sb([128, 16], F32, "cA")
    cB = C.sb([128, 16], F32, "cB")
    ctmp = C.sb([128, 4], F32, "ctmp")
    P.op("dve", lambda e: e.memset(cA[:], 0.0), writes=[(tag, "cA")])
    P.op("dve", lambda e: e.memset(cB[:], 0.0), writes=[(tag, "cB")])
    NB = 2
    bA = [C.sb([128, 512], F32, "bA") for _ in range(NB)]
    bB = [C.sb([128, 512], F32, "bB") for _ in range(NB)]
    w1 = [C.sb([128, 512], F32, "w1") for _ in range(NB)]
    w2 = [C.sb([128, 512], F32, "w2") for _ in range(NB)]
    w3 = [C.sb([128, 512], F32, "w3") for _ in range(NB)]
    w4 = [C.sb([128, 512], F32, "w4") for _ in range(NB)]
    xbf = [C.sb([128, 512], BF16, "xbf") for _ in range(NB)]
    yo = [C.sb([128, 512], F32, "yo") for _ in range(2)]
    gt = [C.sb([128, 512], F32, "gt") for _ in range(2)]
    yob = [C.sb([128, 512], BF16, "yob") for _ in range(2)]
    pA = [C.ps([128, 512], F32, "pA") for _ in range(2)]
    pB = [C.ps([128, 512], F32, "pB") for _ in range(2)]
    yps = [C.ps([128, 512], F32, "yps") for _ in range(2)]
    pe2 = "pool" if scan_pool else "dve"
    uidx = [0]

    def unit(gd, it, last_g, yslot):
        d, g = gd // 8, gd % 8
        s = uidx[0] % NB
        uidx[0] += 1
        ts = slice(it * 512, (it + 1) * 512)
        rcol = sm["e"][:, gd:gd + 1]
        ct, st_ = cosT[:, gd, :], sinT[:, gd, :]
        kct, kst = (tag, "cosT", gd), (tag, "sinT", gd)
        P.op("pe", lambda e: e.matmul(pA[s][:], lhsT=BA[:, gd, :], rhs=uT[:, ts], start=True, stop=True),
             reads=[(tag, "Btab"), (tag, "uT")], writes=[(tag, "pA", s)])
        P.op("pe", lambda e: e.matmul(pB[s][:], lhsT=BB[:, gd, :], rhs=uT[:, ts], start=True, stop=True),
             reads=[(tag, "Btab"), (tag, "uT")], writes=[(tag, "pB", s)])
        P.op("act", lambda e: e.copy(out=bA[s][:], in_=pA[s][:]), writes=[(tag, "bA", s), (tag, "pA", s)])
        P.op("act", lambda e: e.copy(out=bB[s][:], in_=pB[s][:]), writes=[(tag, "bB", s), (tag, "pB", s)])
        P.op("dve", lambda e: e.tensor_tensor(out=w1[s][:], in0=bA[s][:], in1=ct, op=ALU.mult), reads=[(tag, "bA", s), kct], writes=[(tag, "w1", s)])
        P.op("dve", lambda e: e.tensor_tensor(out=w2[s][:], in0=bB[s][:], in1=st_, op=ALU.mult), reads=[(tag, "bB", s), kst], writes=[(tag, "w2", s)])
        P.op("dve", lambda e: e.tensor_tensor(out=w1[s][:], in0=w1[s][:], in1=w2[s][:], op=ALU.add), reads=[(tag, "w1", s), (tag, "w2", s)], writes=[(tag, "w1", s)])
        P.op(pe2, lambda e: e.tensor_tensor(out=w3[s][:], in0=bB[s][:], in1=ct, op=ALU.mult), reads=[(tag, "bB", s), kct], writes=[(tag, "w3", s)])
        P.op(pe2, lambda e: e.tensor_tensor(out=w4[s][:], in0=bA[s][:], in1=st_, op=ALU.mult), reads=[(tag, "bA", s), kst], writes=[(tag, "w4", s)])
        P.op(pe2, lambda e: e.tensor_tensor(out=w3[s][:], in0=w3[s][:], in1=w4[s][:], op=ALU.subtract), reads=[(tag, "w3", s), (tag, "w4", s)], writes=[(tag, "w3", s)])
        rb = rcol.to_broadcast([128, 512])
        if d == 0:
            P.op("dve", lambda e: e.tensor_tensor_scan(out=w2[s][:], data0=rb, data1=w1[s][:], initial=cA[:, gd:gd + 1], op0=ALU.mult, op1=ALU.add),
                 reads=[(tag, "w1", s), k("e"), (tag, "cA")], writes=[(tag, "w2", s)])
            P.op("dve", lambda e: e.tensor_tensor_scan(out=w4[s][:], data0=rb, data1=w3[s][:], initial=cB[:, gd:gd + 1], op0=ALU.mult, op1=ALU.add),
                 reads=[(tag, "w3", s), k("e"), (tag, "cB")], writes=[(tag, "w4", s)])
            lastc = slice(511, 512)
        else:
            P.op("dve", lambda e: e.tensor_tensor_scan(out=w2[s][:, ::-1], data0=rb, data1=w1[s][:, ::-1], initial=cA[:, gd:gd + 1], op0=ALU.mult, op1=ALU.add),
                 reads=[(tag, "w1", s), k("e"), (tag, "cA")], writes=[(tag, "w2", s)])
            P.op("dve", lambda e: e.tensor_tensor_scan(out=w4[s][:, ::-1], data0=rb, data1=w3[s][:, ::-1], initial=cB[:, gd:gd + 1], op0=ALU.mult, op1=ALU.add),
                 reads=[(tag, "w3", s), k("e"), (tag, "cB")], writes=[(tag, "w4", s)])
            lastc = slice(0, 1)
        c5, s5 = sm["c512"][:, gd:gd + 1], sm["s512"][:, gd:gd + 1]
        P.op("dve", lambda e: e.tensor_tensor(out=ctmp[:, 0:1], in0=w2[s][:, lastc], in1=c5, op=ALU.mult), reads=[(tag, "w2", s), k("c512")], writes=[(tag, "ct0")])
        P.op("dve", lambda e: e.tensor_tensor(out=ctmp[:, 1:2], in0=w4[s][:, lastc], in1=s5, op=ALU.mult), reads=[(tag, "w4", s), k("s512")], writes=[(tag, "ct1")])
        P.op("dve", lambda e: e.tensor_tensor(out=ctmp[:, 2:3], in0=w4[s][:, lastc], in1=c5, op=ALU.mult), reads=[(tag, "w4", s), k("c512")], writes=[(tag, "ct2")])
        P.op("dve", lambda e: e.tensor_tensor(out=ctmp[:, 3:4], in0=w2[s][:, lastc], in1=s5, op=ALU.mult), reads=[(tag, "w2", s), k("s512")], writes=[(tag, "ct3")])
        P.op("dve", lambda e: e.tensor_tensor(out=cA[:, gd:gd + 1], in0=ctmp[:, 0:1], in1=ctmp[:, 1:2], op=ALU.subtract),
             reads=[(tag, "ct0"), (tag, "ct1")], writes=[(tag, "cA")])
        P.op("dve", lambda e: e.tensor_tensor(out=cB[:, gd:gd + 1], in0=ctmp[:, 2:3], in1=ctmp[:, 3:4], op=ALU.add),
             reads=[(tag, "ct2"), (tag, "ct3")], writes=[(tag, "cB")])
        P.op(pe2, lambda e: e.tensor_tensor(out=w1[s][:], in0=w2[s][:], in1=ct, op=ALU.mult), reads=[(tag, "w2", s), kct], writes=[(tag, "w1", s)])
        P.op(pe2, lambda e: e.tensor_tensor(out=w3[s][:], in0=w4[s][:], in1=st_, op=ALU.mult), reads=[(tag, "w4", s), kst], writes=[(tag, "w3", s)])
        P.op("dve", lambda e: e.tensor_tensor(out=xbf[s][:], in0=w1[s][:], in1=w3[s][:], op=ALU.subtract), reads=[(tag, "w1", s), (tag, "w3", s)], writes=[(tag, "xbf", s)])
        P.op("pe", lambda e: e.matmul(yps[yslot][:], lhsT=CP[:, gd, :], rhs=xbf[s][:], start=(g == 0), stop=(last_g and d == 1)),
             reads=[(tag, "CP"), (tag, "xbf", s)], writes=[(tag, "yps", yslot)])

    tcount = 0
    for d in (1, 0):
        order = range(NT - 1, -1, -1) if d == 1 else range(NT)
        for it in order:
            ysl = tcount % 2
            tcount += 1
            ts = slice(it * 512, (it + 1) * 512)
            for g in range(8):
                unit(d * 8 + g, it, g == 7, ysl)
            if d == 1:
                P.op("act", lambda e, ysl=ysl, ts=ts: e.copy(out=yb[:, ts], in_=yps[ysl][:]), writes=[(tag, "yb", it), (tag, "yps", ysl)])
            else:
                P.op("pe", lambda e, ysl=ysl, ts=ts: e.matmul(yps[ysl][:], lhsT=Dg[:], rhs=uT[:, ts], start=False, stop=True),
                     reads=[(tag, "Dg"), (tag, "uT")], writes=[(tag, "yps", ysl)])
                P.op("dve", lambda e, ysl=ysl, ts=ts: e.tensor_tensor(out=yo[ysl][:], in0=yps[ysl][:], in1=yb[:, ts], op=ALU.add),
                     reads=[(tag, "yb", it)], writes=[(tag, "yo", ysl), (tag, "yps", ysl)])
                emit_gelu(P, (tag, "g", ysl), yo[ysl][:], (tag, "yo", ysl), gt[ysl][:], yob[ysl][:], (tag, "yob", ysl))
                P.dma("sp", ys5T_d[:, ts], yob[ysl][:], chan=(tag, "yo", ysl), reads=[(tag, "yob", ysl)], writes=[(tag, "ys5T", it)])


def build_A4(L, scan_pool=True):
    nc = bass.Bass("TRN2", target_bir_lowering=False)
    with ExitStack() as es:
        C = Ctx(nc, es)
        fmT_d = C.din("fmT", [7, 128, L], BF16)
        s5p_d = C.din("s5p", [128, 49], F32)
        bre_d = C.din("s5bre", [128, 16, 16], F32)
        bim_d = C.din("s5bim", [128, 16, 16], F32)
        cc_d = C.din("s5cc", [128, 2, 128], F32)
        sc_d = C.din("s5c", [128, S5C_W], F32)
        idf_d = C.din("identf", [128, 128], F32)
        ys5T_d = C.dout("ys5T", [128, L], BF16)
        stage_A4(C, fmT_d, s5p_d[:, :], bre_d[:, :, :], bim_d[:, :, :], cc_d[:, :, :], sc_d[:, :], idf_d[:, :], ys5T_d, L, scan_pool=scan_pool)
        C.P.emit(es)
    return nc


W0_SPECS = (("w_out", 1024, 1024), ("ple_gate_w", 1024, 1024), ("ple_w", 256, 1024), ("s5_glu_w", 256, 256),
            ("ffn_w_up", 1024, 5632), ("ffn_w_down", 2816, 1024))


def stage_W0(C, pairs, tag="w0"):
    P = C.P
    stg = [C.sb([128, 2816], F32, "w0s") for _ in range(2)]
    obf = [C.sb([128, 2816], BF16, "w0o") for _ in range(2)]
    i = 0
    engs = ("dve", "pool", "act")
    for src_d, dst_d, R, N in pairs:
        for r0 in range(0, R, 128):
            for n0 in range(0, N, 2816):
                n1 = min(N, n0 + 2816)
                w = n1 - n0
                s = i % 2
                P.dma("sp", stg[s][:, 0:w], src_d[r0:r0 + 128, n0:n1], chan=(tag, "in", s), writes=[(tag, "stg", s)])
                en = engs[i % 3]
                if en == "act":
                    P.op("act", lambda e, s=s, w=w: e.copy(out=obf[s][:, 0:w], in_=stg[s][:, 0:w]), reads=[(tag, "stg", s)], writes=[(tag, "obf", s)])
                else:
                    P.op(en, lambda e, s=s, w=w: e.tensor_copy(out=obf[s][:, 0:w], in_=stg[s][:, 0:w]), reads=[(tag, "stg", s)], writes=[(tag, "obf", s)])
                P.dma("pool", dst_d[r0:r0 + 128, n0:n1], obf[s][:, 0:w], chan=(tag, "out", s), reads=[(tag, "obf", s)], writes=[(tag, "dst", i)])
                i += 1


def build_W0():
    nc = bass.Bass("TRN2", target_bir_lowering=False)
    with ExitStack() as es:
        C = Ctx(nc, es)
        pairs = []
        for nm, R, N in W0_SPECS:
            pairs.append((C.din(nm, [R, N], F32), C.dout(nm + "_bf", [R, N], BF16), R, N))
        stage_W0(C, pairs)
        C.P.emit(es)
    return nc


def emit_ln(P, tag, h, hkey, st, mv, rstd, epsb, gtab, btab, gkeys, out, okey, eng2="pool"):
    ks, km, kr = (tag, "st"), (tag, "mv"), (tag, "rstd")
    for j in range(2):
        P.op("dve", lambda e, j=j: e.bn_stats(out=st[:, j, :], in_=h[:, j * 512:(j + 1) * 512]), reads=[hkey], writes=[ks])
    P.op("dve", lambda e: e.bn_aggr(out=mv[:], in_=st[:]), reads=[ks], writes=[km])
    P.op("act", lambda e: e.activation(out=rstd[:], in_=mv[:, 1:2], func=AF.Sqrt, bias=epsb[:, 0:1], scale=1.0), reads=[km] + gkeys, writes=[kr])
    P.op("dve", lambda e: e.reciprocal(out=rstd[:], in_=rstd[:]), reads=[kr], writes=[kr])
    P.op("dve", lambda e: e.tensor_scalar(out=h, in0=h, scalar1=mv[:, 0:1], scalar2=rstd[:, 0:1], op0=ALU.subtract, op1=ALU.mult),
         reads=[hkey, km, kr], writes=[hkey])
    P.op(eng2, lambda e: e.tensor_tensor(out=h, in0=h, in1=gtab, op=ALU.mult), reads=[hkey] + gkeys, writes=[hkey])
    P.op(eng2, lambda e: e.tensor_tensor(out=out, in0=h, in1=btab, op=ALU.add), reads=[hkey] + gkeys, writes=[okey])


def emit_to_featmajor(P, C_tag, src32, skey, xbf, pst, dstT, dkey_fn, ident, j):
    tag = C_tag
    P.op("act", lambda e: e.copy(out=xbf, in_=src32), reads=[skey], writes=[(tag, "xbf")])
    for c in range(8):
        P.op("pe", lambda e, c=c: e.transpose(out=pst[:, c, :], in_=xbf[:, c * 128:(c + 1) * 128], identity=ident[:]),
             reads=[(tag, "xbf"), "ident"], writes=[(tag, "pst")])
    P.op("dve", lambda e: e.tensor_copy(out=dstT[:, :, j * 128:(j + 1) * 128], in_=pst[:]), writes=[dkey_fn(j), (tag, "pst")])


def stage_P1(C, ycT_d, x_d, wout_d, glw_d, glb_d, g1_d, b1_d, x1_d, x1T_d, ident, NTOK, tag="p1"):
    P = C.P
    wout = C.sb([128, 8, 1024], BF16, "wout")
    glw = C.sb([128, 2, 256], BF16, "glw")
    glb = C.sb([128, 2], F32, "glb")
    gtab = C.sb([128, 1024], F32, "gtab")
    btab = C.sb([128, 1024], F32, "btab")
    epsb = C.sb([128, 1], F32, "epsb")
    P.dma("sp", wout[:], wout_d.rearrange("(c p) n -> p c n", p=128), chan=(tag, "c0"), writes=[(tag, "wout")])
    P.dma("sp", glw[:], glw_d.rearrange("(c p) n -> p c n", p=128), chan=(tag, "c0"), writes=[(tag, "glw")])
    for oc in range(2):
        P.dma("sp", glb[:, oc:oc + 1], glb_d[oc * 128:(oc + 1) * 128].rearrange("(p o) -> p o", o=1), chan=(tag, "c0"), writes=[(tag, "glb")])
    P.dma("sp", gtab[:], g1_d.partition_broadcast(128), chan=(tag, "c0"), writes=[(tag, "gtab")])
    P.dma("sp", btab[:], b1_d.partition_broadcast(128), chan=(tag, "c0"), writes=[(tag, "btab")])
    P.op("dve", lambda e: e.memset(epsb[:], LN_EPS), writes=[(tag, "epsb")])
    gk = [(tag, "gtab"), (tag, "btab"), (tag, "epsb")]
    yc = [C.sb([128, 8, 512], BF16, "yc") for _ in range(2)]
    ysg = [C.sb([128, 2, 512], BF16, "ysg") for _ in range(2)]
    gsig = C.sb([128, 512], F32, "gsig")
    xin = [C.sb([128, 1024], F32, "xin") for _ in range(2)]
    hh = [C.sb([128, 1024], F32, "hh") for _ in range(2)]
    x1o = [C.sb([128, 1024], F32, "x1o") for _ in range(2)]
    xbf = C.sb([128, 1024], BF16, "xbf")
    x1T = [C.sb([128, 8, 512], BF16, "x1T") for _ in range(2)]
    st = C.sb([128, 2, 6], F32, "st")
    mv = C.sb([128, 2], F32, "mv")
    rstd = C.sb([128, 1], F32, "rstd")
    pgl = [C.ps([128, 512], F32, "pgl") for _ in range(2)]
    pmx = [C.ps([128, 512], F32, "pmx") for _ in range(4)]
    pst = C.ps([128, 8, 128], BF16, "pst")
    ycT_v = ycT_d.rearrange("(c p) t -> p c t", p=128)
    x1T_v = x1T_d.rearrange("(c p) t -> p c t", p=128)
    NTT = NTOK // 512
    kk = 0
    for tt in range(NTT):
        s = tt % 2
        t0 = tt * 512
        P.dma("sp", yc[s][:], ycT_v[:, :, t0:t0 + 512], chan=(tag, "yc", s), writes=[(tag, "yc", s)])
        for oc in range(2):
            for kc in range(2):
                P.op("pe", lambda e, s=s, oc=oc, kc=kc: e.matmul(pgl[oc][:], lhsT=glw[:, kc, oc * 128:(oc + 1) * 128], rhs=yc[s][:, 6 + kc, :],
                                                                 start=(kc == 0), stop=(kc == 1)),
                     reads=[(tag, "glw"), (tag, "yc", s)], writes=[(tag, "pgl", oc)])
            P.op("act", lambda e, oc=oc: e.activation(out=gsig[:], in_=pgl[oc][:], func=AF.Sigmoid, bias=glb[:, oc:oc + 1], scale=1.0),
                 reads=[(tag, "glb")], writes=[(tag, "gsig"), (tag, "pgl", oc)])
            P.op("dve", lambda e, s=s, oc=oc: e.tensor_tensor(out=ysg[s][:, oc, :], in0=gsig[:], in1=yc[s][:, 6 + oc, :], op=ALU.mult),
                 reads=[(tag, "gsig"), (tag, "yc", s)], writes=[(tag, "ysg", s, oc)])
        for j in range(4):
            b = kk % 2
            kk += 1
            r0 = t0 + j * 128
            P.dma("sp", xin[b][:], x_d[r0:r0 + 128, :], chan=(tag, "xin", b), writes=[(tag, "xin", b)])
            for nb in range(2):
                pb = (2 * b + nb)
                for c in range(8):
                    def lhs(c=c, s=s, j=j):
                        return yc[s][:, c, j * 128:(j + 1) * 128] if c < 6 else ysg[s][:, c - 6, j * 128:(j + 1) * 128]
                    P.op("pe", lambda e, pb=pb, c=c, nb=nb, lhs=lhs: e.matmul(pmx[pb][:], lhsT=lhs(), rhs=wout[:, c, nb * 512:(nb + 1) * 512],
                                                                              start=(c == 0), stop=(c == 7)),
                         reads=[(tag, "wout"), (tag, "yc", s), (tag, "ysg", s, 0), (tag, "ysg", s, 1)], writes=[(tag, "pmx", pb)])
                P.op("dve", lambda e, b=b, pb=pb, nb=nb: e.scalar_tensor_tensor(out=hh[b][:, nb * 512:(nb + 1) * 512], in0=xin[b][:, nb * 512:(nb + 1) * 512],
                                                                               scalar=ALPHA, in1=pmx[pb][:], op0=ALU.mult, op1=ALU.add),
                     reads=[(tag, "xin", b)], writes=[(tag, "hh", b), (tag, "pmx", pb)])
            emit_ln(P, (tag, "ln"), hh[b][:], (tag, "hh", b), st, mv, rstd, epsb, gtab[:], btab[:], gk, x1o[b][:], (tag, "x1o", b))
            P.dma("pool", x1_d[r0:r0 + 128, :], x1o[b][:], chan=(tag, "x1o", b), reads=[(tag, "x1o", b)], writes=[(tag, "x1d", tt, j)])
            emit_to_featmajor(P, (tag, "fm"), x1o[b][:], (tag, "x1o", b), xbf[:], pst, x1T[s], lambda jj, s=s: (tag, "x1T", s, jj), ident, j)
        P.dma("pool", x1T_v[:, :, t0:t0 + 512], x1T[s][:], chan=(tag, "x1T", s), reads=[(tag, "x1T", s, jj) for jj in range(4)],
              writes=[(tag, "x1Td", tt)])


def build_P1(NTOK):
    nc = bass.Bass("TRN2", target_bir_lowering=False)
    with ExitStack() as es:
        C = Ctx(nc, es)
        ycT_d = C.din("ycT", [1024, NTOK], BF16)
        x_d = C.din("x", [NTOK, 1024], F32)
        wout_d = C.din("w_out_bf", [1024, 1024], BF16)
        glw_d = C.din("s5_glu_w_bf", [256, 256], BF16)
        glb_d = C.din("glb", [256], F32)
        g1_d = C.din("ln_g", [1024], F32)
        b1_d = C.din("ln_b", [1024], F32)
        id_d = C.din("ident", [128, 128], BF16)
        x1_d = C.dout("x1", [NTOK, 1024], F32)
        x1T_d = C.dout("x1T", [1024, NTOK], BF16)
        ident = C.sb([128, 128], BF16, "ident")
        C.P.dma("sp", ident[:], id_d[:, :], chan="ident", writes=["ident"])
        stage_P1(C, ycT_d, x_d, wout_d, glw_d, glb_d, g1_d, b1_d, x1_d, x1T_d, ident, NTOK)
        C.P.emit(es)
    return nc


NFT = D_FF // 128


def conv_layout(conv_w, conv_b):
    a = np.concatenate([np.asarray(conv_w), np.asarray(conv_b)[None, :]], axis=0)
    return np.ascontiguousarray(a.reshape(4, NFT, 128).transpose(2, 1, 0)).astype(np.float32)


def stage_P2(C, x1T_d, x1_d, p_d, wup_d, wdn_d, plew_d, plegw_d, cwb_d, g2_d, b2_d, x2_d, x2T_d, ident, NTOK, tag="p2"):
    P = C.P
    plegw = C.sb([128, 8, 1024], BF16, "plegw")
    plew = C.sb([128, 2, 1024], BF16, "plew")
    wdn = C.sb([128, NFT, 1024], BF16, "wdn")
    cwb = C.sb([128, NFT, 4], F32, "cwb")
    gtab = C.sb([128, 1024], F32, "gtab")
    btab = C.sb([128, 1024], F32, "btab")
    epsb = C.sb([128, 1], F32, "epsb")
    P.dma("sp", plegw[:], plegw_d.rearrange("(c p) n -> p c n", p=128), chan=(tag, "c0"), writes=[(tag, "plegw")])
    P.dma("sp", plew[:], plew_d.rearrange("(c p) n -> p c n", p=128), chan=(tag, "c0"), writes=[(tag, "plew")])
    P.dma("sp", wdn[:], wdn_d.rearrange("(c p) n -> p c n", p=128), chan=(tag, "c0"), writes=[(tag, "wdn")])
    P.dma("sp", cwb[:], cwb_d, chan=(tag, "c0"), writes=[(tag, "cwb")])
    P.dma("sp", gtab[:], g2_d.partition_broadcast(128), chan=(tag, "c0"), writes=[(tag, "gtab")])
    P.dma("sp", btab[:], b2_d.partition_broadcast(128), chan=(tag, "c0"), writes=[(tag, "btab")])
    P.op("dve", lambda e: e.memset(epsb[:], LN_EPS), writes=[(tag, "epsb")])
    gk = [(tag, "gtab"), (tag, "btab"), (tag, "epsb")]

    xt = [C.sb([128, 8, 514], BF16, "xt") for _ in range(2)]
    wg = [C.sb([128, 8, 128], BF16, "wg") for _ in range(3)]
    wv = [C.sb([128, 8, 128], BF16, "wv") for _ in range(3)]
    hm = C.sb([128, NFT, 512], BF16, "hm")
    racc = [C.sb([128, 1024], F32, "racc") for _ in range(4)]
    x1t = [C.sb([128, 1024], F32, "x1t") for _ in range(2)]
    pin = [C.sb([128, 256], F32, "pin") for _ in range(2)]
    pbf = [C.sb([128, 256], BF16, "pbf") for _ in range(2)]
    pT = C.sb([128, 2, 512], BF16, "pT")
    sg = [C.sb([128, 512], F32, "sg") for _ in range(2)]
    gext = [C.sb([128, 514], F32, "gext") for _ in range(3)]
    cv = [C.sb([128, 512], F32, "cv") for _ in range(3)]
    tmp = [C.sb([128, 512], F32, "tmp") for _ in range(3)]
    oneb = C.sb([128, 1], F32, "oneb")
    P.op("dve", lambda e: e.memset(oneb[:], 1.0), writes=[(tag, "oneb")])
    xbf = C.sb([128, 1024], BF16, "xbf")
    x2T = C.sb([128, 8, 512], BF16, "x2T")
    st = C.sb([128, 2, 6], F32, "st")
    mv = C.sb([128, 2], F32, "mv")
    rstd = C.sb([128, 1], F32, "rstd")
    pgate = [C.ps([128, 512], F32, "pgate") for _ in range(2)]
    pval = [C.ps([128, 512], F32, "pval") for _ in range(2)]
    pd = [C.ps([128, 512], F32, "pd") for _ in range(2)]
    phalo = C.ps([128, 2], F32, "phalo")
    pst = C.ps([128, 8, 128], BF16, "pst")
    x1T_v = x1T_d.rearrange("(c p) t -> p c t", p=128)
    x2T_v = x2T_d.rearrange("(c p) t -> p c t", p=128)
    wup_v = wup_d.rearrange("(c p) n -> p c n", p=128)
    NTT = NTOK // 512
    cnt = {"w": 0, "u": 0, "d": 0, "x": 0, "p": 0, "q": 0}

    def tile(tt):
        s = tt % 2
        t0 = tt * 512
        P.dma("sp", xt[s][:], x1T_v[:, :, t0:t0 + 514], chan=(tag, "xt", s), writes=[(tag, "xt", s)])
        for j in range(4):
            b = cnt["p"] % 2
            cnt["p"] += 1
            r0 = t0 + j * 128
            P.dma("sp", pin[b][:], p_d[r0:r0 + 128, :], chan=(tag, "pin", b), writes=[(tag, "pin", b)])
            P.op("act", lambda e, b=b: e.copy(out=pbf[b][:], in_=pin[b][:]), reads=[(tag, "pin", b)], writes=[(tag, "pbf", b)])
            for c in range(2):
                P.op("pe", lambda e, b=b, c=c: e.transpose(out=pst[:, c, :], in_=pbf[b][:, c * 128:(c + 1) * 128], identity=ident[:]),
                     reads=[(tag, "pbf", b), "ident"], writes=[(tag, "pst")])
            P.op("dve", lambda e, j=j: e.tensor_copy(out=pT[:, :, j * 128:(j + 1) * 128], in_=pst[:, 0:2, :]), writes=[(tag, "pT", j), (tag, "pst")])
        for j in range(4):
            b = cnt["x"] % 2
            cnt["x"] += 1
            r0 = t0 + j * 128
            P.dma("sp", x1t[b][:], x1_d[r0:r0 + 128, :], chan=(tag, "x1t", b), writes=[(tag, "x1t", b)])
            for nb in range(2):
                q = cnt["q"] % 2
                cnt["q"] += 1
                ns = slice(nb * 512, (nb + 1) * 512)
                for c in range(2):
                    P.op("pe", lambda e, c=c, j=j, ns=ns: e.matmul(pd[0][:], lhsT=pT[:, c, j * 128:(j + 1) * 128], rhs=plew[:, c, ns], start=(c == 0), stop=(c == 1)),
                         reads=[(tag, "pT", j), (tag, "plew")], writes=[(tag, "pd", 0)])
                for c in range(8):
                    P.op("pe", lambda e, c=c, j=j, ns=ns, s=s: e.matmul(pd[1][:], lhsT=xt[s][:, c, 1 + j * 128:1 + (j + 1) * 128], rhs=plegw[:, c, ns],
                                                                       start=(c == 0), stop=(c == 7)),
                         reads=[(tag, "xt", s), (tag, "plegw")], writes=[(tag, "pd", 1)])
                P.op("act", lambda e, q=q: e.activation(out=sg[q][:], in_=pd[1][:], func=AF.Sigmoid), writes=[(tag, "sg", q), (tag, "pd", 1)])
                P.op("dve", lambda e, q=q: e.tensor_tensor(out=sg[q][:], in0=pd[0][:], in1=sg[q][:], op=ALU.mult),
                     reads=[(tag, "sg", q)], writes=[(tag, "sg", q), (tag, "pd", 0)])
                P.op("dve", lambda e, q=q, b=b, j=j, ns=ns: e.scalar_tensor_tensor(out=racc[j][:, ns], in0=x1t[b][:, ns], scalar=ALPHA, in1=sg[q][:],
                                                                                  op0=ALU.mult, op1=ALU.add),
                     reads=[(tag, "sg", q), (tag, "x1t", b)], writes=[(tag, "racc", j)])
        def wload(f):
            w = (cnt["w"] + f) % 3
            P.dma("sp", wg[w][:], wup_v[:, :, f * 128:(f + 1) * 128], chan=(tag, "wg", w), writes=[(tag, "wg", w)])
            P.dma("sp", wv[w][:], wup_v[:, :, D_FF + f * 128:D_FF + (f + 1) * 128], chan=(tag, "wv", w), writes=[(tag, "wv", w)])
        wload(0)
        wload(1)
        for f in range(NFT):
            w = (cnt["w"] + f) % 3
            u = cnt["u"] % 2
            g3 = cnt["u"] % 3
            cnt["u"] += 1
            if f + 2 < NFT:
                wload(f + 2)
            for c in range(8):
                P.op("pe", lambda e, c=c, w=w, u=u, s=s: e.matmul(pgate[u][:], lhsT=wg[w][:, c, :], rhs=xt[s][:, c, 1:513], start=(c == 0), stop=(c == 7)),
                     reads=[(tag, "wg", w), (tag, "xt", s)], writes=[(tag, "pgate", u)])
            for c in range(8):
                P.op("pe", lambda e, c=c, w=w, s=s: e.matmul(phalo[:], lhsT=wg[w][:, c, :], rhs=xt[s][:, c, 0:514:513], start=(c == 0), stop=(c == 7)),
                     reads=[(tag, "wg", w), (tag, "xt", s)], writes=[(tag, "phalo")])
            for c in range(8):
                P.op("pe", lambda e, c=c, w=w, u=u, s=s: e.matmul(pval[u][:], lhsT=wv[w][:, c, :], rhs=xt[s][:, c, 1:513], start=(c == 0), stop=(c == 7)),
                     reads=[(tag, "wv", w), (tag, "xt", s)], writes=[(tag, "pval", u)])
            kc, kt = (tag, "cv", g3), (tag, "tmp", g3)
            P.op("act", lambda e, u=u, g3=g3: e.copy(out=gext[g3][:, 1:513], in_=pgate[u][:]), writes=[(tag, "gext", g3), (tag, "pgate", u)])
            P.op("act", lambda e, u=u, g3=g3, f=f: e.activation(out=cv[g3][:], in_=pgate[u][:], func=AF.Identity, scale=cwb[:, f, 1:2], bias=cwb[:, f, 3:4]),
                 reads=[(tag, "cwb")], writes=[kc, (tag, "pgate", u)])
            P.op("act", lambda e, g3=g3: e.copy(out=gext[g3][:, 0:514:513], in_=phalo[:]), writes=[(tag, "gexth", g3), (tag, "phalo")])
            gkeys = [(tag, "gext", g3), (tag, "gexth", g3), (tag, "cwb")]
            P.op("dve", lambda e, g3=g3, f=f: e.scalar_tensor_tensor(out=cv[g3][:], in0=gext[g3][:, 0:512], scalar=cwb[:, f, 0:1], in1=cv[g3][:],
                                                                     op0=ALU.mult, op1=ALU.add), reads=gkeys + [kc], writes=[kc])
            P.op("dve", lambda e, g3=g3, f=f: e.scalar_tensor_tensor(out=cv[g3][:], in0=gext[g3][:, 2:514], scalar=cwb[:, f, 2:3], in1=cv[g3][:],
                                                                     op0=ALU.mult, op1=ALU.add), reads=gkeys + [kc], writes=[kc])
            P.op("act", lambda e, g3=g3: e.activation(out=tmp[g3][:], in_=cv[g3][:], func=AF.Square), reads=[kc], writes=[kt])
            P.op("act", lambda e, g3=g3: e.activation(out=tmp[g3][:], in_=tmp[g3][:], func=AF.Identity, scale=0.044715, bias=oneb[:, 0:1]),
                 reads=[kt, (tag, "oneb")], writes=[kt])
            P.op("pool", lambda e, g3=g3: e.tensor_tensor(out=tmp[g3][:], in0=tmp[g3][:], in1=cv[g3][:], op=ALU.mult), reads=[kt, kc], writes=[kt])
            P.op("act", lambda e, g3=g3: e.activation(out=tmp[g3][:], in_=tmp[g3][:], func=AF.Sigmoid, scale=GELU_K), reads=[kt], writes=[kt])
            P.op("pool", lambda e, g3=g3: e.tensor_tensor(out=tmp[g3][:], in0=tmp[g3][:], in1=cv[g3][:], op=ALU.mult), reads=[kt, kc], writes=[kt])
            P.op("dve", lambda e, u=u, g3=g3, f=f: e.tensor_tensor(out=hm[:, f, :], in0=pval[u][:], in1=tmp[g3][:], op=ALU.mult),
                 reads=[kt], writes=[(tag, "hm", f), (tag, "pval", u)])
        cnt["w"] += NFT
        for j in range(4):
            for nb in range(2):
                d = cnt["d"] % 2
                cnt["d"] += 1
                ns = slice(nb * 512, (nb + 1) * 512)
                for f in range(NFT):
                    P.op("pe", lambda e, f=f, j=j, ns=ns, d=d: e.matmul(pd[d][:], lhsT=hm[:, f, j * 128:(j + 1) * 128], rhs=wdn[:, f, ns],
                                                                       start=(f == 0), stop=(f == NFT - 1)),
                         reads=[(tag, "hm", f), (tag, "wdn")], writes=[(tag, "pd", d)])
                P.op("dve", lambda e, j=j, ns=ns, d=d: e.tensor_tensor(out=racc[j][:, ns], in0=pd[d][:], in1=racc[j][:, ns], op=ALU.add),
                     reads=[(tag, "racc", j)], writes=[(tag, "racc", j), (tag, "pd", d)])
            r0 = t0 + j * 128
            emit_ln(P, (tag, "ln"), racc[j][:], (tag, "racc", j), st, mv, rstd, epsb, gtab[:], btab[:], gk, racc[j][:], (tag, "racc", j))
            P.dma("sp", x2_d[r0:r0 + 128, :], racc[j][:], chan=(tag, "x2o", j), reads=[(tag, "racc", j)], writes=[(tag, "x2d", tt, j)])
            emit_to_featmajor(P, (tag, "fm"), racc[j][:], (tag, "racc", j), xbf[:], pst, x2T, lambda jj: (tag, "x2T", jj), ident, j)
        P.dma("sp", x2T_v[:, :, t0:t0 + 512], x2T[:], chan=(tag, "x2T"), reads=[(tag, "x2T", jj) for jj in range(4)], writes=[(tag, "x2Td", tt)])

    for tt in range(NTT):
        tile(tt)


def build_P2(NTOK):
    nc = bass.Bass("TRN2", target_bir_lowering=False)
    with ExitStack() as es:
        C = Ctx(nc, es)
        x1T_d = C.din("x1Te", [1024, NTOK + 2], BF16)
        x1_d = C.din("x1", [NTOK, 1024], F32)
        p_d = C.din("p", [NTOK, 256], F32)
        wup_d = C.din("ffn_w_up_bf", [1024, 2 * D_FF], BF16)
        wdn_d = C.din("ffn_w_down_bf", [D_FF, 1024], BF16)
        plew_d = C.din("ple_w_bf", [256, 1024], BF16)
        plegw_d = C.din("ple_gate_w_bf", [1024, 1024], BF16)
        cwb_d = C.din("cwb", [128, NFT, 4], F32)
        g2_d = C.din("ln_g", [1024], F32)
        b2_d = C.din("ln_b", [1024], F32)
        id_d = C.din("ident", [128, 128], BF16)
        x2_d = C.dout("x2", [NTOK, 1024], F32)
        x2T_d = C.dout("x2T", [1024, NTOK], BF16)
        ident = C.sb([128, 128], BF16, "ident")
        C.P.dma("sp", ident[:], id_d[:, :], chan="ident", writes=["ident"])
        stage_P2(C, x1T_d, x1_d, p_d, wup_d, wdn_d, plew_d, plegw_d, cwb_d[:, :, :], g2_d, b2_d, x2_d, x2T_d, ident, NTOK)
        C.P.emit(es)
    return nc


def build_W0_split():
    nc = bass.Bass("TRN2", target_bir_lowering=False)
    with ExitStack() as es:
        C = Ctx(nc, es)
        pairs = []
        for i in range(DEPTH):
            for nm, R, N in W0_SPECS:
                r = R // NCORES
                pairs.append((C.din("%s_%d" % (nm, i), [r, N], F32), C.dout("%s_%d_bf" % (nm, i), [r, N], BF16), r, N))
        stage_W0v(C, pairs)
        C.P.emit(es)
    return nc


def stage_W0v(C, pairs, tag="w0"):
    P = C.P
    stg = [C.sb([128, 2816], F32, "w0s") for _ in range(2)]
    obf = [C.sb([128, 2816], BF16, "w0o") for _ in range(2)]
    i = 0
    engs = ("dve", "pool", "act")
    for src_d, dst_d, R, N in pairs:
        for r0 in range(0, R, 128):
            pr = min(128, R - r0)
            for n0 in range(0, N, 2816):
                n1 = min(N, n0 + 2816)
                w = n1 - n0
                s = i % 2
                P.dma("sp", stg[s][0:pr, 0:w], src_d[r0:r0 + pr, n0:n1], chan=(tag, "in", s), writes=[(tag, "stg", s)])
                en = engs[i % 3]
                if en == "act":
                    P.op("act", lambda e, s=s, w=w, pr=pr: e.copy(out=obf[s][0:pr, 0:w], in_=stg[s][0:pr, 0:w]), reads=[(tag, "stg", s)], writes=[(tag, "obf", s)])
                else:
                    P.op(en, lambda e, s=s, w=w, pr=pr: e.tensor_copy(out=obf[s][0:pr, 0:w], in_=stg[s][0:pr, 0:w]), reads=[(tag, "stg", s)], writes=[(tag, "obf", s)])
                P.dma("pool", dst_d[r0:r0 + pr, n0:n1], obf[s][0:pr, 0:w], chan=(tag, "out", s), reads=[(tag, "obf", s)], writes=[(tag, "dst", i)])
                i += 1


def build_fused(L):
    nc = bass.Bass("TRN2", target_bir_lowering=False)

    def din(name, shape, dt):
        return nc.dram_tensor(name, list(shape), dt, kind="ExternalInput").ap()

    def dint(name, shape, dt):
        return nc.dram_tensor(name, list(shape), dt, kind="Internal").ap()

    x_d = din("x", [L, 1024], F32)
    p_d = [din("p%d" % l, [L, 256], F32) for l in range(DEPTH)]
    pos_d = din("pos", [L], I32)
    rc_d = din("rc", [128, 4], F32)
    idb_d = din("ident", [128, 128], BF16)
    idf_d = din("identf", [128, 128], F32)
    s5c_d = din("s5c", [128, S5C_W], F32)
    zero_d = din("zeros", [128, 8], BF16)
    wsrc = {}
    wbf = {}
    for l in range(DEPTH):
        for nm, R, N in W0_SPECS:
            wsrc[(nm, l)] = din("%s_%d" % (nm, l), [R, N], F32)
            wbf[(nm, l)] = dint("%s_%d_bf" % (nm, l), [R, N], BF16)
    wfm_d = {(l, hp): din("wfm_%d_%d" % (l, hp), [1024, NFM * 128], F32) for l in range(DEPTH) for hp in range(2)}
    wtm_d = {(l, hp): din("wtm_%d_%d" % (l, hp), [1024, 512], F32) for l in range(DEPTH) for hp in range(2)}
    lamp_d = [din("lamp%d" % l, [4, 64], F32) for l in range(DEPTH)]
    subg_d = [din("subg%d" % l, [128], F32) for l in range(DEPTH)]
    lc_d = [din("lc%d" % l, [128, 2], F32) for l in range(DEPTH)]
    rt_d = [din("rt%d" % hp, [128, RT_W], F32) for hp in range(2)]
    gng_d = [din("gng%d" % l, [64], F32) for l in range(DEPTH)]
    gnb_d = [din("gnb%d" % l, [64], F32) for l in range(DEPTH)]
    s5_d = {(l, hp): (din("s5p_%d_%d" % (l, hp), [128, 49], F32), din("s5bre_%d_%d" % (l, hp), [128, 16, 16], F32),
                      din("s5bim_%d_%d" % (l, hp), [128, 16, 16], F32), din("s5cc_%d_%d" % (l, hp), [128, 2, 128], F32))
            for l in range(DEPTH) for hp in range(2)}
    glb_d = [din("glb%d" % l, [256], F32) for l in range(DEPTH)]
    ln_d = [[din("ln%d_%s%d" % (k, gb, l), [1024], F32) for gb in ("g", "b")] for l in range(DEPTH) for k in (1, 2)]
    cwb_d = [din("cwb%d" % l, [128, NFT, 4], F32) for l in range(DEPTH)]
    out_d = nc.dram_tensor("out", [L, 1024], F32, kind="ExternalOutput").ap()

    xT = dint("xT_s", [1024, L], BF16)
    fmT = dint("fmT_s", [7, 128, L], BF16)
    vr = dint("vr_s", [L, 384], BF16)
    sgd = dint("sg_s", [L, 128], F32)
    ycT = dint("ycT_s", [1024, L], BF16)
    x1 = dint("x1_s", [L, 1024], F32)
    x1Te = dint("x1Te_s", [1024, L + 2], BF16)
    xmid = dint("xmid_s", [L, 1024], F32)
    ycT_r = ycT.rearrange("(r p) t -> r p t", p=128)

    def scope(fn, with_ident=False):
        with ExitStack() as es:
            C = Ctx(nc, es)
            ident = None
            if with_ident:
                ident = C.sb([128, 128], BF16, "ident")
                C.P.dma("sp", ident[:], idb_d[:, :], chan="ident", writes=["ident"])
            fn(C, ident)
            C.P.emit(es, own_sems=True)
        nc.all_engine_barrier()
        nc.clear_and_free_semaphores(C.P.sem_handles)
        nc.all_engine_barrier()

    def zero_halo(C, ident):
        z = C.sb([128, 8], BF16, "z")
        C.P.dma("sp", z[:], zero_d[:, :], chan="z", writes=["z"])
        v = x1Te.rearrange("(c p) t -> p c t", p=128)
        C.P.dma("sp", v[:, :, 0:1], z[:].unsqueeze(2), chan="z2", reads=["z"], slow=True)
        C.P.dma("sp", v[:, :, L + 1:L + 2], z[:].unsqueeze(2), chan="z2", reads=["z"], slow=True)

    scope(lambda C, i: stage_W0v(C, [(wsrc[(nm, l)], wbf[(nm, l)], R, N) for l in range(DEPTH) for nm, R, N in W0_SPECS]))
    scope(zero_halo)
    scope(lambda C, i: stage_T0(C, x_d, xT, i, L), with_ident=True)
    for l in range(DEPTH):
        xin_d = x_d if l == 0 else xmid
        xout_d = xmid if l == 0 else out_d
        for hp in range(2):
            scope(lambda C, i, l=l, hp=hp: stage_A1(C, xT, wfm_d[(l, hp)], wtm_d[(l, hp)], pos_d, rc_d, fmT, vr, sgd, L))
            scope(lambda C, i, l=l, hp=hp: stage_A2(C, fmT, vr, lamp_d[l], subg_d[l], lc_d[l], ycT_r[2 * hp:2 * hp + 2, :, :], i, L), with_ident=True)
            scope(lambda C, i, l=l, hp=hp: stage_A3(C, fmT, vr, sgd, rt_d[hp], gng_d[l], gnb_d[l], ycT_r[4 + hp, :, :], i, L), with_ident=True)
            scope(lambda C, i, l=l, hp=hp: stage_A4(C, fmT, s5_d[(l, hp)][0][:, :], s5_d[(l, hp)][1][:, :, :], s5_d[(l, hp)][2][:, :, :],
                                                   s5_d[(l, hp)][3][:, :, :], s5c_d[:, :], idf_d[:, :], ycT_r[6 + hp, :, :], L))
        scope(lambda C, i, l=l, xin_d=xin_d: stage_P1(C, ycT, xin_d, wbf[("w_out", l)], wbf[("s5_glu_w", l)], glb_d[l], ln_d[2 * l][0], ln_d[2 * l][1],
                                                     x1, x1Te[:, 1:L + 1], i, L), with_ident=True)
        scope(lambda C, i, l=l, xout_d=xout_d: stage_P2(C, x1Te, x1, p_d[l], wbf[("ffn_w_up", l)], wbf[("ffn_w_down", l)], wbf[("ple_w", l)],
                                                       wbf[("ple_gate_w", l)], cwb_d[l][:, :, :], ln_d[2 * l + 1][0], ln_d[2 * l + 1][1], xout_d, xT, i, L),
              with_ident=True)
    return nc


def fused_inputs(inp, b):
    m = {"x": np.ascontiguousarray(inp["x"][b], dtype=np.float32), "pos": np.ascontiguousarray(inp["positions"][b], dtype=np.int32),
         "rc": rc_const(), "ident": np.eye(128, dtype=np.float32).astype(ml_dtypes.bfloat16), "identf": np.eye(128, dtype=np.float32),
         "s5c": s5_consts(), "zeros": np.zeros((128, 8), ml_dtypes.bfloat16)}
    for l in range(DEPTH):
        lam_init = 0.8 - 0.6 * math.exp(-0.3 * l)
        m["p%d" % l] = np.ascontiguousarray(inp["p"][l][b], dtype=np.float32)
        for nm, R, N in W0_SPECS:
            m["%s_%d" % (nm, l)] = np.ascontiguousarray(inp[nm][l], dtype=np.float32)
        for hp in range(2):
            fm, tm = a1_columns(hp)
            m["wfm_%d_%d" % (l, hp)] = np.ascontiguousarray(inp["w_in"][l][:, fm], dtype=np.float32)
            m["wtm_%d_%d" % (l, hp)] = np.ascontiguousarray(inp["w_in"][l][:, tm], dtype=np.float32)
            s5 = s5_layout(inp, l, hp)
            for k in ("s5p", "s5bre", "s5bim", "s5cc"):
                m["%s_%d_%d" % (k, l, hp)] = s5[k]
        m["lamp%d" % l] = np.stack([inp["da_lambda_q1"][l], inp["da_lambda_k1"][l], inp["da_lambda_q2"][l], inp["da_lambda_k2"][l]]).astype(np.float32)
        m["subg%d" % l] = np.ascontiguousarray(inp["da_subln_g"][l], dtype=np.float32)
        m["lc%d" % l] = np.tile(np.array([[lam_init, 1.0 - lam_init]], np.float32), (128, 1))
        m["gng%d" % l] = np.ascontiguousarray(inp["ret_gn_g"][l], dtype=np.float32)
        m["gnb%d" % l] = np.ascontiguousarray(inp["ret_gn_b"][l], dtype=np.float32)
        m["glb%d" % l] = np.ascontiguousarray(inp["s5_glu_b"][l], dtype=np.float32)
        for k in (1, 2):
            m["ln%d_g%d" % (k, l)] = np.ascontiguousarray(inp["ln%d_g" % k][l], dtype=np.float32)
            m["ln%d_b%d" % (k, l)] = np.ascontiguousarray(inp["ln%d_b" % k][l], dtype=np.float32)
        m["cwb%d" % l] = conv_layout(inp["ffn_conv_w"][l], inp["ffn_conv_b"][l])
    for hp in range(2):
        m["rt%d" % hp] = ret_consts(hp)
    return m


def kernel_fused(**inputs):
    inp = {k: np.asarray(v) for k, v in inputs.items()}
    B, L, _ = inp["x"].shape
    NTOK = L // 2
    nc = _prog("fused", build_fused, L)
    bmaps = [fused_inputs(inp, b) for b in range(B)]
    res = _run(nc, [bmaps[c // 2] for c in range(NCORES)])
    out = np.empty((B, L, D_MODEL), np.float32)
    for c in range(NCORES):
        b, h = c // 2, c % 2
        out[b, h * NTOK:(h + 1) * NTOK] = res[c]["out"][h * NTOK:(h + 1) * NTOK]
    return out


_PROGS = {}


def _prog(name, fn, *args):
    key = (name,) + args
    if key not in _PROGS:
        _PROGS[key] = fn(*args)
    return _PROGS[key]


def _run(nc, maps):
    return run_bass_kernel_spmd(nc, maps, core_ids=list(range(NCORES))).results


def kernel_unfused(**inputs):
    inp = {k: np.asarray(v) for k, v in inputs.items()}
    x = np.ascontiguousarray(inp["x"], dtype=np.float32)
    B, L, _ = x.shape
    assert B * 2 == NCORES
    NTOK = L // 2
    ident_bf = np.eye(128, dtype=np.float32).astype(ml_dtypes.bfloat16)
    identf = np.eye(128, dtype=np.float32)
    rc = rc_const()
    s5c = s5_consts()
    cores = [(c // 2, c % 2) for c in range(NCORES)]

    maps = []
    for c in range(NCORES):
        m = {}
        for i in range(DEPTH):
            for nm, R, N in W0_SPECS:
                r = R // NCORES
                m["%s_%d" % (nm, i)] = np.ascontiguousarray(inp[nm][i][c * r:(c + 1) * r], dtype=np.float32)
        maps.append(m)
    res = _run(_prog("W0", build_W0_split), maps)
    wbf = [{nm: np.concatenate([res[c]["%s_%d_bf" % (nm, i)] for c in range(NCORES)], axis=0) for nm, R, N in W0_SPECS}
           for i in range(DEPTH)]

    res = _run(_prog("T0", build_T0, NTOK), [{"x": np.ascontiguousarray(x[b, h * NTOK:(h + 1) * NTOK]), "ident": ident_bf} for b, h in cores])
    xT = [np.concatenate([res[2 * b]["xT"], res[2 * b + 1]["xT"]], axis=1) for b in range(B)]
    xcur = [np.ascontiguousarray(x[b, h * NTOK:(h + 1) * NTOK]) for b, h in cores]

    for i in range(DEPTH):
        lam_init = 0.8 - 0.6 * math.exp(-0.3 * i)
        w_in = inp["w_in"][i]
        maps = []
        for b, h in cores:
            fm, tm = a1_columns(h)
            maps.append({"xT": xT[b], "wfm": np.ascontiguousarray(w_in[:, fm], dtype=np.float32),
                         "wtm": np.ascontiguousarray(w_in[:, tm], dtype=np.float32),
                         "pos": np.ascontiguousarray(inp["positions"][b], dtype=np.int32), "rc": rc})
        a1 = _run(_prog("A1", build_A1, L), maps)
        lamp = np.stack([inp["da_lambda_q1"][i], inp["da_lambda_k1"][i], inp["da_lambda_q2"][i], inp["da_lambda_k2"][i]]).astype(np.float32)
        lc = np.tile(np.array([[lam_init, 1.0 - lam_init]], np.float32), (128, 1))
        a2 = _run(_prog("A2", build_A2, L), [{"fmT": a1[c]["fmT"], "vr": a1[c]["vr"], "lamp": lamp,
                                              "subg": np.ascontiguousarray(inp["da_subln_g"][i], dtype=np.float32), "lc": lc, "ident": ident_bf}
                                             for c in range(NCORES)])
        a3 = _run(_prog("A3", build_A3, L), [{"fmT": a1[c]["fmT"], "vr": a1[c]["vr"], "sg": a1[c]["sg"], "rt": ret_consts(cores[c][1]),
                                              "gng": np.ascontiguousarray(inp["ret_gn_g"][i], dtype=np.float32),
                                              "gnb": np.ascontiguousarray(inp["ret_gn_b"][i], dtype=np.float32), "ident": ident_bf}
                                             for c in range(NCORES)])
        maps = []
        for c, (b, h) in enumerate(cores):
            m = {"fmT": a1[c]["fmT"], "s5c": s5c, "identf": identf}
            m.update(s5_layout(inp, i, h))
            maps.append(m)
        a4 = _run(_prog("A4", build_A4, L), maps)
        maps = []
        for c, (b, h) in enumerate(cores):
            ts = slice(h * NTOK, (h + 1) * NTOK)
            rows = []
            for hp in range(2):
                rows += [a2[2 * b + hp]["ydaT"][0][:, ts], a2[2 * b + hp]["ydaT"][1][:, ts]]
            rows += [a3[2 * b + hp]["yretT"][:, ts] for hp in range(2)]
            rows += [a4[2 * b + hp]["ys5T"][:, ts] for hp in range(2)]
            maps.append({"ycT": np.ascontiguousarray(np.concatenate(rows, axis=0)), "x": xcur[c], "w_out_bf": wbf[i]["w_out"],
                         "s5_glu_w_bf": wbf[i]["s5_glu_w"], "glb": np.ascontiguousarray(inp["s5_glu_b"][i], dtype=np.float32),
                         "ln_g": np.ascontiguousarray(inp["ln1_g"][i], dtype=np.float32),
                         "ln_b": np.ascontiguousarray(inp["ln1_b"][i], dtype=np.float32), "ident": ident_bf})
        p1 = _run(_prog("P1", build_P1, NTOK), maps)
        cwb = conv_layout(inp["ffn_conv_w"][i], inp["ffn_conv_b"][i])
        maps = []
        for c, (b, h) in enumerate(cores):
            ts = slice(h * NTOK, (h + 1) * NTOK)
            ext = np.zeros((1024, NTOK + 2), dtype=ml_dtypes.bfloat16)
            ext[:, 1:NTOK + 1] = p1[c]["x1T"]
            if h == 1:
                ext[:, 0] = p1[c - 1]["x1T"][:, NTOK - 1]
            else:
                ext[:, NTOK + 1] = p1[c + 1]["x1T"][:, 0]
            maps.append({"x1Te": ext, "x1": p1[c]["x1"], "p": np.ascontiguousarray(inp["p"][i][b][ts], dtype=np.float32),
                         "ffn_w_up_bf": wbf[i]["ffn_w_up"], "ffn_w_down_bf": wbf[i]["ffn_w_down"], "ple_w_bf": wbf[i]["ple_w"],
                         "ple_gate_w_bf": wbf[i]["ple_gate_w"], "cwb": cwb,
                         "ln_g": np.ascontiguousarray(inp["ln2_g"][i], dtype=np.float32),
                         "ln_b": np.ascontiguousarray(inp["ln2_b"][i], dtype=np.float32), "ident": ident_bf})
        p2 = _run(_prog("P2", build_P2, NTOK), maps)
        xcur = [p2[c]["x2"] for c in range(NCORES)]
        xT = [np.concatenate([p2[2 * b]["x2T"], p2[2 * b + 1]["x2T"]], axis=1) for b in range(B)]

    out = np.empty((B, L, D_MODEL), np.float32)
    for c, (b, h) in enumerate(cores):
        out[b, h * NTOK:(h + 1) * NTOK] = xcur[c]
    return out


def kernel(**inputs):
    return kernel_fused(**inputs)
```

```python
import math
from contextlib import ExitStack

import numpy as np
import ml_dtypes

import concourse.bass as bass
import concourse.mybir as mybir
from concourse.bass_utils import run_bass_kernel_spmd

F32 = mybir.dt.float32
BF16 = mybir.dt.bfloat16
I32 = mybir.dt.int32
ALU = mybir.AluOpType
AF = mybir.ActivationFunctionType
AX = mybir.AxisListType

D_MODEL = 1024
BATCH = 4
SEQ = 8192
DEPTH = 2
PLE_DIM = 256
D_FF = 2816
LN_EPS = 1e-5
ALPHA = (2 * DEPTH) ** 0.25
COL_DA_Q = 0
COL_DA_K = 512
COL_DA_V = 1024
COL_RET_Q = 1536
COL_RET_K = 1792
COL_RET_V = 2048
COL_RET_G = 2304
COL_S5_U = 2560
NCORES = 8
TWO_PI = 2.0 * math.pi


class Prog:
    ENGS = ("pe", "act", "dve", "pool", "sp")

    def __init__(self, nc):
        self.nc = nc
        self.ops = {e: [] for e in self.ENGS}
        self.cnt = {}
        self.seen = {e: {} for e in self.ENGS}
        self.last_w = {}
        self.readers = {}
        self.n = 0

    def _deps(self, reads, writes):
        raw, other = set(), set()

        def fix(t):
            if isinstance(t[0], tuple):
                t = (t[0], self.cnt[t[0]])
            return t
        for k in reads:
            t = self.last_w.get(k)
            if t is not None:
                raw.add(fix(t))
        for k in writes:
            t = self.last_w.get(k)
            if t is not None:
                other.add(fix(t))
            for t in self.readers.get(k, ()):
                other.add(fix(t))
        return raw, other

    def _commit(self, tok, reads, writes):
        for k in reads:
            self.readers.setdefault(k, []).append(tok)
        for k in writes:
            self.last_w[k] = tok
            self.readers[k] = []

    def _waits(self, eng, deps):
        raw, other = deps
        best = {}
        for src_set, is_raw in ((raw, True), (other, False)):
            for (sk, v) in src_set:
                if sk == eng and eng not in ("act", "dve", "pool"):
                    continue
                if self.seen[eng].get(sk, 0) >= v:
                    continue
                if best.get(sk, 0) < v:
                    best[sk] = v
        for sk, v in best.items():
            self.seen[eng][sk] = v
        return list(best.items())

    def op(self, eng, fn, reads=(), writes=()):
        deps = self._deps(reads, writes)
        waits = self._waits(eng, deps)
        v = self.cnt.get(eng, 0) + 1
        self.cnt[eng] = v
        tok = (eng, v)
        self.ops[eng].append((waits, fn, (eng, 1)))
        self._commit(tok, reads, writes)
        self.n += 1
        return tok

    def dma(self, q, out, in_, chan, reads=(), writes=(), slow=False):
        deps = self._deps(reads, writes)
        waits = self._waits(q, deps)
        sk = ("dma", chan)
        v = self.cnt.get(sk, 0) + 16
        self.cnt[sk] = v
        tok = (sk, v)
        if slow:
            self.ops[q].append((waits, lambda e, o=out, i=in_: e.dma_start(out=o, in_=i, allow_slow_non_contiguous=True), (sk, 16)))
        else:
            self.ops[q].append((waits, lambda e, o=out, i=in_: e.dma_start(out=o, in_=i), (sk, 16)))
        self._commit(tok, reads, writes)
        self.n += 1
        return tok

    def emit(self, es, own_sems=False):
        nc = self.nc
        self.sem_handles = []
        fin = [(sk, v) for sk, v in self.cnt.items() if isinstance(sk, tuple)]
        sems = {}
        for i, sk in enumerate(self.cnt.keys()):
            _UID[0] += 1
            if own_sems:
                sems[sk] = nc.alloc_semaphore(name="s%d_%d" % (i, _UID[0]))
                self.sem_handles.append(sems[sk])
            else:
                sems[sk] = es.enter_context(nc.semaphore("s%d_%d" % (i, _UID[0])))
        block = es.enter_context(nc.Block())

        def replay(name, e):
            for waits, fn, inc in self.ops[name]:
                for sk, v in waits:
                    e.wait_ge(sems[sk], v)
                ins = fn(e)
                ins.then_inc(sems[inc[0]], inc[1])
            if name == "sp":
                for sk, v in fin:
                    e.wait_ge(sems[sk], v)
                for en in ("pe", "act", "dve", "pool"):
                    if self.cnt.get(en, 0):
                        e.wait_ge(sems[en], self.cnt[en])

        @block.sync
        def _(e):
            replay("sp", e)

        @block.tensor
        def _(e):
            replay("pe", e)

        @block.scalar
        def _(e):
            replay("act", e)

        @block.vector
        def _(e):
            replay("dve", e)

        @block.gpsimd
        def _(e):
            replay("pool", e)


_UID = [0]


class Ctx:
    def __init__(self, nc, es):
        self.nc = nc
        self.es = es
        self.P = Prog(nc)
        self._i = 0

    def sb(self, shape, dt, name=None):
        _UID[0] += 1
        return self.es.enter_context(self.nc.sbuf_tensor("%s_%d" % (name or "t", _UID[0]), list(shape), dt))

    def ps(self, shape, dt, name=None):
        _UID[0] += 1
        return self.es.enter_context(self.nc.psum_tensor("%s_%d" % (name or "p", _UID[0]), list(shape), dt))

    def din(self, name, shape, dt):
        return self.nc.dram_tensor(name, list(shape), dt, kind="ExternalInput").ap()

    def dout(self, name, shape, dt):
        return self.nc.dram_tensor(name, list(shape), dt, kind="ExternalOutput").ap()

    def dint(self, name, shape, dt):
        return self.nc.dram_tensor(name, list(shape), dt, kind="Internal").ap()


def bf(a):
    return np.ascontiguousarray(a).astype(ml_dtypes.bfloat16)


def stage_T0(C, x_d, xT_d, ident_bf, ntok, tag="t0"):
    P = C.P
    xin = [C.sb([128, 1024], F32, "xin") for _ in range(2)]
    xbf = [C.sb([128, 1024], BF16, "xbf") for _ in range(2)]
    xT = [C.sb([128, 8, 512], BF16, "xT") for _ in range(2)]
    pst = [C.ps([128, 8, 128], BF16, "pst") for _ in range(2)]
    xT_v = xT_d.rearrange("(c p) t -> p c t", p=128)
    nt = ntok // 128
    for i in range(nt):
        s = i % 2
        g = (i // 4) % 2
        P.dma("sp", xin[s][:], x_d[i * 128:(i + 1) * 128, :], chan=(tag, "xin", s),
              writes=[(tag, "xin", s)])
        P.op("act", lambda e, s=s: e.copy(out=xbf[s][:], in_=xin[s][:]),
             reads=[(tag, "xin", s)], writes=[(tag, "xbf", s)])
        for c in range(8):
            P.op("pe", lambda e, s=s, c=c: e.transpose(out=pst[s][:, c, :], in_=xbf[s][:, c * 128:(c + 1) * 128],
                                                        identity=ident_bf[:]),
                 reads=[(tag, "xbf", s), "ident"], writes=[(tag, "pst", s, c)])
        P.op("dve", lambda e, s=s, g=g, i=i: e.tensor_copy(out=xT[g][:, :, (i % 4) * 128:(i % 4 + 1) * 128], in_=pst[s][:]),
             reads=[(tag, "pst", s, c) for c in range(8)], writes=[(tag, "xT", g, i % 4)])
        if i % 4 == 3:
            t0 = (i // 4) * 512
            P.dma("pool", xT_v[:, :, t0:t0 + 512], xT[g][:], chan=(tag, "xTo", g),
                  reads=[(tag, "xT", g, j) for j in range(4)], writes=[(tag, "xTd", i // 4)])


def build_T0(ntok):
    nc = bass.Bass("TRN2", target_bir_lowering=False)
    with ExitStack() as es:
        C = Ctx(nc, es)
        x_d = C.din("x", [ntok, 1024], F32)
        id_d = C.din("ident", [128, 128], BF16)
        xT_d = C.dout("xT", [1024, ntok], BF16)
        ident = C.sb([128, 128], BF16, "ident")
        C.P.dma("sp", ident[:], id_d[:, :], chan="ident", writes=["ident"])
        stage_T0(C, x_d, xT_d, ident, ntok)
        C.P.emit(es)
    return nc


NFM = 13
ROPED = 6


def a1_columns(h):
    def swap64(cols):
        cols = np.asarray(cols).reshape(-1, 64)
        return np.concatenate([cols[:, 32:], cols[:, :32]], axis=1).reshape(-1)
    tiles = []
    for base in (COL_DA_Q, COL_DA_K):
        for hh in range(2):
            tiles.append(base + (2 * h + hh) * 128 + np.arange(128))
    tiles.append(COL_RET_Q + 2 * h * 64 + np.arange(128))
    tiles.append(COL_RET_K + 2 * h * 64 + np.arange(128))
    sw = [swap64(t) for t in tiles]
    u = COL_S5_U + 128 * h + np.arange(128)
    fm = np.concatenate(tiles + sw + [u])
    tm = np.concatenate([COL_DA_V + 2 * h * 128 + np.arange(256),
                         COL_RET_V + 2 * h * 64 + np.arange(128),
                         COL_RET_G + 2 * h * 64 + np.arange(128)])
    return fm, tm


def rope_consts():
    inv = 10000.0 ** (-np.arange(0, 64, 2, dtype=np.float64) / 64.0)
    p = np.arange(128)
    invf = inv[p % 32].astype(np.float32).reshape(128, 1)
    sgn = np.where((p % 64) < 32, -1.0, 1.0).astype(np.float32).reshape(128, 1)
    return invf, sgn


CW1 = float(np.float32(6.28125))
CW2 = float(np.float32(TWO_PI - 6.28125))


def sincos(P, eng, tag, ang, ang_key, sarg, carg, tmp_i, tmp_f):
    ki, kf, ks, kc = (tag, "rr_i"), (tag, "rr_f"), (tag, "sarg"), (tag, "carg")
    P.op(eng, lambda e: e.tensor_scalar(out=tmp_i[:], in0=ang[:], scalar1=1.0 / TWO_PI, scalar2=None, op0=ALU.mult),
         reads=[ang_key], writes=[ki])
    P.op(eng, lambda e: e.tensor_copy(out=tmp_f[:], in_=tmp_i[:]), reads=[ki], writes=[kf])
    P.op(eng, lambda e: e.scalar_tensor_tensor(out=sarg[:], in0=tmp_f[:], scalar=-CW1, in1=ang[:], op0=ALU.mult, op1=ALU.add),
         reads=[kf, ang_key], writes=[ks])
    P.op(eng, lambda e: e.scalar_tensor_tensor(out=sarg[:], in0=tmp_f[:], scalar=-CW2, in1=sarg[:], op0=ALU.mult, op1=ALU.add),
         reads=[kf, ks], writes=[ks])

    def wrap(t, key):
        P.op(eng, lambda e: e.tensor_scalar(out=tmp_f[:], in0=t[:], scalar1=math.pi, scalar2=-TWO_PI, op0=ALU.is_gt, op1=ALU.mult),
             reads=[key], writes=[kf])
        P.op(eng, lambda e: e.tensor_tensor(out=t[:], in0=t[:], in1=tmp_f[:], op=ALU.add), reads=[key, kf], writes=[key])
        P.op(eng, lambda e: e.tensor_scalar(out=tmp_f[:], in0=t[:], scalar1=-math.pi, scalar2=TWO_PI, op0=ALU.is_lt, op1=ALU.mult),
             reads=[key], writes=[kf])
        P.op(eng, lambda e: e.tensor_tensor(out=t[:], in0=t[:], in1=tmp_f[:], op=ALU.add), reads=[key, kf], writes=[key])
    wrap(sarg, ks)
    P.op(eng, lambda e: e.tensor_scalar(out=carg[:], in0=sarg[:], scalar1=0.5 * math.pi, scalar2=None, op0=ALU.add),
         reads=[ks], writes=[kc])
    wrap(carg, kc)


def stage_A1(C, xT_d, wfm_d, wtm_d, pos_d, rc_d, fmT_d, vr_d, sg_d, L, tag="a1"):
    P = C.P
    NW = NFM * 128
    wfm = C.sb([128, 8, NW], BF16, "wfm")
    wtm = C.sb([128, 8, 512], BF16, "wtm")
    stg = [C.sb([128, NW], F32, "wstg") for _ in range(2)]
    rc = C.sb([128, 4], F32, "rc")
    P.dma("sp", rc[:], rc_d[:, :], chan=(tag, "rc"), writes=[(tag, "rc")])
    wfm_v = wfm_d.rearrange("(c p) n -> p c n", p=128)
    wtm_v = wtm_d.rearrange("(c p) n -> p c n", p=128)
    for c in range(8):
        s = c % 2
        P.dma("sp", stg[s][:], wfm_v[:, c, :], chan=(tag, "wstg", s), writes=[(tag, "wstg", s)])
        P.op("pool", lambda e, s=s, c=c: e.tensor_copy(out=wfm[:, c, :], in_=stg[s][:]),
             reads=[(tag, "wstg", s)], writes=[(tag, "wfm", c)])
    for c in range(8):
        s = c % 2
        P.dma("sp", stg[s][:, 0:512], wtm_v[:, c, :], chan=(tag, "wstg", s), writes=[(tag, "wstg", s)])
        P.op("pool", lambda e, s=s, c=c: e.tensor_copy(out=wtm[:, c, :], in_=stg[s][:, 0:512]),
             reads=[(tag, "wstg", s)], writes=[(tag, "wtm", c)])
    wfm_k = [(tag, "wfm", c) for c in range(8)]
    wtm_k = [(tag, "wtm", c) for c in range(8)]

    xT = [C.sb([128, 8, 512], BF16, "xT") for _ in range(2)]
    posi = [C.sb([128, 512], I32, "posi") for _ in range(2)]
    posf = C.sb([128, 512], F32, "posf")
    a0 = C.sb([128, 512], F32, "a0")
    sarg = C.sb([128, 512], F32, "sarg")
    carg = C.sb([128, 512], F32, "carg")
    rr_i = C.sb([128, 512], I32, "rr_i")
    rr_f = C.sb([128, 512], F32, "rr_f")
    cosT = [C.sb([128, 512], F32, "cosT") for _ in range(2)]
    sinT = [C.sb([128, 512], F32, "sinT") for _ in range(2)]
    t1 = [C.sb([128, 512], F32, "t1") for _ in range(2)]
    t2 = [C.sb([128, 512], F32, "t2") for _ in range(2)]
    fmo = [C.sb([128, 7, 512], BF16, "fmo") for _ in range(2)]
    tmo = [C.sb([128, 4, 384], BF16, "tmo") for _ in range(2)]
    sgo = [C.sb([128, 4, 128], F32, "sgo") for _ in range(2)]
    psA = [C.ps([128, 512], F32, "psA") for _ in range(2)]
    psB = [C.ps([128, 512], F32, "psB") for _ in range(2)]
    psC = [C.ps([128, 512], F32, "psC") for _ in range(3)]
    xT_v = xT_d.rearrange("(c p) t -> p c t", p=128)
    fmT_v = fmT_d.rearrange("r p t -> p r t")
    nt = L // 512
    pc = 0
    cc = 0
    for it in range(nt):
        s = it % 2
        tk0 = it * 512
        P.dma("sp", xT[s][:], xT_v[:, :, tk0:tk0 + 512], chan=(tag, "xT", s), writes=[(tag, "xT", s)])
        P.dma("sp", posi[s][:], pos_d[tk0:tk0 + 512].partition_broadcast(128), chan=(tag, "pos", s),
              writes=[(tag, "pos", s)])
        P.op("dve", lambda e, s=s: e.tensor_copy(out=posf[:], in_=posi[s][:]),
             reads=[(tag, "pos", s)], writes=[(tag, "posf")])
        P.op("dve", lambda e: e.tensor_scalar(out=a0[:], in0=posf[:], scalar1=rc[:, 0:1], scalar2=None, op0=ALU.mult),
             reads=[(tag, "posf"), (tag, "rc")], writes=[(tag, "a0")])
        sincos(P, "dve", tag, a0, (tag, "a0"), sarg, carg, rr_i, rr_f)
        P.op("act", lambda e, s=s: e.activation(out=sinT[s][:], in_=sarg[:], func=AF.Sin, scale=rc[:, 1:2]),
             reads=[(tag, "sarg"), (tag, "rc")], writes=[(tag, "sinT", s)])
        P.op("act", lambda e, s=s: e.activation(out=cosT[s][:], in_=carg[:], func=AF.Sin),
             reads=[(tag, "carg")], writes=[(tag, "cosT", s)])
        for r in range(ROPED):
            b = pc % 2
            pc += 1
            for c in range(8):
                P.op("pe", lambda e, b=b, r=r, c=c, s=s: e.matmul(psA[b][:], lhsT=wfm[:, c, r * 128:(r + 1) * 128],
                                                                 rhs=xT[s][:, c, :], start=(c == 0), stop=(c == 7)),
                     reads=[(tag, "xT", s), (tag, "wfm", c)], writes=[(tag, "psA", b)])
            for c in range(8):
                P.op("pe", lambda e, b=b, r=r, c=c, s=s: e.matmul(psB[b][:], lhsT=wfm[:, c, (ROPED + r) * 128:(ROPED + r + 1) * 128],
                                                                 rhs=xT[s][:, c, :], start=(c == 0), stop=(c == 7)),
                     reads=[(tag, "xT", s), (tag, "wfm", c)], writes=[(tag, "psB", b)])
            P.op("dve", lambda e, b=b, s=s: e.tensor_tensor(out=t1[b][:], in0=psA[b][:], in1=cosT[s][:], op=ALU.mult),
                 reads=[(tag, "psA", b), (tag, "cosT", s)], writes=[(tag, "t1", b)])
            P.op("dve", lambda e, b=b, s=s: e.tensor_tensor(out=t2[b][:], in0=psB[b][:], in1=sinT[s][:], op=ALU.mult),
                 reads=[(tag, "psB", b), (tag, "sinT", s)], writes=[(tag, "t2", b)])
            P.op("pool", lambda e, b=b, s=s, r=r: e.tensor_tensor(out=fmo[s][:, r, :], in0=t1[b][:], in1=t2[b][:], op=ALU.add),
                 reads=[(tag, "t1", b), (tag, "t2", b)], writes=[(tag, "fmo", s, r)])
        b = cc % 3
        cc += 1
        for c in range(8):
            P.op("pe", lambda e, b=b, c=c, s=s: e.matmul(psC[b][:], lhsT=wfm[:, c, 12 * 128:13 * 128],
                                                        rhs=xT[s][:, c, :], start=(c == 0), stop=(c == 7)),
                 reads=[(tag, "xT", s), (tag, "wfm", c)], writes=[(tag, "psC", b)])
        P.op("act", lambda e, b=b, s=s: e.copy(out=fmo[s][:, 6, :], in_=psC[b][:]),
             reads=[(tag, "psC", b)], writes=[(tag, "fmo", s, 6)])
        P.dma("pool", fmT_v[:, :, tk0:tk0 + 512], fmo[s][:], chan=(tag, "fmo", s),
              reads=[(tag, "fmo", s, r) for r in range(7)], writes=[(tag, "fmT", it)])
        for j in range(4):
            b = cc % 3
            cc += 1
            for c in range(8):
                P.op("pe", lambda e, b=b, c=c, s=s, j=j: e.matmul(psC[b][:], lhsT=xT[s][:, c, j * 128:(j + 1) * 128],
                                                                 rhs=wtm[:, c, :], start=(c == 0), stop=(c == 7)),
                     reads=[(tag, "xT", s), (tag, "wtm", c)], writes=[(tag, "psC", b)])
            P.op("act", lambda e, b=b, s=s, j=j: e.copy(out=tmo[s][:, j, :], in_=psC[b][:, 0:384]),
                 reads=[(tag, "psC", b)], writes=[(tag, "tmo", s, j)])
            P.op("act", lambda e, b=b, s=s, j=j: e.activation(out=sgo[s][:, j, :], in_=psC[b][:, 384:512], func=AF.Silu),
                 reads=[(tag, "psC", b)], writes=[(tag, "sgo", s, j)])
        P.dma("pool", vr_d[tk0:tk0 + 512, :].rearrange("(j p) c -> p j c", p=128), tmo[s][:], chan=(tag, "tmo", s),
              reads=[(tag, "tmo", s, j) for j in range(4)], writes=[(tag, "vr", it)])
        P.dma("pool", sg_d[tk0:tk0 + 512, :].rearrange("(j p) c -> p j c", p=128), sgo[s][:], chan=(tag, "sgo", s),
              reads=[(tag, "sgo", s, j) for j in range(4)], writes=[(tag, "sg", it)])


def build_A1(L):
    nc = bass.Bass("TRN2", target_bir_lowering=False)
    with ExitStack() as es:
        C = Ctx(nc, es)
        xT_d = C.din("xT", [1024, L], BF16)
        wfm_d = C.din("wfm", [1024, NFM * 128], F32)
        wtm_d = C.din("wtm", [1024, 512], F32)
        pos_d = C.din("pos", [L], I32)
        rc_d = C.din("rc", [128, 4], F32)
        fmT_d = C.dout("fmT", [7, 128, L], BF16)
        vr_d = C.dout("vr", [L, 384], BF16)
        sg_d = C.dout("sg", [L, 128], F32)
        stage_A1(C, xT_d, wfm_d, wtm_d, pos_d, rc_d, fmT_d, vr_d, sg_d, L)
        C.P.emit(es)
    return nc


def rc_const():
    invf, sgn = rope_consts()
    return np.concatenate([invf, sgn, -math.pi * sgn, np.full((128, 1), -math.pi, np.float32)], axis=1).astype(np.float32)


def stage_A2(C, fmT_d, vr_d, lamp_d, subg_d, lc_d, ydaT_d, ident, L, tag="a2", dbg=None):
    P = C.P
    NKB = L // 128
    NQB = L // 512
    lamp = C.sb([128, 4, 64], F32, "lamp")
    lc = C.sb([128, 2], F32, "lc")
    gcol = C.sb([128, 1], F32, "gcol")
    lprod = C.sb([128, 2, 64], F32, "lprod")
    lsum = C.sb([128, 2], F32, "lsum")
    lexp = C.sb([128, 2], F32, "lexp")
    neglam = C.sb([128, 1], F32, "neglam")
    epsb = C.sb([128, 1], F32, "epsb")
    ones = C.sb([128, 128], F32, "ones")
    P.dma("sp", lamp[:], lamp_d.rearrange("a d -> (a d)").partition_broadcast(128).rearrange("p (a d) -> p a d", a=4),
          chan=(tag, "c0"), writes=[(tag, "lamp")])
    P.dma("sp", lc[:], lc_d[:, :], chan=(tag, "c0"), writes=[(tag, "lc")])
    P.dma("sp", gcol[:], subg_d.rearrange("(p o) -> p o", o=1), chan=(tag, "c0"), writes=[(tag, "gcol")])
    P.op("dve", lambda e: e.tensor_tensor(out=lprod[:, 0, :], in0=lamp[:, 0, :], in1=lamp[:, 1, :], op=ALU.mult),
         reads=[(tag, "lamp")], writes=[(tag, "lprod")])
    P.op("dve", lambda e: e.tensor_tensor(out=lprod[:, 1, :], in0=lamp[:, 2, :], in1=lamp[:, 3, :], op=ALU.mult),
         reads=[(tag, "lamp")], writes=[(tag, "lprod")])
    P.op("dve", lambda e: e.tensor_reduce(out=lsum[:], in_=lprod[:], axis=AX.X, op=ALU.add),
         reads=[(tag, "lprod")], writes=[(tag, "lsum")])
    P.op("act", lambda e: e.activation(out=lexp[:], in_=lsum[:], func=AF.Exp), reads=[(tag, "lsum")], writes=[(tag, "lexp")])
    P.op("dve", lambda e: e.tensor_tensor(out=neglam[:], in0=lexp[:, 1:2], in1=lexp[:, 0:1], op=ALU.subtract),
         reads=[(tag, "lexp")], writes=[(tag, "neglam")])
    P.op("dve", lambda e: e.tensor_tensor(out=neglam[:], in0=neglam[:], in1=lc[:, 0:1], op=ALU.subtract),
         reads=[(tag, "neglam"), (tag, "lc")], writes=[(tag, "neglam")])
    P.op("dve", lambda e: e.tensor_tensor(out=gcol[:], in0=gcol[:], in1=lc[:, 1:2], op=ALU.mult),
         reads=[(tag, "gcol"), (tag, "lc")], writes=[(tag, "gcol")])
    P.op("dve", lambda e: e.memset(epsb[:], 1e-6), writes=[(tag, "epsb")])
    P.op("pool", lambda e: e.memset(ones[:], 1.0), writes=[(tag, "ones")])

    qT = [C.sb([128, L], BF16, "qT") for _ in range(2)]
    kT = [[C.sb([128, L], BF16, "kT") for _ in range(2)] for _ in range(2)]
    va = [C.sb([128, NKB, 128], BF16, "va") for _ in range(2)]
    yT = [C.sb([128, L], BF16, "yT") for _ in range(2)]
    NPT = 4
    PT = [C.sb([128, 512], BF16, "PT") for _ in range(NPT)]
    accS = [[C.sb([128, 512], F32, "accS") for _ in range(2)] for _ in range(2)]
    oacc = C.sb([128, 2, 512], F32, "oacc")
    rinv = C.sb([128, 2, 512], F32, "rinv")
    o = C.sb([128, 512], F32, "o")
    t2 = C.sb([128, 512], F32, "t2")
    sq = C.sb([128, 512], F32, "sq")
    rs = C.sb([128, 512], F32, "rs")
    S = [C.ps([128, 512], F32, "S") for _ in range(3)]
    accO = [C.ps([128, 512], F32, "accO") for _ in range(2)]
    Rps = [C.ps([128, 512], F32, "Rps") for _ in range(2)]

    for h in range(2):
        P.dma("sp", qT[h][:], fmT_d[h, :, :], chan=(tag, "ld", h), writes=[(tag, "qT", h)])
        for m in range(2):
            P.op("pool", lambda e, h=h, m=m: e.memset(kT[h][m][(1 - m) * 64:(2 - m) * 64, :], 0.0), writes=[(tag, "kTz", h, m)])
            P.dma("sp", kT[h][m][m * 64:(m + 1) * 64, :], fmT_d[2 + h, m * 64:(m + 1) * 64, :], chan=(tag, "ld", h), writes=[(tag, "kT", h)])
        P.dma("sp", va[h][:], vr_d[:, h * 128:(h + 1) * 128].rearrange("(kb p) c -> p kb c", p=128),
              chan=(tag, "ld", h), writes=[(tag, "va", h)])
    step = 0
    blk = 0
    for h in range(2):
        for qb in range(NQB):
            q0 = qb * 512
            nst = NKB * 2
            par = blk % 2
            blk += 1

            def qk(i, h=h, q0=q0, step=step):
                kb, m = i // 2, i % 2
                sl = (step + i) % 3
                P.op("pe", lambda e: e.matmul(S[sl][:], lhsT=kT[h][m][:, kb * 128:(kb + 1) * 128],
                                               rhs=qT[h][:, q0:q0 + 512], start=True, stop=True),
                     reads=[(tag, "kT", h), (tag, "kTz", h, 0), (tag, "kTz", h, 1), (tag, "qT", h)], writes=[(tag, "S", sl)])

            def ex(i, step=step):
                sl = (step + i) % 3
                pl = (step + i) % NPT
                P.op("act", lambda e: e.activation(out=PT[pl][:], in_=S[sl][:], func=AF.Exp, scale=0.125),
                     writes=[(tag, "PT", pl), (tag, "S", sl)])

            def av(i, h=h, step=step, par=par):
                kb, m = i // 2, i % 2
                pl = (step + i) % NPT
                P.op("pe", lambda e: e.matmul(accO[m][:], lhsT=va[h][:, kb, :], rhs=PT[pl][:], start=(kb == 0), stop=(kb == NKB - 1)),
                     reads=[(tag, "PT", pl), (tag, "va", h)], writes=[(tag, "accO", m)])
                if kb == 0:
                    P.op("dve", lambda e: e.tensor_copy(out=accS[par][m][:], in_=PT[pl][:]), reads=[(tag, "PT", pl)], writes=[(tag, "accS", par, m)])
                else:
                    P.op("dve", lambda e: e.tensor_tensor(out=accS[par][m][:], in0=accS[par][m][:], in1=PT[pl][:], op=ALU.add),
                         reads=[(tag, "PT", pl), (tag, "accS", par, m)], writes=[(tag, "accS", par, m)])

            qk(0); ex(0)
            qk(1); ex(1)
            for i in range(nst):
                if i + 2 < nst:
                    qk(i + 2); ex(i + 2)
                av(i)
            step += nst

            def post(h=h, q0=q0, par=par):
                for m in range(2):
                    P.op("pe", lambda e, m=m: e.matmul(Rps[m][:], lhsT=ones[:], rhs=accS[par][m][:], start=True, stop=True),
                         reads=[(tag, "ones"), (tag, "accS", par, m)], writes=[(tag, "Rps", m)])
                P.op("act", lambda e: e.copy(out=oacc[:, 0, :], in_=accO[0][:]), writes=[(tag, "oacc", 0), (tag, "accO", 0)])
                P.op("dve", lambda e: e.tensor_copy(out=oacc[:, 1, :], in_=accO[1][:]), writes=[(tag, "oacc", 1), (tag, "accO", 1)])
                for m in range(2):
                    P.op("dve", lambda e, m=m: e.reciprocal(out=rinv[:, m, :], in_=Rps[m][:]), writes=[(tag, "rinv", m), (tag, "Rps", m)])
                P.op("dve", lambda e: e.tensor_tensor(out=o[:], in0=oacc[:, 0, :], in1=rinv[:, 0, :], op=ALU.mult),
                     reads=[(tag, "oacc", 0), (tag, "rinv", 0)], writes=[(tag, "o")])
                P.op("pool", lambda e: e.tensor_tensor(out=t2[:], in0=oacc[:, 1, :], in1=rinv[:, 1, :], op=ALU.mult),
                     reads=[(tag, "oacc", 1), (tag, "rinv", 1)], writes=[(tag, "t2")])
                P.op("dve", lambda e: e.scalar_tensor_tensor(out=o[:], in0=t2[:], scalar=neglam[:, 0:1], in1=o[:], op0=ALU.mult, op1=ALU.add),
                     reads=[(tag, "t2"), (tag, "o"), (tag, "neglam")], writes=[(tag, "o")])
                P.op("act", lambda e: e.activation(out=sq[:], in_=o[:], func=AF.Square), reads=[(tag, "o")], writes=[(tag, "sq")])
                P.op("pe", lambda e: e.matmul(Rps[0][:], lhsT=ones[:], rhs=sq[:], start=True, stop=True),
                     reads=[(tag, "ones"), (tag, "sq")], writes=[(tag, "Rps", 0)])
                P.op("act", lambda e: e.activation(out=rs[:], in_=Rps[0][:], func=AF.Sqrt, bias=epsb[:, 0:1], scale=1.0 / 128.0),
                     reads=[(tag, "epsb")], writes=[(tag, "rs"), (tag, "Rps", 0)])
                P.op("dve", lambda e: e.reciprocal(out=rs[:], in_=rs[:]), reads=[(tag, "rs")], writes=[(tag, "rs")])
                P.op("dve", lambda e: e.scalar_tensor_tensor(out=yT[h][:, q0:q0 + 512], in0=o[:], scalar=gcol[:, 0:1], in1=rs[:],
                                                              op0=ALU.mult, op1=ALU.mult),
                     reads=[(tag, "o"), (tag, "rs"), (tag, "gcol")], writes=[(tag, "yT", h)])
            post()
        P.dma("sp", ydaT_d[h, :, :], yT[h][:], chan=(tag, "yo", h), reads=[(tag, "yT", h)], writes=[(tag, "ydaT", h)])


def build_A2(L, debug=False):
    nc = bass.Bass("TRN2", target_bir_lowering=False)
    with ExitStack() as es:
        C = Ctx(nc, es)
        fmT_d = C.din("fmT", [7, 128, L], BF16)
        vr_d = C.din("vr", [L, 384], BF16)
        lamp_d = C.din("lamp", [4, 64], F32)
        subg_d = C.din("subg", [128], F32)
        lc_d = C.din("lc", [128, 2], F32)
        id_d = C.din("ident", [128, 128], BF16)
        ydaT_d = C.dout("ydaT", [2, 128, L], BF16)
        ident = C.sb([128, 128], BF16, "ident")
        C.P.dma("sp", ident[:], id_d[:, :], chan="ident", writes=["ident"])
        dbg = None
        stage_A2(C, fmT_d, vr_d, lamp_d, subg_d, lc_d, ydaT_d, ident, L, dbg=dbg)
        C.P.emit(es)
    return nc


RT_W = 256 + 128 + 128 + 4 + 1


def ret_consts(h):
    idx = np.arange(128, dtype=np.float64)
    out = np.zeros((128, RT_W), np.float64)
    for hh in range(2):
        lg = math.log(1.0 - 2.0 ** (-5.0 - (2 * h + hh)))
        out[:, hh * 128:(hh + 1) * 128] = np.exp(lg * np.abs(idx[None, :] - idx[:, None])) / 8.0
        out[:, 256 + hh * 64:256 + (hh + 1) * 64] = (np.exp(lg * (127 - idx)) / 8.0)[:, None]
        out[:, 384 + hh * 64:384 + (hh + 1) * 64] = (np.exp(lg * idx) / 8.0)[:, None]
        out[:, 512 + 2 * hh] = np.exp(lg * (idx + 1))
        out[:, 512 + 2 * hh + 1] = np.exp(lg * (128 - idx))
        out[hh * 64:(hh + 1) * 64, 516] = math.exp(lg * 128)
    return out.astype(np.float32)


def stage_A3(C, fmT_d, vr_d, sg_d, rt_d, gng_d, gnb_d, yretT_d, ident, L, tag="a3"):
    P = C.P
    NCH = L // 128
    rt = C.sb([128, RT_W], F32, "rt")
    gng = C.sb([128, 2, 64], F32, "gng")
    gnb = C.sb([128, 2, 64], F32, "gnb")
    epsb = C.sb([128, 1], F32, "epsb")
    P.dma("sp", rt[:], rt_d[:, :], chan=(tag, "c0"), writes=[(tag, "rt")])
    for hh in range(2):
        P.dma("sp", gng[:, hh, :], gng_d.partition_broadcast(128), chan=(tag, "c0"), writes=[(tag, "gng")])
        P.dma("sp", gnb[:, hh, :], gnb_d.partition_broadcast(128), chan=(tag, "c0"), writes=[(tag, "gnb")])
    P.op("dve", lambda e: e.memset(epsb[:], LN_EPS), writes=[(tag, "epsb")])
    Dm = rt[:, 0:256].rearrange("p (a n) -> p a n", a=2)
    decF = rt[:, 256:384]
    decB = rt[:, 384:512]
    gC = rt[:, 516:517]

    rqT = C.sb([128, L], BF16, "rqT")
    rkT = C.sb([128, L], BF16, "rkT")
    rv = C.sb([128, NCH, 128], BF16, "rv")
    sg = C.sb([128, NCH, 128], F32, "sg")
    yT = C.sb([128, L], BF16, "yT")
    Gs = C.sb([128, NCH, 64], BF16, "Gs")
    G = C.sb([128, 64], F32, "G")
    F = C.sb([128, 64], F32, "F")
    Fbf = [C.sb([128, 64], BF16, "Fbf") for _ in range(2)]
    kd = [C.sb([128, 128], BF16, "kd") for _ in range(2)]
    Sm = [C.sb([128, 2, 128], BF16, "Sm") for _ in range(2)]
    t = [C.sb([128, 2, 64], F32, "t") for _ in range(2)]
    st = C.sb([128, 2, 6], F32, "st")
    mv = C.sb([128, 2, 2], F32, "mv")
    rstd = C.sb([128, 2], F32, "rstd")
    ybf = [C.sb([128, 128], BF16, "ybf") for _ in range(2)]
    pk = [C.ps([128, 128], BF16, "pk") for _ in range(2)]
    pkv = [C.ps([128, 128], F32, "pkv") for _ in range(2)]
    pS = [C.ps([128, 128], F32, "pS") for _ in range(2)]
    po = [C.ps([128, 3, 64], F32, "po") for _ in range(2)]
    P.dma("sp", rqT[:], fmT_d[4, :, :], chan=(tag, "ld"), writes=[(tag, "rqT")])
    P.dma("sp", rkT[:], fmT_d[5, :, :], chan=(tag, "ld"), writes=[(tag, "rkT")])
    P.dma("sp", rv[:], vr_d[:, 256:384].rearrange("(c p) n -> p c n", p=128), chan=(tag, "ld"), writes=[(tag, "rv")])
    P.dma("sp", sg[:], sg_d.rearrange("(c p) n -> p c n", p=128), chan=(tag, "ld"), writes=[(tag, "sg")])
    P.op("dve", lambda e: e.memset(G[:], 0.0), writes=[(tag, "G")])
    P.op("dve", lambda e: e.memset(F[:], 0.0), writes=[(tag, "F")])

    def kv_step(c, dec, state, skey, i):
        s = i % 2
        P.op("pe", lambda e: e.transpose(out=pk[s][:], in_=rkT[:, c * 128:(c + 1) * 128], identity=ident[:]),
             reads=[(tag, "rkT"), "ident"], writes=[(tag, "pk", s)])
        P.op("dve", lambda e: e.tensor_tensor(out=kd[s][:], in0=pk[s][:], in1=dec, op=ALU.mult),
             reads=[(tag, "rt")], writes=[(tag, "kd", s), (tag, "pk", s)])
        P.op("pe", lambda e: e.matmul(pkv[s][:], lhsT=kd[s][:], rhs=rv[:, c, :], start=True, stop=True),
             reads=[(tag, "kd", s), (tag, "rv")], writes=[(tag, "pkv", s)])
        for hh in range(2):
            hs = slice(hh * 64, (hh + 1) * 64)
            P.op("dve", lambda e, hs=hs: e.scalar_tensor_tensor(out=state[hs, :], in0=state[hs, :], scalar=gC[hs, :], in1=pkv[s][hs, hs],
                                                                 op0=ALU.mult, op1=ALU.add),
                 reads=[skey, (tag, "rt")], writes=[skey, (tag, "pkv", s)])

    it = 0
    for c in range(NCH - 1, -1, -1):
        P.op("act", lambda e, c=c: e.copy(out=Gs[:, c, :], in_=G[:]), reads=[(tag, "G")], writes=[(tag, "Gs", c)])
        if c > 0:
            kv_step(c, decB, G, (tag, "G"), it)
            it += 1
    def chunk2(c, it):
        s = c % 2
        cs = slice(c * 128, (c + 1) * 128)
        P.op("act", lambda e, s=s: e.copy(out=Fbf[s][:], in_=F[:]), reads=[(tag, "F")], writes=[(tag, "Fbf", s)])
        for hh in range(2):
            hs = slice(hh * 64, (hh + 1) * 64)
            P.op("pe", lambda e, hh=hh, hs=hs: e.matmul(pS[hh][:], lhsT=rkT[hs, cs], rhs=rqT[hs, cs], start=True, stop=True),
                 reads=[(tag, "rkT"), (tag, "rqT")], writes=[(tag, "pS", hh)])
            P.op("dve", lambda e, s=s, hh=hh: e.tensor_tensor(out=Sm[s][:, hh, :], in0=pS[hh][:], in1=Dm[:, hh, :], op=ALU.mult),
                 reads=[(tag, "rt")], writes=[(tag, "Sm", s, hh), (tag, "pS", hh)])
        for hh in range(2):
            hs = slice(hh * 64, (hh + 1) * 64)
            P.op("pe", lambda e, hh=hh, hs=hs: e.matmul(po[hh][:, 0, :], lhsT=Sm[s][:, hh, :], rhs=rv[:, c, hs], start=True, stop=False),
                 reads=[(tag, "Sm", s, hh), (tag, "rv")], writes=[(tag, "po", hh)])
            P.op("pe", lambda e, hh=hh, hs=hs: e.matmul(po[hh][:, 1, :], lhsT=rqT[hs, cs], rhs=Fbf[s][hs, :], start=False, stop=False),
                 reads=[(tag, "rqT"), (tag, "Fbf", s)], writes=[(tag, "po", hh)])
            P.op("pe", lambda e, hh=hh, hs=hs, c=c: e.matmul(po[hh][:, 2, :], lhsT=rqT[hs, cs], rhs=Gs[hs, c, :], start=False, stop=True),
                 reads=[(tag, "rqT"), (tag, "Gs", c)], writes=[(tag, "po", hh)])
        if c < NCH - 1:
            kv_step(c, decF, F, (tag, "F"), it)
        for hh in range(2):
            P.op("dve", lambda e, s=s, hh=hh: e.tensor_copy(out=t[s][:, hh, :], in_=po[hh][:, 0, :]),
                 writes=[(tag, "t", s), (tag, "po", hh)])
            for j in (1, 2):
                P.op("dve", lambda e, s=s, hh=hh, j=j: e.scalar_tensor_tensor(
                    out=t[s][:, hh, :], in0=po[hh][:, j, :], scalar=rt[:, 512 + 2 * hh + j - 1:512 + 2 * hh + j],
                    in1=t[s][:, hh, :], op0=ALU.mult, op1=ALU.add),
                    reads=[(tag, "t", s), (tag, "rt")], writes=[(tag, "t", s), (tag, "po", hh)])
        for hh in range(2):
            P.op("dve", lambda e, s=s, hh=hh: e.bn_stats(out=st[:, hh, :], in_=t[s][:, hh, :]), reads=[(tag, "t", s)], writes=[(tag, "st")])
            P.op("dve", lambda e, hh=hh: e.bn_aggr(out=mv[:, hh, :], in_=st[:, hh, :]), reads=[(tag, "st")], writes=[(tag, "mv")])
        P.op("act", lambda e: e.activation(out=rstd[:], in_=mv[:, :, 1], func=AF.Sqrt, bias=epsb[:, 0:1], scale=1.0),
             reads=[(tag, "mv"), (tag, "epsb")], writes=[(tag, "rstd")])
        P.op("dve", lambda e: e.reciprocal(out=rstd[:], in_=rstd[:]), reads=[(tag, "rstd")], writes=[(tag, "rstd")])
        for hh in range(2):
            P.op("dve", lambda e, s=s, hh=hh: e.tensor_scalar(out=t[s][:, hh, :], in0=t[s][:, hh, :], scalar1=mv[:, hh, 0:1],
                                                              scalar2=rstd[:, hh:hh + 1], op0=ALU.subtract, op1=ALU.mult),
                 reads=[(tag, "t", s), (tag, "mv"), (tag, "rstd")], writes=[(tag, "t", s)])
        P.op("pool", lambda e, s=s: e.tensor_tensor(out=t[s][:], in0=t[s][:], in1=gng[:], op=ALU.mult),
             reads=[(tag, "t", s), (tag, "gng")], writes=[(tag, "t", s)])
        P.op("pool", lambda e, s=s: e.tensor_tensor(out=t[s][:], in0=t[s][:], in1=gnb[:], op=ALU.add),
             reads=[(tag, "t", s), (tag, "gnb")], writes=[(tag, "t", s)])
        P.op("pool", lambda e, s=s, c=c: e.tensor_tensor(out=ybf[s][:], in0=t[s][:].rearrange("p a b -> p (a b)"), in1=sg[:, c, :], op=ALU.mult),
             reads=[(tag, "t", s), (tag, "sg")], writes=[(tag, "ybf", s)])
        P.op("pe", lambda e, s=s: e.transpose(out=pk[s][:], in_=ybf[s][:], identity=ident[:]),
             reads=[(tag, "ybf", s), "ident"], writes=[(tag, "pk", s)])
        P.op("act", lambda e, s=s, cs=cs: e.copy(out=yT[:, cs], in_=pk[s][:]), writes=[(tag, "yT"), (tag, "pk", s)])
    for c in range(NCH):
        chunk2(c, it)
        it += 1
    P.dma("pool", yretT_d[:, :], yT[:], chan=(tag, "yo"), reads=[(tag, "yT")], writes=[(tag, "yretT")])


def build_A3(L):
    nc = bass.Bass("TRN2", target_bir_lowering=False)
    with ExitStack() as es:
        C = Ctx(nc, es)
        fmT_d = C.din("fmT", [7, 128, L], BF16)
        vr_d = C.din("vr", [L, 384], BF16)
        sg_d = C.din("sg", [L, 128], F32)
        rt_d = C.din("rt", [128, RT_W], F32)
        gng_d = C.din("gng", [64], F32)
        gnb_d = C.din("gnb", [64], F32)
        id_d = C.din("ident", [128, 128], BF16)
        yretT_d = C.dout("yretT", [128, L], BF16)
        ident = C.sb([128, 128], BF16, "ident")
        C.P.dma("sp", ident[:], id_d[:, :], chan="ident", writes=["ident"])
        stage_A3(C, fmT_d, vr_d, sg_d, rt_d, gng_d, gnb_d, yretT_d, ident, L)
        C.P.emit(es)
    return nc


S5C_W = 1 + 1 + 1 + 8 + 512 + 512
GELU_K = 2.0 * math.sqrt(2.0 / math.pi)


def s5_consts():
    p = np.arange(128)
    out = np.zeros((128, S5C_W), np.float32)
    out[:, 0] = (p < 64)
    out[:, 1] = (p >= 64)
    out[:, 2] = np.where(p < 64, 1.0, -1.0)
    for g in range(8):
        out[:, 3 + g] = (p // 16 == g)
    out[:, 11:11 + 512] = np.arange(512)[None, :]
    out[:, 11 + 512:11 + 1024] = (511 - np.arange(512))[None, :]
    return out


def s5_layout(inp, i, h):
    gs = slice(8 * h, 8 * h + 8)

    def pg(a):
        a = np.asarray(a[i][:, gs, :]).transpose(2, 0, 1).reshape(64, 16)
        return np.ascontiguousarray(np.concatenate([a, a], axis=0)).astype(np.float32)
    ldt = np.asarray(inp['s5_log_dt'][i][:, gs]).reshape(1, 16)
    ldt = np.ascontiguousarray(np.broadcast_to(ldt, (128, 16))).astype(np.float32)

    def pb(a):
        a = np.asarray(a[i][:, gs]).transpose(2, 0, 1, 3).reshape(64, 16, 16)
        return np.ascontiguousarray(np.concatenate([a, a], axis=0)).astype(np.float32)
    cre = np.asarray(inp['s5_C_re'][i][:, gs])
    cim = np.asarray(inp['s5_C_im'][i][:, gs])
    cc = np.stack([cre, cim], axis=2)
    cc = cc.transpose(1, 3, 0, 2, 4).reshape(128, 2, 128)
    dv = np.asarray(inp['s5_D'][i][128 * h:128 * h + 128]).reshape(128, 1)
    sp = np.concatenate([pg(inp['s5_A_re']), pg(inp['s5_A_im']), ldt, dv.astype(np.float32)], axis=1)
    return {"s5p": np.ascontiguousarray(sp), "s5bre": pb(inp['s5_B_re']), "s5bim": pb(inp['s5_B_im']),
            "s5cc": np.ascontiguousarray(cc).astype(np.float32)}


def emit_gelu(P, tag, y, ykey, tmp, out, okey):
    kt = (tag, "gelu_tmp")
    P.op("dve", lambda e: e.tensor_tensor(out=tmp, in0=y, in1=y, op=ALU.mult), reads=[ykey], writes=[kt])
    P.op("dve", lambda e: e.tensor_scalar(out=tmp, in0=tmp, scalar1=0.044715, scalar2=1.0, op0=ALU.mult, op1=ALU.add),
         reads=[kt], writes=[kt])
    P.op("dve", lambda e: e.tensor_tensor(out=tmp, in0=tmp, in1=y, op=ALU.mult), reads=[kt, ykey], writes=[kt])
    P.op("act", lambda e: e.activation(out=tmp, in_=tmp, func=AF.Sigmoid, scale=GELU_K), reads=[kt], writes=[kt])
    P.op("dve", lambda e: e.tensor_tensor(out=out, in0=tmp, in1=y, op=ALU.mult), reads=[kt, ykey], writes=[okey])


def stage_A4(C, fmT_d, s5p_d, bre_d, bim_d, cc_d, sc_d, identf_d, ys5T_d, L, tag="a4", scan_pool=True):
    P = C.P
    NT = L // 512
    sc = C.sb([128, S5C_W], F32, "sc")
    sp = C.sb([128, 49], F32, "sp")
    bre = C.sb([128, 16, 16], F32, "bre")
    bim = C.sb([128, 16, 16], F32, "bim")
    cc = C.sb([128, 2, 128], F32, "cc")
    idf = C.sb([128, 128], F32, "idf")
    for tl, src_, k in ((sc, sc_d, "sc"), (sp, s5p_d, "sp"), (bre, bre_d, "bre"), (bim, bim_d, "bim"), (cc, cc_d, "cc"), (idf, identf_d, "idf")):
        P.dma("sp", tl[:], src_, chan=(tag, "c0"), writes=[(tag, k)])
    mtop, mbot, sgnC = sc[:, 0:1], sc[:, 1:2], sc[:, 2:3]
    tau = [sc[:, 11:11 + 512], sc[:, 11 + 512:11 + 1024]]
    ar, ai, ldt, dv = sp[:, 0:16], sp[:, 16:32], sp[:, 32:48], sp[:, 48:49]

    sm = {}
    for nm in ("step", "zr", "th", "pp", "em1", "e", "sarg", "carg", "c1", "s1", "sh", "cm1", "nr", "ni", "den", "t0", "t1",
               "cre", "cim", "s1A", "s2A", "s1B", "c512", "s512", "a512", "rrf"):
        sm[nm] = C.sb([128, 16], F32, "s5_" + nm)
    rri = C.sb([128, 16], I32, "s5_rri")

    def S(nm):
        return sm[nm][:]

    def k(nm):
        return (tag, "sm", nm)

    def dv_op(fn, reads, writes):
        P.op("dve", fn, reads=[k(r) if isinstance(r, str) else r for r in reads], writes=[k(w) for w in writes])
    kp = (tag, "sp")
    dv_op(lambda e: e.tensor_copy(out=S("t0"), in_=ldt), [kp], ["t0"])
    P.op("act", lambda e: e.activation(out=S("step"), in_=S("t0"), func=AF.Exp), reads=[k("t0")], writes=[k("step")])
    dv_op(lambda e: e.tensor_tensor(out=S("zr"), in0=S("step"), in1=ar, op=ALU.mult), ["step", kp], ["zr"])
    dv_op(lambda e: e.tensor_tensor(out=S("th"), in0=S("step"), in1=ai, op=ALU.mult), ["step", kp], ["th"])
    dv_op(lambda e: e.tensor_scalar(out=S("pp"), in0=S("zr"), scalar1=1.0 / 6.0, scalar2=1.0, op0=ALU.mult, op1=ALU.add), ["zr"], ["pp"])
    for cdiv in (5.0, 4.0, 3.0, 2.0):
        dv_op(lambda e, cdiv=cdiv: e.scalar_tensor_tensor(out=S("pp"), in0=S("zr"), scalar=1.0 / cdiv, in1=S("pp"), op0=ALU.mult, op1=ALU.mult),
              ["zr", "pp"], ["pp"])
        dv_op(lambda e: e.tensor_scalar(out=S("pp"), in0=S("pp"), scalar1=1.0, scalar2=None, op0=ALU.add), ["pp"], ["pp"])
    dv_op(lambda e: e.tensor_tensor(out=S("em1"), in0=S("zr"), in1=S("pp"), op=ALU.mult), ["zr", "pp"], ["em1"])
    dv_op(lambda e: e.tensor_scalar(out=S("e"), in0=S("em1"), scalar1=1.0, scalar2=None, op0=ALU.add), ["em1"], ["e"])
    sincos(P, "dve", (tag, "sc1"), sm["th"], k("th"), sm["sarg"], sm["carg"], rri, sm["rrf"])
    ks, kc = ((tag, "sc1"), "sarg"), ((tag, "sc1"), "carg")
    P.op("act", lambda e: e.activation(out=S("s1"), in_=S("sarg"), func=AF.Sin), reads=[ks], writes=[k("s1")])
    P.op("act", lambda e: e.activation(out=S("sh"), in_=S("sarg"), func=AF.Sin, scale=0.5), reads=[ks], writes=[k("sh")])
    dv_op(lambda e: e.scalar_tensor_tensor(out=S("cm1"), in0=S("sh"), scalar=-2.0, in1=S("sh"), op0=ALU.mult, op1=ALU.mult), ["sh"], ["cm1"])
    dv_op(lambda e: e.tensor_scalar(out=S("c1"), in0=S("cm1"), scalar1=1.0, scalar2=None, op0=ALU.add), ["cm1"], ["c1"])
    dv_op(lambda e: e.tensor_tensor(out=S("nr"), in0=S("em1"), in1=S("c1"), op=ALU.mult), ["em1", "c1"], ["nr"])
    dv_op(lambda e: e.tensor_tensor(out=S("nr"), in0=S("nr"), in1=S("cm1"), op=ALU.add), ["nr", "cm1"], ["nr"])
    dv_op(lambda e: e.tensor_tensor(out=S("ni"), in0=S("e"), in1=S("s1"), op=ALU.mult), ["e", "s1"], ["ni"])
    dv_op(lambda e: e.tensor_tensor(out=S("den"), in0=ar, in1=ar, op=ALU.mult), [kp], ["den"])
    dv_op(lambda e: e.tensor_tensor(out=S("t0"), in0=ai, in1=ai, op=ALU.mult), [kp], ["t0"])
    dv_op(lambda e: e.tensor_tensor(out=S("den"), in0=S("den"), in1=S("t0"), op=ALU.add), ["den", "t0"], ["den"])
    dv_op(lambda e: e.reciprocal(out=S("den"), in_=S("den")), ["den"], ["den"])
    dv_op(lambda e: e.tensor_tensor(out=S("t0"), in0=S("nr"), in1=ar, op=ALU.mult), ["nr", kp], ["t0"])
    dv_op(lambda e: e.tensor_tensor(out=S("t1"), in0=S("ni"), in1=ai, op=ALU.mult), ["ni", kp], ["t1"])
    dv_op(lambda e: e.tensor_tensor(out=S("t0"), in0=S("t0"), in1=S("t1"), op=ALU.add), ["t0", "t1"], ["t0"])
    dv_op(lambda e: e.tensor_tensor(out=S("cre"), in0=S("t0"), in1=S("den"), op=ALU.mult), ["t0", "den"], ["cre"])
    dv_op(lambda e: e.tensor_tensor(out=S("t0"), in0=S("ni"), in1=ar, op=ALU.mult), ["ni", kp], ["t0"])
    dv_op(lambda e: e.tensor_tensor(out=S("t1"), in0=S("nr"), in1=ai, op=ALU.mult), ["nr", kp], ["t1"])
    dv_op(lambda e: e.tensor_tensor(out=S("t0"), in0=S("t0"), in1=S("t1"), op=ALU.subtract), ["t0", "t1"], ["t0"])
    dv_op(lambda e: e.tensor_tensor(out=S("cim"), in0=S("t0"), in1=S("den"), op=ALU.mult), ["t0", "den"], ["cim"])
    ksc = (tag, "sc")
    dv_op(lambda e: e.tensor_scalar(out=S("s1A"), in0=S("cre"), scalar1=mtop, scalar2=None, op0=ALU.mult), ["cre", ksc], ["s1A"])
    dv_op(lambda e: e.scalar_tensor_tensor(out=S("s1A"), in0=S("cim"), scalar=mbot, in1=S("s1A"), op0=ALU.mult, op1=ALU.add), ["cim", "s1A", ksc], ["s1A"])
    dv_op(lambda e: e.tensor_scalar(out=S("s2A"), in0=S("cre"), scalar1=mbot, scalar2=None, op0=ALU.mult), ["cre", ksc], ["s2A"])
    dv_op(lambda e: e.tensor_scalar(out=S("t0"), in0=S("cim"), scalar1=mtop, scalar2=None, op0=ALU.mult), ["cim", ksc], ["t0"])
    dv_op(lambda e: e.tensor_tensor(out=S("s2A"), in0=S("s2A"), in1=S("t0"), op=ALU.subtract), ["s2A", "t0"], ["s2A"])
    dv_op(lambda e: e.tensor_scalar(out=S("s1B"), in0=S("s1A"), scalar1=sgnC, scalar2=None, op0=ALU.mult), ["s1A", ksc], ["s1B"])
    dv_op(lambda e: e.tensor_scalar(out=S("s1B"), in0=S("cim"), scalar1=mtop, scalar2=None, op0=ALU.mult), ["cim", ksc], ["s1B"])
    dv_op(lambda e: e.tensor_scalar(out=S("t0"), in0=S("cre"), scalar1=mbot, scalar2=None, op0=ALU.mult), ["cre", ksc], ["t0"])
    dv_op(lambda e: e.tensor_tensor(out=S("s1B"), in0=S("s1B"), in1=S("t0"), op=ALU.subtract), ["s1B", "t0"], ["s1B"])
    dv_op(lambda e: e.tensor_scalar(out=S("a512"), in0=S("th"), scalar1=512.0, scalar2=None, op0=ALU.mult), ["th"], ["a512"])
    sincos(P, "dve", (tag, "sc2"), sm["a512"], k("a512"), sm["sarg"], sm["carg"], rri, sm["rrf"])
    ks2, kc2 = ((tag, "sc2"), "sarg"), ((tag, "sc2"), "carg")
    P.op("act", lambda e: e.activation(out=S("s512"), in_=S("sarg"), func=AF.Sin), reads=[ks2], writes=[k("s512")])
    P.op("act", lambda e: e.activation(out=S("c512"), in_=S("carg"), func=AF.Sin), reads=[kc2], writes=[k("c512")])

    BA = C.sb([128, 16, 128], BF16, "BA")
    BB = C.sb([128, 16, 128], BF16, "BB")
    CP = C.sb([128, 16, 128], BF16, "CP")
    Dg = C.sb([128, 128], BF16, "Dg")
    xa = C.sb([128, 8, 16], F32, "xa")
    xb = C.sb([128, 8, 16], F32, "xb")
    cfull = C.sb([128, 128], F32, "cfull")
    pset = C.ps([128, 128], F32, "pset")
    P.op("pool", lambda e: e.memset(CP[:], 0.0), writes=[(tag, "CP")])
    P.op("dve", lambda e: e.tensor_scalar(out=Dg[:], in0=idf[:], scalar1=dv, scalar2=None, op0=ALU.mult),
         reads=[(tag, "idf"), kp], writes=[(tag, "Dg")])
    for d in range(2):
        for var, (sa, sb_), dst in ((0, ("s1A", "s2A"), BA), (1, ("s1B", "s1A"), BB)):
            def bc(nm, d=d):
                return sm[nm][:, d * 8:(d + 1) * 8].unsqueeze(2).to_broadcast([128, 8, 16])
            P.op("dve", lambda e, d=d, sa=sa, bc=bc: e.tensor_tensor(out=xa[:], in0=bre[:, d * 8:(d + 1) * 8, :], in1=bc(sa), op=ALU.mult),
                 reads=[(tag, "bre"), k(sa)], writes=[(tag, "xa")])
            P.op("dve", lambda e, d=d, sb_=sb_, bc=bc: e.tensor_tensor(out=xb[:], in0=bim[:, d * 8:(d + 1) * 8, :], in1=bc(sb_), op=ALU.mult),
                 reads=[(tag, "bim"), k(sb_)], writes=[(tag, "xb")])
            P.op("dve", lambda e: e.tensor_tensor(out=xa[:], in0=xa[:], in1=xb[:], op=ALU.add),
                 reads=[(tag, "xa"), (tag, "xb")], writes=[(tag, "xa")])
            P.op("pe", lambda e: e.transpose(out=pset[:], in_=xa[:].rearrange("p a b -> p (a b)"), identity=idf[:]),
                 reads=[(tag, "xa"), (tag, "idf")], writes=[(tag, "pset")])
            for g in range(8):
                P.op("dve", lambda e, g=g, d=d, dst=dst: e.tensor_scalar(out=dst[:, d * 8 + g, :], in0=pset[:], scalar1=sc[:, 3 + g:4 + g],
                                                                      scalar2=None, op0=ALU.mult),
                     reads=[ksc], writes=[(tag, "Btab"), (tag, "pset")])
        P.op("pe", lambda e, d=d: e.transpose(out=pset[:], in_=cc[:, d, :], identity=idf[:]),
             reads=[(tag, "cc"), (tag, "idf")], writes=[(tag, "pset")])
        P.op("dve", lambda e: e.tensor_copy(out=cfull[:], in_=pset[:]), writes=[(tag, "cfull"), (tag, "pset")])
        for g in range(8):
            P.op("dve", lambda e, g=g, d=d: e.tensor_scalar(out=CP[:, d * 8 + g, 16 * g:16 * g + 16], in0=cfull[:, 16 * g:16 * g + 16],
                                                            scalar1=sgnC, scalar2=None, op0=ALU.mult),
                 reads=[(tag, "cfull"), ksc], writes=[(tag, "CP")])

    cosT = C.sb([128, 16, 512], F32, "cosT")
    sinT = C.sb([128, 16, 512], F32, "sinT")
    ang = C.sb([128, 512], F32, "ang")
    rsa = C.sb([128, 512], F32, "rsa")
    rca = C.sb([128, 512], F32, "rca")
    rr_i = C.sb([128, 512], I32, "rr_i")
    rr_f = C.sb([128, 512], F32, "rr_f")
    for gd in range(16):
        d = gd // 8
        P.op("dve", lambda e, gd=gd, d=d: e.tensor_scalar(out=ang[:], in0=tau[d], scalar1=sm["th"][:, gd:gd + 1], scalar2=None, op0=ALU.mult),
             reads=[ksc, k("th")], writes=[(tag, "ang")])
        sincos(P, "dve", (tag, "rt"), ang, (tag, "ang"), rsa, rca, rr_i, rr_f)
        P.op("act", lambda e, gd=gd: e.activation(out=sinT[:, gd, :], in_=rsa[:], func=AF.Sin), reads=[((tag, "rt"), "sarg")], writes=[(tag, "sinT", gd)])
        P.op("act", lambda e, gd=gd: e.activation(out=cosT[:, gd, :], in_=rca[:], func=AF.Sin), reads=[((tag, "rt"), "carg")], writes=[(tag, "cosT", gd)])

    uT = C.sb([128, L], BF16, "uT")
    yb = C.sb([128, L], F32, "yb")
    P.dma("sp", uT[:], fmT_d[6, :, :], chan=(tag, "ld"), writes=[(tag, "uT")])
    cA = C.sb([128, 16], F32, "cA")
    cB = C.sb([128, 16], F32, "cB")
    ctmp = C.sb([128, 4], F32, "ctmp")
    P.op("dve", lambda e: e.memset(cA[:], 0.0), writes=[(tag, "cA")])
    P.op("dve", lambda e: e.memset(cB[:], 0.0), writes=[(tag, "cB")])
    NB = 2
    bA = [C.sb([128, 512], F32, "bA") for _ in range(NB)]
    bB = [C.sb([128, 512], F32, "bB") for _ in range(NB)]
    w1 = [C.sb([128, 512], F32, "w1") for _ in range(NB)]
    w2 = [C.sb([128, 512], F32, "w2") for _ in range(NB)]
    w3 = [C.sb([128, 512], F32, "w3") for _ in range(NB)]
    w4 = [C.sb([128, 512], F32, "w4") for _ in range(NB)]
    xbf = [C.sb([128, 512], BF16, "xbf") for _ in range(NB)]
    yo = [C.sb([128, 512], F32, "yo") for _ in range(2)]
    gt = [C.sb([128, 512], F32, "gt") for _ in range(2)]
    yob = [C.sb([128, 512], BF16, "yob") for _ in range(2)]
    pA = [C.ps([128, 512], F32, "pA") for _ in range(2)]
    pB = [C.ps([128, 512], F32, "pB") for _ in range(2)]
    yps = [C.ps([128, 512], F32, "yps") for _ in range(2)]
    pe2 = "pool" if scan_pool else "dve"
    uidx = [0]

    def unit(gd, it, last_g, yslot):
        d, g = gd // 8, gd % 8
        s = uidx[0] % NB
        uidx[0] += 1
        ts = slice(it * 512, (it + 1) * 512)
        rcol = sm["e"][:, gd:gd + 1]
        ct, st_ = cosT[:, gd, :], sinT[:, gd, :]
        kct, kst = (tag, "cosT", gd), (tag, "sinT", gd)
        P.op("pe", lambda e: e.matmul(pA[s][:], lhsT=BA[:, gd, :], rhs=uT[:, ts], start=True, stop=True),
             reads=[(tag, "Btab"), (tag, "uT")], writes=[(tag, "pA", s)])
        P.op("pe", lambda e: e.matmul(pB[s][:], lhsT=BB[:, gd, :], rhs=uT[:, ts], start=True, stop=True),
             reads=[(tag, "Btab"), (tag, "uT")], writes=[(tag, "pB", s)])
        P.op("act", lambda e: e.copy(out=bA[s][:], in_=pA[s][:]), writes=[(tag, "bA", s), (tag, "pA", s)])
        P.op("act", lambda e: e.copy(out=bB[s][:], in_=pB[s][:]), writes=[(tag, "bB", s), (tag, "pB", s)])
        P.op("dve", lambda e: e.tensor_tensor(out=w1[s][:], in0=bA[s][:], in1=ct, op=ALU.mult), reads=[(tag, "bA", s), kct], writes=[(tag, "w1", s)])
        P.op("dve", lambda e: e.tensor_tensor(out=w2[s][:], in0=bB[s][:], in1=st_, op=ALU.mult), reads=[(tag, "bB", s), kst], writes=[(tag, "w2", s)])
        P.op("dve", lambda e: e.tensor_tensor(out=w1[s][:], in0=w1[s][:], in1=w2[s][:], op=ALU.add), reads=[(tag, "w1", s), (tag, "w2", s)], writes=[(tag, "w1", s)])
        P.op(pe2, lambda e: e.tensor_tensor(out=w3[s][:], in0=bB[s][:], in1=ct, op=ALU.mult), reads=[(tag, "bB", s), kct], writes=[(tag, "w3", s)])
        P.op(pe2, lambda e: e.tensor_tensor(out=w4[s][:], in0=bA[s][:], in1=st_, op=ALU.mult), reads=[(tag, "bA", s), kst], writes=[(tag, "w4", s)])
        P.op(pe2, lambda e: e.tensor_tensor(out=w3[s][:], in0=w3[s][:], in1=w4[s][:], op=ALU.subtract), reads=[(tag, "w3", s), (tag, "w4", s)], writes=[(tag, "w3", s)])
        rb = rcol.to_broadcast([128, 512])
        if d == 0:
            P.op("dve", lambda e: e.tensor_tensor_scan(out=w2[s][:], data0=rb, data1=w1[s][:], initial=cA[:, gd:gd + 1], op0=ALU.mult, op1=ALU.add),
                 reads=[(tag, "w1", s), k("e"), (tag, "cA")], writes=[(tag, "w2", s)])
            P.op("dve", lambda e: e.tensor_tensor_scan(out=w4[s][:], data0=rb, data1=w3[s][:], initial=cB[:, gd:gd + 1], op0=ALU.mult, op1=ALU.add),
                 reads=[(tag, "w3", s), k("e"), (tag, "cB")], writes=[(tag, "w4", s)])
            lastc = slice(511, 512)
        else:
            P.op("dve", lambda e: e.tensor_tensor_scan(out=w2[s][:, ::-1], data0=rb, data1=w1[s][:, ::-1], initial=cA[:, gd:gd + 1], op0=ALU.mult, op1=ALU.add),
                 reads=[(tag, "w1", s), k("e"), (tag, "cA")], writes=[(tag, "w2", s)])
            P.op("dve", lambda e: e.tensor_tensor_scan(out=w4[s][:, ::-1], data0=rb, data1=w3[s][:, ::-1], initial=cB[:, gd:gd + 1], op0=ALU.mult, op1=ALU.add),
                 reads=[(tag, "w3", s), k("e"), (tag, "cB")], writes=[(tag, "w4", s)])
            lastc = slice(0, 1)
        c5, s5 = sm["c512"][:, gd:gd + 1], sm["s512"][:, gd:gd + 1]
        P.op("dve", lambda e: e.tensor_tensor(out=ctmp[:, 0:1], in0=w2[s][:, lastc], in1=c5, op=ALU.mult), reads=[(tag, "w2", s), k("c512")], writes=[(tag, "ct0")])
        P.op("dve", lambda e: e.tensor_tensor(out=ctmp[:, 1:2], in0=w4[s][:, lastc], in1=s5, op=ALU.mult), reads=[(tag, "w4", s), k("s512")], writes=[(tag, "ct1")])
        P.op("dve", lambda e: e.tensor_tensor(out=ctmp[:, 2:3], in0=w4[s][:, lastc], in1=c5, op=ALU.mult), reads=[(tag, "w4", s), k("c512")], writes=[(tag, "ct2")])
        P.op("dve", lambda e: e.tensor_tensor(out=ctmp[:, 3:4], in0=w2[s][:, lastc], in1=s5, op=ALU.mult), reads=[(tag, "w2", s), k("s512")], writes=[(tag, "ct3")])
        P.op("dve", lambda e: e.tensor_tensor(out=cA[:, gd:gd + 1], in0=ctmp[:, 0:1], in1=ctmp[:, 1:2], op=ALU.subtract),
             reads=[(tag, "ct0"), (tag, "ct1")], writes=[(tag, "cA")])
        P.op("dve", lambda e: e.tensor_tensor(out=cB[:, gd:gd + 1], in0=ctmp[:, 2:3], in1=ctmp[:, 3:4], op=ALU.add),
             reads=[(tag, "ct2"), (tag, "ct3")], writes=[(tag, "cB")])
        P.op(pe2, lambda e: e.tensor_tensor(out=w1[s][:], in0=w2[s][:], in1=ct, op=ALU.mult), reads=[(tag, "w2", s), kct], writes=[(tag, "w1", s)])
        P.op(pe2, lambda e: e.tensor_tensor(out=w3[s][:], in0=w4[s][:], in1=st_, op=ALU.mult), reads=[(tag, "w4", s), kst], writes=[(tag, "w3", s)])
        P.op("dve", lambda e: e.tensor_tensor(out=xbf[s][:], in0=w1[s][:], in1=w3[s][:], op=ALU.subtract), reads=[(tag, "w1", s), (tag, "w3", s)], writes=[(tag, "xbf", s)])
        P.op("pe", lambda e: e.matmul(yps[yslot][:], lhsT=CP[:, gd, :], rhs=xbf[s][:], start=(g == 0), stop=(last_g and d == 1)),
             reads=[(tag, "CP"), (tag, "xbf", s)], writes=[(tag, "yps", yslot)])

    tcount = 0
    for d in (1, 0):
        order = range(NT - 1, -1, -1) if d == 1 else range(NT)
        for it in order:
            ysl = tcount % 2
            tcount += 1
            ts = slice(it * 512, (it + 1) * 512)
            for g in range(8):
                unit(d * 8 + g, it, g == 7, ysl)
            if d == 1:
                P.op("act", lambda e, ysl=ysl, ts=ts: e.copy(out=yb[:, ts], in_=yps[ysl][:]), writes=[(tag, "yb", it), (tag, "yps", ysl)])
            else:
                P.op("pe", lambda e, ysl=ysl, ts=ts: e.matmul(yps[ysl][:], lhsT=Dg[:], rhs=uT[:, ts], start=False, stop=True),
                     reads=[(tag, "Dg"), (tag, "uT")], writes=[(tag, "yps", ysl)])
                P.op("dve", lambda e, ysl=ysl, ts=ts: e.tensor_tensor(out=yo[ysl][:], in0=yps[ysl][:], in1=yb[:, ts], op=ALU.add),
                     reads=[(tag, "yb", it)], writes=[(tag, "yo", ysl), (tag, "yps", ysl)])
                emit_gelu(P, (tag, "g", ysl), yo[ysl][:], (tag, "yo", ysl), gt[ysl][:], yob[ysl][:], (tag, "yob", ysl))
                P.dma("sp", ys5T_d[:, ts], yob[ysl][:], chan=(tag, "yo", ysl), reads=[(tag, "yob", ysl)], writes=[(tag, "ys5T", it)])


def build_A4(L, scan_pool=True):
    nc = bass.Bass("TRN2", target_bir_lowering=False)
    with ExitStack() as es:
        C = Ctx(nc, es)
        fmT_d = C.din("fmT", [7, 128, L], BF16)
        s5p_d = C.din("s5p", [128, 49], F32)
        bre_d = C.din("s5bre", [128, 16, 16], F32)
        bim_d = C.din("s5bim", [128, 16, 16], F32)
        cc_d = C.din("s5cc", [128, 2, 128], F32)
        sc_d = C.din("s5c", [128, S5C_W], F32)
        idf_d = C.din("identf", [128, 128], F32)
        ys5T_d = C.dout("ys5T", [128, L], BF16)
        stage_A4(C, fmT_d, s5p_d[:, :], bre_d[:, :, :], bim_d[:, :, :], cc_d[:, :, :], sc_d[:, :], idf_d[:, :], ys5T_d, L, scan_pool=scan_pool)
        C.P.emit(es)
    return nc


W0_SPECS = (("w_out", 1024, 1024), ("ple_gate_w", 1024, 1024), ("ple_w", 256, 1024), ("s5_glu_w", 256, 256),
            ("ffn_w_up", 1024, 5632), ("ffn_w_down", 2816, 1024))


def stage_W0(C, pairs, tag="w0"):
    P = C.P
    stg = [C.sb([128, 2816], F32, "w0s") for _ in range(2)]
    obf = [C.sb([128, 2816], BF16, "w0o") for _ in range(2)]
    i = 0
    engs = ("dve", "pool", "act")
    for src_d, dst_d, R, N in pairs:
        for r0 in range(0, R, 128):
            for n0 in range(0, N, 2816):
                n1 = min(N, n0 + 2816)
                w = n1 - n0
                s = i % 2
                P.dma("sp", stg[s][:, 0:w], src_d[r0:r0 + 128, n0:n1], chan=(tag, "in", s), writes=[(tag, "stg", s)])
                en = engs[i % 3]
                if en == "act":
                    P.op("act", lambda e, s=s, w=w: e.copy(out=obf[s][:, 0:w], in_=stg[s][:, 0:w]), reads=[(tag, "stg", s)], writes=[(tag, "obf", s)])
                else:
                    P.op(en, lambda e, s=s, w=w: e.tensor_copy(out=obf[s][:, 0:w], in_=stg[s][:, 0:w]), reads=[(tag, "stg", s)], writes=[(tag, "obf", s)])
                P.dma("pool", dst_d[r0:r0 + 128, n0:n1], obf[s][:, 0:w], chan=(tag, "out", s), reads=[(tag, "obf", s)], writes=[(tag, "dst", i)])
                i += 1


def build_W0():
    nc = bass.Bass("TRN2", target_bir_lowering=False)
    with ExitStack() as es:
        C = Ctx(nc, es)
        pairs = []
        for nm, R, N in W0_SPECS:
            pairs.append((C.din(nm, [R, N], F32), C.dout(nm + "_bf", [R, N], BF16), R, N))
        stage_W0(C, pairs)
        C.P.emit(es)
    return nc


def emit_ln(P, tag, h, hkey, st, mv, rstd, epsb, gtab, btab, gkeys, out, okey, eng2="pool"):
    ks, km, kr = (tag, "st"), (tag, "mv"), (tag, "rstd")
    for j in range(2):
        P.op("dve", lambda e, j=j: e.bn_stats(out=st[:, j, :], in_=h[:, j * 512:(j + 1) * 512]), reads=[hkey], writes=[ks])
    P.op("dve", lambda e: e.bn_aggr(out=mv[:], in_=st[:]), reads=[ks], writes=[km])
    P.op("act", lambda e: e.activation(out=rstd[:], in_=mv[:, 1:2], func=AF.Sqrt, bias=epsb[:, 0:1], scale=1.0), reads=[km] + gkeys, writes=[kr])
    P.op("dve", lambda e: e.reciprocal(out=rstd[:], in_=rstd[:]), reads=[kr], writes=[kr])
    P.op("dve", lambda e: e.tensor_scalar(out=h, in0=h, scalar1=mv[:, 0:1], scalar2=rstd[:, 0:1], op0=ALU.subtract, op1=ALU.mult),
         reads=[hkey, km, kr], writes=[hkey])
    P.op(eng2, lambda e: e.tensor_tensor(out=h, in0=h, in1=gtab, op=ALU.mult), reads=[hkey] + gkeys, writes=[hkey])
    P.op(eng2, lambda e: e.tensor_tensor(out=out, in0=h, in1=btab, op=ALU.add), reads=[hkey] + gkeys, writes=[okey])


def emit_to_featmajor(P, C_tag, src32, skey, xbf, pst, dstT, dkey_fn, ident, j):
    tag = C_tag
    P.op("act", lambda e: e.copy(out=xbf, in_=src32), reads=[skey], writes=[(tag, "xbf")])
    for c in range(8):
        P.op("pe", lambda e, c=c: e.transpose(out=pst[:, c, :], in_=xbf[:, c * 128:(c + 1) * 128], identity=ident[:]),
             reads=[(tag, "xbf"), "ident"], writes=[(tag, "pst")])
    P.op("dve", lambda e: e.tensor_copy(out=dstT[:, :, j * 128:(j + 1) * 128], in_=pst[:]), writes=[dkey_fn(j), (tag, "pst")])


def stage_P1(C, ycT_d, x_d, wout_d, glw_d, glb_d, g1_d, b1_d, x1_d, x1T_d, ident, NTOK, tag="p1"):
    P = C.P
    wout = C.sb([128, 8, 1024], BF16, "wout")
    glw = C.sb([128, 2, 256], BF16, "glw")
    glb = C.sb([128, 2], F32, "glb")
    gtab = C.sb([128, 1024], F32, "gtab")
    btab = C.sb([128, 1024], F32, "btab")
    epsb = C.sb([128, 1], F32, "epsb")
    P.dma("sp", wout[:], wout_d.rearrange("(c p) n -> p c n", p=128), chan=(tag, "c0"), writes=[(tag, "wout")])
    P.dma("sp", glw[:], glw_d.rearrange("(c p) n -> p c n", p=128), chan=(tag, "c0"), writes=[(tag, "glw")])
    for oc in range(2):
        P.dma("sp", glb[:, oc:oc + 1], glb_d[oc * 128:(oc + 1) * 128].rearrange("(p o) -> p o", o=1), chan=(tag, "c0"), writes=[(tag, "glb")])
    P.dma("sp", gtab[:], g1_d.partition_broadcast(128), chan=(tag, "c0"), writes=[(tag, "gtab")])
    P.dma("sp", btab[:], b1_d.partition_broadcast(128), chan=(tag, "c0"), writes=[(tag, "btab")])
    P.op("dve", lambda e: e.memset(epsb[:], LN_EPS), writes=[(tag, "epsb")])
    gk = [(tag, "gtab"), (tag, "btab"), (tag, "epsb")]
    yc = [C.sb([128, 8, 512], BF16, "yc") for _ in range(2)]
    ysg = [C.sb([128, 2, 512], BF16, "ysg") for _ in range(2)]
    gsig = C.sb([128, 512], F32, "gsig")
    xin = [C.sb([128, 1024], F32, "xin") for _ in range(2)]
    hh = [C.sb([128, 1024], F32, "hh") for _ in range(2)]
    x1o = [C.sb([128, 1024], F32, "x1o") for _ in range(2)]
    xbf = C.sb([128, 1024], BF16, "xbf")
    x1T = [C.sb([128, 8, 512], BF16, "x1T") for _ in range(2)]
    st = C.sb([128, 2, 6], F32, "st")
    mv = C.sb([128, 2], F32, "mv")
    rstd = C.sb([128, 1], F32, "rstd")
    pgl = [C.ps([128, 512], F32, "pgl") for _ in range(2)]
    pmx = [C.ps([128, 512], F32, "pmx") for _ in range(4)]
    pst = C.ps([128, 8, 128], BF16, "pst")
    ycT_v = ycT_d.rearrange("(c p) t -> p c t", p=128)
    x1T_v = x1T_d.rearrange("(c p) t -> p c t", p=128)
    NTT = NTOK // 512
    kk = 0
    for tt in range(NTT):
        s = tt % 2
        t0 = tt * 512
        P.dma("sp", yc[s][:], ycT_v[:, :, t0:t0 + 512], chan=(tag, "yc", s), writes=[(tag, "yc", s)])
        for oc in range(2):
            for kc in range(2):
                P.op("pe", lambda e, s=s, oc=oc, kc=kc: e.matmul(pgl[oc][:], lhsT=glw[:, kc, oc * 128:(oc + 1) * 128], rhs=yc[s][:, 6 + kc, :],
                                                                 start=(kc == 0), stop=(kc == 1)),
                     reads=[(tag, "glw"), (tag, "yc", s)], writes=[(tag, "pgl", oc)])
            P.op("act", lambda e, oc=oc: e.activation(out=gsig[:], in_=pgl[oc][:], func=AF.Sigmoid, bias=glb[:, oc:oc + 1], scale=1.0),
                 reads=[(tag, "glb")], writes=[(tag, "gsig"), (tag, "pgl", oc)])
            P.op("dve", lambda e, s=s, oc=oc: e.tensor_tensor(out=ysg[s][:, oc, :], in0=gsig[:], in1=yc[s][:, 6 + oc, :], op=ALU.mult),
                 reads=[(tag, "gsig"), (tag, "yc", s)], writes=[(tag, "ysg", s, oc)])
        for j in range(4):
            b = kk % 2
            kk += 1
            r0 = t0 + j * 128
            P.dma("sp", xin[b][:], x_d[r0:r0 + 128, :], chan=(tag, "xin", b), writes=[(tag, "xin", b)])
            for nb in range(2):
                pb = (2 * b + nb)
                for c in range(8):
                    def lhs(c=c, s=s, j=j):
                        return yc[s][:, c, j * 128:(j + 1) * 128] if c < 6 else ysg[s][:, c - 6, j * 128:(j + 1) * 128]
                    P.op("pe", lambda e, pb=pb, c=c, nb=nb, lhs=lhs: e.matmul(pmx[pb][:], lhsT=lhs(), rhs=wout[:, c, nb * 512:(nb + 1) * 512],
                                                                              start=(c == 0), stop=(c == 7)),
                         reads=[(tag, "wout"), (tag, "yc", s), (tag, "ysg", s, 0), (tag, "ysg", s, 1)], writes=[(tag, "pmx", pb)])
                P.op("dve", lambda e, b=b, pb=pb, nb=nb: e.scalar_tensor_tensor(out=hh[b][:, nb * 512:(nb + 1) * 512], in0=xin[b][:, nb * 512:(nb + 1) * 512],
                                                                               scalar=ALPHA, in1=pmx[pb][:], op0=ALU.mult, op1=ALU.add),
                     reads=[(tag, "xin", b)], writes=[(tag, "hh", b), (tag, "pmx", pb)])
            emit_ln(P, (tag, "ln"), hh[b][:], (tag, "hh", b), st, mv, rstd, epsb, gtab[:], btab[:], gk, x1o[b][:], (tag, "x1o", b))
            P.dma("pool", x1_d[r0:r0 + 128, :], x1o[b][:], chan=(tag, "x1o", b), reads=[(tag, "x1o", b)], writes=[(tag, "x1d", tt, j)])
            emit_to_featmajor(P, (tag, "fm"), x1o[b][:], (tag, "x1o", b), xbf[:], pst, x1T[s], lambda jj, s=s: (tag, "x1T", s, jj), ident, j)
        P.dma("pool", x1T_v[:, :, t0:t0 + 512], x1T[s][:], chan=(tag, "x1T", s), reads=[(tag, "x1T", s, jj) for jj in range(4)],
              writes=[(tag, "x1Td", tt)])


def build_P1(NTOK):
    nc = bass.Bass("TRN2", target_bir_lowering=False)
    with ExitStack() as es:
        C = Ctx(nc, es)
        ycT_d = C.din("ycT", [1024, NTOK], BF16)
        x_d = C.din("x", [NTOK, 1024], F32)
        wout_d = C.din("w_out_bf", [1024, 1024], BF16)
        glw_d = C.din("s5_glu_w_bf", [256, 256], BF16)
        glb_d = C.din("glb", [256], F32)
        g1_d = C.din("ln_g", [1024], F32)
        b1_d = C.din("ln_b", [1024], F32)
        id_d = C.din("ident", [128, 128], BF16)
        x1_d = C.dout("x1", [NTOK, 1024], F32)
        x1T_d = C.dout("x1T", [1024, NTOK], BF16)
        ident = C.sb([128, 128], BF16, "ident")
        C.P.dma("sp", ident[:], id_d[:, :], chan="ident", writes=["ident"])
        stage_P1(C, ycT_d, x_d, wout_d, glw_d, glb_d, g1_d, b1_d, x1_d, x1T_d, ident, NTOK)
        C.P.emit(es)
    return nc


NFT = D_FF // 128


def conv_layout(conv_w, conv_b):
    a = np.concatenate([np.asarray(conv_w), np.asarray(conv_b)[None, :]], axis=0)
    return np.ascontiguousarray(a.reshape(4, NFT, 128).transpose(2, 1, 0)).astype(np.float32)


def stage_P2(C, x1T_d, x1_d, p_d, wup_d, wdn_d, plew_d, plegw_d, cwb_d, g2_d, b2_d, x2_d, x2T_d, ident, NTOK, tag="p2"):
    P = C.P
    plegw = C.sb([128, 8, 1024], BF16, "plegw")
    plew = C.sb([128, 2, 1024], BF16, "plew")
    wdn = C.sb([128, NFT, 1024], BF16, "wdn")
    cwb = C.sb([128, NFT, 4], F32, "cwb")
    gtab = C.sb([128, 1024], F32, "gtab")
    btab = C.sb([128, 1024], F32, "btab")
    epsb = C.sb([128, 1], F32, "epsb")
    P.dma("sp", plegw[:], plegw_d.rearrange("(c p) n -> p c n", p=128), chan=(tag, "c0"), writes=[(tag, "plegw")])
    P.dma("sp", plew[:], plew_d.rearrange("(c p) n -> p c n", p=128), chan=(tag, "c0"), writes=[(tag, "plew")])
    P.dma("sp", wdn[:], wdn_d.rearrange("(c p) n -> p c n", p=128), chan=(tag, "c0"), writes=[(tag, "wdn")])
    P.dma("sp", cwb[:], cwb_d, chan=(tag, "c0"), writes=[(tag, "cwb")])
    P.dma("sp", gtab[:], g2_d.partition_broadcast(128), chan=(tag, "c0"), writes=[(tag, "gtab")])
    P.dma("sp", btab[:], b2_d.partition_broadcast(128), chan=(tag, "c0"), writes=[(tag, "btab")])
    P.op("dve", lambda e: e.memset(epsb[:], LN_EPS), writes=[(tag, "epsb")])
    gk = [(tag, "gtab"), (tag, "btab"), (tag, "epsb")]

    xt = [C.sb([128, 8, 514], BF16, "xt") for _ in range(2)]
    wg = [C.sb([128, 8, 128], BF16, "wg") for _ in range(3)]
    wv = [C.sb([128, 8, 128], BF16, "wv") for _ in range(3)]
    hm = C.sb([128, NFT, 512], BF16, "hm")
    racc = [C.sb([128, 1024], F32, "racc") for _ in range(4)]
    x1t = [C.sb([128, 1024], F32, "x1t") for _ in range(2)]
    pin = [C.sb([128, 256], F32, "pin") for _ in range(2)]
    pbf = [C.sb([128, 256], BF16, "pbf") for _ in range(2)]
    pT = C.sb([128, 2, 512], BF16, "pT")
    sg = [C.sb([128, 512], F32, "sg") for _ in range(2)]
    gext = [C.sb([128, 514], F32, "gext") for _ in range(3)]
    cv = [C.sb([128, 512], F32, "cv") for _ in range(3)]
    tmp = [C.sb([128, 512], F32, "tmp") for _ in range(3)]
    oneb = C.sb([128, 1], F32, "oneb")
    P.op("dve", lambda e: e.memset(oneb[:], 1.0), writes=[(tag, "oneb")])
    xbf = C.sb([128, 1024], BF16, "xbf")
    x2T = C.sb([128, 8, 512], BF16, "x2T")
    st = C.sb([128, 2, 6], F32, "st")
    mv = C.sb([128, 2], F32, "mv")
    rstd = C.sb([128, 1], F32, "rstd")
    pgate = [C.ps([128, 512], F32, "pgate") for _ in range(2)]
    pval = [C.ps([128, 512], F32, "pval") for _ in range(2)]
    pd = [C.ps([128, 512], F32, "pd") for _ in range(2)]
    phalo = C.ps([128, 2], F32, "phalo")
    pst = C.ps([128, 8, 128], BF16, "pst")
    x1T_v = x1T_d.rearrange("(c p) t -> p c t", p=128)
    x2T_v = x2T_d.rearrange("(c p) t -> p c t", p=128)
    wup_v = wup_d.rearrange("(c p) n -> p c n", p=128)
    NTT = NTOK // 512
    cnt = {"w": 0, "u": 0, "d": 0, "x": 0, "p": 0, "q": 0}

    def tile(tt):
        s = tt % 2
        t0 = tt * 512
        P.dma("sp", xt[s][:], x1T_v[:, :, t0:t0 + 514], chan=(tag, "xt", s), writes=[(tag, "xt", s)])
        for j in range(4):
            b = cnt["p"] % 2
            cnt["p"] += 1
            r0 = t0 + j * 128
            P.dma("sp", pin[b][:], p_d[r0:r0 + 128, :], chan=(tag, "pin", b), writes=[(tag, "pin", b)])
            P.op("act", lambda e, b=b: e.copy(out=pbf[b][:], in_=pin[b][:]), reads=[(tag, "pin", b)], writes=[(tag, "pbf", b)])
            for c in range(2):
                P.op("pe", lambda e, b=b, c=c: e.transpose(out=pst[:, c, :], in_=pbf[b][:, c * 128:(c + 1) * 128], identity=ident[:]),
                     reads=[(tag, "pbf", b), "ident"], writes=[(tag, "pst")])
            P.op("dve", lambda e, j=j: e.tensor_copy(out=pT[:, :, j * 128:(j + 1) * 128], in_=pst[:, 0:2, :]), writes=[(tag, "pT", j), (tag, "pst")])
        for j in range(4):
            b = cnt["x"] % 2
            cnt["x"] += 1
            r0 = t0 + j * 128
            P.dma("sp", x1t[b][:], x1_d[r0:r0 + 128, :], chan=(tag, "x1t", b), writes=[(tag, "x1t", b)])
            for nb in range(2):
                q = cnt["q"] % 2
                cnt["q"] += 1
                ns = slice(nb * 512, (nb + 1) * 512)
                for c in range(2):
                    P.op("pe", lambda e, c=c, j=j, ns=ns: e.matmul(pd[0][:], lhsT=pT[:, c, j * 128:(j + 1) * 128], rhs=plew[:, c, ns], start=(c == 0), stop=(c == 1)),
                         reads=[(tag, "pT", j), (tag, "plew")], writes=[(tag, "pd", 0)])
                for c in range(8):
                    P.op("pe", lambda e, c=c, j=j, ns=ns, s=s: e.matmul(pd[1][:], lhsT=xt[s][:, c, 1 + j * 128:1 + (j + 1) * 128], rhs=plegw[:, c, ns],
                                                                       start=(c == 0), stop=(c == 7)),
                         reads=[(tag, "xt", s), (tag, "plegw")], writes=[(tag, "pd", 1)])
                P.op("act", lambda e, q=q: e.activation(out=sg[q][:], in_=pd[1][:], func=AF.Sigmoid), writes=[(tag, "sg", q), (tag, "pd", 1)])
                P.op("dve", lambda e, q=q: e.tensor_tensor(out=sg[q][:], in0=pd[0][:], in1=sg[q][:], op=ALU.mult),
                     reads=[(tag, "sg", q)], writes=[(tag, "sg", q), (tag, "pd", 0)])
                P.op("dve", lambda e, q=q, b=b, j=j, ns=ns: e.scalar_tensor_tensor(out=racc[j][:, ns], in0=x1t[b][:, ns], scalar=ALPHA, in1=sg[q][:],
                                                                                  op0=ALU.mult, op1=ALU.add),
                     reads=[(tag, "sg", q), (tag, "x1t", b)], writes=[(tag, "racc", j)])
        def wload(f):
            w = (cnt["w"] + f) % 3
            P.dma("sp", wg[w][:], wup_v[:, :, f * 128:(f + 1) * 128], chan=(tag, "wg", w), writes=[(tag, "wg", w)])
            P.dma("sp", wv[w][:], wup_v[:, :, D_FF + f * 128:D_FF + (f + 1) * 128], chan=(tag, "wv", w), writes=[(tag, "wv", w)])
        wload(0)
        wload(1)
        for f in range(NFT):
            w = (cnt["w"] + f) % 3
            u = cnt["u"] % 2
            g3 = cnt["u"] % 3
            cnt["u"] += 1
            if f + 2 < NFT:
                wload(f + 2)
            for c in range(8):
                P.op("pe", lambda e, c=c, w=w, u=u, s=s: e.matmul(pgate[u][:], lhsT=wg[w][:, c, :], rhs=xt[s][:, c, 1:513], start=(c == 0), stop=(c == 7)),
                     reads=[(tag, "wg", w), (tag, "xt", s)], writes=[(tag, "pgate", u)])
            for c in range(8):
                P.op("pe", lambda e, c=c, w=w, s=s: e.matmul(phalo[:], lhsT=wg[w][:, c, :], rhs=xt[s][:, c, 0:514:513], start=(c == 0), stop=(c == 7)),
                     reads=[(tag, "wg", w), (tag, "xt", s)], writes=[(tag, "phalo")])
            for c in range(8):
                P.op("pe", lambda e, c=c, w=w, u=u, s=s: e.matmul(pval[u][:], lhsT=wv[w][:, c, :], rhs=xt[s][:, c, 1:513], start=(c == 0), stop=(c == 7)),
                     reads=[(tag, "wv", w), (tag, "xt", s)], writes=[(tag, "pval", u)])
            kc, kt = (tag, "cv", g3), (tag, "tmp", g3)
            P.op("act", lambda e, u=u, g3=g3: e.copy(out=gext[g3][:, 1:513], in_=pgate[u][:]), writes=[(tag, "gext", g3), (tag, "pgate", u)])
            P.op("act", lambda e, u=u, g3=g3, f=f: e.activation(out=cv[g3][:], in_=pgate[u][:], func=AF.Identity, scale=cwb[:, f, 1:2], bias=cwb[:, f, 3:4]),
                 reads=[(tag, "cwb")], writes=[kc, (tag, "pgate", u)])
            P.op("act", lambda e, g3=g3: e.copy(out=gext[g3][:, 0:514:513], in_=phalo[:]), writes=[(tag, "gexth", g3), (tag, "phalo")])
            gkeys = [(tag, "gext", g3), (tag, "gexth", g3), (tag, "cwb")]
            P.op("dve", lambda e, g3=g3, f=f: e.scalar_tensor_tensor(out=cv[g3][:], in0=gext[g3][:, 0:512], scalar=cwb[:, f, 0:1], in1=cv[g3][:],
                                                                     op0=ALU.mult, op1=ALU.add), reads=gkeys + [kc], writes=[kc])
            P.op("dve", lambda e, g3=g3, f=f: e.scalar_tensor_tensor(out=cv[g3][:], in0=gext[g3][:, 2:514], scalar=cwb[:, f, 2:3], in1=cv[g3][:],
                                                                     op0=ALU.mult, op1=ALU.add), reads=gkeys + [kc], writes=[kc])
            P.op("act", lambda e, g3=g3: e.activation(out=tmp[g3][:], in_=cv[g3][:], func=AF.Square), reads=[kc], writes=[kt])
            P.op("act", lambda e, g3=g3: e.activation(out=tmp[g3][:], in_=tmp[g3][:], func=AF.Identity, scale=0.044715, bias=oneb[:, 0:1]),
                 reads=[kt, (tag, "oneb")], writes=[kt])
            P.op("pool", lambda e, g3=g3: e.tensor_tensor(out=tmp[g3][:], in0=tmp[g3][:], in1=cv[g3][:], op=ALU.mult), reads=[kt, kc], writes=[kt])
            P.op("act", lambda e, g3=g3: e.activation(out=tmp[g3][:], in_=tmp[g3][:], func=AF.Sigmoid, scale=GELU_K), reads=[kt], writes=[kt])
            P.op("pool", lambda e, g3=g3: e.tensor_tensor(out=tmp[g3][:], in0=tmp[g3][:], in1=cv[g3][:], op=ALU.mult), reads=[kt, kc], writes=[kt])
            P.op("dve", lambda e, u=u, g3=g3, f=f: e.tensor_tensor(out=hm[:, f, :], in0=pval[u][:], in1=tmp[g3][:], op=ALU.mult),
                 reads=[kt], writes=[(tag, "hm", f), (tag, "pval", u)])
        cnt["w"] += NFT
        for j in range(4):
            for nb in range(2):
                d = cnt["d"] % 2
                cnt["d"] += 1
                ns = slice(nb * 512, (nb + 1) * 512)
                for f in range(NFT):
                    P.op("pe", lambda e, f=f, j=j, ns=ns, d=d: e.matmul(pd[d][:], lhsT=hm[:, f, j * 128:(j + 1) * 128], rhs=wdn[:, f, ns],
                                                                       start=(f == 0), stop=(f == NFT - 1)),
                         reads=[(tag, "hm", f), (tag, "wdn")], writes=[(tag, "pd", d)])
                P.op("dve", lambda e, j=j, ns=ns, d=d: e.tensor_tensor(out=racc[j][:, ns], in0=pd[d][:], in1=racc[j][:, ns], op=ALU.add),
                     reads=[(tag, "racc", j)], writes=[(tag, "racc", j), (tag, "pd", d)])
            r0 = t0 + j * 128
            emit_ln(P, (tag, "ln"), racc[j][:], (tag, "racc", j), st, mv, rstd, epsb, gtab[:], btab[:], gk, racc[j][:], (tag, "racc", j))
            P.dma("sp", x2_d[r0:r0 + 128, :], racc[j][:], chan=(tag, "x2o", j), reads=[(tag, "racc", j)], writes=[(tag, "x2d", tt, j)])
            emit_to_featmajor(P, (tag, "fm"), racc[j][:], (tag, "racc", j), xbf[:], pst, x2T, lambda jj: (tag, "x2T", jj), ident, j)
        P.dma("sp", x2T_v[:, :, t0:t0 + 512], x2T[:], chan=(tag, "x2T"), reads=[(tag, "x2T", jj) for jj in range(4)], writes=[(tag, "x2Td", tt)])

    for tt in range(NTT):
        tile(tt)


def build_P2(NTOK):
    nc = bass.Bass("TRN2", target_bir_lowering=False)
    with ExitStack() as es:
        C = Ctx(nc, es)
        x1T_d = C.din("x1Te", [1024, NTOK + 2], BF16)
        x1_d = C.din("x1", [NTOK, 1024], F32)
        p_d = C.din("p", [NTOK, 256], F32)
        wup_d = C.din("ffn_w_up_bf", [1024, 2 * D_FF], BF16)
        wdn_d = C.din("ffn_w_down_bf", [D_FF, 1024], BF16)
        plew_d = C.din("ple_w_bf", [256, 1024], BF16)
        plegw_d = C.din("ple_gate_w_bf", [1024, 1024], BF16)
        cwb_d = C.din("cwb", [128, NFT, 4], F32)
        g2_d = C.din("ln_g", [1024], F32)
        b2_d = C.din("ln_b", [1024], F32)
        id_d = C.din("ident", [128, 128], BF16)
        x2_d = C.dout("x2", [NTOK, 1024], F32)
        x2T_d = C.dout("x2T", [1024, NTOK], BF16)
        ident = C.sb([128, 128], BF16, "ident")
        C.P.dma("sp", ident[:], id_d[:, :], chan="ident", writes=["ident"])
        stage_P2(C, x1T_d, x1_d, p_d, wup_d, wdn_d, plew_d, plegw_d, cwb_d[:, :, :], g2_d, b2_d, x2_d, x2T_d, ident, NTOK)
        C.P.emit(es)
    return nc


def build_W0_split():
    nc = bass.Bass("TRN2", target_bir_lowering=False)
    with ExitStack() as es:
        C = Ctx(nc, es)
        pairs = []
        for i in range(DEPTH):
            for nm, R, N in W0_SPECS:
                r = R // NCORES
                pairs.append((C.din("%s_%d" % (nm, i), [r, N], F32), C.dout("%s_%d_bf" % (nm, i), [r, N], BF16), r, N))
        stage_W0v(C, pairs)
        C.P.emit(es)
    return nc


def stage_W0v(C, pairs, tag="w0"):
    P = C.P
    stg = [C.sb([128, 2816], F32, "w0s") for _ in range(2)]
    obf = [C.sb([128, 2816], BF16, "w0o") for _ in range(2)]
    i = 0
    engs = ("dve", "pool", "act")
    for src_d, dst_d, R, N in pairs:
        for r0 in range(0, R, 128):
            pr = min(128, R - r0)
            for n0 in range(0, N, 2816):
                n1 = min(N, n0 + 2816)
                w = n1 - n0
                s = i % 2
                P.dma("sp", stg[s][0:pr, 0:w], src_d[r0:r0 + pr, n0:n1], chan=(tag, "in", s), writes=[(tag, "stg", s)])
                en = engs[i % 3]
                if en == "act":
                    P.op("act", lambda e, s=s, w=w, pr=pr: e.copy(out=obf[s][0:pr, 0:w], in_=stg[s][0:pr, 0:w]), reads=[(tag, "stg", s)], writes=[(tag, "obf", s)])
                else:
                    P.op(en, lambda e, s=s, w=w, pr=pr: e.tensor_copy(out=obf[s][0:pr, 0:w], in_=stg[s][0:pr, 0:w]), reads=[(tag, "stg", s)], writes=[(tag, "obf", s)])
                P.dma("pool", dst_d[r0:r0 + pr, n0:n1], obf[s][0:pr, 0:w], chan=(tag, "out", s), reads=[(tag, "obf", s)], writes=[(tag, "dst", i)])
                i += 1


def build_fused(L):
    nc = bass.Bass("TRN2", target_bir_lowering=False)

    def din(name, shape, dt):
        return nc.dram_tensor(name, list(shape), dt, kind="ExternalInput").ap()

    def dint(name, shape, dt):
        return nc.dram_tensor(name, list(shape), dt, kind="Internal").ap()

    x_d = din("x", [L, 1024], F32)
    p_d = [din("p%d" % l, [L, 256], F32) for l in range(DEPTH)]
    pos_d = din("pos", [L], I32)
    rc_d = din("rc", [128, 4], F32)
    idb_d = din("ident", [128, 128], BF16)
    idf_d = din("identf", [128, 128], F32)
    s5c_d = din("s5c", [128, S5C_W], F32)
    zero_d = din("zeros", [128, 8], BF16)
    wsrc = {}
    wbf = {}
    for l in range(DEPTH):
        for nm, R, N in W0_SPECS:
            wsrc[(nm, l)] = din("%s_%d" % (nm, l), [R, N], F32)
            wbf[(nm, l)] = dint("%s_%d_bf" % (nm, l), [R, N], BF16)
    wfm_d = {(l, hp): din("wfm_%d_%d" % (l, hp), [1024, NFM * 128], F32) for l in range(DEPTH) for hp in range(2)}
    wtm_d = {(l, hp): din("wtm_%d_%d" % (l, hp), [1024, 512], F32) for l in range(DEPTH) for hp in range(2)}
    lamp_d = [din("lamp%d" % l, [4, 64], F32) for l in range(DEPTH)]
    subg_d = [din("subg%d" % l, [128], F32) for l in range(DEPTH)]
    lc_d = [din("lc%d" % l, [128, 2], F32) for l in range(DEPTH)]
    rt_d = [din("rt%d" % hp, [128, RT_W], F32) for hp in range(2)]
    gng_d = [din("gng%d" % l, [64], F32) for l in range(DEPTH)]
    gnb_d = [din("gnb%d" % l, [64], F32) for l in range(DEPTH)]
    s5_d = {(l, hp): (din("s5p_%d_%d" % (l, hp), [128, 49], F32), din("s5bre_%d_%d" % (l, hp), [128, 16, 16], F32),
                      din("s5bim_%d_%d" % (l, hp), [128, 16, 16], F32), din("s5cc_%d_%d" % (l, hp), [128, 2, 128], F32))
            for l in range(DEPTH) for hp in range(2)}
    glb_d = [din("glb%d" % l, [256], F32) for l in range(DEPTH)]
    ln_d = [[din("ln%d_%s%d" % (k, gb, l), [1024], F32) for gb in ("g", "b")] for l in range(DEPTH) for k in (1, 2)]
    cwb_d = [din("cwb%d" % l, [128, NFT, 4], F32) for l in range(DEPTH)]
    out_d = nc.dram_tensor("out", [L, 1024], F32, kind="ExternalOutput").ap()

    xT = dint("xT_s", [1024, L], BF16)
    fmT = dint("fmT_s", [7, 128, L], BF16)
    vr = dint("vr_s", [L, 384], BF16)
    sgd = dint("sg_s", [L, 128], F32)
    ycT = dint("ycT_s", [1024, L], BF16)
    x1 = dint("x1_s", [L, 1024], F32)
    x1Te = dint("x1Te_s", [1024, L + 2], BF16)
    xmid = dint("xmid_s", [L, 1024], F32)
    ycT_r = ycT.rearrange("(r p) t -> r p t", p=128)

    def scope(fn, with_ident=False):
        with ExitStack() as es:
            C = Ctx(nc, es)
            ident = None
            if with_ident:
                ident = C.sb([128, 128], BF16, "ident")
                C.P.dma("sp", ident[:], idb_d[:, :], chan="ident", writes=["ident"])
            fn(C, ident)
            C.P.emit(es, own_sems=True)
        nc.all_engine_barrier()
        nc.clear_and_free_semaphores(C.P.sem_handles)
        nc.all_engine_barrier()

    def zero_halo(C, ident):
        z = C.sb([128, 8], BF16, "z")
        C.P.dma("sp", z[:], zero_d[:, :], chan="z", writes=["z"])
        v = x1Te.rearrange("(c p) t -> p c t", p=128)
        C.P.dma("sp", v[:, :, 0:1], z[:].unsqueeze(2), chan="z2", reads=["z"], slow=True)
        C.P.dma("sp", v[:, :, L + 1:L + 2], z[:].unsqueeze(2), chan="z2", reads=["z"], slow=True)

    scope(lambda C, i: stage_W0v(C, [(wsrc[(nm, l)], wbf[(nm, l)], R, N) for l in range(DEPTH) for nm, R, N in W0_SPECS]))
    scope(zero_halo)
    scope(lambda C, i: stage_T0(C, x_d, xT, i, L), with_ident=True)
    for l in range(DEPTH):
        xin_d = x_d if l == 0 else xmid
        xout_d = xmid if l == 0 else out_d
        for hp in range(2):
            scope(lambda C, i, l=l, hp=hp: stage_A1(C, xT, wfm_d[(l, hp)], wtm_d[(l, hp)], pos_d, rc_d, fmT, vr, sgd, L))
            scope(lambda C, i, l=l, hp=hp: stage_A2(C, fmT, vr, lamp_d[l], subg_d[l], lc_d[l], ycT_r[2 * hp:2 * hp + 2, :, :], i, L), with_ident=True)
            scope(lambda C, i, l=l, hp=hp: stage_A3(C, fmT, vr, sgd, rt_d[hp], gng_d[l], gnb_d[l], ycT_r[4 + hp, :, :], i, L), with_ident=True)
            scope(lambda C, i, l=l, hp=hp: stage_A4(C, fmT, s5_d[(l, hp)][0][:, :], s5_d[(l, hp)][1][:, :, :], s5_d[(l, hp)][2][:, :, :],
                                                   s5_d[(l, hp)][3][:, :, :], s5c_d[:, :], idf_d[:, :], ycT_r[6 + hp, :, :], L))
        scope(lambda C, i, l=l, xin_d=xin_d: stage_P1(C, ycT, xin_d, wbf[("w_out", l)], wbf[("s5_glu_w", l)], glb_d[l], ln_d[2 * l][0], ln_d[2 * l][1],
                                                     x1, x1Te[:, 1:L + 1], i, L), with_ident=True)
        scope(lambda C, i, l=l, xout_d=xout_d: stage_P2(C, x1Te, x1, p_d[l], wbf[("ffn_w_up", l)], wbf[("ffn_w_down", l)], wbf[("ple_w", l)],
                                                       wbf[("ple_gate_w", l)], cwb_d[l][:, :, :], ln_d[2 * l + 1][0], ln_d[2 * l + 1][1], xout_d, xT, i, L),
              with_ident=True)
    return nc


def fused_inputs(inp, b):
    m = {"x": np.ascontiguousarray(inp["x"][b], dtype=np.float32), "pos": np.ascontiguousarray(inp["positions"][b], dtype=np.int32),
         "rc": rc_const(), "ident": np.eye(128, dtype=np.float32).astype(ml_dtypes.bfloat16), "identf": np.eye(128, dtype=np.float32),
         "s5c": s5_consts(), "zeros": np.zeros((128, 8), ml_dtypes.bfloat16)}
    for l in range(DEPTH):
        lam_init = 0.8 - 0.6 * math.exp(-0.3 * l)
        m["p%d" % l] = np.ascontiguousarray(inp["p"][l][b], dtype=np.float32)
        for nm, R, N in W0_SPECS:
            m["%s_%d" % (nm, l)] = np.ascontiguousarray(inp[nm][l], dtype=np.float32)
        for hp in range(2):
            fm, tm = a1_columns(hp)
            m["wfm_%d_%d" % (l, hp)] = np.ascontiguousarray(inp["w_in"][l][:, fm], dtype=np.float32)
            m["wtm_%d_%d" % (l, hp)] = np.ascontiguousarray(inp["w_in"][l][:, tm], dtype=np.float32)
            s5 = s5_layout(inp, l, hp)
            for k in ("s5p", "s5bre", "s5bim", "s5cc"):
                m["%s_%d_%d" % (k, l, hp)] = s5[k]
        m["lamp%d" % l] = np.stack([inp["da_lambda_q1"][l], inp["da_lambda_k1"][l], inp["da_lambda_q2"][l], inp["da_lambda_k2"][l]]).astype(np.float32)
        m["subg%d" % l] = np.ascontiguousarray(inp["da_subln_g"][l], dtype=np.float32)
        m["lc%d" % l] = np.tile(np.array([[lam_init, 1.0 - lam_init]], np.float32), (128, 1))
        m["gng%d" % l] = np.ascontiguousarray(inp["ret_gn_g"][l], dtype=np.float32)
        m["gnb%d" % l] = np.ascontiguousarray(inp["ret_gn_b"][l], dtype=np.float32)
        m["glb%d" % l] = np.ascontiguousarray(inp["s5_glu_b"][l], dtype=np.float32)
        for k in (1, 2):
            m["ln%d_g%d" % (k, l)] = np.ascontiguousarray(inp["ln%d_g" % k][l], dtype=np.float32)
            m["ln%d_b%d" % (k, l)] = np.ascontiguousarray(inp["ln%d_b" % k][l], dtype=np.float32)
        m["cwb%d" % l] = conv_layout(inp["ffn_conv_w"][l], inp["ffn_conv_b"][l])
    for hp in range(2):
        m["rt%d" % hp] = ret_consts(hp)
    return m


def kernel_fused(**inputs):
    inp = {k: np.asarray(v) for k, v in inputs.items()}
    B, L, _ = inp["x"].shape
    NTOK = L // 2
    nc = _prog("fused", build_fused, L)
    bmaps = [fused_inputs(inp, b) for b in range(B)]
    res = _run(nc, [bmaps[c // 2] for c in range(NCORES)])
    out = np.empty((B, L, D_MODEL), np.float32)
    for c in range(NCORES):
        b, h = c // 2, c % 2
        out[b, h * NTOK:(h + 1) * NTOK] = res[c]["out"][h * NTOK:(h + 1) * NTOK]
    return out


_PROGS = {}


def _prog(name, fn, *args):
    key = (name,) + args
    if key not in _PROGS:
        _PROGS[key] = fn(*args)
    return _PROGS[key]


def _run(nc, maps):
    return run_bass_kernel_spmd(nc, maps, core_ids=list(range(NCORES))).results


def kernel_unfused(**inputs):
    inp = {k: np.asarray(v) for k, v in inputs.items()}
    x = np.ascontiguousarray(inp["x"], dtype=np.float32)
    B, L, _ = x.shape
    assert B * 2 == NCORES
    NTOK = L // 2
    ident_bf = np.eye(128, dtype=np.float32).astype(ml_dtypes.bfloat16)
    identf = np.eye(128, dtype=np.float32)
    rc = rc_const()
    s5c = s5_consts()
    cores = [(c // 2, c % 2) for c in range(NCORES)]

    maps = []
    for c in range(NCORES):
        m = {}
        for i in range(DEPTH):
            for nm, R, N in W0_SPECS:
                r = R // NCORES
                m["%s_%d" % (nm, i)] = np.ascontiguousarray(inp[nm][i][c * r:(c + 1) * r], dtype=np.float32)
        maps.append(m)
    res = _run(_prog("W0", build_W0_split), maps)
    wbf = [{nm: np.concatenate([res[c]["%s_%d_bf" % (nm, i)] for c in range(NCORES)], axis=0) for nm, R, N in W0_SPECS}
           for i in range(DEPTH)]

    res = _run(_prog("T0", build_T0, NTOK), [{"x": np.ascontiguousarray(x[b, h * NTOK:(h + 1) * NTOK]), "ident": ident_bf} for b, h in cores])
    xT = [np.concatenate([res[2 * b]["xT"], res[2 * b + 1]["xT"]], axis=1) for b in range(B)]
    xcur = [np.ascontiguousarray(x[b, h * NTOK:(h + 1) * NTOK]) for b, h in cores]

    for i in range(DEPTH):
        lam_init = 0.8 - 0.6 * math.exp(-0.3 * i)
        w_in = inp["w_in"][i]
        maps = []
        for b, h in cores:
            fm, tm = a1_columns(h)
            maps.append({"xT": xT[b], "wfm": np.ascontiguousarray(w_in[:, fm], dtype=np.float32),
                         "wtm": np.ascontiguousarray(w_in[:, tm], dtype=np.float32),
                         "pos": np.ascontiguousarray(inp["positions"][b], dtype=np.int32), "rc": rc})
        a1 = _run(_prog("A1", build_A1, L), maps)
        lamp = np.stack([inp["da_lambda_q1"][i], inp["da_lambda_k1"][i], inp["da_lambda_q2"][i], inp["da_lambda_k2"][i]]).astype(np.float32)
        lc = np.tile(np.array([[lam_init, 1.0 - lam_init]], np.float32), (128, 1))
        a2 = _run(_prog("A2", build_A2, L), [{"fmT": a1[c]["fmT"], "vr": a1[c]["vr"], "lamp": lamp,
                                              "subg": np.ascontiguousarray(inp["da_subln_g"][i], dtype=np.float32), "lc": lc, "ident": ident_bf}
                                             for c in range(NCORES)])
        a3 = _run(_prog("A3", build_A3, L), [{"fmT": a1[c]["fmT"], "vr": a1[c]["vr"], "sg": a1[c]["sg"], "rt": ret_consts(cores[c][1]),
                                              "gng": np.ascontiguousarray(inp["ret_gn_g"][i], dtype=np.float32),
                                              "gnb": np.ascontiguousarray(inp["ret_gn_b"][i], dtype=np.float32), "ident": ident_bf}
                                             for c in range(NCORES)])
        maps = []
        for c, (b, h) in enumerate(cores):
            m = {"fmT": a1[c]["fmT"], "s5c": s5c, "identf": identf}
            m.update(s5_layout(inp, i, h))
            maps.append(m)
        a4 = _run(_prog("A4", build_A4, L), maps)
        maps = []
        for c, (b, h) in enumerate(cores):
            ts = slice(h * NTOK, (h + 1) * NTOK)
            rows = []
            for hp in range(2):
                rows += [a2[2 * b + hp]["ydaT"][0][:, ts], a2[2 * b + hp]["ydaT"][1][:, ts]]
            rows += [a3[2 * b + hp]["yretT"][:, ts] for hp in range(2)]
            rows += [a4[2 * b + hp]["ys5T"][:, ts] for hp in range(2)]
            maps.append({"ycT": np.ascontiguousarray(np.concatenate(rows, axis=0)), "x": xcur[c], "w_out_bf": wbf[i]["w_out"],
                         "s5_glu_w_bf": wbf[i]["s5_glu_w"], "glb": np.ascontiguousarray(inp["s5_glu_b"][i], dtype=np.float32),
                         "ln_g": np.ascontiguousarray(inp["ln1_g"][i], dtype=np.float32),
                         "ln_b": np.ascontiguousarray(inp["ln1_b"][i], dtype=np.float32), "ident": ident_bf})
        p1 = _run(_prog("P1", build_P1, NTOK), maps)
        cwb = conv_layout(inp["ffn_conv_w"][i], inp["ffn_conv_b"][i])
        maps = []
        for c, (b, h) in enumerate(cores):
            ts = slice(h * NTOK, (h + 1) * NTOK)
            ext = np.zeros((1024, NTOK + 2), dtype=ml_dtypes.bfloat16)
            ext[:, 1:NTOK + 1] = p1[c]["x1T"]
            if h == 1:
                ext[:, 0] = p1[c - 1]["x1T"][:, NTOK - 1]
            else:
                ext[:, NTOK + 1] = p1[c + 1]["x1T"][:, 0]
            maps.append({"x1Te": ext, "x1": p1[c]["x1"], "p": np.ascontiguousarray(inp["p"][i][b][ts], dtype=np.float32),
                         "ffn_w_up_bf": wbf[i]["ffn_w_up"], "ffn_w_down_bf": wbf[i]["ffn_w_down"], "ple_w_bf": wbf[i]["ple_w"],
                         "ple_gate_w_bf": wbf[i]["ple_gate_w"], "cwb": cwb,
                         "ln_g": np.ascontiguousarray(inp["ln2_g"][i], dtype=np.float32),
                         "ln_b": np.ascontiguousarray(inp["ln2_b"][i], dtype=np.float32), "ident": ident_bf})
        p2 = _run(_prog("P2", build_P2, NTOK), maps)
        xcur = [p2[c]["x2"] for c in range(NCORES)]
        xT = [np.concatenate([p2[2 * b]["x2T"], p2[2 * b + 1]["x2T"]], axis=1) for b in range(B)]

    out = np.empty((B, L, D_MODEL), np.float32)
    for c, (b, h) in enumerate(cores):
        out[b, h * NTOK:(h + 1) * NTOK] = xcur[c]
    return out


def kernel(**inputs):
    return kernel_fused(**inputs)
```

```python
import math
from contextlib import ExitStack

import numpy as np
import ml_dtypes

import concourse.bass as bass
import concourse.mybir as mybir
from concourse.bass_utils import run_bass_kernel_spmd

F32 = mybir.dt.float32
BF16 = mybir.dt.bfloat16
I32 = mybir.dt.int32
ALU = mybir.AluOpType
AF = mybir.ActivationFunctionType
AX = mybir.AxisListType

D_MODEL = 1024
BATCH = 4
SEQ = 8192
DEPTH = 2
PLE_DIM = 256
D_FF = 2816
LN_EPS = 1e-5
ALPHA = (2 * DEPTH) ** 0.25
COL_DA_Q = 0
COL_DA_K = 512
COL_DA_V = 1024
COL_RET_Q = 1536
COL_RET_K = 1792
COL_RET_V = 2048
COL_RET_G = 2304
COL_S5_U = 2560
NCORES = 8
TWO_PI = 2.0 * math.pi


class Prog:
    ENGS = ("pe", "act", "dve", "pool", "sp")

    def __init__(self, nc):
        self.nc = nc
        self.ops = {e: [] for e in self.ENGS}
        self.cnt = {}
        self.seen = {e: {} for e in self.ENGS}
        self.last_w = {}
        self.readers = {}
        self.n = 0

    def _deps(self, reads, writes):
        raw, other = set(), set()

        def fix(t):
            if isinstance(t[0], tuple):
                t = (t[0], self.cnt[t[0]])
            return t
        for k in reads:
            t = self.last_w.get(k)
            if t is not None:
                raw.add(fix(t))
        for k in writes:
            t = self.last_w.get(k)
            if t is not None:
                other.add(fix(t))
            for t in self.readers.get(k, ()):
                other.add(fix(t))
        return raw, other

    def _commit(self, tok, reads, writes):
        for k in reads:
            self.readers.setdefault(k, []).append(tok)
        for k in writes:
            self.last_w[k] = tok
            self.readers[k] = []

    def _waits(self, eng, deps):
        raw, other = deps
        best = {}
        for src_set, is_raw in ((raw, True), (other, False)):
            for (sk, v) in src_set:
                if sk == eng and eng not in ("act", "dve", "pool"):
                    continue
                if self.seen[eng].get(sk, 0) >= v:
                    continue
                if best.get(sk, 0) < v:
                    best[sk] = v
        for sk, v in best.items():
            self.seen[eng][sk] = v
        return list(best.items())

    def op(self, eng, fn, reads=(), writes=()):
        deps = self._deps(reads, writes)
        waits = self._waits(eng, deps)
        v = self.cnt.get(eng, 0) + 1
        self.cnt[eng] = v
        tok = (eng, v)
        self.ops[eng].append((waits, fn, (eng, 1)))
        self._commit(tok, reads, writes)
        self.n += 1
        return tok

    def dma(self, q, out, in_, chan, reads=(), writes=(), slow=False):
        deps = self._deps(reads, writes)
        waits = self._waits(q, deps)
        sk = ("dma", chan)
        v = self.cnt.get(sk, 0) + 16
        self.cnt[sk] = v
        tok = (sk, v)
        if slow:
            self.ops[q].append((waits, lambda e, o=out, i=in_: e.dma_start(out=o, in_=i, allow_slow_non_contiguous=True), (sk, 16)))
        else:
            self.ops[q].append((waits, lambda e, o=out, i=in_: e.dma_start(out=o, in_=i), (sk, 16)))
        self._commit(tok, reads, writes)
        self.n += 1
        return tok

    def emit(self, es, own_sems=False):
        nc = self.nc
        self.sem_handles = []
        fin = [(sk, v) for sk, v in self.cnt.items() if isinstance(sk, tuple)]
        sems = {}
        for i, sk in enumerate(self.cnt.keys()):
            _UID[0] += 1
            if own_sems:
                sems[sk] = nc.alloc_semaphore(name="s%d_%d" % (i, _UID[0]))
                self.sem_handles.append(sems[sk])
            else:
                sems[sk] = es.enter_context(nc.semaphore("s%d_%d" % (i, _UID[0])))
        block = es.enter_context(nc.Block())

        def replay(name, e):
            for waits, fn, inc in self.ops[name]:
                for sk, v in waits:
                    e.wait_ge(sems[sk], v)
                ins = fn(e)
                ins.then_inc(sems[inc[0]], inc[1])
            if name == "sp":
                for sk, v in fin:
                    e.wait_ge(sems[sk], v)
                for en in ("pe", "act", "dve", "pool"):
                    if self.cnt.get(en, 0):
                        e.wait_ge(sems[en], self.cnt[en])

        @block.sync
        def _(e):
            replay("sp", e)

        @block.tensor
        def _(e):
            replay("pe", e)

        @block.scalar
        def _(e):
            replay("act", e)

        @block.vector
        def _(e):
            replay("dve", e)

        @block.gpsimd
        def _(e):
            replay("pool", e)


_UID = [0]


class Ctx:
    def __init__(self, nc, es):
        self.nc = nc
        self.es = es
        self.P = Prog(nc)
        self._i = 0

    def sb(self, shape, dt, name=None):
        _UID[0] += 1
        return self.es.enter_context(self.nc.sbuf_tensor("%s_%d" % (name or "t", _UID[0]), list(shape), dt))

    def ps(self, shape, dt, name=None):
        _UID[0] += 1
        return self.es.enter_context(self.nc.psum_tensor("%s_%d" % (name or "p", _UID[0]), list(shape), dt))

    def din(self, name, shape, dt):
        return self.nc.dram_tensor(name, list(shape), dt, kind="ExternalInput").ap()

    def dout(self, name, shape, dt):
        return self.nc.dram_tensor(name, list(shape), dt, kind="ExternalOutput").ap()

    def dint(self, name, shape, dt):
        return self.nc.dram_tensor(name, list(shape), dt, kind="Internal").ap()


def bf(a):
    return np.ascontiguousarray(a).astype(ml_dtypes.bfloat16)


def stage_T0(C, x_d, xT_d, ident_bf, ntok, tag="t0"):
    P = C.P
    xin = [C.sb([128, 1024], F32, "xin") for _ in range(2)]
    xbf = [C.sb([128, 1024], BF16, "xbf") for _ in range(2)]
    xT = [C.sb([128, 8, 512], BF16, "xT") for _ in range(2)]
    pst = [C.ps([128, 8, 128], BF16, "pst") for _ in range(2)]
    xT_v = xT_d.rearrange("(c p) t -> p c t", p=128)
    nt = ntok // 128
    for i in range(nt):
        s = i % 2
        g = (i // 4) % 2
        P.dma("sp", xin[s][:], x_d[i * 128:(i + 1) * 128, :], chan=(tag, "xin", s),
              writes=[(tag, "xin", s)])
        P.op("act", lambda e, s=s: e.copy(out=xbf[s][:], in_=xin[s][:]),
             reads=[(tag, "xin", s)], writes=[(tag, "xbf", s)])
        for c in range(8):
            P.op("pe", lambda e, s=s, c=c: e.transpose(out=pst[s][:, c, :], in_=xbf[s][:, c * 128:(c + 1) * 128],
                                                        identity=ident_bf[:]),
                 reads=[(tag, "xbf", s), "ident"], writes=[(tag, "pst", s, c)])
        P.op("dve", lambda e, s=s, g=g, i=i: e.tensor_copy(out=xT[g][:, :, (i % 4) * 128:(i % 4 + 1) * 128], in_=pst[s][:]),
             reads=[(tag, "pst", s, c) for c in range(8)], writes=[(tag, "xT", g, i % 4)])
        if i % 4 == 3:
            t0 = (i // 4) * 512
            P.dma("pool", xT_v[:, :, t0:t0 + 512], xT[g][:], chan=(tag, "xTo", g),
                  reads=[(tag, "xT", g, j) for j in range(4)], writes=[(tag, "xTd", i // 4)])


def build_T0(ntok):
    nc = bass.Bass("TRN2", target_bir_lowering=False)
    with ExitStack() as es:
        C = Ctx(nc, es)
        x_d = C.din("x", [ntok, 1024], F32)
        id_d = C.din("ident", [128, 128], BF16)
        xT_d = C.dout("xT", [1024, ntok], BF16)
        ident = C.sb([128, 128], BF16, "ident")
        C.P.dma("sp", ident[:], id_d[:, :], chan="ident", writes=["ident"])
        stage_T0(C, x_d, xT_d, ident, ntok)
        C.P.emit(es)
    return nc


NFM = 13
ROPED = 6


def a1_columns(h):
    def swap64(cols):
        cols = np.asarray(cols).reshape(-1, 64)
        return np.concatenate([cols[:, 32:], cols[:, :32]], axis=1).reshape(-1)
    tiles = []
    for base in (COL_DA_Q, COL_DA_K):
        for hh in range(2):
            tiles.append(base + (2 * h + hh) * 128 + np.arange(128))
    tiles.append(COL_RET_Q + 2 * h * 64 + np.arange(128))
    tiles.append(COL_RET_K + 2 * h * 64 + np.arange(128))
    sw = [swap64(t) for t in tiles]
    u = COL_S5_U + 128 * h + np.arange(128)
    fm = np.concatenate(tiles + sw + [u])
    tm = np.concatenate([COL_DA_V + 2 * h * 128 + np.arange(256),
                         COL_RET_V + 2 * h * 64 + np.arange(128),
                         COL_RET_G + 2 * h * 64 + np.arange(128)])
    return fm, tm


def rope_consts():
    inv = 10000.0 ** (-np.arange(0, 64, 2, dtype=np.float64) / 64.0)
    p = np.arange(128)
    invf = inv[p % 32].astype(np.float32).reshape(128, 1)
    sgn = np.where((p % 64) < 32, -1.0, 1.0).astype(np.float32).reshape(128, 1)
    return invf, sgn


CW1 = float(np.float32(6.28125))
CW2 = float(np.float32(TWO_PI - 6.28125))


def sincos(P, eng, tag, ang, ang_key, sarg, carg, tmp_i, tmp_f):
    ki, kf, ks, kc = (tag, "rr_i"), (tag, "rr_f"), (tag, "sarg"), (tag, "carg")
    P.op(eng, lambda e: e.tensor_scalar(out=tmp_i[:], in0=ang[:], scalar1=1.0 / TWO_PI, scalar2=None, op0=ALU.mult),
         reads=[ang_key], writes=[ki])
    P.op(eng, lambda e: e.tensor_copy(out=tmp_f[:], in_=tmp_i[:]), reads=[ki], writes=[kf])
    P.op(eng, lambda e: e.scalar_tensor_tensor(out=sarg[:], in0=tmp_f[:], scalar=-CW1, in1=ang[:], op0=ALU.mult, op1=ALU.add),
         reads=[kf, ang_key], writes=[ks])
    P.op(eng, lambda e: e.scalar_tensor_tensor(out=sarg[:], in0=tmp_f[:], scalar=-CW2, in1=sarg[:], op0=ALU.mult, op1=ALU.add),
         reads=[kf, ks], writes=[ks])

    def wrap(t, key):
        P.op(eng, lambda e: e.tensor_scalar(out=tmp_f[:], in0=t[:], scalar1=math.pi, scalar2=-TWO_PI, op0=ALU.is_gt, op1=ALU.mult),
             reads=[key], writes=[kf])
        P.op(eng, lambda e: e.tensor_tensor(out=t[:], in0=t[:], in1=tmp_f[:], op=ALU.add), reads=[key, kf], writes=[key])
        P.op(eng, lambda e: e.tensor_scalar(out=tmp_f[:], in0=t[:], scalar1=-math.pi, scalar2=TWO_PI, op0=ALU.is_lt, op1=ALU.mult),
             reads=[key], writes=[kf])
        P.op(eng, lambda e: e.tensor_tensor(out=t[:], in0=t[:], in1=tmp_f[:], op=ALU.add), reads=[key, kf], writes=[key])
    wrap(sarg, ks)
    P.op(eng, lambda e: e.tensor_scalar(out=carg[:], in0=sarg[:], scalar1=0.5 * math.pi, scalar2=None, op0=ALU.add),
         reads=[ks], writes=[kc])
    wrap(carg, kc)


def stage_A1(C, xT_d, wfm_d, wtm_d, pos_d, rc_d, fmT_d, vr_d, sg_d, L, tag="a1"):
    P = C.P
    NW = NFM * 128
    wfm = C.sb([128, 8, NW], BF16, "wfm")
    wtm = C.sb([128, 8, 512], BF16, "wtm")
    stg = [C.sb([128, NW], F32, "wstg") for _ in range(2)]
    rc = C.sb([128, 4], F32, "rc")
    P.dma("sp", rc[:], rc_d[:, :], chan=(tag, "rc"), writes=[(tag, "rc")])
    wfm_v = wfm_d.rearrange("(c p) n -> p c n", p=128)
    wtm_v = wtm_d.rearrange("(c p) n -> p c n", p=128)
    for c in range(8):
        s = c % 2
        P.dma("sp", stg[s][:], wfm_v[:, c, :], chan=(tag, "wstg", s), writes=[(tag, "wstg", s)])
        P.op("pool", lambda e, s=s, c=c: e.tensor_copy(out=wfm[:, c, :], in_=stg[s][:]),
             reads=[(tag, "wstg", s)], writes=[(tag, "wfm", c)])
    for c in range(8):
        s = c % 2
        P.dma("sp", stg[s][:, 0:512], wtm_v[:, c, :], chan=(tag, "wstg", s), writes=[(tag, "wstg", s)])
        P.op("pool", lambda e, s=s, c=c: e.tensor_copy(out=wtm[:, c, :], in_=stg[s][:, 0:512]),
             reads=[(tag, "wstg", s)], writes=[(tag, "wtm", c)])
    wfm_k = [(tag, "wfm", c) for c in range(8)]
    wtm_k = [(tag, "wtm", c) for c in range(8)]

    xT = [C.sb([128, 8, 512], BF16, "xT") for _ in range(2)]
    posi = [C.sb([128, 512], I32, "posi") for _ in range(2)]
    posf = C.sb([128, 512], F32, "posf")
    a0 = C.sb([128, 512], F32, "a0")
    sarg = C.sb([128, 512], F32, "sarg")
    carg = C.sb([128, 512], F32, "carg")
    rr_i = C.sb([128, 512], I32, "rr_i")
    rr_f = C.sb([128, 512], F32, "rr_f")
    cosT = [C.sb([128, 512], F32, "cosT") for _ in range(2)]
    sinT = [C.sb([128, 512], F32, "sinT") for _ in range(2)]
    t1 = [C.sb([128, 512], F32, "t1") for _ in range(2)]
    t2 = [C.sb([128, 512], F32, "t2") for _ in range(2)]
    fmo = [C.sb([128, 7, 512], BF16, "fmo") for _ in range(2)]
    tmo = [C.sb([128, 4, 384], BF16, "tmo") for _ in range(2)]
    sgo = [C.sb([128, 4, 128], F32, "sgo") for _ in range(2)]
    psA = [C.ps([128, 512], F32, "psA") for _ in range(2)]
    psB = [C.ps([128, 512], F32, "psB") for _ in range(2)]
    psC = [C.ps([128, 512], F32, "psC") for _ in range(3)]
    xT_v = xT_d.rearrange("(c p) t -> p c t", p=128)
    fmT_v = fmT_d.rearrange("r p t -> p r t")
    nt = L // 512
    pc = 0
    cc = 0
    for it in range(nt):
        s = it % 2
        tk0 = it * 512
        P.dma("sp", xT[s][:], xT_v[:, :, tk0:tk0 + 512], chan=(tag, "xT", s), writes=[(tag, "xT", s)])
        P.dma("sp", posi[s][:], pos_d[tk0:tk0 + 512].partition_broadcast(128), chan=(tag, "pos", s),
              writes=[(tag, "pos", s)])
        P.op("dve", lambda e, s=s: e.tensor_copy(out=posf[:], in_=posi[s][:]),
             reads=[(tag, "pos", s)], writes=[(tag, "posf")])
        P.op("dve", lambda e: e.tensor_scalar(out=a0[:], in0=posf[:], scalar1=rc[:, 0:1], scalar2=None, op0=ALU.mult),
             reads=[(tag, "posf"), (tag, "rc")], writes=[(tag, "a0")])
        sincos(P, "dve", tag, a0, (tag, "a0"), sarg, carg, rr_i, rr_f)
        P.op("act", lambda e, s=s: e.activation(out=sinT[s][:], in_=sarg[:], func=AF.Sin, scale=rc[:, 1:2]),
             reads=[(tag, "sarg"), (tag, "rc")], writes=[(tag, "sinT", s)])
        P.op("act", lambda e, s=s: e.activation(out=cosT[s][:], in_=carg[:], func=AF.Sin),
             reads=[(tag, "carg")], writes=[(tag, "cosT", s)])
        for r in range(ROPED):
            b = pc % 2
            pc += 1
            for c in range(8):
                P.op("pe", lambda e, b=b, r=r, c=c, s=s: e.matmul(psA[b][:], lhsT=wfm[:, c, r * 128:(r + 1) * 128],
                                                                 rhs=xT[s][:, c, :], start=(c == 0), stop=(c == 7)),
                     reads=[(tag, "xT", s), (tag, "wfm", c)], writes=[(tag, "psA", b)])
            for c in range(8):
                P.op("pe", lambda e, b=b, r=r, c=c, s=s: e.matmul(psB[b][:], lhsT=wfm[:, c, (ROPED + r) * 128:(ROPED + r + 1) * 128],
                                                                 rhs=xT[s][:, c, :], start=(c == 0), stop=(c == 7)),
                     reads=[(tag, "xT", s), (tag, "wfm", c)], writes=[(tag, "psB", b)])
            P.op("dve", lambda e, b=b, s=s: e.tensor_tensor(out=t1[b][:], in0=psA[b][:], in1=cosT[s][:], op=ALU.mult),
                 reads=[(tag, "psA", b), (tag, "cosT", s)], writes=[(tag, "t1", b)])
            P.op("dve", lambda e, b=b, s=s: e.tensor_tensor(out=t2[b][:], in0=psB[b][:], in1=sinT[s][:], op=ALU.mult),
                 reads=[(tag, "psB", b), (tag, "sinT", s)], writes=[(tag, "t2", b)])
            P.op("pool", lambda e, b=b, s=s, r=r: e.tensor_tensor(out=fmo[s][:, r, :], in0=t1[b][:], in1=t2[b][:], op=ALU.add),
                 reads=[(tag, "t1", b), (tag, "t2", b)], writes=[(tag, "fmo", s, r)])
        b = cc % 3
        cc += 1
        for c in range(8):
            P.op("pe", lambda e, b=b, c=c, s=s: e.matmul(psC[b][:], lhsT=wfm[:, c, 12 * 128:13 * 128],
                                                        rhs=xT[s][:, c, :], start=(c == 0), stop=(c == 7)),
                 reads=[(tag, "xT", s), (tag, "wfm", c)], writes=[(tag, "psC", b)])
        P.op("act", lambda e, b=b, s=s: e.copy(out=fmo[s][:, 6, :], in_=psC[b][:]),
             reads=[(tag, "psC", b)], writes=[(tag, "fmo", s, 6)])
        P.dma("sp", fmT_v[:, :, tk0:tk0 + 512], fmo[s][:], chan=(tag, "fmo", s),
              reads=[(tag, "fmo", s, r) for r in range(7)], writes=[(tag, "fmT", it)])
        for j in range(4):
            b = cc % 3
            cc += 1
            for c in range(8):
                P.op("pe", lambda e, b=b, c=c, s=s, j=j: e.matmul(psC[b][:], lhsT=xT[s][:, c, j * 128:(j + 1) * 128],
                                                                 rhs=wtm[:, c, :], start=(c == 0), stop=(c == 7)),
                     reads=[(tag, "xT", s), (tag, "wtm", c)], writes=[(tag, "psC", b)])
            P.op("act", lambda e, b=b, s=s, j=j: e.copy(out=tmo[s][:, j, :], in_=psC[b][:, 0:384]),
                 reads=[(tag, "psC", b)], writes=[(tag, "tmo", s, j)])
            P.op("act", lambda e, b=b, s=s, j=j: e.activation(out=sgo[s][:, j, :], in_=psC[b][:, 384:512], func=AF.Silu),
                 reads=[(tag, "psC", b)], writes=[(tag, "sgo", s, j)])
        P.dma("sp", vr_d[tk0:tk0 + 512, :].rearrange("(j p) c -> p j c", p=128), tmo[s][:], chan=(tag, "tmo", s),
              reads=[(tag, "tmo", s, j) for j in range(4)], writes=[(tag, "vr", it)])
        P.dma("sp", sg_d[tk0:tk0 + 512, :].rearrange("(j p) c -> p j c", p=128), sgo[s][:], chan=(tag, "sgo", s),
              reads=[(tag, "sgo", s, j) for j in range(4)], writes=[(tag, "sg", it)])


def build_A1(L):
    nc = bass.Bass("TRN2", target_bir_lowering=False)
    with ExitStack() as es:
        C = Ctx(nc, es)
        xT_d = C.din("xT", [1024, L], BF16)
        wfm_d = C.din("wfm", [1024, NFM * 128], F32)
        wtm_d = C.din("wtm", [1024, 512], F32)
        pos_d = C.din("pos", [L], I32)
        rc_d = C.din("rc", [128, 4], F32)
        fmT_d = C.dout("fmT", [7, 128, L], BF16)
        vr_d = C.dout("vr", [L, 384], BF16)
        sg_d = C.dout("sg", [L, 128], F32)
        stage_A1(C, xT_d, wfm_d, wtm_d, pos_d, rc_d, fmT_d, vr_d, sg_d, L)
        C.P.emit(es)
    return nc


def rc_const():
    invf, sgn = rope_consts()
    return np.concatenate([invf, sgn, -math.pi * sgn, np.full((128, 1), -math.pi, np.float32)], axis=1).astype(np.float32)


def stage_A2(C, fmT_d, vr_d, lamp_d, subg_d, lc_d, ydaT_d, ident, L, tag="a2", dbg=None):
    P = C.P
    NKB = L // 128
    NQB = L // 512
    lamp = C.sb([128, 4, 64], F32, "lamp")
    lc = C.sb([128, 2], F32, "lc")
    gcol = C.sb([128, 1], F32, "gcol")
    lprod = C.sb([128, 2, 64], F32, "lprod")
    lsum = C.sb([128, 2], F32, "lsum")
    lexp = C.sb([128, 2], F32, "lexp")
    neglam = C.sb([128, 1], F32, "neglam")
    epsb = C.sb([128, 1], F32, "epsb")
    ones = C.sb([128, 128], F32, "ones")
    P.dma("sp", lamp[:], lamp_d.rearrange("a d -> (a d)").partition_broadcast(128).rearrange("p (a d) -> p a d", a=4),
          chan=(tag, "c0"), writes=[(tag, "lamp")])
    P.dma("sp", lc[:], lc_d[:, :], chan=(tag, "c0"), writes=[(tag, "lc")])
    P.dma("sp", gcol[:], subg_d.rearrange("(p o) -> p o", o=1), chan=(tag, "c0"), writes=[(tag, "gcol")])
    P.op("dve", lambda e: e.tensor_tensor(out=lprod[:, 0, :], in0=lamp[:, 0, :], in1=lamp[:, 1, :], op=ALU.mult),
         reads=[(tag, "lamp")], writes=[(tag, "lprod")])
    P.op("dve", lambda e: e.tensor_tensor(out=lprod[:, 1, :], in0=lamp[:, 2, :], in1=lamp[:, 3, :], op=ALU.mult),
         reads=[(tag, "lamp")], writes=[(tag, "lprod")])
    P.op("dve", lambda e: e.tensor_reduce(out=lsum[:], in_=lprod[:], axis=AX.X, op=ALU.add),
         reads=[(tag, "lprod")], writes=[(tag, "lsum")])
    P.op("act", lambda e: e.activation(out=lexp[:], in_=lsum[:], func=AF.Exp), reads=[(tag, "lsum")], writes=[(tag, "lexp")])
    P.op("dve", lambda e: e.tensor_tensor(out=neglam[:], in0=lexp[:, 1:2], in1=lexp[:, 0:1], op=ALU.subtract),
         reads=[(tag, "lexp")], writes=[(tag, "neglam")])
    P.op("dve", lambda e: e.tensor_tensor(out=neglam[:], in0=neglam[:], in1=lc[:, 0:1], op=ALU.subtract),
         reads=[(tag, "neglam"), (tag, "lc")], writes=[(tag, "neglam")])
    P.op("dve", lambda e: e.tensor_tensor(out=gcol[:], in0=gcol[:], in1=lc[:, 1:2], op=ALU.mult),
         reads=[(tag, "gcol"), (tag, "lc")], writes=[(tag, "gcol")])
    P.op("dve", lambda e: e.memset(epsb[:], 1e-6), writes=[(tag, "epsb")])
    P.op("pool", lambda e: e.memset(ones[:], 1.0), writes=[(tag, "ones")])

    qT = [C.sb([128, L], BF16, "qT") for _ in range(2)]
    kT = [[C.sb([128, L], BF16, "kT") for _ in range(2)] for _ in range(2)]
    va = [C.sb([128, NKB, 128], BF16, "va") for _ in range(2)]
    yT = [C.sb([128, L], BF16, "yT") for _ in range(2)]
    NPT = 4
    PT = [C.sb([128, 512], BF16, "PT") for _ in range(NPT)]
    accS = [[C.sb([128, 512], F32, "accS") for _ in range(2)] for _ in range(2)]
    oacc = C.sb([128, 2, 512], F32, "oacc")
    rinv = C.sb([128, 2, 512], F32, "rinv")
    o = C.sb([128, 512], F32, "o")
    t2 = C.sb([128, 512], F32, "t2")
    sq = C.sb([128, 512], F32, "sq")
    rs = C.sb([128, 512], F32, "rs")
    S = [C.ps([128, 512], F32, "S") for _ in range(3)]
    accO = [C.ps([128, 512], F32, "accO") for _ in range(2)]
    Rps = [C.ps([128, 512], F32, "Rps") for _ in range(2)]

    for h in range(2):
        P.dma("sp", qT[h][:], fmT_d[h, :, :], chan=(tag, "ld", h), writes=[(tag, "qT", h)])
        for m in range(2):
            P.op("pool", lambda e, h=h, m=m: e.memset(kT[h][m][(1 - m) * 64:(2 - m) * 64, :], 0.0), writes=[(tag, "kTz", h, m)])
            P.dma("sp", kT[h][m][m * 64:(m + 1) * 64, :], fmT_d[2 + h, m * 64:(m + 1) * 64, :], chan=(tag, "ld", h), writes=[(tag, "kT", h)])
        P.dma("sp", va[h][:], vr_d[:, h * 128:(h + 1) * 128].rearrange("(kb p) c -> p kb c", p=128),
              chan=(tag, "ld", h), writes=[(tag, "va", h)])
    step = 0
    blk = 0
    for h in range(2):
        for qb in range(NQB):
            q0 = qb * 512
            nst = NKB * 2
            par = blk % 2
            blk += 1

            def qk(i, h=h, q0=q0, step=step):
                kb, m = i // 2, i % 2
                sl = (step + i) % 3
                P.op("pe", lambda e: e.matmul(S[sl][:], lhsT=kT[h][m][:, kb * 128:(kb + 1) * 128],
                                               rhs=qT[h][:, q0:q0 + 512], start=True, stop=True),
                     reads=[(tag, "kT", h), (tag, "kTz", h, 0), (tag, "kTz", h, 1), (tag, "qT", h)], writes=[(tag, "S", sl)])

            def ex(i, step=step):
                sl = (step + i) % 3
                pl = (step + i) % NPT
                P.op("act", lambda e: e.activation(out=PT[pl][:], in_=S[sl][:], func=AF.Exp, scale=0.125),
                     writes=[(tag, "PT", pl), (tag, "S", sl)])

            def av(i, h=h, step=step, par=par):
                kb, m = i // 2, i % 2
                pl = (step + i) % NPT
                P.op("pe", lambda e: e.matmul(accO[m][:], lhsT=va[h][:, kb, :], rhs=PT[pl][:], start=(kb == 0), stop=(kb == NKB - 1)),
                     reads=[(tag, "PT", pl), (tag, "va", h)], writes=[(tag, "accO", m)])
                if kb == 0:
                    P.op("dve", lambda e: e.tensor_copy(out=accS[par][m][:], in_=PT[pl][:]), reads=[(tag, "PT", pl)], writes=[(tag, "accS", par, m)])
                else:
                    P.op("dve", lambda e: e.tensor_tensor(out=accS[par][m][:], in0=accS[par][m][:], in1=PT[pl][:], op=ALU.add),
                         reads=[(tag, "PT", pl), (tag, "accS", par, m)], writes=[(tag, "accS", par, m)])

            qk(0); ex(0)
            qk(1); ex(1)
            for i in range(nst):
                if i + 2 < nst:
                    qk(i + 2); ex(i + 2)
                av(i)
            step += nst

            def post(h=h, q0=q0, par=par):
                for m in range(2):
                    P.op("pe", lambda e, m=m: e.matmul(Rps[m][:], lhsT=ones[:], rhs=accS[par][m][:], start=True, stop=True),
                         reads=[(tag, "ones"), (tag, "accS", par, m)], writes=[(tag, "Rps", m)])
                P.op("act", lambda e: e.copy(out=oacc[:, 0, :], in_=accO[0][:]), writes=[(tag, "oacc", 0), (tag, "accO", 0)])
                P.op("dve", lambda e: e.tensor_copy(out=oacc[:, 1, :], in_=accO[1][:]), writes=[(tag, "oacc", 1), (tag, "accO", 1)])
                for m in range(2):
                    P.op("dve", lambda e, m=m: e.reciprocal(out=rinv[:, m, :], in_=Rps[m][:]), writes=[(tag, "rinv", m), (tag, "Rps", m)])
                P.op("dve", lambda e: e.tensor_tensor(out=o[:], in0=oacc[:, 0, :], in1=rinv[:, 0, :], op=ALU.mult),
                     reads=[(tag, "oacc", 0), (tag, "rinv", 0)], writes=[(tag, "o")])
                P.op("pool", lambda e: e.tensor_tensor(out=t2[:], in0=oacc[:, 1, :], in1=rinv[:, 1, :], op=ALU.mult),
                     reads=[(tag, "oacc", 1), (tag, "rinv", 1)], writes=[(tag, "t2")])
                P.op("dve", lambda e: e.scalar_tensor_tensor(out=o[:], in0=t2[:], scalar=neglam[:, 0:1], in1=o[:], op0=ALU.mult, op1=ALU.add),
                     reads=[(tag, "t2"), (tag, "o"), (tag, "neglam")], writes=[(tag, "o")])
                P.op("act", lambda e: e.activation(out=sq[:], in_=o[:], func=AF.Square), reads=[(tag, "o")], writes=[(tag, "sq")])
                P.op("pe", lambda e: e.matmul(Rps[0][:], lhsT=ones[:], rhs=sq[:], start=True, stop=True),
                     reads=[(tag, "ones"), (tag, "sq")], writes=[(tag, "Rps", 0)])
                P.op("act", lambda e: e.activation(out=rs[:], in_=Rps[0][:], func=AF.Sqrt, bias=epsb[:, 0:1], scale=1.0 / 128.0),
                     reads=[(tag, "epsb")], writes=[(tag, "rs"), (tag, "Rps", 0)])
                P.op("dve", lambda e: e.reciprocal(out=rs[:], in_=rs[:]), reads=[(tag, "rs")], writes=[(tag, "rs")])
                P.op("dve", lambda e: e.scalar_tensor_tensor(out=yT[h][:, q0:q0 + 512], in0=o[:], scalar=gcol[:, 0:1], in1=rs[:],
                                                              op0=ALU.mult, op1=ALU.mult),
                     reads=[(tag, "o"), (tag, "rs"), (tag, "gcol")], writes=[(tag, "yT", h)])
            post()
        P.dma("sp", ydaT_d[h, :, :], yT[h][:], chan=(tag, "yo", h), reads=[(tag, "yT", h)], writes=[(tag, "ydaT", h)])


def build_A2(L, debug=False):
    nc = bass.Bass("TRN2", target_bir_lowering=False)
    with ExitStack() as es:
        C = Ctx(nc, es)
        fmT_d = C.din("fmT", [7, 128, L], BF16)
        vr_d = C.din("vr", [L, 384], BF16)
        lamp_d = C.din("lamp", [4, 64], F32)
        subg_d = C.din("subg", [128], F32)
        lc_d = C.din("lc", [128, 2], F32)
        id_d = C.din("ident", [128, 128], BF16)
        ydaT_d = C.dout("ydaT", [2, 128, L], BF16)
        ident = C.sb([128, 128], BF16, "ident")
        C.P.dma("sp", ident[:], id_d[:, :], chan="ident", writes=["ident"])
        dbg = None
        stage_A2(C, fmT_d, vr_d, lamp_d, subg_d, lc_d, ydaT_d, ident, L, dbg=dbg)
        C.P.emit(es)
    return nc


RT_W = 256 + 128 + 128 + 4 + 1


def ret_consts(h):
    idx = np.arange(128, dtype=np.float64)
    out = np.zeros((128, RT_W), np.float64)
    for hh in range(2):
        lg = math.log(1.0 - 2.0 ** (-5.0 - (2 * h + hh)))
        out[:, hh * 128:(hh + 1) * 128] = np.exp(lg * np.abs(idx[None, :] - idx[:, None])) / 8.0
        out[:, 256 + hh * 64:256 + (hh + 1) * 64] = (np.exp(lg * (127 - idx)) / 8.0)[:, None]
        out[:, 384 + hh * 64:384 + (hh + 1) * 64] = (np.exp(lg * idx) / 8.0)[:, None]
        out[:, 512 + 2 * hh] = np.exp(lg * (idx + 1))
        out[:, 512 + 2 * hh + 1] = np.exp(lg * (128 - idx))
        out[hh * 64:(hh + 1) * 64, 516] = math.exp(lg * 128)
    return out.astype(np.float32)


def stage_A3(C, fmT_d, vr_d, sg_d, rt_d, gng_d, gnb_d, yretT_d, ident, L, tag="a3"):
    P = C.P
    NCH = L // 128
    rt = C.sb([128, RT_W], F32, "rt")
    gng = C.sb([128, 2, 64], F32, "gng")
    gnb = C.sb([128, 2, 64], F32, "gnb")
    epsb = C.sb([128, 1], F32, "epsb")
    P.dma("sp", rt[:], rt_d[:, :], chan=(tag, "c0"), writes=[(tag, "rt")])
    for hh in range(2):
        P.dma("sp", gng[:, hh, :], gng_d.partition_broadcast(128), chan=(tag, "c0"), writes=[(tag, "gng")])
        P.dma("sp", gnb[:, hh, :], gnb_d.partition_broadcast(128), chan=(tag, "c0"), writes=[(tag, "gnb")])
    P.op("dve", lambda e: e.memset(epsb[:], LN_EPS), writes=[(tag, "epsb")])
    Dm = rt[:, 0:256].rearrange("p (a n) -> p a n", a=2)
    decF = rt[:, 256:384]
    decB = rt[:, 384:512]
    gC = rt[:, 516:517]

    rqT = C.sb([128, L], BF16, "rqT")
    rkT = C.sb([128, L], BF16, "rkT")
    rv = C.sb([128, NCH, 128], BF16, "rv")
    sg = C.sb([128, NCH, 128], F32, "sg")
    yT = C.sb([128, L], BF16, "yT")
    Gs = C.sb([128, NCH, 64], BF16, "Gs")
    G = C.sb([128, 64], F32, "G")
    F = C.sb([128, 64], F32, "F")
    Fbf = [C.sb([128, 64], BF16, "Fbf") for _ in range(2)]
    kd = [C.sb([128, 128], BF16, "kd") for _ in range(2)]
    Sm = [C.sb([128, 2, 128], BF16, "Sm") for _ in range(2)]
    t = [C.sb([128, 2, 64], F32, "t") for _ in range(2)]
    st = C.sb([128, 2, 6], F32, "st")
    mv = C.sb([128, 2, 2], F32, "mv")
    rstd = C.sb([128, 2], F32, "rstd")
    ybf = [C.sb([128, 128], BF16, "ybf") for _ in range(2)]
    pk = [C.ps([128, 128], BF16, "pk") for _ in range(2)]
    pkv = [C.ps([128, 128], F32, "pkv") for _ in range(2)]
    pS = [C.ps([128, 128], F32, "pS") for _ in range(2)]
    po = [C.ps([128, 3, 64], F32, "po") for _ in range(2)]
    P.dma("sp", rqT[:], fmT_d[4, :, :], chan=(tag, "ld"), writes=[(tag, "rqT")])
    P.dma("sp", rkT[:], fmT_d[5, :, :], chan=(tag, "ld"), writes=[(tag, "rkT")])
    P.dma("sp", rv[:], vr_d[:, 256:384].rearrange("(c p) n -> p c n", p=128), chan=(tag, "ld"), writes=[(tag, "rv")])
    P.dma("sp", sg[:], sg_d.rearrange("(c p) n -> p c n", p=128), chan=(tag, "ld"), writes=[(tag, "sg")])
    P.op("dve", lambda e: e.memset(G[:], 0.0), writes=[(tag, "G")])
    P.op("dve", lambda e: e.memset(F[:], 0.0), writes=[(tag, "F")])

    def kv_step(c, dec, state, skey, i):
        s = i % 2
        P.op("pe", lambda e: e.transpose(out=pk[s][:], in_=rkT[:, c * 128:(c + 1) * 128], identity=ident[:]),
             reads=[(tag, "rkT"), "ident"], writes=[(tag, "pk", s)])
        P.op("dve", lambda e: e.tensor_tensor(out=kd[s][:], in0=pk[s][:], in1=dec, op=ALU.mult),
             reads=[(tag, "rt")], writes=[(tag, "kd", s), (tag, "pk", s)])
        P.op("pe", lambda e: e.matmul(pkv[s][:], lhsT=kd[s][:], rhs=rv[:, c, :], start=True, stop=True),
             reads=[(tag, "kd", s), (tag, "rv")], writes=[(tag, "pkv", s)])
        for hh in range(2):
            hs = slice(hh * 64, (hh + 1) * 64)
            P.op("dve", lambda e, hs=hs: e.scalar_tensor_tensor(out=state[hs, :], in0=state[hs, :], scalar=gC[hs, :], in1=pkv[s][hs, hs],
                                                                 op0=ALU.mult, op1=ALU.add),
                 reads=[skey, (tag, "rt")], writes=[skey, (tag, "pkv", s)])

    it = 0
    for c in range(NCH - 1, -1, -1):
        P.op("act", lambda e, c=c: e.copy(out=Gs[:, c, :], in_=G[:]), reads=[(tag, "G")], writes=[(tag, "Gs", c)])
        if c > 0:
            kv_step(c, decB, G, (tag, "G"), it)
            it += 1
    def chunk2(c, it):
        s = c % 2
        cs = slice(c * 128, (c + 1) * 128)
        P.op("act", lambda e, s=s: e.copy(out=Fbf[s][:], in_=F[:]), reads=[(tag, "F")], writes=[(tag, "Fbf", s)])
        for hh in range(2):
            hs = slice(hh * 64, (hh + 1) * 64)
            P.op("pe", lambda e, hh=hh, hs=hs: e.matmul(pS[hh][:], lhsT=rkT[hs, cs], rhs=rqT[hs, cs], start=True, stop=True),
                 reads=[(tag, "rkT"), (tag, "rqT")], writes=[(tag, "pS", hh)])
            P.op("dve", lambda e, s=s, hh=hh: e.tensor_tensor(out=Sm[s][:, hh, :], in0=pS[hh][:], in1=Dm[:, hh, :], op=ALU.mult),
                 reads=[(tag, "rt")], writes=[(tag, "Sm", s, hh), (tag, "pS", hh)])
        for hh in range(2):
            hs = slice(hh * 64, (hh + 1) * 64)
            P.op("pe", lambda e, hh=hh, hs=hs: e.matmul(po[hh][:, 0, :], lhsT=Sm[s][:, hh, :], rhs=rv[:, c, hs], start=True, stop=False),
                 reads=[(tag, "Sm", s, hh), (tag, "rv")], writes=[(tag, "po", hh)])
            P.op("pe", lambda e, hh=hh, hs=hs: e.matmul(po[hh][:, 1, :], lhsT=rqT[hs, cs], rhs=Fbf[s][hs, :], start=False, stop=False),
                 reads=[(tag, "rqT"), (tag, "Fbf", s)], writes=[(tag, "po", hh)])
            P.op("pe", lambda e, hh=hh, hs=hs, c=c: e.matmul(po[hh][:, 2, :], lhsT=rqT[hs, cs], rhs=Gs[hs, c, :], start=False, stop=True),
                 reads=[(tag, "rqT"), (tag, "Gs", c)], writes=[(tag, "po", hh)])
        if c < NCH - 1:
            kv_step(c, decF, F, (tag, "F"), it)
        for hh in range(2):
            P.op("dve", lambda e, s=s, hh=hh: e.tensor_copy(out=t[s][:, hh, :], in_=po[hh][:, 0, :]),
                 writes=[(tag, "t", s), (tag, "po", hh)])
            for j in (1, 2):
                P.op("dve", lambda e, s=s, hh=hh, j=j: e.scalar_tensor_tensor(
                    out=t[s][:, hh, :], in0=po[hh][:, j, :], scalar=rt[:, 512 + 2 * hh + j - 1:512 + 2 * hh + j],
                    in1=t[s][:, hh, :], op0=ALU.mult, op1=ALU.add),
                    reads=[(tag, "t", s), (tag, "rt")], writes=[(tag, "t", s), (tag, "po", hh)])
        for hh in range(2):
            P.op("dve", lambda e, s=s, hh=hh: e.bn_stats(out=st[:, hh, :], in_=t[s][:, hh, :]), reads=[(tag, "t", s)], writes=[(tag, "st")])
            P.op("dve", lambda e, hh=hh: e.bn_aggr(out=mv[:, hh, :], in_=st[:, hh, :]), reads=[(tag, "st")], writes=[(tag, "mv")])
        P.op("act", lambda e: e.activation(out=rstd[:], in_=mv[:, :, 1], func=AF.Sqrt, bias=epsb[:, 0:1], scale=1.0),
             reads=[(tag, "mv"), (tag, "epsb")], writes=[(tag, "rstd")])
        P.op("dve", lambda e: e.reciprocal(out=rstd[:], in_=rstd[:]), reads=[(tag, "rstd")], writes=[(tag, "rstd")])
        for hh in range(2):
            P.op("dve", lambda e, s=s, hh=hh: e.tensor_scalar(out=t[s][:, hh, :], in0=t[s][:, hh, :], scalar1=mv[:, hh, 0:1],
                                                              scalar2=rstd[:, hh:hh + 1], op0=ALU.subtract, op1=ALU.mult),
                 reads=[(tag, "t", s), (tag, "mv"), (tag, "rstd")], writes=[(tag, "t", s)])
        P.op("pool", lambda e, s=s: e.tensor_tensor(out=t[s][:], in0=t[s][:], in1=gng[:], op=ALU.mult),
             reads=[(tag, "t", s), (tag, "gng")], writes=[(tag, "t", s)])
        P.op("pool", lambda e, s=s: e.tensor_tensor(out=t[s][:], in0=t[s][:], in1=gnb[:], op=ALU.add),
             reads=[(tag, "t", s), (tag, "gnb")], writes=[(tag, "t", s)])
        P.op("pool", lambda e, s=s, c=c: e.tensor_tensor(out=ybf[s][:], in0=t[s][:].rearrange("p a b -> p (a b)"), in1=sg[:, c, :], op=ALU.mult),
             reads=[(tag, "t", s), (tag, "sg")], writes=[(tag, "ybf", s)])
        P.op("pe", lambda e, s=s: e.transpose(out=pk[s][:], in_=ybf[s][:], identity=ident[:]),
             reads=[(tag, "ybf", s), "ident"], writes=[(tag, "pk", s)])
        P.op("act", lambda e, s=s, cs=cs: e.copy(out=yT[:, cs], in_=pk[s][:]), writes=[(tag, "yT"), (tag, "pk", s)])
    for c in range(NCH):
        chunk2(c, it)
        it += 1
    P.dma("pool", yretT_d[:, :], yT[:], chan=(tag, "yo"), reads=[(tag, "yT")], writes=[(tag, "yretT")])


def build_A3(L):
    nc = bass.Bass("TRN2", target_bir_lowering=False)
    with ExitStack() as es:
        C = Ctx(nc, es)
        fmT_d = C.din("fmT", [7, 128, L], BF16)
        vr_d = C.din("vr", [L, 384], BF16)
        sg_d = C.din("sg", [L, 128], F32)
        rt_d = C.din("rt", [128, RT_W], F32)
        gng_d = C.din("gng", [64], F32)
        gnb_d = C.din("gnb", [64], F32)
        id_d = C.din("ident", [128, 128], BF16)
        yretT_d = C.dout("yretT", [128, L], BF16)
        ident = C.sb([128, 128], BF16, "ident")
        C.P.dma("sp", ident[:], id_d[:, :], chan="ident", writes=["ident"])
        stage_A3(C, fmT_d, vr_d, sg_d, rt_d, gng_d, gnb_d, yretT_d, ident, L)
        C.P.emit(es)
    return nc


S5C_W = 1 + 1 + 1 + 8 + 512 + 512
GELU_K = 2.0 * math.sqrt(2.0 / math.pi)


def s5_consts():
    p = np.arange(128)
    out = np.zeros((128, S5C_W), np.float32)
    out[:, 0] = (p < 64)
    out[:, 1] = (p >= 64)
    out[:, 2] = np.where(p < 64, 1.0, -1.0)
    for g in range(8):
        out[:, 3 + g] = (p // 16 == g)
    out[:, 11:11 + 512] = np.arange(512)[None, :]
    out[:, 11 + 512:11 + 1024] = (511 - np.arange(512))[None, :]
    return out


def s5_layout(inp, i, h):
    gs = slice(8 * h, 8 * h + 8)

    def pg(a):
        a = np.asarray(a[i][:, gs, :]).transpose(2, 0, 1).reshape(64, 16)
        return np.ascontiguousarray(np.concatenate([a, a], axis=0)).astype(np.float32)
    ldt = np.asarray(inp['s5_log_dt'][i][:, gs]).reshape(1, 16)
    ldt = np.ascontiguousarray(np.broadcast_to(ldt, (128, 16))).astype(np.float32)

    def pb(a):
        a = np.asarray(a[i][:, gs]).transpose(2, 0, 1, 3).reshape(64, 16, 16)
        return np.ascontiguousarray(np.concatenate([a, a], axis=0)).astype(np.float32)
    cre = np.asarray(inp['s5_C_re'][i][:, gs])
    cim = np.asarray(inp['s5_C_im'][i][:, gs])
    cc = np.stack([cre, cim], axis=2)
    cc = cc.transpose(1, 3, 0, 2, 4).reshape(128, 2, 128)
    dv = np.asarray(inp['s5_D'][i][128 * h:128 * h + 128]).reshape(128, 1)
    sp = np.concatenate([pg(inp['s5_A_re']), pg(inp['s5_A_im']), ldt, dv.astype(np.float32)], axis=1)
    return {"s5p": np.ascontiguousarray(sp), "s5bre": pb(inp['s5_B_re']), "s5bim": pb(inp['s5_B_im']),
            "s5cc": np.ascontiguousarray(cc).astype(np.float32)}


def emit_gelu(P, tag, y, ykey, tmp, out, okey):
    kt = (tag, "gelu_tmp")
    P.op("dve", lambda e: e.tensor_tensor(out=tmp, in0=y, in1=y, op=ALU.mult), reads=[ykey], writes=[kt])
    P.op("dve", lambda e: e.tensor_scalar(out=tmp, in0=tmp, scalar1=0.044715, scalar2=1.0, op0=ALU.mult, op1=ALU.add),
         reads=[kt], writes=[kt])
    P.op("dve", lambda e: e.tensor_tensor(out=tmp, in0=tmp, in1=y, op=ALU.mult), reads=[kt, ykey], writes=[kt])
    P.op("act", lambda e: e.activation(out=tmp, in_=tmp, func=AF.Sigmoid, scale=GELU_K), reads=[kt], writes=[kt])
    P.op("dve", lambda e: e.tensor_tensor(out=out, in0=tmp, in1=y, op=ALU.mult), reads=[kt, ykey], writes=[okey])


def stage_A4(C, fmT_d, s5p_d, bre_d, bim_d, cc_d, sc_d, identf_d, ys5T_d, L, tag="a4", scan_pool=True):
    P = C.P
    NT = L // 512
    sc = C.sb([128, S5C_W], F32, "sc")
    sp = C.sb([128, 49], F32, "sp")
    bre = C.sb([128, 16, 16], F32, "bre")
    bim = C.sb([128, 16, 16], F32, "bim")
    cc = C.sb([128, 2, 128], F32, "cc")
    idf = C.sb([128, 128], F32, "idf")
    for tl, src_, k in ((sc, sc_d, "sc"), (sp, s5p_d, "sp"), (bre, bre_d, "bre"), (bim, bim_d, "bim"), (cc, cc_d, "cc"), (idf, identf_d, "idf")):
        P.dma("sp", tl[:], src_, chan=(tag, "c0"), writes=[(tag, k)])
    mtop, mbot, sgnC = sc[:, 0:1], sc[:, 1:2], sc[:, 2:3]
    tau = [sc[:, 11:11 + 512], sc[:, 11 + 512:11 + 1024]]
    ar, ai, ldt, dv = sp[:, 0:16], sp[:, 16:32], sp[:, 32:48], sp[:, 48:49]

    sm = {}
    for nm in ("step", "zr", "th", "pp", "em1", "e", "sarg", "carg", "c1", "s1", "sh", "cm1", "nr", "ni", "den", "t0", "t1",
               "cre", "cim", "s1A", "s2A", "s1B", "c512", "s512", "a512", "rrf"):
        sm[nm] = C.sb([128, 16], F32, "s5_" + nm)
    rri = C.sb([128, 16], I32, "s5_rri")

    def S(nm):
        return sm[nm][:]

    def k(nm):
        return (tag, "sm", nm)

    def dv_op(fn, reads, writes):
        P.op("dve", fn, reads=[k(r) if isinstance(r, str) else r for r in reads], writes=[k(w) for w in writes])
    kp = (tag, "sp")
    dv_op(lambda e: e.tensor_copy(out=S("t0"), in_=ldt), [kp], ["t0"])
    P.op("act", lambda e: e.activation(out=S("step"), in_=S("t0"), func=AF.Exp), reads=[k("t0")], writes=[k("step")])
    dv_op(lambda e: e.tensor_tensor(out=S("zr"), in0=S("step"), in1=ar, op=ALU.mult), ["step", kp], ["zr"])
    dv_op(lambda e: e.tensor_tensor(out=S("th"), in0=S("step"), in1=ai, op=ALU.mult), ["step", kp], ["th"])
    dv_op(lambda e: e.tensor_scalar(out=S("pp"), in0=S("zr"), scalar1=1.0 / 6.0, scalar2=1.0, op0=ALU.mult, op1=ALU.add), ["zr"], ["pp"])
    for cdiv in (5.0, 4.0, 3.0, 2.0):
        dv_op(lambda e, cdiv=cdiv: e.scalar_tensor_tensor(out=S("pp"), in0=S("zr"), scalar=1.0 / cdiv, in1=S("pp"), op0=ALU.mult, op1=ALU.mult),
              ["zr", "pp"], ["pp"])
        dv_op(lambda e: e.tensor_scalar(out=S("pp"), in0=S("pp"), scalar1=1.0, scalar2=None, op0=ALU.add), ["pp"], ["pp"])
    dv_op(lambda e: e.tensor_tensor(out=S("em1"), in0=S("zr"), in1=S("pp"), op=ALU.mult), ["zr", "pp"], ["em1"])
    dv_op(lambda e: e.tensor_scalar(out=S("e"), in0=S("em1"), scalar1=1.0, scalar2=None, op0=ALU.add), ["em1"], ["e"])
    sincos(P, "dve", (tag, "sc1"), sm["th"], k("th"), sm["sarg"], sm["carg"], rri, sm["rrf"])
    ks, kc = ((tag, "sc1"), "sarg"), ((tag, "sc1"), "carg")
    P.op("act", lambda e: e.activation(out=S("s1"), in_=S("sarg"), func=AF.Sin), reads=[ks], writes=[k("s1")])
    P.op("act", lambda e: e.activation(out=S("sh"), in_=S("sarg"), func=AF.Sin, scale=0.5), reads=[ks], writes=[k("sh")])
    dv_op(lambda e: e.scalar_tensor_tensor(out=S("cm1"), in0=S("sh"), scalar=-2.0, in1=S("sh"), op0=ALU.mult, op1=ALU.mult), ["sh"], ["cm1"])
    dv_op(lambda e: e.tensor_scalar(out=S("c1"), in0=S("cm1"), scalar1=1.0, scalar2=None, op0=ALU.add), ["cm1"], ["c1"])
    dv_op(lambda e: e.tensor_tensor(out=S("nr"), in0=S("em1"), in1=S("c1"), op=ALU.mult), ["em1", "c1"], ["nr"])
    dv_op(lambda e: e.tensor_tensor(out=S("nr"), in0=S("nr"), in1=S("cm1"), op=ALU.add), ["nr", "cm1"], ["nr"])
    dv_op(lambda e: e.tensor_tensor(out=S("ni"), in0=S("e"), in1=S("s1"), op=ALU.mult), ["e", "s1"], ["ni"])
    dv_op(lambda e: e.tensor_tensor(out=S("den"), in0=ar, in1=ar, op=ALU.mult), [kp], ["den"])
    dv_op(lambda e: e.tensor_tensor(out=S("t0"), in0=ai, in1=ai, op=ALU.mult), [kp], ["t0"])
    dv_op(lambda e: e.tensor_tensor(out=S("den"), in0=S("den"), in1=S("t0"), op=ALU.add), ["den", "t0"], ["den"])
    dv_op(lambda e: e.reciprocal(out=S("den"), in_=S("den")), ["den"], ["den"])
    dv_op(lambda e: e.tensor_tensor(out=S("t0"), in0=S("nr"), in1=ar, op=ALU.mult), ["nr", kp], ["t0"])
    dv_op(lambda e: e.tensor_tensor(out=S("t1"), in0=S("ni"), in1=ai, op=ALU.mult), ["ni", kp], ["t1"])
    dv_op(lambda e: e.tensor_tensor(out=S("t0"), in0=S("t0"), in1=S("t1"), op=ALU.add), ["t0", "t1"], ["t0"])
    dv_op(lambda e: e.tensor_tensor(out=S("cre"), in0=S("t0"), in1=S("den"), op=ALU.mult), ["t0", "den"], ["cre"])
    dv_op(lambda e: e.tensor_tensor(out=S("t0"), in0=S("ni"), in1=ar, op=ALU.mult), ["ni", kp], ["t0"])
    dv_op(lambda e: e.tensor_tensor(out=S("t1"), in0=S("nr"), in1=ai, op=ALU.mult), ["nr", kp], ["t1"])
    dv_op(lambda e: e.tensor_tensor(out=S("t0"), in0=S("t0"), in1=S("t1"), op=ALU.subtract), ["t0", "t1"], ["t0"])
    dv_op(lambda e: e.tensor_tensor(out=S("cim"), in0=S("t0"), in1=S("den"), op=ALU.mult), ["t0", "den"], ["cim"])
    ksc = (tag, "sc")
    dv_op(lambda e: e.tensor_scalar(out=S("s1A"), in0=S("cre"), scalar1=mtop, scalar2=None, op0=ALU.mult), ["cre", ksc], ["s1A"])
    dv_op(lambda e: e.scalar_tensor_tensor(out=S("s1A"), in0=S("cim"), scalar=mbot, in1=S("s1A"), op0=ALU.mult, op1=ALU.add), ["cim", "s1A", ksc], ["s1A"])
    dv_op(lambda e: e.tensor_scalar(out=S("s2A"), in0=S("cre"), scalar1=mbot, scalar2=None, op0=ALU.mult), ["cre", ksc], ["s2A"])
    dv_op(lambda e: e.tensor_scalar(out=S("t0"), in0=S("cim"), scalar1=mtop, scalar2=None, op0=ALU.mult), ["cim", ksc], ["t0"])
    dv_op(lambda e: e.tensor_tensor(out=S("s2A"), in0=S("s2A"), in1=S("t0"), op=ALU.subtract), ["s2A", "t0"], ["s2A"])
    dv_op(lambda e: e.tensor_scalar(out=S("s1B"), in0=S("s1A"), scalar1=sgnC, scalar2=None, op0=ALU.mult), ["s1A", ksc], ["s1B"])
    dv_op(lambda e: e.tensor_scalar(out=S("s1B"), in0=S("cim"), scalar1=mtop, scalar2=None, op0=ALU.mult), ["cim", ksc], ["s1B"])
    dv_op(lambda e: e.tensor_scalar(out=S("t0"), in0=S("cre"), scalar1=mbot, scalar2=None, op0=ALU.mult), ["cre", ksc], ["t0"])
    dv_op(lambda e: e.tensor_tensor(out=S("s1B"), in0=S("s1B"), in1=S("t0"), op=ALU.subtract), ["s1B", "t0"], ["s1B"])
    dv_op(lambda e: e.tensor_scalar(out=S("a512"), in0=S("th"), scalar1=512.0, scalar2=None, op0=ALU.mult), ["th"], ["a512"])
    sincos(P, "dve", (tag, "sc2"), sm["a512"], k("a512"), sm["sarg"], sm["carg"], rri, sm["rrf"])
    ks2, kc2 = ((tag, "sc2"), "sarg"), ((tag, "sc2"), "carg")
    P.op("act", lambda e: e.activation(out=S("s512"), in_=S("sarg"), func=AF.Sin), reads=[ks2], writes=[k("s512")])
    P.op("act", lambda e: e.activation(out=S("c512"), in_=S("carg"), func=AF.Sin), reads=[kc2], writes=[k("c512")])

    BA = C.sb([128, 16, 128], BF16, "BA")
    BB = C.sb([128, 16, 128], BF16, "BB")
    CP = C.sb([128, 16, 128], BF16, "CP")
    Dg = C.sb([128, 128], BF16, "Dg")
    xa = C.sb([128, 8, 16], F32, "xa")
    xb = C.sb([128, 8, 16], F32, "xb")
    cfull = C.sb([128, 128], F32, "cfull")
    pset = C.ps([128, 128], F32, "pset")
    P.op("pool", lambda e: e.memset(CP[:], 0.0), writes=[(tag, "CP")])
    P.op("dve", lambda e: e.tensor_scalar(out=Dg[:], in0=idf[:], scalar1=dv, scalar2=None, op0=ALU.mult),
         reads=[(tag, "idf"), kp], writes=[(tag, "Dg")])
    for d in range(2):
        for var, (sa, sb_), dst in ((0, ("s1A", "s2A"), BA), (1, ("s1B", "s1A"), BB)):
            def bc(nm, d=d):
                return sm[nm][:, d * 8:(d + 1) * 8].unsqueeze(2).to_broadcast([128, 8, 16])
            P.op("dve", lambda e, d=d, sa=sa, bc=bc: e.tensor_tensor(out=xa[:], in0=bre[:, d * 8:(d + 1) * 8, :], in1=bc(sa), op=ALU.mult),
                 reads=[(tag, "bre"), k(sa)], writes=[(tag, "xa")])
            P.op("dve", lambda e, d=d, sb_=sb_, bc=bc: e.tensor_tensor(out=xb[:], in0=bim[:, d * 8:(d + 1) * 8, :], in1=bc(sb_), op=ALU.mult),
                 reads=[(tag, "bim"), k(sb_)], writes=[(tag, "xb")])
            P.op("dve", lambda e: e.tensor_tensor(out=xa[:], in0=xa[:], in1=xb[:], op=ALU.add),
                 reads=[(tag, "xa"), (tag, "xb")], writes=[(tag, "xa")])
            P.op("pe", lambda e: e.transpose(out=pset[:], in_=xa[:].rearrange("p a b -> p (a b)"), identity=idf[:]),
                 reads=[(tag, "xa"), (tag, "idf")], writes=[(tag, "pset")])
            for g in range(8):
                P.op("dve", lambda e, g=g, d=d, dst=dst: e.tensor_scalar(out=dst[:, d * 8 + g, :], in0=pset[:], scalar1=sc[:, 3 + g:4 + g],
                                                                      scalar2=None, op0=ALU.mult),
                     reads=[ksc], writes=[(tag, "Btab"), (tag, "pset")])
        P.op("pe", lambda e, d=d: e.transpose(out=pset[:], in_=cc[:, d, :], identity=idf[:]),
             reads=[(tag, "cc"), (tag, "idf")], writes=[(tag, "pset")])
        P.op("dve", lambda e: e.tensor_copy(out=cfull[:], in_=pset[:]), writes=[(tag, "cfull"), (tag, "pset")])
        for g in range(8):
            P.op("dve", lambda e, g=g, d=d: e.tensor_scalar(out=CP[:, d * 8 + g, 16 * g:16 * g + 16], in0=cfull[:, 16 * g:16 * g + 16],
                                                            scalar1=sgnC, scalar2=None, op0=ALU.mult),
                 reads=[(tag, "cfull"), ksc], writes=[(tag, "CP")])

    cosT = C.sb([128, 16, 512], F32, "cosT")
    sinT = C.sb([128, 16, 512], F32, "sinT")
    ang = C.sb([128, 512], F32, "ang")
    rsa = C.sb([128, 512], F32, "rsa")
    rca = C.sb([128, 512], F32, "rca")
    rr_i = C.sb([128, 512], I32, "rr_i")
    rr_f = C.sb([128, 512], F32, "rr_f")
    for gd in range(16):
        d = gd // 8
        P.op("dve", lambda e, gd=gd, d=d: e.tensor_scalar(out=ang[:], in0=tau[d], scalar1=sm["th"][:, gd:gd + 1], scalar2=None, op0=ALU.mult),
             reads=[ksc, k("th")], writes=[(tag, "ang")])
        sincos(P, "dve", (tag, "rt"), ang, (tag, "ang"), rsa, rca, rr_i, rr_f)
        P.op("act", lambda e, gd=gd: e.activation(out=sinT[:, gd, :], in_=rsa[:], func=AF.Sin), reads=[((tag, "rt"), "sarg")], writes=[(tag, "sinT", gd)])
        P.op("act", lambda e, gd=gd: e.activation(out=cosT[:, gd, :], in_=rca[:], func=AF.Sin), reads=[((tag, "rt"), "carg")], writes=[(tag, "cosT", gd)])

    uT = C.sb([128, L], BF16, "uT")
    yb = C.sb([128, L], F32, "yb")
    P.dma("sp", uT[:], fmT_d[6, :, :], chan=(tag, "ld"), writes=[(tag, "uT")])
    cA = C.sb([128, 16], F32, "cA")
    cB = C.sb([128, 16], F32, "cB")
    ctmp = C.sb([128, 4], F32, "ctmp")
    P.op("dve", lambda e: e.memset(cA[:], 0.0), writes=[(tag, "cA", gd) for gd in range(16)])
    P.op("dve", lambda e: e.memset(cB[:], 0.0), writes=[(tag, "cB", gd) for gd in range(16)])
    bA = [C.sb([128, 512], F32, "bA") for _ in range(2)]
    bB = [C.sb([128, 512], F32, "bB") for _ in range(2)]
    w1 = [C.sb([128, 512], F32, "w1") for _ in range(3)]
    w2 = [C.sb([128, 512], F32, "w2") for _ in range(3)]
    w3 = [C.sb([128, 512], F32, "w3") for _ in range(3)]
    w4 = [C.sb([128, 512], F32, "w4") for _ in range(3)]
    xbf = [C.sb([128, 512], BF16, "xbf") for _ in range(2)]
    yo = [C.sb([128, 512], F32, "yo") for _ in range(2)]
    gt = [C.sb([128, 512], F32, "gt") for _ in range(2)]
    yob = [C.sb([128, 512], BF16, "yob") for _ in range(2)]
    pA = [C.ps([128, 512], F32, "pA") for _ in range(2)]
    pB = [C.ps([128, 512], F32, "pB") for _ in range(2)]
    yps = [C.ps([128, 512], F32, "yps") for _ in range(2)]

    units = []
    tcount = 0
    for d in (1, 0):
        order = range(NT - 1, -1, -1) if d == 1 else range(NT)
        for it in order:
            for g in range(8):
                units.append((d, it, g, tcount % 2))
            tcount += 1
    NU = len(units)

    def ph_a(i):
        d, it, g, ysl = units[i]
        gd = d * 8 + g
        s = i % 2
        ts = slice(it * 512, (it + 1) * 512)
        P.op("pe", lambda e: e.matmul(pA[s][:], lhsT=BA[:, gd, :], rhs=uT[:, ts], start=True, stop=True),
             reads=[(tag, "Btab"), (tag, "uT")], writes=[(tag, "pA", s)])
        P.op("pe", lambda e: e.matmul(pB[s][:], lhsT=BB[:, gd, :], rhs=uT[:, ts], start=True, stop=True),
             reads=[(tag, "Btab"), (tag, "uT")], writes=[(tag, "pB", s)])
        P.op("act", lambda e: e.copy(out=bA[s][:], in_=pA[s][:]), writes=[(tag, "bA", s), (tag, "pA", s)])
        P.op("act", lambda e: e.copy(out=bB[s][:], in_=pB[s][:]), writes=[(tag, "bB", s), (tag, "pB", s)])

    def tabs(i):
        d, it, g, ysl = units[i]
        gd = d * 8 + g
        return gd, cosT[:, gd, :], sinT[:, gd, :], (tag, "cosT", gd), (tag, "sinT", gd)

    def ph_c(i):
        gd, ct, st_, kct, kst = tabs(i)
        s, w = i % 2, i % 3
        P.op("pool", lambda e: e.tensor_tensor(out=w3[w][:], in0=bB[s][:], in1=ct, op=ALU.mult), reads=[(tag, "bB", s), kct], writes=[(tag, "w3", w)])
        P.op("pool", lambda e: e.tensor_tensor(out=w4[w][:], in0=bA[s][:], in1=st_, op=ALU.mult), reads=[(tag, "bA", s), kst], writes=[(tag, "w4", w)])
        P.op("pool", lambda e: e.tensor_tensor(out=w3[w][:], in0=w3[w][:], in1=w4[w][:], op=ALU.subtract), reads=[(tag, "w3", w), (tag, "w4", w)], writes=[(tag, "w3", w)])

    def ph_df(i_scan, i_chain):
        ops_scan, ops_chain = [], []
        if i_scan is not None:
            d, it, g, ysl = units[i_scan]
            gd = d * 8 + g
            w = i_scan % 3
            rb = sm["e"][:, gd:gd + 1].to_broadcast([128, 512])
            if d == 0:
                oA, iA, oB, iB, lastc = w2[w][:], w1[w][:], w4[w][:], w3[w][:], slice(511, 512)
            else:
                oA, iA, oB, iB, lastc = w2[w][:, ::-1], w1[w][:, ::-1], w4[w][:, ::-1], w3[w][:, ::-1], slice(0, 1)
            c5, s5 = sm["c512"][:, gd:gd + 1], sm["s512"][:, gd:gd + 1]
            ops_scan = [
                (lambda e: e.tensor_tensor_scan(out=oA, data0=rb, data1=iA, initial=cA[:, gd:gd + 1], op0=ALU.mult, op1=ALU.add),
                 [(tag, "w1", w), k("e"), (tag, "cA", gd)], [(tag, "w2", w)]),
                (lambda e: e.tensor_tensor_scan(out=oB, data0=rb, data1=iB, initial=cB[:, gd:gd + 1], op0=ALU.mult, op1=ALU.add),
                 [(tag, "w3", w), k("e"), (tag, "cB", gd)], [(tag, "w4", w)]),
                (lambda e: e.tensor_tensor(out=ctmp[:, 0:1], in0=w4[w][:, lastc], in1=s5, op=ALU.mult), [(tag, "w4", w), k("s512")], [(tag, "ct0")]),
                (lambda e: e.tensor_tensor(out=ctmp[:, 1:2], in0=w2[w][:, lastc], in1=s5, op=ALU.mult), [(tag, "w2", w), k("s512")], [(tag, "ct1")]),
                (lambda e: e.scalar_tensor_tensor(out=cA[:, gd:gd + 1], in0=w2[w][:, lastc], scalar=c5, in1=ctmp[:, 0:1], op0=ALU.mult, op1=ALU.subtract),
                 [(tag, "w2", w), k("c512"), (tag, "ct0")], [(tag, "cA", gd)]),
                (lambda e: e.scalar_tensor_tensor(out=cB[:, gd:gd + 1], in0=w4[w][:, lastc], scalar=c5, in1=ctmp[:, 1:2], op0=ALU.mult, op1=ALU.add),
                 [(tag, "w4", w), k("c512"), (tag, "ct1")], [(tag, "cB", gd)]),
            ]
        if i_chain is not None:
            gd2, ct, st_, kct, kst = tabs(i_chain)
            s, w_ = i_chain % 2, i_chain % 3
            ops_chain = [
                (lambda e: e.tensor_tensor(out=w1[w_][:], in0=bA[s][:], in1=ct, op=ALU.mult), [(tag, "bA", s), kct], [(tag, "w1", w_)]),
                (lambda e: e.tensor_tensor(out=w2[w_][:], in0=bB[s][:], in1=st_, op=ALU.mult), [(tag, "bB", s), kst], [(tag, "w2", w_)]),
                (lambda e: e.tensor_tensor(out=w1[w_][:], in0=w1[w_][:], in1=w2[w_][:], op=ALU.add), [(tag, "w1", w_), (tag, "w2", w_)], [(tag, "w1", w_)]),
            ]
        order = []
        sq, ch = list(ops_scan), list(ops_chain)
        pattern = ["s", "c", "s", "c", "s", "s", "c", "s", "s"]
        for p_ in pattern:
            if p_ == "s" and sq:
                order.append(sq.pop(0))
            elif p_ == "c" and ch:
                order.append(ch.pop(0))
        order += sq + ch
        for fn, rd, wr in order:
            P.op("dve", fn, reads=rd, writes=wr)

    def ph_e(i):
        gd, ct, st_, kct, kst = tabs(i)
        w = i % 3
        P.op("pool", lambda e: e.tensor_tensor(out=w1[w][:], in0=w2[w][:], in1=ct, op=ALU.mult), reads=[(tag, "w2", w), kct], writes=[(tag, "w1", w)])
        P.op("pool", lambda e: e.tensor_tensor(out=w3[w][:], in0=w4[w][:], in1=st_, op=ALU.mult), reads=[(tag, "w4", w), kst], writes=[(tag, "w3", w)])

    def ph_b(i):
        d, it, g, ysl = units[i]
        gd = d * 8 + g
        w, s = i % 3, i % 2
        ts = slice(it * 512, (it + 1) * 512)
        P.op("dve", lambda e: e.tensor_tensor(out=xbf[s][:], in0=w1[w][:], in1=w3[w][:], op=ALU.subtract), reads=[(tag, "w1", w), (tag, "w3", w)], writes=[(tag, "xbf", s)])
        P.op("pe", lambda e: e.matmul(yps[ysl][:], lhsT=CP[:, gd, :], rhs=xbf[s][:], start=(g == 0), stop=(g == 7 and d == 1)),
             reads=[(tag, "CP"), (tag, "xbf", s)], writes=[(tag, "yps", ysl)])
        if g != 7:
            return
        if d == 1:
            P.op("act", lambda e: e.copy(out=yb[:, ts], in_=yps[ysl][:]), writes=[(tag, "yb", it), (tag, "yps", ysl)])
        else:
            P.op("pe", lambda e: e.matmul(yps[ysl][:], lhsT=Dg[:], rhs=uT[:, ts], start=False, stop=True),
                 reads=[(tag, "Dg"), (tag, "uT")], writes=[(tag, "yps", ysl)])
            P.op("dve", lambda e: e.tensor_tensor(out=yo[ysl][:], in0=yps[ysl][:], in1=yb[:, ts], op=ALU.add),
                 reads=[(tag, "yb", it)], writes=[(tag, "yo", ysl), (tag, "yps", ysl)])
            emit_gelu(P, (tag, "g", ysl), yo[ysl][:], (tag, "yo", ysl), gt[ysl][:], yob[ysl][:], (tag, "yob", ysl))
            P.dma("sp", ys5T_d[:, ts], yob[ysl][:], chan=(tag, "yo", ysl), reads=[(tag, "yob", ysl)], writes=[(tag, "ys5T", it)])

    ph_a(0)
    for idx in range(NU + 2):
        if idx + 1 < NU:
            ph_a(idx + 1)
        if 0 <= idx - 2 < NU:
            ph_b(idx - 2)
        if idx < NU:
            ph_c(idx)
        ph_df(idx - 1 if 0 <= idx - 1 < NU else None, idx if idx < NU else None)
        if 0 <= idx - 1 < NU:
            ph_e(idx - 1)


def build_A4(L, scan_pool=True):
    nc = bass.Bass("TRN2", target_bir_lowering=False)
    with ExitStack() as es:
        C = Ctx(nc, es)
        fmT_d = C.din("fmT", [7, 128, L], BF16)
        s5p_d = C.din("s5p", [128, 49], F32)
        bre_d = C.din("s5bre", [128, 16, 16], F32)
        bim_d = C.din("s5bim", [128, 16, 16], F32)
        cc_d = C.din("s5cc", [128, 2, 128], F32)
        sc_d = C.din("s5c", [128, S5C_W], F32)
        idf_d = C.din("identf", [128, 128], F32)
        ys5T_d = C.dout("ys5T", [128, L], BF16)
        stage_A4(C, fmT_d, s5p_d[:, :], bre_d[:, :, :], bim_d[:, :, :], cc_d[:, :, :], sc_d[:, :], idf_d[:, :], ys5T_d, L, scan_pool=scan_pool)
        C.P.emit(es)
    return nc


W0_SPECS = (("w_out", 1024, 1024), ("ple_gate_w", 1024, 1024), ("ple_w", 256, 1024), ("s5_glu_w", 256, 256),
            ("ffn_w_up", 1024, 5632), ("ffn_w_down", 2816, 1024))


def stage_W0(C, pairs, tag="w0"):
    P = C.P
    stg = [C.sb([128, 2816], F32, "w0s") for _ in range(2)]
    obf = [C.sb([128, 2816], BF16, "w0o") for _ in range(2)]
    i = 0
    engs = ("dve", "pool", "act")
    for src_d, dst_d, R, N in pairs:
        for r0 in range(0, R, 128):
            for n0 in range(0, N, 2816):
                n1 = min(N, n0 + 2816)
                w = n1 - n0
                s = i % 2
                P.dma("sp", stg[s][:, 0:w], src_d[r0:r0 + 128, n0:n1], chan=(tag, "in", s), writes=[(tag, "stg", s)])
                en = engs[i % 3]
                if en == "act":
                    P.op("act", lambda e, s=s, w=w: e.copy(out=obf[s][:, 0:w], in_=stg[s][:, 0:w]), reads=[(tag, "stg", s)], writes=[(tag, "obf", s)])
                else:
                    P.op(en, lambda e, s=s, w=w: e.tensor_copy(out=obf[s][:, 0:w], in_=stg[s][:, 0:w]), reads=[(tag, "stg", s)], writes=[(tag, "obf", s)])
                P.dma("pool", dst_d[r0:r0 + 128, n0:n1], obf[s][:, 0:w], chan=(tag, "out", s), reads=[(tag, "obf", s)], writes=[(tag, "dst", i)])
                i += 1


def build_W0():
    nc = bass.Bass("TRN2", target_bir_lowering=False)
    with ExitStack() as es:
        C = Ctx(nc, es)
        pairs = []
        for nm, R, N in W0_SPECS:
            pairs.append((C.din(nm, [R, N], F32), C.dout(nm + "_bf", [R, N], BF16), R, N))
        stage_W0(C, pairs)
        C.P.emit(es)
    return nc


def emit_ln(P, tag, h, hkey, st, mv, rstd, epsb, gtab, btab, gkeys, out, okey, eng2="pool"):
    ks, km, kr = (tag, "st"), (tag, "mv"), (tag, "rstd")
    for j in range(2):
        P.op("dve", lambda e, j=j: e.bn_stats(out=st[:, j, :], in_=h[:, j * 512:(j + 1) * 512]), reads=[hkey], writes=[ks])
    P.op("dve", lambda e: e.bn_aggr(out=mv[:], in_=st[:]), reads=[ks], writes=[km])
    P.op("act", lambda e: e.activation(out=rstd[:], in_=mv[:, 1:2], func=AF.Sqrt, bias=epsb[:, 0:1], scale=1.0), reads=[km] + gkeys, writes=[kr])
    P.op("dve", lambda e: e.reciprocal(out=rstd[:], in_=rstd[:]), reads=[kr], writes=[kr])
    P.op("dve", lambda e: e.tensor_scalar(out=h, in0=h, scalar1=mv[:, 0:1], scalar2=rstd[:, 0:1], op0=ALU.subtract, op1=ALU.mult),
         reads=[hkey, km, kr], writes=[hkey])
    P.op(eng2, lambda e: e.tensor_tensor(out=h, in0=h, in1=gtab, op=ALU.mult), reads=[hkey] + gkeys, writes=[hkey])
    P.op(eng2, lambda e: e.tensor_tensor(out=out, in0=h, in1=btab, op=ALU.add), reads=[hkey] + gkeys, writes=[okey])


def emit_to_featmajor(P, C_tag, src32, skey, xbf, pst, dstT, dkey_fn, ident, j):
    tag = C_tag
    P.op("act", lambda e: e.copy(out=xbf, in_=src32), reads=[skey], writes=[(tag, "xbf")])
    for c in range(8):
        P.op("pe", lambda e, c=c: e.transpose(out=pst[:, c, :], in_=xbf[:, c * 128:(c + 1) * 128], identity=ident[:]),
             reads=[(tag, "xbf"), "ident"], writes=[(tag, "pst")])
    P.op("dve", lambda e: e.tensor_copy(out=dstT[:, :, j * 128:(j + 1) * 128], in_=pst[:]), writes=[dkey_fn(j), (tag, "pst")])


def stage_P1(C, ycT_d, x_d, wout_d, glw_d, glb_d, g1_d, b1_d, x1_d, x1T_d, ident, NTOK, tag="p1"):
    P = C.P
    wout = C.sb([128, 8, 1024], BF16, "wout")
    glw = C.sb([128, 2, 256], BF16, "glw")
    glb = C.sb([128, 2], F32, "glb")
    gtab = C.sb([128, 1024], F32, "gtab")
    btab = C.sb([128, 1024], F32, "btab")
    epsb = C.sb([128, 1], F32, "epsb")
    P.dma("sp", wout[:], wout_d.rearrange("(c p) n -> p c n", p=128), chan=(tag, "c0"), writes=[(tag, "wout")])
    P.dma("sp", glw[:], glw_d.rearrange("(c p) n -> p c n", p=128), chan=(tag, "c0"), writes=[(tag, "glw")])
    for oc in range(2):
        P.dma("sp", glb[:, oc:oc + 1], glb_d[oc * 128:(oc + 1) * 128].rearrange("(p o) -> p o", o=1), chan=(tag, "c0"), writes=[(tag, "glb")])
    P.dma("sp", gtab[:], g1_d.partition_broadcast(128), chan=(tag, "c0"), writes=[(tag, "gtab")])
    P.dma("sp", btab[:], b1_d.partition_broadcast(128), chan=(tag, "c0"), writes=[(tag, "btab")])
    P.op("dve", lambda e: e.memset(epsb[:], LN_EPS), writes=[(tag, "epsb")])
    gk = [(tag, "gtab"), (tag, "btab"), (tag, "epsb")]
    yc = [C.sb([128, 8, 512], BF16, "yc") for _ in range(2)]
    ysg = [C.sb([128, 2, 512], BF16, "ysg") for _ in range(2)]
    gsig = C.sb([128, 512], F32, "gsig")
    xin = [C.sb([128, 1024], F32, "xin") for _ in range(2)]
    hh = [C.sb([128, 1024], F32, "hh") for _ in range(2)]
    x1o = [C.sb([128, 1024], F32, "x1o") for _ in range(2)]
    xbf = C.sb([128, 1024], BF16, "xbf")
    x1T = [C.sb([128, 8, 512], BF16, "x1T") for _ in range(2)]
    st = C.sb([128, 2, 6], F32, "st")
    mv = C.sb([128, 2], F32, "mv")
    rstd = C.sb([128, 1], F32, "rstd")
    pgl = [C.ps([128, 512], F32, "pgl") for _ in range(2)]
    pmx = [C.ps([128, 512], F32, "pmx") for _ in range(4)]
    pst = C.ps([128, 8, 128], BF16, "pst")
    ycT_v = ycT_d.rearrange("(c p) t -> p c t", p=128)
    x1T_v = x1T_d.rearrange("(c p) t -> p c t", p=128)
    NTT = NTOK // 512
    kk = 0
    for tt in range(NTT):
        s = tt % 2
        t0 = tt * 512
        P.dma("sp", yc[s][:], ycT_v[:, :, t0:t0 + 512], chan=(tag, "yc", s), writes=[(tag, "yc", s)])
        for oc in range(2):
            for kc in range(2):
                P.op("pe", lambda e, s=s, oc=oc, kc=kc: e.matmul(pgl[oc][:], lhsT=glw[:, kc, oc * 128:(oc + 1) * 128], rhs=yc[s][:, 6 + kc, :],
                                                                 start=(kc == 0), stop=(kc == 1)),
                     reads=[(tag, "glw"), (tag, "yc", s)], writes=[(tag, "pgl", oc)])
            P.op("act", lambda e, oc=oc: e.activation(out=gsig[:], in_=pgl[oc][:], func=AF.Sigmoid, bias=glb[:, oc:oc + 1], scale=1.0),
                 reads=[(tag, "glb")], writes=[(tag, "gsig"), (tag, "pgl", oc)])
            P.op("dve", lambda e, s=s, oc=oc: e.tensor_tensor(out=ysg[s][:, oc, :], in0=gsig[:], in1=yc[s][:, 6 + oc, :], op=ALU.mult),
                 reads=[(tag, "gsig"), (tag, "yc", s)], writes=[(tag, "ysg", s, oc)])
        for j in range(4):
            b = kk % 2
            kk += 1
            r0 = t0 + j * 128
            P.dma("sp", xin[b][:], x_d[r0:r0 + 128, :], chan=(tag, "xin", b), writes=[(tag, "xin", b)])
            for nb in range(2):
                pb = (2 * b + nb)
                for c in range(8):
                    def lhs(c=c, s=s, j=j):
                        return yc[s][:, c, j * 128:(j + 1) * 128] if c < 6 else ysg[s][:, c - 6, j * 128:(j + 1) * 128]
                    P.op("pe", lambda e, pb=pb, c=c, nb=nb, lhs=lhs: e.matmul(pmx[pb][:], lhsT=lhs(), rhs=wout[:, c, nb * 512:(nb + 1) * 512],
                                                                              start=(c == 0), stop=(c == 7)),
                         reads=[(tag, "wout"), (tag, "yc", s), (tag, "ysg", s, 0), (tag, "ysg", s, 1)], writes=[(tag, "pmx", pb)])
                P.op("dve", lambda e, b=b, pb=pb, nb=nb: e.scalar_tensor_tensor(out=hh[b][:, nb * 512:(nb + 1) * 512], in0=xin[b][:, nb * 512:(nb + 1) * 512],
                                                                               scalar=ALPHA, in1=pmx[pb][:], op0=ALU.mult, op1=ALU.add),
                     reads=[(tag, "xin", b)], writes=[(tag, "hh", b), (tag, "pmx", pb)])
            emit_ln(P, (tag, "ln"), hh[b][:], (tag, "hh", b), st, mv, rstd, epsb, gtab[:], btab[:], gk, x1o[b][:], (tag, "x1o", b))
            P.dma("sp", x1_d[r0:r0 + 128, :], x1o[b][:], chan=(tag, "x1o", b), reads=[(tag, "x1o", b)], writes=[(tag, "x1d", tt, j)])
            emit_to_featmajor(P, (tag, "fm"), x1o[b][:], (tag, "x1o", b), xbf[:], pst, x1T[s], lambda jj, s=s: (tag, "x1T", s, jj), ident, j)
        P.dma("sp", x1T_v[:, :, t0:t0 + 512], x1T[s][:], chan=(tag, "x1T", s), reads=[(tag, "x1T", s, jj) for jj in range(4)],
              writes=[(tag, "x1Td", tt)])


def build_P1(NTOK):
    nc = bass.Bass("TRN2", target_bir_lowering=False)
    with ExitStack() as es:
        C = Ctx(nc, es)
        ycT_d = C.din("ycT", [1024, NTOK], BF16)
        x_d = C.din("x", [NTOK, 1024], F32)
        wout_d = C.din("w_out_bf", [1024, 1024], BF16)
        glw_d = C.din("s5_glu_w_bf", [256, 256], BF16)
        glb_d = C.din("glb", [256], F32)
        g1_d = C.din("ln_g", [1024], F32)
        b1_d = C.din("ln_b", [1024], F32)
        id_d = C.din("ident", [128, 128], BF16)
        x1_d = C.dout("x1", [NTOK, 1024], F32)
        x1T_d = C.dout("x1T", [1024, NTOK], BF16)
        ident = C.sb([128, 128], BF16, "ident")
        C.P.dma("sp", ident[:], id_d[:, :], chan="ident", writes=["ident"])
        stage_P1(C, ycT_d, x_d, wout_d, glw_d, glb_d, g1_d, b1_d, x1_d, x1T_d, ident, NTOK)
        C.P.emit(es)
    return nc


NFT = D_FF // 128


def conv_layout(conv_w, conv_b):
    a = np.concatenate([np.asarray(conv_w), np.asarray(conv_b)[None, :]], axis=0)
    return np.ascontiguousarray(a.reshape(4, NFT, 128).transpose(2, 1, 0)).astype(np.float32)


def stage_P2(C, x1T_d, x1_d, p_d, wup_d, wdn_d, plew_d, plegw_d, cwb_d, g2_d, b2_d, x2_d, x2T_d, ident, NTOK, tag="p2"):
    P = C.P
    plegw = C.sb([128, 8, 1024], BF16, "plegw")
    plew = C.sb([128, 2, 1024], BF16, "plew")
    wdn = C.sb([128, NFT, 1024], BF16, "wdn")
    cwb = C.sb([128, NFT, 4], F32, "cwb")
    gtab = C.sb([128, 1024], F32, "gtab")
    btab = C.sb([128, 1024], F32, "btab")
    epsb = C.sb([128, 1], F32, "epsb")
    P.dma("sp", plegw[:], plegw_d.rearrange("(c p) n -> p c n", p=128), chan=(tag, "c0"), writes=[(tag, "plegw")])
    P.dma("sp", plew[:], plew_d.rearrange("(c p) n -> p c n", p=128), chan=(tag, "c0"), writes=[(tag, "plew")])
    P.dma("sp", wdn[:], wdn_d.rearrange("(c p) n -> p c n", p=128), chan=(tag, "c0"), writes=[(tag, "wdn")])
    P.dma("sp", cwb[:], cwb_d, chan=(tag, "c0"), writes=[(tag, "cwb")])
    P.dma("sp", gtab[:], g2_d.partition_broadcast(128), chan=(tag, "c0"), writes=[(tag, "gtab")])
    P.dma("sp", btab[:], b2_d.partition_broadcast(128), chan=(tag, "c0"), writes=[(tag, "btab")])
    P.op("dve", lambda e: e.memset(epsb[:], LN_EPS), writes=[(tag, "epsb")])
    gk = [(tag, "gtab"), (tag, "btab"), (tag, "epsb")]

    xt = [C.sb([128, 8, 514], BF16, "xt") for _ in range(2)]
    wg = [C.sb([128, 8, 128], BF16, "wg") for _ in range(3)]
    wv = [C.sb([128, 8, 128], BF16, "wv") for _ in range(3)]
    hm = C.sb([128, NFT, 512], BF16, "hm")
    racc = [C.sb([128, 1024], F32, "racc") for _ in range(4)]
    x1t = [C.sb([128, 1024], F32, "x1t") for _ in range(2)]
    pin = [C.sb([128, 256], F32, "pin") for _ in range(2)]
    pbf = [C.sb([128, 256], BF16, "pbf") for _ in range(2)]
    pT = C.sb([128, 2, 512], BF16, "pT")
    sg = [C.sb([128, 512], F32, "sg") for _ in range(2)]
    gext = [C.sb([128, 514], F32, "gext") for _ in range(3)]
    cv = [C.sb([128, 512], F32, "cv") for _ in range(3)]
    tmp = [C.sb([128, 512], F32, "tmp") for _ in range(3)]
    oneb = C.sb([128, 1], F32, "oneb")
    P.op("dve", lambda e: e.memset(oneb[:], 1.0), writes=[(tag, "oneb")])
    xbf = C.sb([128, 1024], BF16, "xbf")
    x2T = C.sb([128, 8, 512], BF16, "x2T")
    st = C.sb([128, 2, 6], F32, "st")
    mv = C.sb([128, 2], F32, "mv")
    rstd = C.sb([128, 1], F32, "rstd")
    pgate = [C.ps([128, 512], F32, "pgate") for _ in range(2)]
    pval = [C.ps([128, 512], F32, "pval") for _ in range(2)]
    pd = [C.ps([128, 512], F32, "pd") for _ in range(2)]
    phalo = C.ps([128, 2], F32, "phalo")
    pst = C.ps([128, 8, 128], BF16, "pst")
    x1T_v = x1T_d.rearrange("(c p) t -> p c t", p=128)
    x2T_v = x2T_d.rearrange("(c p) t -> p c t", p=128)
    wup_v = wup_d.rearrange("(c p) n -> p c n", p=128)
    NTT = NTOK // 512
    cnt = {"w": 0, "u": 0, "d": 0, "x": 0, "p": 0, "q": 0}

    def tile(tt):
        s = tt % 2
        t0 = tt * 512
        P.dma("sp", xt[s][:], x1T_v[:, :, t0:t0 + 514], chan=(tag, "xt", s), writes=[(tag, "xt", s)])
        for j in range(4):
            b = cnt["p"] % 2
            cnt["p"] += 1
            r0 = t0 + j * 128
            P.dma("sp", pin[b][:], p_d[r0:r0 + 128, :], chan=(tag, "pin", b), writes=[(tag, "pin", b)])
            P.op("act", lambda e, b=b: e.copy(out=pbf[b][:], in_=pin[b][:]), reads=[(tag, "pin", b)], writes=[(tag, "pbf", b)])
            for c in range(2):
                P.op("pe", lambda e, b=b, c=c: e.transpose(out=pst[:, c, :], in_=pbf[b][:, c * 128:(c + 1) * 128], identity=ident[:]),
                     reads=[(tag, "pbf", b), "ident"], writes=[(tag, "pst")])
            P.op("dve", lambda e, j=j: e.tensor_copy(out=pT[:, :, j * 128:(j + 1) * 128], in_=pst[:, 0:2, :]), writes=[(tag, "pT", j), (tag, "pst")])
        for j in range(4):
            b = cnt["x"] % 2
            cnt["x"] += 1
            r0 = t0 + j * 128
            P.dma("sp", x1t[b][:], x1_d[r0:r0 + 128, :], chan=(tag, "x1t", b), writes=[(tag, "x1t", b)])
            for nb in range(2):
                q = cnt["q"] % 2
                cnt["q"] += 1
                ns = slice(nb * 512, (nb + 1) * 512)
                for c in range(2):
                    P.op("pe", lambda e, c=c, j=j, ns=ns: e.matmul(pd[0][:], lhsT=pT[:, c, j * 128:(j + 1) * 128], rhs=plew[:, c, ns], start=(c == 0), stop=(c == 1)),
                         reads=[(tag, "pT", j), (tag, "plew")], writes=[(tag, "pd", 0)])
                for c in range(8):
                    P.op("pe", lambda e, c=c, j=j, ns=ns, s=s: e.matmul(pd[1][:], lhsT=xt[s][:, c, 1 + j * 128:1 + (j + 1) * 128], rhs=plegw[:, c, ns],
                                                                       start=(c == 0), stop=(c == 7)),
                         reads=[(tag, "xt", s), (tag, "plegw")], writes=[(tag, "pd", 1)])
                P.op("act", lambda e, q=q: e.activation(out=sg[q][:], in_=pd[1][:], func=AF.Sigmoid), writes=[(tag, "sg", q), (tag, "pd", 1)])
                P.op("dve", lambda e, q=q: e.tensor_tensor(out=sg[q][:], in0=pd[0][:], in1=sg[q][:], op=ALU.mult),
                     reads=[(tag, "sg", q)], writes=[(tag, "sg", q), (tag, "pd", 0)])
                P.op("dve", lambda e, q=q, b=b, j=j, ns=ns: e.scalar_tensor_tensor(out=racc[j][:, ns], in0=x1t[b][:, ns], scalar=ALPHA, in1=sg[q][:],
                                                                                  op0=ALU.mult, op1=ALU.add),
                     reads=[(tag, "sg", q), (tag, "x1t", b)], writes=[(tag, "racc", j)])
        def wload(f):
            w = (cnt["w"] + f) % 3
            P.dma("sp", wg[w][:], wup_v[:, :, f * 128:(f + 1) * 128], chan=(tag, "wg", w), writes=[(tag, "wg", w)])
            P.dma("sp", wv[w][:], wup_v[:, :, D_FF + f * 128:D_FF + (f + 1) * 128], chan=(tag, "wv", w), writes=[(tag, "wv", w)])
        wload(0)
        wload(1)
        for f in range(NFT):
            w = (cnt["w"] + f) % 3
            u = cnt["u"] % 2
            g3 = cnt["u"] % 3
            cnt["u"] += 1
            if f + 2 < NFT:
                wload(f + 2)
            for c in range(8):
                P.op("pe", lambda e, c=c, w=w, u=u, s=s: e.matmul(pgate[u][:], lhsT=wg[w][:, c, :], rhs=xt[s][:, c, 1:513], start=(c == 0), stop=(c == 7)),
                     reads=[(tag, "wg", w), (tag, "xt", s)], writes=[(tag, "pgate", u)])
            for c in range(8):
                P.op("pe", lambda e, c=c, w=w, s=s: e.matmul(phalo[:], lhsT=wg[w][:, c, :], rhs=xt[s][:, c, 0:514:513], start=(c == 0), stop=(c == 7)),
                     reads=[(tag, "wg", w), (tag, "xt", s)], writes=[(tag, "phalo")])
            for c in range(8):
                P.op("pe", lambda e, c=c, w=w, u=u, s=s: e.matmul(pval[u][:], lhsT=wv[w][:, c, :], rhs=xt[s][:, c, 1:513], start=(c == 0), stop=(c == 7)),
                     reads=[(tag, "wv", w), (tag, "xt", s)], writes=[(tag, "pval", u)])
            kc, kt = (tag, "cv", g3), (tag, "tmp", g3)
            P.op("act", lambda e, u=u, g3=g3: e.copy(out=gext[g3][:, 1:513], in_=pgate[u][:]), writes=[(tag, "gext", g3), (tag, "pgate", u)])
            P.op("act", lambda e, u=u, g3=g3, f=f: e.activation(out=cv[g3][:], in_=pgate[u][:], func=AF.Identity, scale=cwb[:, f, 1:2], bias=cwb[:, f, 3:4]),
                 reads=[(tag, "cwb")], writes=[kc, (tag, "pgate", u)])
            P.op("act", lambda e, g3=g3: e.copy(out=gext[g3][:, 0:514:513], in_=phalo[:]), writes=[(tag, "gexth", g3), (tag, "phalo")])
            gkeys = [(tag, "gext", g3), (tag, "gexth", g3), (tag, "cwb")]
            P.op("dve", lambda e, g3=g3, f=f: e.scalar_tensor_tensor(out=cv[g3][:], in0=gext[g3][:, 0:512], scalar=cwb[:, f, 0:1], in1=cv[g3][:],
                                                                     op0=ALU.mult, op1=ALU.add), reads=gkeys + [kc], writes=[kc])
            P.op("dve", lambda e, g3=g3, f=f: e.scalar_tensor_tensor(out=cv[g3][:], in0=gext[g3][:, 2:514], scalar=cwb[:, f, 2:3], in1=cv[g3][:],
                                                                     op0=ALU.mult, op1=ALU.add), reads=gkeys + [kc], writes=[kc])
            P.op("act", lambda e, g3=g3: e.activation(out=tmp[g3][:], in_=cv[g3][:], func=AF.Square), reads=[kc], writes=[kt])
            P.op("act", lambda e, g3=g3: e.activation(out=tmp[g3][:], in_=tmp[g3][:], func=AF.Identity, scale=0.044715, bias=oneb[:, 0:1]),
                 reads=[kt, (tag, "oneb")], writes=[kt])
            P.op("pool", lambda e, g3=g3: e.tensor_tensor(out=tmp[g3][:], in0=tmp[g3][:], in1=cv[g3][:], op=ALU.mult), reads=[kt, kc], writes=[kt])
            P.op("act", lambda e, g3=g3: e.activation(out=tmp[g3][:], in_=tmp[g3][:], func=AF.Sigmoid, scale=GELU_K), reads=[kt], writes=[kt])
            P.op("pool", lambda e, g3=g3: e.tensor_tensor(out=tmp[g3][:], in0=tmp[g3][:], in1=cv[g3][:], op=ALU.mult), reads=[kt, kc], writes=[kt])
            P.op("dve", lambda e, u=u, g3=g3, f=f: e.tensor_tensor(out=hm[:, f, :], in0=pval[u][:], in1=tmp[g3][:], op=ALU.mult),
                 reads=[kt], writes=[(tag, "hm", f), (tag, "pval", u)])
        cnt["w"] += NFT
        for j in range(4):
            for nb in range(2):
                d = cnt["d"] % 2
                cnt["d"] += 1
                ns = slice(nb * 512, (nb + 1) * 512)
                for f in range(NFT):
                    P.op("pe", lambda e, f=f, j=j, ns=ns, d=d: e.matmul(pd[d][:], lhsT=hm[:, f, j * 128:(j + 1) * 128], rhs=wdn[:, f, ns],
                                                                       start=(f == 0), stop=(f == NFT - 1)),
                         reads=[(tag, "hm", f), (tag, "wdn")], writes=[(tag, "pd", d)])
                P.op("dve", lambda e, j=j, ns=ns, d=d: e.tensor_tensor(out=racc[j][:, ns], in0=pd[d][:], in1=racc[j][:, ns], op=ALU.add),
                     reads=[(tag, "racc", j)], writes=[(tag, "racc", j), (tag, "pd", d)])
            r0 = t0 + j * 128
            emit_ln(P, (tag, "ln"), racc[j][:], (tag, "racc", j), st, mv, rstd, epsb, gtab[:], btab[:], gk, racc[j][:], (tag, "racc", j))
            P.dma("sp", x2_d[r0:r0 + 128, :], racc[j][:], chan=(tag, "x2o", j), reads=[(tag, "racc", j)], writes=[(tag, "x2d", tt, j)])
            emit_to_featmajor(P, (tag, "fm"), racc[j][:], (tag, "racc", j), xbf[:], pst, x2T, lambda jj: (tag, "x2T", jj), ident, j)
        P.dma("sp", x2T_v[:, :, t0:t0 + 512], x2T[:], chan=(tag, "x2T"), reads=[(tag, "x2T", jj) for jj in range(4)], writes=[(tag, "x2Td", tt)])

    for tt in range(NTT):
        tile(tt)


def build_P2(NTOK):
    nc = bass.Bass("TRN2", target_bir_lowering=False)
    with ExitStack() as es:
        C = Ctx(nc, es)
        x1T_d = C.din("x1Te", [1024, NTOK + 2], BF16)
        x1_d = C.din("x1", [NTOK, 1024], F32)
        p_d = C.din("p", [NTOK, 256], F32)
        wup_d = C.din("ffn_w_up_bf", [1024, 2 * D_FF], BF16)
        wdn_d = C.din("ffn_w_down_bf", [D_FF, 1024], BF16)
        plew_d = C.din("ple_w_bf", [256, 1024], BF16)
        plegw_d = C.din("ple_gate_w_bf", [1024, 1024], BF16)
        cwb_d = C.din("cwb", [128, NFT, 4], F32)
        g2_d = C.din("ln_g", [1024], F32)
        b2_d = C.din("ln_b", [1024], F32)
        id_d = C.din("ident", [128, 128], BF16)
        x2_d = C.dout("x2", [NTOK, 1024], F32)
        x2T_d = C.dout("x2T", [1024, NTOK], BF16)
        ident = C.sb([128, 128], BF16, "ident")
        C.P.dma("sp", ident[:], id_d[:, :], chan="ident", writes=["ident"])
        stage_P2(C, x1T_d, x1_d, p_d, wup_d, wdn_d, plew_d, plegw_d, cwb_d[:, :, :], g2_d, b2_d, x2_d, x2T_d, ident, NTOK)
        C.P.emit(es)
    return nc


def build_W0_split():
    nc = bass.Bass("TRN2", target_bir_lowering=False)
    with ExitStack() as es:
        C = Ctx(nc, es)
        pairs = []
        for i in range(DEPTH):
            for nm, R, N in W0_SPECS:
                r = R // NCORES
                pairs.append((C.din("%s_%d" % (nm, i), [r, N], F32), C.dout("%s_%d_bf" % (nm, i), [r, N], BF16), r, N))
        stage_W0v(C, pairs)
        C.P.emit(es)
    return nc


def stage_W0v(C, pairs, tag="w0"):
    P = C.P
    stg = [C.sb([128, 2816], F32, "w0s") for _ in range(2)]
    obf = [C.sb([128, 2816], BF16, "w0o") for _ in range(2)]
    i = 0
    engs = ("dve", "pool", "act")
    for src_d, dst_d, R, N in pairs:
        for r0 in range(0, R, 128):
            pr = min(128, R - r0)
            for n0 in range(0, N, 2816):
                n1 = min(N, n0 + 2816)
                w = n1 - n0
                s = i % 2
                P.dma("sp", stg[s][0:pr, 0:w], src_d[r0:r0 + pr, n0:n1], chan=(tag, "in", s), writes=[(tag, "stg", s)])
                en = engs[i % 3]
                if en == "act":
                    P.op("act", lambda e, s=s, w=w, pr=pr: e.copy(out=obf[s][0:pr, 0:w], in_=stg[s][0:pr, 0:w]), reads=[(tag, "stg", s)], writes=[(tag, "obf", s)])
                else:
                    P.op(en, lambda e, s=s, w=w, pr=pr: e.tensor_copy(out=obf[s][0:pr, 0:w], in_=stg[s][0:pr, 0:w]), reads=[(tag, "stg", s)], writes=[(tag, "obf", s)])
                P.dma("pool", dst_d[r0:r0 + pr, n0:n1], obf[s][0:pr, 0:w], chan=(tag, "out", s), reads=[(tag, "obf", s)], writes=[(tag, "dst", i)])
                i += 1


def build_fused(L):
    nc = bass.Bass("TRN2", target_bir_lowering=False)

    def din(name, shape, dt):
        return nc.dram_tensor(name, list(shape), dt, kind="ExternalInput").ap()

    def dint(name, shape, dt):
        return nc.dram_tensor(name, list(shape), dt, kind="Internal").ap()

    x_d = din("x", [L, 1024], F32)
    p_d = [din("p%d" % l, [L, 256], F32) for l in range(DEPTH)]
    pos_d = din("pos", [L], I32)
    rc_d = din("rc", [128, 4], F32)
    idb_d = din("ident", [128, 128], BF16)
    idf_d = din("identf", [128, 128], F32)
    s5c_d = din("s5c", [128, S5C_W], F32)
    zero_d = din("zeros", [128, 8], BF16)
    wsrc = {}
    wbf = {}
    for l in range(DEPTH):
        for nm, R, N in W0_SPECS:
            wsrc[(nm, l)] = din("%s_%d" % (nm, l), [R, N], F32)
            wbf[(nm, l)] = dint("%s_%d_bf" % (nm, l), [R, N], BF16)
    wfm_d = {(l, hp): din("wfm_%d_%d" % (l, hp), [1024, NFM * 128], F32) for l in range(DEPTH) for hp in range(2)}
    wtm_d = {(l, hp): din("wtm_%d_%d" % (l, hp), [1024, 512], F32) for l in range(DEPTH) for hp in range(2)}
    lamp_d = [din("lamp%d" % l, [4, 64], F32) for l in range(DEPTH)]
    subg_d = [din("subg%d" % l, [128], F32) for l in range(DEPTH)]
    lc_d = [din("lc%d" % l, [128, 2], F32) for l in range(DEPTH)]
    rt_d = [din("rt%d" % hp, [128, RT_W], F32) for hp in range(2)]
    gng_d = [din("gng%d" % l, [64], F32) for l in range(DEPTH)]
    gnb_d = [din("gnb%d" % l, [64], F32) for l in range(DEPTH)]
    s5_d = {(l, hp): (din("s5p_%d_%d" % (l, hp), [128, 49], F32), din("s5bre_%d_%d" % (l, hp), [128, 16, 16], F32),
                      din("s5bim_%d_%d" % (l, hp), [128, 16, 16], F32), din("s5cc_%d_%d" % (l, hp), [128, 2, 128], F32))
            for l in range(DEPTH) for hp in range(2)}
    glb_d = [din("glb%d" % l, [256], F32) for l in range(DEPTH)]
    ln_d = [[din("ln%d_%s%d" % (k, gb, l), [1024], F32) for gb in ("g", "b")] for l in range(DEPTH) for k in (1, 2)]
    cwb_d = [din("cwb%d" % l, [128, NFT, 4], F32) for l in range(DEPTH)]
    out_d = nc.dram_tensor("out", [L, 1024], F32, kind="ExternalOutput").ap()

    xT = dint("xT_s", [1024, L], BF16)
    fmT = dint("fmT_s", [7, 128, L], BF16)
    vr = dint("vr_s", [L, 384], BF16)
    sgd = dint("sg_s", [L, 128], F32)
    ycT = dint("ycT_s", [1024, L], BF16)
    x1 = dint("x1_s", [L, 1024], F32)
    x1Te = dint("x1Te_s", [1024, L + 2], BF16)
    xmid = dint("xmid_s", [L, 1024], F32)
    ycT_r = ycT.rearrange("(r p) t -> r p t", p=128)

    def scope(fn, with_ident=False):
        with ExitStack() as es:
            C = Ctx(nc, es)
            ident = None
            if with_ident:
                ident = C.sb([128, 128], BF16, "ident")
                C.P.dma("sp", ident[:], idb_d[:, :], chan="ident", writes=["ident"])
            fn(C, ident)
            C.P.emit(es, own_sems=True)
        nc.all_engine_barrier()
        nc.clear_and_free_semaphores(C.P.sem_handles)
        nc.all_engine_barrier()

    def zero_halo(C, ident):
        z = C.sb([128, 8], BF16, "z")
        C.P.dma("sp", z[:], zero_d[:, :], chan="z", writes=["z"])
        v = x1Te.rearrange("(c p) t -> p c t", p=128)
        C.P.dma("sp", v[:, :, 0:1], z[:].unsqueeze(2), chan="z2", reads=["z"], slow=True)
        C.P.dma("sp", v[:, :, L + 1:L + 2], z[:].unsqueeze(2), chan="z2", reads=["z"], slow=True)

    scope(lambda C, i: stage_W0v(C, [(wsrc[(nm, l)], wbf[(nm, l)], R, N) for l in range(DEPTH) for nm, R, N in W0_SPECS]))
    scope(zero_halo)
    scope(lambda C, i: stage_T0(C, x_d, xT, i, L), with_ident=True)
    for l in range(DEPTH):
        xin_d = x_d if l == 0 else xmid
        xout_d = xmid if l == 0 else out_d
        for hp in range(2):
            scope(lambda C, i, l=l, hp=hp: stage_A1(C, xT, wfm_d[(l, hp)], wtm_d[(l, hp)], pos_d, rc_d, fmT, vr, sgd, L))
            scope(lambda C, i, l=l, hp=hp: stage_A2(C, fmT, vr, lamp_d[l], subg_d[l], lc_d[l], ycT_r[2 * hp:2 * hp + 2, :, :], i, L), with_ident=True)
            scope(lambda C, i, l=l, hp=hp: stage_A3(C, fmT, vr, sgd, rt_d[hp], gng_d[l], gnb_d[l], ycT_r[4 + hp, :, :], i, L), with_ident=True)
            scope(lambda C, i, l=l, hp=hp: stage_A4(C, fmT, s5_d[(l, hp)][0][:, :], s5_d[(l, hp)][1][:, :, :], s5_d[(l, hp)][2][:, :, :],
                                                   s5_d[(l, hp)][3][:, :, :], s5c_d[:, :], idf_d[:, :], ycT_r[6 + hp, :, :], L))
        scope(lambda C, i, l=l, xin_d=xin_d: stage_P1(C, ycT, xin_d, wbf[("w_out", l)], wbf[("s5_glu_w", l)], glb_d[l], ln_d[2 * l][0], ln_d[2 * l][1],
                                                     x1, x1Te[:, 1:L + 1], i, L), with_ident=True)
        scope(lambda C, i, l=l, xout_d=xout_d: stage_P2(C, x1Te, x1, p_d[l], wbf[("ffn_w_up", l)], wbf[("ffn_w_down", l)], wbf[("ple_w", l)],
                                                       wbf[("ple_gate_w", l)], cwb_d[l][:, :, :], ln_d[2 * l + 1][0], ln_d[2 * l + 1][1], xout_d, xT, i, L),
              with_ident=True)
    return nc


def fused_inputs(inp, b):
    m = {"x": np.ascontiguousarray(inp["x"][b], dtype=np.float32), "pos": np.ascontiguousarray(inp["positions"][b], dtype=np.int32),
         "rc": rc_const(), "ident": np.eye(128, dtype=np.float32).astype(ml_dtypes.bfloat16), "identf": np.eye(128, dtype=np.float32),
         "s5c": s5_consts(), "zeros": np.zeros((128, 8), ml_dtypes.bfloat16)}
    for l in range(DEPTH):
        lam_init = 0.8 - 0.6 * math.exp(-0.3 * l)
        m["p%d" % l] = np.ascontiguousarray(inp["p"][l][b], dtype=np.float32)
        for nm, R, N in W0_SPECS:
            m["%s_%d" % (nm, l)] = np.ascontiguousarray(inp[nm][l], dtype=np.float32)
        for hp in range(2):
            fm, tm = a1_columns(hp)
            m["wfm_%d_%d" % (l, hp)] = np.ascontiguousarray(inp["w_in"][l][:, fm], dtype=np.float32)
            m["wtm_%d_%d" % (l, hp)] = np.ascontiguousarray(inp["w_in"][l][:, tm], dtype=np.float32)
            s5 = s5_layout(inp, l, hp)
            for k in ("s5p", "s5bre", "s5bim", "s5cc"):
                m["%s_%d_%d" % (k, l, hp)] = s5[k]
        m["lamp%d" % l] = np.stack([inp["da_lambda_q1"][l], inp["da_lambda_k1"][l], inp["da_lambda_q2"][l], inp["da_lambda_k2"][l]]).astype(np.float32)
        m["subg%d" % l] = np.ascontiguousarray(inp["da_subln_g"][l], dtype=np.float32)
        m["lc%d" % l] = np.tile(np.array([[lam_init, 1.0 - lam_init]], np.float32), (128, 1))
        m["gng%d" % l] = np.ascontiguousarray(inp["ret_gn_g"][l], dtype=np.float32)
        m["gnb%d" % l] = np.ascontiguousarray(inp["ret_gn_b"][l], dtype=np.float32)
        m["glb%d" % l] = np.ascontiguousarray(inp["s5_glu_b"][l], dtype=np.float32)
        for k in (1, 2):
            m["ln%d_g%d" % (k, l)] = np.ascontiguousarray(inp["ln%d_g" % k][l], dtype=np.float32)
            m["ln%d_b%d" % (k, l)] = np.ascontiguousarray(inp["ln%d_b" % k][l], dtype=np.float32)
        m["cwb%d" % l] = conv_layout(inp["ffn_conv_w"][l], inp["ffn_conv_b"][l])
    for hp in range(2):
        m["rt%d" % hp] = ret_consts(hp)
    return m


def kernel_fused(**inputs):
    inp = {k: np.asarray(v) for k, v in inputs.items()}
    B, L, _ = inp["x"].shape
    NTOK = L // 2
    nc = _prog("fused", build_fused, L)
    bmaps = [fused_inputs(inp, b) for b in range(B)]
    res = _run(nc, [bmaps[c // 2] for c in range(NCORES)])
    out = np.empty((B, L, D_MODEL), np.float32)
    for c in range(NCORES):
        b, h = c // 2, c % 2
        out[b, h * NTOK:(h + 1) * NTOK] = res[c]["out"][h * NTOK:(h + 1) * NTOK]
    return out


_PROGS = {}


def _prog(name, fn, *args):
    key = (name,) + args
    if key not in _PROGS:
        _PROGS[key] = fn(*args)
    return _PROGS[key]


def _run(nc, maps):
    return run_bass_kernel_spmd(nc, maps, core_ids=list(range(NCORES))).results


def kernel_unfused(**inputs):
    inp = {k: np.asarray(v) for k, v in inputs.items()}
    x = np.ascontiguousarray(inp["x"], dtype=np.float32)
    B, L, _ = x.shape
    assert B * 2 == NCORES
    NTOK = L // 2
    ident_bf = np.eye(128, dtype=np.float32).astype(ml_dtypes.bfloat16)
    identf = np.eye(128, dtype=np.float32)
    rc = rc_const()
    s5c = s5_consts()
    cores = [(c // 2, c % 2) for c in range(NCORES)]

    maps = []
    for c in range(NCORES):
        m = {}
        for i in range(DEPTH):
            for nm, R, N in W0_SPECS:
                r = R // NCORES
                m["%s_%d" % (nm, i)] = np.ascontiguousarray(inp[nm][i][c * r:(c + 1) * r], dtype=np.float32)
        maps.append(m)
    res = _run(_prog("W0", build_W0_split), maps)
    wbf = [{nm: np.concatenate([res[c]["%s_%d_bf" % (nm, i)] for c in range(NCORES)], axis=0) for nm, R, N in W0_SPECS}
           for i in range(DEPTH)]

    res = _run(_prog("T0", build_T0, NTOK), [{"x": np.ascontiguousarray(x[b, h * NTOK:(h + 1) * NTOK]), "ident": ident_bf} for b, h in cores])
    xT = [np.concatenate([res[2 * b]["xT"], res[2 * b + 1]["xT"]], axis=1) for b in range(B)]
    xcur = [np.ascontiguousarray(x[b, h * NTOK:(h + 1) * NTOK]) for b, h in cores]

    for i in range(DEPTH):
        lam_init = 0.8 - 0.6 * math.exp(-0.3 * i)
        w_in = inp["w_in"][i]
        maps = []
        for b, h in cores:
            fm, tm = a1_columns(h)
            maps.append({"xT": xT[b], "wfm": np.ascontiguousarray(w_in[:, fm], dtype=np.float32),
                         "wtm": np.ascontiguousarray(w_in[:, tm], dtype=np.float32),
                         "pos": np.ascontiguousarray(inp["positions"][b], dtype=np.int32), "rc": rc})
        a1 = _run(_prog("A1", build_A1, L), maps)
        lamp = np.stack([inp["da_lambda_q1"][i], inp["da_lambda_k1"][i], inp["da_lambda_q2"][i], inp["da_lambda_k2"][i]]).astype(np.float32)
        lc = np.tile(np.array([[lam_init, 1.0 - lam_init]], np.float32), (128, 1))
        a2 = _run(_prog("A2", build_A2, L), [{"fmT": a1[c]["fmT"], "vr": a1[c]["vr"], "lamp": lamp,
                                              "subg": np.ascontiguousarray(inp["da_subln_g"][i], dtype=np.float32), "lc": lc, "ident": ident_bf}
                                             for c in range(NCORES)])
        a3 = _run(_prog("A3", build_A3, L), [{"fmT": a1[c]["fmT"], "vr": a1[c]["vr"], "sg": a1[c]["sg"], "rt": ret_consts(cores[c][1]),
                                              "gng": np.ascontiguousarray(inp["ret_gn_g"][i], dtype=np.float32),
                                              "gnb": np.ascontiguousarray(inp["ret_gn_b"][i], dtype=np.float32), "ident": ident_bf}
                                             for c in range(NCORES)])
        maps = []
        for c, (b, h) in enumerate(cores):
            m = {"fmT": a1[c]["fmT"], "s5c": s5c, "identf": identf}
            m.update(s5_layout(inp, i, h))
            maps.append(m)
        a4 = _run(_prog("A4", build_A4, L), maps)
        maps = []
        for c, (b, h) in enumerate(cores):
            ts = slice(h * NTOK, (h + 1) * NTOK)
            rows = []
            for hp in range(2):
                rows += [a2[2 * b + hp]["ydaT"][0][:, ts], a2[2 * b + hp]["ydaT"][1][:, ts]]
            rows += [a3[2 * b + hp]["yretT"][:, ts] for hp in range(2)]
            rows += [a4[2 * b + hp]["ys5T"][:, ts] for hp in range(2)]
            maps.append({"ycT": np.ascontiguousarray(np.concatenate(rows, axis=0)), "x": xcur[c], "w_out_bf": wbf[i]["w_out"],
                         "s5_glu_w_bf": wbf[i]["s5_glu_w"], "glb": np.ascontiguousarray(inp["s5_glu_b"][i], dtype=np.float32),
                         "ln_g": np.ascontiguousarray(inp["ln1_g"][i], dtype=np.float32),
                         "ln_b": np.ascontiguousarray(inp["ln1_b"][i], dtype=np.float32), "ident": ident_bf})
        p1 = _run(_prog("P1", build_P1, NTOK), maps)
        cwb = conv_layout(inp["ffn_conv_w"][i], inp["ffn_conv_b"][i])
        maps = []
        for c, (b, h) in enumerate(cores):
            ts = slice(h * NTOK, (h + 1) * NTOK)
            ext = np.zeros((1024, NTOK + 2), dtype=ml_dtypes.bfloat16)
            ext[:, 1:NTOK + 1] = p1[c]["x1T"]
            if h == 1:
                ext[:, 0] = p1[c - 1]["x1T"][:, NTOK - 1]
            else:
                ext[:, NTOK + 1] = p1[c + 1]["x1T"][:, 0]
            maps.append({"x1Te": ext, "x1": p1[c]["x1"], "p": np.ascontiguousarray(inp["p"][i][b][ts], dtype=np.float32),
                         "ffn_w_up_bf": wbf[i]["ffn_w_up"], "ffn_w_down_bf": wbf[i]["ffn_w_down"], "ple_w_bf": wbf[i]["ple_w"],
                         "ple_gate_w_bf": wbf[i]["ple_gate_w"], "cwb": cwb,
                         "ln_g": np.ascontiguousarray(inp["ln2_g"][i], dtype=np.float32),
                         "ln_b": np.ascontiguousarray(inp["ln2_b"][i], dtype=np.float32), "ident": ident_bf})
        p2 = _run(_prog("P2", build_P2, NTOK), maps)
        xcur = [p2[c]["x2"] for c in range(NCORES)]
        xT = [np.concatenate([p2[2 * b]["x2T"], p2[2 * b + 1]["x2T"]], axis=1) for b in range(B)]

    out = np.empty((B, L, D_MODEL), np.float32)
    for c, (b, h) in enumerate(cores):
        out[b, h * NTOK:(h + 1) * NTOK] = xcur[c]
    return out


def kernel(**inputs):
    return kernel_fused(**inputs)
```

```python
import math
from contextlib import ExitStack

import numpy as np
import ml_dtypes

import concourse.bass as bass
import concourse.mybir as mybir
from concourse.bass_utils import run_bass_kernel_spmd

F32 = mybir.dt.float32
BF16 = mybir.dt.bfloat16
I32 = mybir.dt.int32
ALU = mybir.AluOpType
AF = mybir.ActivationFunctionType
AX = mybir.AxisListType

D_MODEL = 1024
BATCH = 4
SEQ = 8192
DEPTH = 2
PLE_DIM = 256
D_FF = 2816
LN_EPS = 1e-5
ALPHA = (2 * DEPTH) ** 0.25
COL_DA_Q = 0
COL_DA_K = 512
COL_DA_V = 1024
COL_RET_Q = 1536
COL_RET_K = 1792
COL_RET_V = 2048
COL_RET_G = 2304
COL_S5_U = 2560
NCORES = 8
TWO_PI = 2.0 * math.pi


class Prog:
    ENGS = ("pe", "act", "dve", "pool", "sp")

    def __init__(self, nc):
        self.nc = nc
        self.ops = {e: [] for e in self.ENGS}
        self.cnt = {}
        self.seen = {e: {} for e in self.ENGS}
        self.last_w = {}
        self.readers = {}
        self.n = 0

    def _deps(self, reads, writes):
        raw, other = set(), set()

        def fix(t):
            if isinstance(t[0], tuple):
                t = (t[0], self.cnt[t[0]])
            return t
        for k in reads:
            t = self.last_w.get(k)
            if t is not None:
                raw.add(fix(t))
        for k in writes:
            t = self.last_w.get(k)
            if t is not None:
                other.add(fix(t))
            for t in self.readers.get(k, ()):
                other.add(fix(t))
        return raw, other

    def _commit(self, tok, reads, writes):
        for k in reads:
            self.readers.setdefault(k, []).append(tok)
        for k in writes:
            self.last_w[k] = tok
            self.readers[k] = []

    def _waits(self, eng, deps):
        raw, other = deps
        best = {}
        for src_set, is_raw in ((raw, True), (other, False)):
            for (sk, v) in src_set:
                if sk == eng and eng not in ("act", "dve", "pool"):
                    continue
                if self.seen[eng].get(sk, 0) >= v:
                    continue
                if best.get(sk, 0) < v:
                    best[sk] = v
        for sk, v in best.items():
            self.seen[eng][sk] = v
        return list(best.items())

    def op(self, eng, fn, reads=(), writes=()):
        deps = self._deps(reads, writes)
        waits = self._waits(eng, deps)
        v = self.cnt.get(eng, 0) + 1
        self.cnt[eng] = v
        tok = (eng, v)
        self.ops[eng].append((waits, fn, (eng, 1)))
        self._commit(tok, reads, writes)
        self.n += 1
        return tok

    def dma(self, q, out, in_, chan, reads=(), writes=(), slow=False):
        deps = self._deps(reads, writes)
        waits = self._waits(q, deps)
        sk = ("dma", chan)
        v = self.cnt.get(sk, 0) + 16
        self.cnt[sk] = v
        tok = (sk, v)
        if slow:
            self.ops[q].append((waits, lambda e, o=out, i=in_: e.dma_start(out=o, in_=i, allow_slow_non_contiguous=True), (sk, 16)))
        else:
            self.ops[q].append((waits, lambda e, o=out, i=in_: e.dma_start(out=o, in_=i), (sk, 16)))
        self._commit(tok, reads, writes)
        self.n += 1
        return tok

    def emit(self, es, own_sems=False):
        nc = self.nc
        self.sem_handles = []
        fin = [(sk, v) for sk, v in self.cnt.items() if isinstance(sk, tuple)]
        sems = {}
        for i, sk in enumerate(self.cnt.keys()):
            _UID[0] += 1
            if own_sems:
                sems[sk] = nc.alloc_semaphore(name="s%d_%d" % (i, _UID[0]))
                self.sem_handles.append(sems[sk])
            else:
                sems[sk] = es.enter_context(nc.semaphore("s%d_%d" % (i, _UID[0])))
        block = es.enter_context(nc.Block())

        def replay(name, e):
            for waits, fn, inc in self.ops[name]:
                for sk, v in waits:
                    e.wait_ge(sems[sk], v)
                ins = fn(e)
                ins.then_inc(sems[inc[0]], inc[1])
            if name == "sp":
                for sk, v in fin:
                    e.wait_ge(sems[sk], v)
                for en in ("pe", "act", "dve", "pool"):
                    if self.cnt.get(en, 0):
                        e.wait_ge(sems[en], self.cnt[en])

        @block.sync
        def _(e):
            replay("sp", e)

        @block.tensor
        def _(e):
            replay("pe", e)

        @block.scalar
        def _(e):
            replay("act", e)

        @block.vector
        def _(e):
            replay("dve", e)

        @block.gpsimd
        def _(e):
            replay("pool", e)


_UID = [0]


class Ctx:
    def __init__(self, nc, es):
        self.nc = nc
        self.es = es
        self.P = Prog(nc)
        self._i = 0

    def sb(self, shape, dt, name=None):
        _UID[0] += 1
        return self.es.enter_context(self.nc.sbuf_tensor("%s_%d" % (name or "t", _UID[0]), list(shape), dt))

    def ps(self, shape, dt, name=None):
        _UID[0] += 1
        return self.es.enter_context(self.nc.psum_tensor("%s_%d" % (name or "p", _UID[0]), list(shape), dt))

    def din(self, name, shape, dt):
        return self.nc.dram_tensor(name, list(shape), dt, kind="ExternalInput").ap()

    def dout(self, name, shape, dt):
        return self.nc.dram_tensor(name, list(shape), dt, kind="ExternalOutput").ap()

    def dint(self, name, shape, dt):
        return self.nc.dram_tensor(name, list(shape), dt, kind="Internal").ap()


def bf(a):
    return np.ascontiguousarray(a).astype(ml_dtypes.bfloat16)


def stage_T0(C, x_d, xT_d, ident_bf, ntok, tag="t0"):
    P = C.P
    xin = [C.sb([128, 1024], F32, "xin") for _ in range(2)]
    xbf = [C.sb([128, 1024], BF16, "xbf") for _ in range(2)]
    xT = [C.sb([128, 8, 512], BF16, "xT") for _ in range(2)]
    pst = [C.ps([128, 8, 128], BF16, "pst") for _ in range(2)]
    xT_v = xT_d.rearrange("(c p) t -> p c t", p=128)
    nt = ntok // 128
    for i in range(nt):
        s = i % 2
        g = (i // 4) % 2
        P.dma("sp", xin[s][:], x_d[i * 128:(i + 1) * 128, :], chan=(tag, "xin", s),
              writes=[(tag, "xin", s)])
        P.op("act", lambda e, s=s: e.copy(out=xbf[s][:], in_=xin[s][:]),
             reads=[(tag, "xin", s)], writes=[(tag, "xbf", s)])
        for c in range(8):
            P.op("pe", lambda e, s=s, c=c: e.transpose(out=pst[s][:, c, :], in_=xbf[s][:, c * 128:(c + 1) * 128],
                                                        identity=ident_bf[:]),
                 reads=[(tag, "xbf", s), "ident"], writes=[(tag, "pst", s, c)])
        P.op("dve", lambda e, s=s, g=g, i=i: e.tensor_copy(out=xT[g][:, :, (i % 4) * 128:(i % 4 + 1) * 128], in_=pst[s][:]),
             reads=[(tag, "pst", s, c) for c in range(8)], writes=[(tag, "xT", g, i % 4)])
        if i % 4 == 3:
            t0 = (i // 4) * 512
            P.dma("pool", xT_v[:, :, t0:t0 + 512], xT[g][:], chan=(tag, "xTo", g),
                  reads=[(tag, "xT", g, j) for j in range(4)], writes=[(tag, "xTd", i // 4)])


def build_T0(ntok):
    nc = bass.Bass("TRN2", target_bir_lowering=False)
    with ExitStack() as es:
        C = Ctx(nc, es)
        x_d = C.din("x", [ntok, 1024], F32)
        id_d = C.din("ident", [128, 128], BF16)
        xT_d = C.dout("xT", [1024, ntok], BF16)
        ident = C.sb([128, 128], BF16, "ident")
        C.P.dma("sp", ident[:], id_d[:, :], chan="ident", writes=["ident"])
        stage_T0(C, x_d, xT_d, ident, ntok)
        C.P.emit(es)
    return nc


NFM = 13
ROPED = 6


def a1_columns(h):
    def swap64(cols):
        cols = np.asarray(cols).reshape(-1, 64)
        return np.concatenate([cols[:, 32:], cols[:, :32]], axis=1).reshape(-1)
    tiles = []
    for base in (COL_DA_Q, COL_DA_K):
        for hh in range(2):
            tiles.append(base + (2 * h + hh) * 128 + np.arange(128))
    tiles.append(COL_RET_Q + 2 * h * 64 + np.arange(128))
    tiles.append(COL_RET_K + 2 * h * 64 + np.arange(128))
    sw = [swap64(t) for t in tiles]
    u = COL_S5_U + 128 * h + np.arange(128)
    fm = np.concatenate(tiles + sw + [u])
    tm = np.concatenate([COL_DA_V + 2 * h * 128 + np.arange(256),
                         COL_RET_V + 2 * h * 64 + np.arange(128),
                         COL_RET_G + 2 * h * 64 + np.arange(128)])
    return fm, tm


def rope_consts():
    inv = 10000.0 ** (-np.arange(0, 64, 2, dtype=np.float64) / 64.0)
    p = np.arange(128)
    invf = inv[p % 32].astype(np.float32).reshape(128, 1)
    sgn = np.where((p % 64) < 32, -1.0, 1.0).astype(np.float32).reshape(128, 1)
    return invf, sgn


CW1 = float(np.float32(6.28125))
CW2 = float(np.float32(TWO_PI - 6.28125))


def sincos(P, eng, tag, ang, ang_key, sarg, carg, tmp_i, tmp_f):
    ki, kf, ks, kc = (tag, "rr_i"), (tag, "rr_f"), (tag, "sarg"), (tag, "carg")
    P.op(eng, lambda e: e.tensor_scalar(out=tmp_i[:], in0=ang[:], scalar1=1.0 / TWO_PI, scalar2=None, op0=ALU.mult),
         reads=[ang_key], writes=[ki])
    P.op(eng, lambda e: e.tensor_copy(out=tmp_f[:], in_=tmp_i[:]), reads=[ki], writes=[kf])
    P.op(eng, lambda e: e.scalar_tensor_tensor(out=sarg[:], in0=tmp_f[:], scalar=-CW1, in1=ang[:], op0=ALU.mult, op1=ALU.add),
         reads=[kf, ang_key], writes=[ks])
    P.op(eng, lambda e: e.scalar_tensor_tensor(out=sarg[:], in0=tmp_f[:], scalar=-CW2, in1=sarg[:], op0=ALU.mult, op1=ALU.add),
         reads=[kf, ks], writes=[ks])

    def wrap(t, key):
        P.op(eng, lambda e: e.tensor_scalar(out=tmp_f[:], in0=t[:], scalar1=math.pi, scalar2=-TWO_PI, op0=ALU.is_gt, op1=ALU.mult),
             reads=[key], writes=[kf])
        P.op(eng, lambda e: e.tensor_tensor(out=t[:], in0=t[:], in1=tmp_f[:], op=ALU.add), reads=[key, kf], writes=[key])
        P.op(eng, lambda e: e.tensor_scalar(out=tmp_f[:], in0=t[:], scalar1=-math.pi, scalar2=TWO_PI, op0=ALU.is_lt, op1=ALU.mult),
             reads=[key], writes=[kf])
        P.op(eng, lambda e: e.tensor_tensor(out=t[:], in0=t[:], in1=tmp_f[:], op=ALU.add), reads=[key, kf], writes=[key])
    wrap(sarg, ks)
    P.op(eng, lambda e: e.tensor_scalar(out=carg[:], in0=sarg[:], scalar1=0.5 * math.pi, scalar2=None, op0=ALU.add),
         reads=[ks], writes=[kc])
    wrap(carg, kc)


def stage_A1(C, xT_d, wfm_d, wtm_d, pos_d, rc_d, fmT_d, vr_d, sg_d, L, tag="a1"):
    P = C.P
    NW = NFM * 128
    wfm = C.sb([128, 8, NW], BF16, "wfm")
    wtm = C.sb([128, 8, 512], BF16, "wtm")
    stg = [C.sb([128, NW], F32, "wstg") for _ in range(2)]
    rc = C.sb([128, 4], F32, "rc")
    P.dma("sp", rc[:], rc_d[:, :], chan=(tag, "rc"), writes=[(tag, "rc")])
    wfm_v = wfm_d.rearrange("(c p) n -> p c n", p=128)
    wtm_v = wtm_d.rearrange("(c p) n -> p c n", p=128)
    for c in range(8):
        s = c % 2
        P.dma("sp", stg[s][:], wfm_v[:, c, :], chan=(tag, "wstg", s), writes=[(tag, "wstg", s)])
        P.op("pool", lambda e, s=s, c=c: e.tensor_copy(out=wfm[:, c, :], in_=stg[s][:]),
             reads=[(tag, "wstg", s)], writes=[(tag, "wfm", c)])
    for c in range(8):
        s = c % 2
        P.dma("sp", stg[s][:, 0:512], wtm_v[:, c, :], chan=(tag, "wstg", s), writes=[(tag, "wstg", s)])
        P.op("pool", lambda e, s=s, c=c: e.tensor_copy(out=wtm[:, c, :], in_=stg[s][:, 0:512]),
             reads=[(tag, "wstg", s)], writes=[(tag, "wtm", c)])
    wfm_k = [(tag, "wfm", c) for c in range(8)]
    wtm_k = [(tag, "wtm", c) for c in range(8)]

    xT = [C.sb([128, 8, 512], BF16, "xT") for _ in range(2)]
    posi = [C.sb([128, 512], I32, "posi") for _ in range(2)]
    posf = C.sb([128, 512], F32, "posf")
    a0 = C.sb([128, 512], F32, "a0")
    sarg = C.sb([128, 512], F32, "sarg")
    carg = C.sb([128, 512], F32, "carg")
    rr_i = C.sb([128, 512], I32, "rr_i")
    rr_f = C.sb([128, 512], F32, "rr_f")
    cosT = [C.sb([128, 512], F32, "cosT") for _ in range(2)]
    sinT = [C.sb([128, 512], F32, "sinT") for _ in range(2)]
    t1 = [C.sb([128, 512], F32, "t1") for _ in range(2)]
    t2 = [C.sb([128, 512], F32, "t2") for _ in range(2)]
    fmo = [C.sb([128, 7, 512], BF16, "fmo") for _ in range(2)]
    tmo = [C.sb([128, 4, 384], BF16, "tmo") for _ in range(2)]
    sgo = [C.sb([128, 4, 128], F32, "sgo") for _ in range(2)]
    psA = [C.ps([128, 512], F32, "psA") for _ in range(2)]
    psB = [C.ps([128, 512], F32, "psB") for _ in range(2)]
    psC = [C.ps([128, 512], F32, "psC") for _ in range(3)]
    xT_v = xT_d.rearrange("(c p) t -> p c t", p=128)
    fmT_v = fmT_d.rearrange("r p t -> p r t")
    nt = L // 512
    pc = 0
    cc = 0
    for it in range(nt):
        s = it % 2
        tk0 = it * 512
        P.dma("sp", xT[s][:], xT_v[:, :, tk0:tk0 + 512], chan=(tag, "xT", s), writes=[(tag, "xT", s)])
        P.dma("sp", posi[s][:], pos_d[tk0:tk0 + 512].partition_broadcast(128), chan=(tag, "pos", s),
              writes=[(tag, "pos", s)])
        P.op("dve", lambda e, s=s: e.tensor_copy(out=posf[:], in_=posi[s][:]),
             reads=[(tag, "pos", s)], writes=[(tag, "posf")])
        P.op("dve", lambda e: e.tensor_scalar(out=a0[:], in0=posf[:], scalar1=rc[:, 0:1], scalar2=None, op0=ALU.mult),
             reads=[(tag, "posf"), (tag, "rc")], writes=[(tag, "a0")])
        sincos(P, "dve", tag, a0, (tag, "a0"), sarg, carg, rr_i, rr_f)
        P.op("act", lambda e, s=s: e.activation(out=sinT[s][:], in_=sarg[:], func=AF.Sin, scale=rc[:, 1:2]),
             reads=[(tag, "sarg"), (tag, "rc")], writes=[(tag, "sinT", s)])
        P.op("act", lambda e, s=s: e.activation(out=cosT[s][:], in_=carg[:], func=AF.Sin),
             reads=[(tag, "carg")], writes=[(tag, "cosT", s)])
        for r in range(ROPED):
            b = pc % 2
            pc += 1
            for c in range(8):
                P.op("pe", lambda e, b=b, r=r, c=c, s=s: e.matmul(psA[b][:], lhsT=wfm[:, c, r * 128:(r + 1) * 128],
                                                                 rhs=xT[s][:, c, :], start=(c == 0), stop=(c == 7)),
                     reads=[(tag, "xT", s), (tag, "wfm", c)], writes=[(tag, "psA", b)])
            for c in range(8):
                P.op("pe", lambda e, b=b, r=r, c=c, s=s: e.matmul(psB[b][:], lhsT=wfm[:, c, (ROPED + r) * 128:(ROPED + r + 1) * 128],
                                                                 rhs=xT[s][:, c, :], start=(c == 0), stop=(c == 7)),
                     reads=[(tag, "xT", s), (tag, "wfm", c)], writes=[(tag, "psB", b)])
            P.op("dve", lambda e, b=b, s=s: e.tensor_tensor(out=t1[b][:], in0=psA[b][:], in1=cosT[s][:], op=ALU.mult),
                 reads=[(tag, "psA", b), (tag, "cosT", s)], writes=[(tag, "t1", b)])
            P.op("dve", lambda e, b=b, s=s: e.tensor_tensor(out=t2[b][:], in0=psB[b][:], in1=sinT[s][:], op=ALU.mult),
                 reads=[(tag, "psB", b), (tag, "sinT", s)], writes=[(tag, "t2", b)])
            P.op("pool", lambda e, b=b, s=s, r=r: e.tensor_tensor(out=fmo[s][:, r, :], in0=t1[b][:], in1=t2[b][:], op=ALU.add),
                 reads=[(tag, "t1", b), (tag, "t2", b)], writes=[(tag, "fmo", s, r)])
        b = cc % 3
        cc += 1
        for c in range(8):
            P.op("pe", lambda e, b=b, c=c, s=s: e.matmul(psC[b][:], lhsT=wfm[:, c, 12 * 128:13 * 128],
                                                        rhs=xT[s][:, c, :], start=(c == 0), stop=(c == 7)),
                 reads=[(tag, "xT", s), (tag, "wfm", c)], writes=[(tag, "psC", b)])
        P.op("act", lambda e, b=b, s=s: e.copy(out=fmo[s][:, 6, :], in_=psC[b][:]),
             reads=[(tag, "psC", b)], writes=[(tag, "fmo", s, 6)])
        P.dma("pool", fmT_v[:, :, tk0:tk0 + 512], fmo[s][:], chan=(tag, "fmo", s),
              reads=[(tag, "fmo", s, r) for r in range(7)], writes=[(tag, "fmT", it)])
        for j in range(4):
            b = cc % 3
            cc += 1
            for c in range(8):
                P.op("pe", lambda e, b=b, c=c, s=s, j=j: e.matmul(psC[b][:], lhsT=xT[s][:, c, j * 128:(j + 1) * 128],
                                                                 rhs=wtm[:, c, :], start=(c == 0), stop=(c == 7)),
                     reads=[(tag, "xT", s), (tag, "wtm", c)], writes=[(tag, "psC", b)])
            P.op("act", lambda e, b=b, s=s, j=j: e.copy(out=tmo[s][:, j, :], in_=psC[b][:, 0:384]),
                 reads=[(tag, "psC", b)], writes=[(tag, "tmo", s, j)])
            P.op("act", lambda e, b=b, s=s, j=j: e.activation(out=sgo[s][:, j, :], in_=psC[b][:, 384:512], func=AF.Silu),
                 reads=[(tag, "psC", b)], writes=[(tag, "sgo", s, j)])
        P.dma("pool", vr_d[tk0:tk0 + 512, :].rearrange("(j p) c -> p j c", p=128), tmo[s][:], chan=(tag, "tmo", s),
              reads=[(tag, "tmo", s, j) for j in range(4)], writes=[(tag, "vr", it)])
        P.dma("pool", sg_d[tk0:tk0 + 512, :].rearrange("(j p) c -> p j c", p=128), sgo[s][:], chan=(tag, "sgo", s),
              reads=[(tag, "sgo", s, j) for j in range(4)], writes=[(tag, "sg", it)])


def build_A1(L):
    nc = bass.Bass("TRN2", target_bir_lowering=False)
    with ExitStack() as es:
        C = Ctx(nc, es)
        xT_d = C.din("xT", [1024, L], BF16)
        wfm_d = C.din("wfm", [1024, NFM * 128], F32)
        wtm_d = C.din("wtm", [1024, 512], F32)
        pos_d = C.din("pos", [L], I32)
        rc_d = C.din("rc", [128, 4], F32)
        fmT_d = C.dout("fmT", [7, 128, L], BF16)
        vr_d = C.dout("vr", [L, 384], BF16)
        sg_d = C.dout("sg", [L, 128], F32)
        stage_A1(C, xT_d, wfm_d, wtm_d, pos_d, rc_d, fmT_d, vr_d, sg_d, L)
        C.P.emit(es)
    return nc


def rc_const():
    invf, sgn = rope_consts()
    return np.concatenate([invf, sgn, -math.pi * sgn, np.full((128, 1), -math.pi, np.float32)], axis=1).astype(np.float32)


def stage_A2(C, fmT_d, vr_d, lamp_d, subg_d, lc_d, ydaT_d, ident, L, tag="a2", dbg=None):
    P = C.P
    NKB = L // 128
    NQB = L // 512
    lamp = C.sb([128, 4, 64], F32, "lamp")
    lc = C.sb([128, 2], F32, "lc")
    gcol = C.sb([128, 1], F32, "gcol")
    lprod = C.sb([128, 2, 64], F32, "lprod")
    lsum = C.sb([128, 2], F32, "lsum")
    lexp = C.sb([128, 2], F32, "lexp")
    neglam = C.sb([128, 1], F32, "neglam")
    epsb = C.sb([128, 1], F32, "epsb")
    ones = C.sb([128, 128], F32, "ones")
    P.dma("sp", lamp[:], lamp_d.rearrange("a d -> (a d)").partition_broadcast(128).rearrange("p (a d) -> p a d", a=4),
          chan=(tag, "c0"), writes=[(tag, "lamp")])
    P.dma("sp", lc[:], lc_d[:, :], chan=(tag, "c0"), writes=[(tag, "lc")])
    P.dma("sp", gcol[:], subg_d.rearrange("(p o) -> p o", o=1), chan=(tag, "c0"), writes=[(tag, "gcol")])
    P.op("dve", lambda e: e.tensor_tensor(out=lprod[:, 0, :], in0=lamp[:, 0, :], in1=lamp[:, 1, :], op=ALU.mult),
         reads=[(tag, "lamp")], writes=[(tag, "lprod")])
    P.op("dve", lambda e: e.tensor_tensor(out=lprod[:, 1, :], in0=lamp[:, 2, :], in1=lamp[:, 3, :], op=ALU.mult),
         reads=[(tag, "lamp")], writes=[(tag, "lprod")])
    P.op("dve", lambda e: e.tensor_reduce(out=lsum[:], in_=lprod[:], axis=AX.X, op=ALU.add),
         reads=[(tag, "lprod")], writes=[(tag, "lsum")])
    P.op("act", lambda e: e.activation(out=lexp[:], in_=lsum[:], func=AF.Exp), reads=[(tag, "lsum")], writes=[(tag, "lexp")])
    P.op("dve", lambda e: e.tensor_tensor(out=neglam[:], in0=lexp[:, 1:2], in1=lexp[:, 0:1], op=ALU.subtract),
         reads=[(tag, "lexp")], writes=[(tag, "neglam")])
    P.op("dve", lambda e: e.tensor_tensor(out=neglam[:], in0=neglam[:], in1=lc[:, 0:1], op=ALU.subtract),
         reads=[(tag, "neglam"), (tag, "lc")], writes=[(tag, "neglam")])
    P.op("dve", lambda e: e.tensor_tensor(out=gcol[:], in0=gcol[:], in1=lc[:, 1:2], op=ALU.mult),
         reads=[(tag, "gcol"), (tag, "lc")], writes=[(tag, "gcol")])
    P.op("dve", lambda e: e.memset(epsb[:], 1e-6), writes=[(tag, "epsb")])
    P.op("pool", lambda e: e.memset(ones[:], 1.0), writes=[(tag, "ones")])

    qT = [C.sb([128, L], BF16, "qT") for _ in range(2)]
    kT = [[C.sb([128, L], BF16, "kT") for _ in range(2)] for _ in range(2)]
    va = [C.sb([128, NKB, 128], BF16, "va") for _ in range(2)]
    yT = [C.sb([128, L], BF16, "yT") for _ in range(2)]
    NPT = 4
    PT = [C.sb([128, 512], BF16, "PT") for _ in range(NPT)]
    accS = [[C.sb([128, 512], F32, "accS") for _ in range(2)] for _ in range(2)]
    oacc = C.sb([128, 2, 512], F32, "oacc")
    rinv = C.sb([128, 2, 512], F32, "rinv")
    o = C.sb([128, 512], F32, "o")
    t2 = C.sb([128, 512], F32, "t2")
    sq = C.sb([128, 512], F32, "sq")
    rs = C.sb([128, 512], F32, "rs")
    S = [C.ps([128, 512], F32, "S") for _ in range(3)]
    accO = [C.ps([128, 512], F32, "accO") for _ in range(2)]
    Rps = [C.ps([128, 512], F32, "Rps") for _ in range(2)]

    for h in range(2):
        P.dma("sp", qT[h][:], fmT_d[h, :, :], chan=(tag, "ld", h), writes=[(tag, "qT", h)])
        for m in range(2):
            P.op("pool", lambda e, h=h, m=m: e.memset(kT[h][m][(1 - m) * 64:(2 - m) * 64, :], 0.0), writes=[(tag, "kTz", h, m)])
            P.dma("sp", kT[h][m][m * 64:(m + 1) * 64, :], fmT_d[2 + h, m * 64:(m + 1) * 64, :], chan=(tag, "ld", h), writes=[(tag, "kT", h)])
        P.dma("sp", va[h][:], vr_d[:, h * 128:(h + 1) * 128].rearrange("(kb p) c -> p kb c", p=128),
              chan=(tag, "ld", h), writes=[(tag, "va", h)])
    step = 0
    blk = 0
    for h in range(2):
        for qb in range(NQB):
            q0 = qb * 512
            nst = NKB * 2
            par = blk % 2
            blk += 1

            def qk(i, h=h, q0=q0, step=step):
                kb, m = i // 2, i % 2
                sl = (step + i) % 3
                P.op("pe", lambda e: e.matmul(S[sl][:], lhsT=kT[h][m][:, kb * 128:(kb + 1) * 128],
                                               rhs=qT[h][:, q0:q0 + 512], start=True, stop=True),
                     reads=[(tag, "kT", h), (tag, "kTz", h, 0), (tag, "kTz", h, 1), (tag, "qT", h)], writes=[(tag, "S", sl)])

            def ex(i, step=step):
                sl = (step + i) % 3
                pl = (step + i) % NPT
                P.op("act", lambda e: e.activation(out=PT[pl][:], in_=S[sl][:], func=AF.Exp, scale=0.125),
                     writes=[(tag, "PT", pl), (tag, "S", sl)])

            def av(i, h=h, step=step, par=par):
                kb, m = i // 2, i % 2
                pl = (step + i) % NPT
                P.op("pe", lambda e: e.matmul(accO[m][:], lhsT=va[h][:, kb, :], rhs=PT[pl][:], start=(kb == 0), stop=(kb == NKB - 1)),
                     reads=[(tag, "PT", pl), (tag, "va", h)], writes=[(tag, "accO", m)])
                if kb == 0:
                    P.op("dve", lambda e: e.tensor_copy(out=accS[par][m][:], in_=PT[pl][:]), reads=[(tag, "PT", pl)], writes=[(tag, "accS", par, m)])
                else:
                    P.op("dve", lambda e: e.tensor_tensor(out=accS[par][m][:], in0=accS[par][m][:], in1=PT[pl][:], op=ALU.add),
                         reads=[(tag, "PT", pl), (tag, "accS", par, m)], writes=[(tag, "accS", par, m)])

            qk(0); ex(0)
            qk(1); ex(1)
            for i in range(nst):
                if i + 2 < nst:
                    qk(i + 2); ex(i + 2)
                av(i)
            step += nst

            def post(h=h, q0=q0, par=par):
                for m in range(2):
                    P.op("pe", lambda e, m=m: e.matmul(Rps[m][:], lhsT=ones[:], rhs=accS[par][m][:], start=True, stop=True),
                         reads=[(tag, "ones"), (tag, "accS", par, m)], writes=[(tag, "Rps", m)])
                P.op("act", lambda e: e.copy(out=oacc[:, 0, :], in_=accO[0][:]), writes=[(tag, "oacc", 0), (tag, "accO", 0)])
                P.op("dve", lambda e: e.tensor_copy(out=oacc[:, 1, :], in_=accO[1][:]), writes=[(tag, "oacc", 1), (tag, "accO", 1)])
                for m in range(2):
                    P.op("dve", lambda e, m=m: e.reciprocal(out=rinv[:, m, :], in_=Rps[m][:]), writes=[(tag, "rinv", m), (tag, "Rps", m)])
                P.op("dve", lambda e: e.tensor_tensor(out=o[:], in0=oacc[:, 0, :], in1=rinv[:, 0, :], op=ALU.mult),
                     reads=[(tag, "oacc", 0), (tag, "rinv", 0)], writes=[(tag, "o")])
                P.op("pool", lambda e: e.tensor_tensor(out=t2[:], in0=oacc[:, 1, :], in1=rinv[:, 1, :], op=ALU.mult),
                     reads=[(tag, "oacc", 1), (tag, "rinv", 1)], writes=[(tag, "t2")])
                P.op("dve", lambda e: e.scalar_tensor_tensor(out=o[:], in0=t2[:], scalar=neglam[:, 0:1], in1=o[:], op0=ALU.mult, op1=ALU.add),
                     reads=[(tag, "t2"), (tag, "o"), (tag, "neglam")], writes=[(tag, "o")])
                P.op("act", lambda e: e.activation(out=sq[:], in_=o[:], func=AF.Square), reads=[(tag, "o")], writes=[(tag, "sq")])
                P.op("pe", lambda e: e.matmul(Rps[0][:], lhsT=ones[:], rhs=sq[:], start=True, stop=True),
                     reads=[(tag, "ones"), (tag, "sq")], writes=[(tag, "Rps", 0)])
                P.op("act", lambda e: e.activation(out=rs[:], in_=Rps[0][:], func=AF.Sqrt, bias=epsb[:, 0:1], scale=1.0 / 128.0),
                     reads=[(tag, "epsb")], writes=[(tag, "rs"), (tag, "Rps", 0)])
                P.op("dve", lambda e: e.reciprocal(out=rs[:], in_=rs[:]), reads=[(tag, "rs")], writes=[(tag, "rs")])
                P.op("dve", lambda e: e.scalar_tensor_tensor(out=yT[h][:, q0:q0 + 512], in0=o[:], scalar=gcol[:, 0:1], in1=rs[:],
                                                              op0=ALU.mult, op1=ALU.mult),
                     reads=[(tag, "o"), (tag, "rs"), (tag, "gcol")], writes=[(tag, "yT", h)])
            post()
        P.dma("sp", ydaT_d[h, :, :], yT[h][:], chan=(tag, "yo", h), reads=[(tag, "yT", h)], writes=[(tag, "ydaT", h)])


def build_A2(L, debug=False):
    nc = bass.Bass("TRN2", target_bir_lowering=False)
    with ExitStack() as es:
        C = Ctx(nc, es)
        fmT_d = C.din("fmT", [7, 128, L], BF16)
        vr_d = C.din("vr", [L, 384], BF16)
        lamp_d = C.din("lamp", [4, 64], F32)
        subg_d = C.din("subg", [128], F32)
        lc_d = C.din("lc", [128, 2], F32)
        id_d = C.din("ident", [128, 128], BF16)
        ydaT_d = C.dout("ydaT", [2, 128, L], BF16)
        ident = C.sb([128, 128], BF16, "ident")
        C.P.dma("sp", ident[:], id_d[:, :], chan="ident", writes=["ident"])
        dbg = None
        stage_A2(C, fmT_d, vr_d, lamp_d, subg_d, lc_d, ydaT_d, ident, L, dbg=dbg)
        C.P.emit(es)
    return nc


RT_W = 256 + 128 + 128 + 4 + 1


def ret_consts(h):
    idx = np.arange(128, dtype=np.float64)
    out = np.zeros((128, RT_W), np.float64)
    for hh in range(2):
        lg = math.log(1.0 - 2.0 ** (-5.0 - (2 * h + hh)))
        out[:, hh * 128:(hh + 1) * 128] = np.exp(lg * np.abs(idx[None, :] - idx[:, None])) / 8.0
        out[:, 256 + hh * 64:256 + (hh + 1) * 64] = (np.exp(lg * (127 - idx)) / 8.0)[:, None]
        out[:, 384 + hh * 64:384 + (hh + 1) * 64] = (np.exp(lg * idx) / 8.0)[:, None]
        out[:, 512 + 2 * hh] = np.exp(lg * (idx + 1))
        out[:, 512 + 2 * hh + 1] = np.exp(lg * (128 - idx))
        out[hh * 64:(hh + 1) * 64, 516] = math.exp(lg * 128)
    return out.astype(np.float32)


def stage_A3(C, fmT_d, vr_d, sg_d, rt_d, gng_d, gnb_d, yretT_d, ident, L, tag="a3"):
    P = C.P
    NCH = L // 128
    rt = C.sb([128, RT_W], F32, "rt")
    gng = C.sb([128, 2, 64], F32, "gng")
    gnb = C.sb([128, 2, 64], F32, "gnb")
    epsb = C.sb([128, 1], F32, "epsb")
    P.dma("sp", rt[:], rt_d[:, :], chan=(tag, "c0"), writes=[(tag, "rt")])
    for hh in range(2):
        P.dma("sp", gng[:, hh, :], gng_d.partition_broadcast(128), chan=(tag, "c0"), writes=[(tag, "gng")])
        P.dma("sp", gnb[:, hh, :], gnb_d.partition_broadcast(128), chan=(tag, "c0"), writes=[(tag, "gnb")])
    P.op("dve", lambda e: e.memset(epsb[:], LN_EPS), writes=[(tag, "epsb")])
    Dm = rt[:, 0:256].rearrange("p (a n) -> p a n", a=2)
    decF = rt[:, 256:384]
    decB = rt[:, 384:512]
    gC = rt[:, 516:517]

    rqT = C.sb([128, L], BF16, "rqT")
    rkT = C.sb([128, L], BF16, "rkT")
    rv = C.sb([128, NCH, 128], BF16, "rv")
    sg = C.sb([128, NCH, 128], F32, "sg")
    yT = C.sb([128, L], BF16, "yT")
    Gs = C.sb([128, NCH, 64], BF16, "Gs")
    G = C.sb([128, 64], F32, "G")
    F = C.sb([128, 64], F32, "F")
    Fbf = [C.sb([128, 64], BF16, "Fbf") for _ in range(2)]
    kd = [C.sb([128, 128], BF16, "kd") for _ in range(2)]
    Sm = [C.sb([128, 2, 128], BF16, "Sm") for _ in range(2)]
    t = [C.sb([128, 2, 64], F32, "t") for _ in range(2)]
    st = C.sb([128, 2, 6], F32, "st")
    mv = C.sb([128, 2, 2], F32, "mv")
    rstd = C.sb([128, 2], F32, "rstd")
    ybf = [C.sb([128, 128], BF16, "ybf") for _ in range(2)]
    pk = [C.ps([128, 128], BF16, "pk") for _ in range(2)]
    pkv = [C.ps([128, 128], F32, "pkv") for _ in range(2)]
    pS = [C.ps([128, 128], F32, "pS") for _ in range(2)]
    po = [C.ps([128, 3, 64], F32, "po") for _ in range(2)]
    P.dma("sp", rqT[:], fmT_d[4, :, :], chan=(tag, "ld"), writes=[(tag, "rqT")])
    P.dma("sp", rkT[:], fmT_d[5, :, :], chan=(tag, "ld"), writes=[(tag, "rkT")])
    P.dma("sp", rv[:], vr_d[:, 256:384].rearrange("(c p) n -> p c n", p=128), chan=(tag, "ld"), writes=[(tag, "rv")])
    P.dma("sp", sg[:], sg_d.rearrange("(c p) n -> p c n", p=128), chan=(tag, "ld"), writes=[(tag, "sg")])
    P.op("dve", lambda e: e.memset(G[:], 0.0), writes=[(tag, "G")])
    P.op("dve", lambda e: e.memset(F[:], 0.0), writes=[(tag, "F")])

    def kv_step(c, dec, state, skey, i):
        s = i % 2
        P.op("pe", lambda e: e.transpose(out=pk[s][:], in_=rkT[:, c * 128:(c + 1) * 128], identity=ident[:]),
             reads=[(tag, "rkT"), "ident"], writes=[(tag, "pk", s)])
        P.op("dve", lambda e: e.tensor_tensor(out=kd[s][:], in0=pk[s][:], in1=dec, op=ALU.mult),
             reads=[(tag, "rt")], writes=[(tag, "kd", s), (tag, "pk", s)])
        P.op("pe", lambda e: e.matmul(pkv[s][:], lhsT=kd[s][:], rhs=rv[:, c, :], start=True, stop=True),
             reads=[(tag, "kd", s), (tag, "rv")], writes=[(tag, "pkv", s)])
        for hh in range(2):
            hs = slice(hh * 64, (hh + 1) * 64)
            P.op("dve", lambda e, hs=hs: e.scalar_tensor_tensor(out=state[hs, :], in0=state[hs, :], scalar=gC[hs, :], in1=pkv[s][hs, hs],
                                                                 op0=ALU.mult, op1=ALU.add),
                 reads=[skey, (tag, "rt")], writes=[skey, (tag, "pkv", s)])

    it = 0
    for c in range(NCH - 1, -1, -1):
        P.op("act", lambda e, c=c: e.copy(out=Gs[:, c, :], in_=G[:]), reads=[(tag, "G")], writes=[(tag, "Gs", c)])
        if c > 0:
            kv_step(c, decB, G, (tag, "G"), it)
            it += 1
    def chunk2(c, it):
        s = c % 2
        cs = slice(c * 128, (c + 1) * 128)
        P.op("act", lambda e, s=s: e.copy(out=Fbf[s][:], in_=F[:]), reads=[(tag, "F")], writes=[(tag, "Fbf", s)])
        for hh in range(2):
            hs = slice(hh * 64, (hh + 1) * 64)
            P.op("pe", lambda e, hh=hh, hs=hs: e.matmul(pS[hh][:], lhsT=rkT[hs, cs], rhs=rqT[hs, cs], start=True, stop=True),
                 reads=[(tag, "rkT"), (tag, "rqT")], writes=[(tag, "pS", hh)])
            P.op("dve", lambda e, s=s, hh=hh: e.tensor_tensor(out=Sm[s][:, hh, :], in0=pS[hh][:], in1=Dm[:, hh, :], op=ALU.mult),
                 reads=[(tag, "rt")], writes=[(tag, "Sm", s, hh), (tag, "pS", hh)])
        for hh in range(2):
            hs = slice(hh * 64, (hh + 1) * 64)
            P.op("pe", lambda e, hh=hh, hs=hs: e.matmul(po[hh][:, 0, :], lhsT=Sm[s][:, hh, :], rhs=rv[:, c, hs], start=True, stop=False),
                 reads=[(tag, "Sm", s, hh), (tag, "rv")], writes=[(tag, "po", hh)])
            P.op("pe", lambda e, hh=hh, hs=hs: e.matmul(po[hh][:, 1, :], lhsT=rqT[hs, cs], rhs=Fbf[s][hs, :], start=False, stop=False),
                 reads=[(tag, "rqT"), (tag, "Fbf", s)], writes=[(tag, "po", hh)])
            P.op("pe", lambda e, hh=hh, hs=hs, c=c: e.matmul(po[hh][:, 2, :], lhsT=rqT[hs, cs], rhs=Gs[hs, c, :], start=False, stop=True),
                 reads=[(tag, "rqT"), (tag, "Gs", c)], writes=[(tag, "po", hh)])
        if c < NCH - 1:
            kv_step(c, decF, F, (tag, "F"), it)
        for hh in range(2):
            P.op("dve", lambda e, s=s, hh=hh: e.tensor_copy(out=t[s][:, hh, :], in_=po[hh][:, 0, :]),
                 writes=[(tag, "t", s), (tag, "po", hh)])
            for j in (1, 2):
                P.op("dve", lambda e, s=s, hh=hh, j=j: e.scalar_tensor_tensor(
                    out=t[s][:, hh, :], in0=po[hh][:, j, :], scalar=rt[:, 512 + 2 * hh + j - 1:512 + 2 * hh + j],
                    in1=t[s][:, hh, :], op0=ALU.mult, op1=ALU.add),
                    reads=[(tag, "t", s), (tag, "rt")], writes=[(tag, "t", s), (tag, "po", hh)])
        for hh in range(2):
            P.op("dve", lambda e, s=s, hh=hh: e.bn_stats(out=st[:, hh, :], in_=t[s][:, hh, :]), reads=[(tag, "t", s)], writes=[(tag, "st")])
            P.op("dve", lambda e, hh=hh: e.bn_aggr(out=mv[:, hh, :], in_=st[:, hh, :]), reads=[(tag, "st")], writes=[(tag, "mv")])
        P.op("act", lambda e: e.activation(out=rstd[:], in_=mv[:, :, 1], func=AF.Sqrt, bias=epsb[:, 0:1], scale=1.0),
             reads=[(tag, "mv"), (tag, "epsb")], writes=[(tag, "rstd")])
        P.op("dve", lambda e: e.reciprocal(out=rstd[:], in_=rstd[:]), reads=[(tag, "rstd")], writes=[(tag, "rstd")])
        for hh in range(2):
            P.op("dve", lambda e, s=s, hh=hh: e.tensor_scalar(out=t[s][:, hh, :], in0=t[s][:, hh, :], scalar1=mv[:, hh, 0:1],
                                                              scalar2=rstd[:, hh:hh + 1], op0=ALU.subtract, op1=ALU.mult),
                 reads=[(tag, "t", s), (tag, "mv"), (tag, "rstd")], writes=[(tag, "t", s)])
        P.op("pool", lambda e, s=s: e.tensor_tensor(out=t[s][:], in0=t[s][:], in1=gng[:], op=ALU.mult),
             reads=[(tag, "t", s), (tag, "gng")], writes=[(tag, "t", s)])
        P.op("pool", lambda e, s=s: e.tensor_tensor(out=t[s][:], in0=t[s][:], in1=gnb[:], op=ALU.add),
             reads=[(tag, "t", s), (tag, "gnb")], writes=[(tag, "t", s)])
        P.op("pool", lambda e, s=s, c=c: e.tensor_tensor(out=ybf[s][:], in0=t[s][:].rearrange("p a b -> p (a b)"), in1=sg[:, c, :], op=ALU.mult),
             reads=[(tag, "t", s), (tag, "sg")], writes=[(tag, "ybf", s)])
        P.op("pe", lambda e, s=s: e.transpose(out=pk[s][:], in_=ybf[s][:], identity=ident[:]),
             reads=[(tag, "ybf", s), "ident"], writes=[(tag, "pk", s)])
        P.op("act", lambda e, s=s, cs=cs: e.copy(out=yT[:, cs], in_=pk[s][:]), writes=[(tag, "yT"), (tag, "pk", s)])
    for c in range(NCH):
        chunk2(c, it)
        it += 1
    P.dma("pool", yretT_d[:, :], yT[:], chan=(tag, "yo"), reads=[(tag, "yT")], writes=[(tag, "yretT")])


def build_A3(L):
    nc = bass.Bass("TRN2", target_bir_lowering=False)
    with ExitStack() as es:
        C = Ctx(nc, es)
        fmT_d = C.din("fmT", [7, 128, L], BF16)
        vr_d = C.din("vr", [L, 384], BF16)
        sg_d = C.din("sg", [L, 128], F32)
        rt_d = C.din("rt", [128, RT_W], F32)
        gng_d = C.din("gng", [64], F32)
        gnb_d = C.din("gnb", [64], F32)
        id_d = C.din("ident", [128, 128], BF16)
        yretT_d = C.dout("yretT", [128, L], BF16)
        ident = C.sb([128, 128], BF16, "ident")
        C.P.dma("sp", ident[:], id_d[:, :], chan="ident", writes=["ident"])
        stage_A3(C, fmT_d, vr_d, sg_d, rt_d, gng_d, gnb_d, yretT_d, ident, L)
        C.P.emit(es)
    return nc


S5C_W = 1 + 1 + 1 + 8 + 512 + 512
GELU_K = 2.0 * math.sqrt(2.0 / math.pi)


def s5_consts():
    p = np.arange(128)
    out = np.zeros((128, S5C_W), np.float32)
    out[:, 0] = (p < 64)
    out[:, 1] = (p >= 64)
    out[:, 2] = np.where(p < 64, 1.0, -1.0)
    for g in range(8):
        out[:, 3 + g] = (p // 16 == g)
    out[:, 11:11 + 512] = np.arange(512)[None, :]
    out[:, 11 + 512:11 + 1024] = (511 - np.arange(512))[None, :]
    return out


def s5_layout(inp, i, h):
    gs = slice(8 * h, 8 * h + 8)

    def pg(a):
        a = np.asarray(a[i][:, gs, :]).transpose(2, 0, 1).reshape(64, 16)
        return np.ascontiguousarray(np.concatenate([a, a], axis=0)).astype(np.float32)
    ldt = np.asarray(inp['s5_log_dt'][i][:, gs]).reshape(1, 16)
    ldt = np.ascontiguousarray(np.broadcast_to(ldt, (128, 16))).astype(np.float32)

    def pb(a):
        a = np.asarray(a[i][:, gs]).transpose(2, 0, 1, 3).reshape(64, 16, 16)
        return np.ascontiguousarray(np.concatenate([a, a], axis=0)).astype(np.float32)
    cre = np.asarray(inp['s5_C_re'][i][:, gs])
    cim = np.asarray(inp['s5_C_im'][i][:, gs])
    cc = np.stack([cre, cim], axis=2)
    cc = cc.transpose(1, 3, 0, 2, 4).reshape(128, 2, 128)
    dv = np.asarray(inp['s5_D'][i][128 * h:128 * h + 128]).reshape(128, 1)
    sp = np.concatenate([pg(inp['s5_A_re']), pg(inp['s5_A_im']), ldt, dv.astype(np.float32)], axis=1)
    return {"s5p": np.ascontiguousarray(sp), "s5bre": pb(inp['s5_B_re']), "s5bim": pb(inp['s5_B_im']),
            "s5cc": np.ascontiguousarray(cc).astype(np.float32)}


def emit_gelu(P, tag, y, ykey, tmp, out, okey):
    kt = (tag, "gelu_tmp")
    P.op("dve", lambda e: e.tensor_tensor(out=tmp, in0=y, in1=y, op=ALU.mult), reads=[ykey], writes=[kt])
    P.op("dve", lambda e: e.tensor_scalar(out=tmp, in0=tmp, scalar1=0.044715, scalar2=1.0, op0=ALU.mult, op1=ALU.add),
         reads=[kt], writes=[kt])
    P.op("dve", lambda e: e.tensor_tensor(out=tmp, in0=tmp, in1=y, op=ALU.mult), reads=[kt, ykey], writes=[kt])
    P.op("act", lambda e: e.activation(out=tmp, in_=tmp, func=AF.Sigmoid, scale=GELU_K), reads=[kt], writes=[kt])
    P.op("dve", lambda e: e.tensor_tensor(out=out, in0=tmp, in1=y, op=ALU.mult), reads=[kt, ykey], writes=[okey])


def stage_A4(C, fmT_d, s5p_d, bre_d, bim_d, cc_d, sc_d, identf_d, ys5T_d, L, tag="a4", scan_pool=True):
    P = C.P
    NT = L // 512
    sc = C.sb([128, S5C_W], F32, "sc")
    sp = C.sb([128, 49], F32, "sp")
    bre = C.sb([128, 16, 16], F32, "bre")
    bim = C.sb([128, 16, 16], F32, "bim")
    cc = C.sb([128, 2, 128], F32, "cc")
    idf = C.sb([128, 128], F32, "idf")
    for tl, src_, k in ((sc, sc_d, "sc"), (sp, s5p_d, "sp"), (bre, bre_d, "bre"), (bim, bim_d, "bim"), (cc, cc_d, "cc"), (idf, identf_d, "idf")):
        P.dma("sp", tl[:], src_, chan=(tag, "c0"), writes=[(tag, k)])
    mtop, mbot, sgnC = sc[:, 0:1], sc[:, 1:2], sc[:, 2:3]
    tau = [sc[:, 11:11 + 512], sc[:, 11 + 512:11 + 1024]]
    ar, ai, ldt, dv = sp[:, 0:16], sp[:, 16:32], sp[:, 32:48], sp[:, 48:49]

    sm = {}
    for nm in ("step", "zr", "th", "pp", "em1", "e", "sarg", "carg", "c1", "s1", "sh", "cm1", "nr", "ni", "den", "t0", "t1",
               "cre", "cim", "s1A", "s2A", "s1B", "c512", "s512", "a512", "rrf"):
        sm[nm] = C.sb([128, 16], F32, "s5_" + nm)
    rri = C.sb([128, 16], I32, "s5_rri")

    def S(nm):
        return sm[nm][:]

    def k(nm):
        return (tag, "sm", nm)

    def dv_op(fn, reads, writes):
        P.op("dve", fn, reads=[k(r) if isinstance(r, str) else r for r in reads], writes=[k(w) for w in writes])
    kp = (tag, "sp")
    dv_op(lambda e: e.tensor_copy(out=S("t0"), in_=ldt), [kp], ["t0"])
    P.op("act", lambda e: e.activation(out=S("step"), in_=S("t0"), func=AF.Exp), reads=[k("t0")], writes=[k("step")])
    dv_op(lambda e: e.tensor_tensor(out=S("zr"), in0=S("step"), in1=ar, op=ALU.mult), ["step", kp], ["zr"])
    dv_op(lambda e: e.tensor_tensor(out=S("th"), in0=S("step"), in1=ai, op=ALU.mult), ["step", kp], ["th"])
    dv_op(lambda e: e.tensor_scalar(out=S("pp"), in0=S("zr"), scalar1=1.0 / 6.0, scalar2=1.0, op0=ALU.mult, op1=ALU.add), ["zr"], ["pp"])
    for cdiv in (5.0, 4.0, 3.0, 2.0):
        dv_op(lambda e, cdiv=cdiv: e.scalar_tensor_tensor(out=S("pp"), in0=S("zr"), scalar=1.0 / cdiv, in1=S("pp"), op0=ALU.mult, op1=ALU.mult),
              ["zr", "pp"], ["pp"])
        dv_op(lambda e: e.tensor_scalar(out=S("pp"), in0=S("pp"), scalar1=1.0, scalar2=None, op0=ALU.add), ["pp"], ["pp"])
    dv_op(lambda e: e.tensor_tensor(out=S("em1"), in0=S("zr"), in1=S("pp"), op=ALU.mult), ["zr", "pp"], ["em1"])
    dv_op(lambda e: e.tensor_scalar(out=S("e"), in0=S("em1"), scalar1=1.0, scalar2=None, op0=ALU.add), ["em1"], ["e"])
    sincos(P, "dve", (tag, "sc1"), sm["th"], k("th"), sm["sarg"], sm["carg"], rri, sm["rrf"])
    ks, kc = ((tag, "sc1"), "sarg"), ((tag, "sc1"), "carg")
    P.op("act", lambda e: e.activation(out=S("s1"), in_=S("sarg"), func=AF.Sin), reads=[ks], writes=[k("s1")])
    P.op("act", lambda e: e.activation(out=S("sh"), in_=S("sarg"), func=AF.Sin, scale=0.5), reads=[ks], writes=[k("sh")])
    dv_op(lambda e: e.scalar_tensor_tensor(out=S("cm1"), in0=S("sh"), scalar=-2.0, in1=S("sh"), op0=ALU.mult, op1=ALU.mult), ["sh"], ["cm1"])
    dv_op(lambda e: e.tensor_scalar(out=S("c1"), in0=S("cm1"), scalar1=1.0, scalar2=None, op0=ALU.add), ["cm1"], ["c1"])
    dv_op(lambda e: e.tensor_tensor(out=S("nr"), in0=S("em1"), in1=S("c1"), op=ALU.mult), ["em1", "c1"], ["nr"])
    dv_op(lambda e: e.tensor_tensor(out=S("nr"), in0=S("nr"), in1=S("cm1"), op=ALU.add), ["nr", "cm1"], ["nr"])
    dv_op(lambda e: e.tensor_tensor(out=S("ni"), in0=S("e"), in1=S("s1"), op=ALU.mult), ["e", "s1"], ["ni"])
    dv_op(lambda e: e.tensor_tensor(out=S("den"), in0=ar, in1=ar, op=ALU.mult), [kp], ["den"])
    dv_op(lambda e: e.tensor_tensor(out=S("t0"), in0=ai, in1=ai, op=ALU.mult), [kp], ["t0"])
    dv_op(lambda e: e.tensor_tensor(out=S("den"), in0=S("den"), in1=S("t0"), op=ALU.add), ["den", "t0"], ["den"])
    dv_op(lambda e: e.reciprocal(out=S("den"), in_=S("den")), ["den"], ["den"])
    dv_op(lambda e: e.tensor_tensor(out=S("t0"), in0=S("nr"), in1=ar, op=ALU.mult), ["nr", kp], ["t0"])
    dv_op(lambda e: e.tensor_tensor(out=S("t1"), in0=S("ni"), in1=ai, op=ALU.mult), ["ni", kp], ["t1"])
    dv_op(lambda e: e.tensor_tensor(out=S("t0"), in0=S("t0"), in1=S("t1"), op=ALU.add), ["t0", "t1"], ["t0"])
    dv_op(lambda e: e.tensor_tensor(out=S("cre"), in0=S("t0"), in1=S("den"), op=ALU.mult), ["t0", "den"], ["cre"])
    dv_op(lambda e: e.tensor_tensor(out=S("t0"), in0=S("ni"), in1=ar, op=ALU.mult), ["ni", kp], ["t0"])
    dv_op(lambda e: e.tensor_tensor(out=S("t1"), in0=S("nr"), in1=ai, op=ALU.mult), ["nr", kp], ["t1"])
    dv_op(lambda e: e.tensor_tensor(out=S("t0"), in0=S("t0"), in1=S("t1"), op=ALU.subtract), ["t0", "t1"], ["t0"])
    dv_op(lambda e: e.tensor_tensor(out=S("cim"), in0=S("t0"), in1=S("den"), op=ALU.mult), ["t0", "den"], ["cim"])
    ksc = (tag, "sc")
    dv_op(lambda e: e.tensor_scalar(out=S("s1A"), in0=S("cre"), scalar1=mtop, scalar2=None, op0=ALU.mult), ["cre", ksc], ["s1A"])
    dv_op(lambda e: e.scalar_tensor_tensor(out=S("s1A"), in0=S("cim"), scalar=mbot, in1=S("s1A"), op0=ALU.mult, op1=ALU.add), ["cim", "s1A", ksc], ["s1A"])
    dv_op(lambda e: e.tensor_scalar(out=S("s2A"), in0=S("cre"), scalar1=mbot, scalar2=None, op0=ALU.mult), ["cre", ksc], ["s2A"])
    dv_op(lambda e: e.tensor_scalar(out=S("t0"), in0=S("cim"), scalar1=mtop, scalar2=None, op0=ALU.mult), ["cim", ksc], ["t0"])
    dv_op(lambda e: e.tensor_tensor(out=S("s2A"), in0=S("s2A"), in1=S("t0"), op=ALU.subtract), ["s2A", "t0"], ["s2A"])
    dv_op(lambda e: e.tensor_scalar(out=S("s1B"), in0=S("s1A"), scalar1=sgnC, scalar2=None, op0=ALU.mult), ["s1A", ksc], ["s1B"])
    dv_op(lambda e: e.tensor_scalar(out=S("s1B"), in0=S("cim"), scalar1=mtop, scalar2=None, op0=ALU.mult), ["cim", ksc], ["s1B"])
    dv_op(lambda e: e.tensor_scalar(out=S("t0"), in0=S("cre"), scalar1=mbot, scalar2=None, op0=ALU.mult), ["cre", ksc], ["t0"])
    dv_op(lambda e: e.tensor_tensor(out=S("s1B"), in0=S("s1B"), in1=S("t0"), op=ALU.subtract), ["s1B", "t0"], ["s1B"])
    dv_op(lambda e: e.tensor_scalar(out=S("a512"), in0=S("th"), scalar1=512.0, scalar2=None, op0=ALU.mult), ["th"], ["a512"])
    sincos(P, "dve", (tag, "sc2"), sm["a512"], k("a512"), sm["sarg"], sm["carg"], rri, sm["rrf"])
    ks2, kc2 = ((tag, "sc2"), "sarg"), ((tag, "sc2"), "carg")
    P.op("act", lambda e: e.activation(out=S("s512"), in_=S("sarg"), func=AF.Sin), reads=[ks2], writes=[k("s512")])
    P.op("act", lambda e: e.activation(out=S("c512"), in_=S("carg"), func=AF.Sin), reads=[kc2], writes=[k("c512")])

    BA = C.sb([128, 16, 128], BF16, "BA")
    BB = C.sb([128, 16, 128], BF16, "BB")
    CP = C.sb([128, 16, 128], BF16, "CP")
    Dg = C.sb([128, 128], BF16, "Dg")
    xa = C.sb([128, 8, 16], F32, "xa")
    xb = C.sb([128, 8, 16], F32, "xb")
    cfull = C.sb([128, 128], F32, "cfull")
    pset = C.ps([128, 128], F32, "pset")
    P.op("pool", lambda e: e.memset(CP[:], 0.0), writes=[(tag, "CP")])
    P.op("dve", lambda e: e.tensor_scalar(out=Dg[:], in0=idf[:], scalar1=dv, scalar2=None, op0=ALU.mult),
         reads=[(tag, "idf"), kp], writes=[(tag, "Dg")])
    for d in range(2):
        for var, (sa, sb_), dst in ((0, ("s1A", "s2A"), BA), (1, ("s1B", "s1A"), BB)):
            def bc(nm, d=d):
                return sm[nm][:, d * 8:(d + 1) * 8].unsqueeze(2).to_broadcast([128, 8, 16])
            P.op("dve", lambda e, d=d, sa=sa, bc=bc: e.tensor_tensor(out=xa[:], in0=bre[:, d * 8:(d + 1) * 8, :], in1=bc(sa), op=ALU.mult),
                 reads=[(tag, "bre"), k(sa)], writes=[(tag, "xa")])
            P.op("dve", lambda e, d=d, sb_=sb_, bc=bc: e.tensor_tensor(out=xb[:], in0=bim[:, d * 8:(d + 1) * 8, :], in1=bc(sb_), op=ALU.mult),
                 reads=[(tag, "bim"), k(sb_)], writes=[(tag, "xb")])
            P.op("dve", lambda e: e.tensor_tensor(out=xa[:], in0=xa[:], in1=xb[:], op=ALU.add),
                 reads=[(tag, "xa"), (tag, "xb")], writes=[(tag, "xa")])
            P.op("pe", lambda e: e.transpose(out=pset[:], in_=xa[:].rearrange("p a b -> p (a b)"), identity=idf[:]),
                 reads=[(tag, "xa"), (tag, "idf")], writes=[(tag, "pset")])
            for g in range(8):
                P.op("dve", lambda e, g=g, d=d, dst=dst: e.tensor_scalar(out=dst[:, d * 8 + g, :], in0=pset[:], scalar1=sc[:, 3 + g:4 + g],
                                                                      scalar2=None, op0=ALU.mult),
                     reads=[ksc], writes=[(tag, "Btab"), (tag, "pset")])
        P.op("pe", lambda e, d=d: e.transpose(out=pset[:], in_=cc[:, d, :], identity=idf[:]),
             reads=[(tag, "cc"), (tag, "idf")], writes=[(tag, "pset")])
        P.op("dve", lambda e: e.tensor_copy(out=cfull[:], in_=pset[:]), writes=[(tag, "cfull"), (tag, "pset")])
        for g in range(8):
            P.op("dve", lambda e, g=g, d=d: e.tensor_scalar(out=CP[:, d * 8 + g, 16 * g:16 * g + 16], in0=cfull[:, 16 * g:16 * g + 16],
                                                            scalar1=sgnC, scalar2=None, op0=ALU.mult),
                 reads=[(tag, "cfull"), ksc], writes=[(tag, "CP")])

    cosT = C.sb([128, 16, 512], F32, "cosT")
    sinT = C.sb([128, 16, 512], F32, "sinT")
    ang = C.sb([128, 512], F32, "ang")
    rsa = C.sb([128, 512], F32, "rsa")
    rca = C.sb([128, 512], F32, "rca")
    rr_i = C.sb([128, 512], I32, "rr_i")
    rr_f = C.sb([128, 512], F32, "rr_f")
    for gd in range(16):
        d = gd // 8
        P.op("dve", lambda e, gd=gd, d=d: e.tensor_scalar(out=ang[:], in0=tau[d], scalar1=sm["th"][:, gd:gd + 1], scalar2=None, op0=ALU.mult),
             reads=[ksc, k("th")], writes=[(tag, "ang")])
        sincos(P, "dve", (tag, "rt"), ang, (tag, "ang"), rsa, rca, rr_i, rr_f)
        P.op("act", lambda e, gd=gd: e.activation(out=sinT[:, gd, :], in_=rsa[:], func=AF.Sin), reads=[((tag, "rt"), "sarg")], writes=[(tag, "sinT", gd)])
        P.op("act", lambda e, gd=gd: e.activation(out=cosT[:, gd, :], in_=rca[:], func=AF.Sin), reads=[((tag, "rt"), "carg")], writes=[(tag, "cosT", gd)])

    uT = C.sb([128, L], BF16, "uT")
    yb = C.sb([128, L], F32, "yb")
    P.dma("sp", uT[:], fmT_d[6, :, :], chan=(tag, "ld"), writes=[(tag, "uT")])
    cA = C.sb([128, 16], F32, "cA")
    cB = C.sb([128, 16], F32, "cB")
    ctmp = C.sb([128, 4], F32, "ctmp")
    P.op("dve", lambda e: e.memset(cA[:], 0.0), writes=[(tag, "cA", gd) for gd in range(16)])
    P.op("dve", lambda e: e.memset(cB[:], 0.0), writes=[(tag, "cB", gd) for gd in range(16)])
    bA = [C.sb([128, 512], F32, "bA") for _ in range(2)]
    bB = [C.sb([128, 512], F32, "bB") for _ in range(2)]
    w1 = [C.sb([128, 512], F32, "w1") for _ in range(3)]
    w2 = [C.sb([128, 512], F32, "w2") for _ in range(3)]
    w3 = [C.sb([128, 512], F32, "w3") for _ in range(3)]
    w4 = [C.sb([128, 512], F32, "w4") for _ in range(3)]
    xbf = [C.sb([128, 512], BF16, "xbf") for _ in range(2)]
    yo = [C.sb([128, 512], F32, "yo") for _ in range(2)]
    gt = [C.sb([128, 512], F32, "gt") for _ in range(2)]
    yob = [C.sb([128, 512], BF16, "yob") for _ in range(2)]
    pA = [C.ps([128, 512], F32, "pA") for _ in range(2)]
    pB = [C.ps([128, 512], F32, "pB") for _ in range(2)]
    yps = [C.ps([128, 512], F32, "yps") for _ in range(2)]

    units = []
    tcount = 0
    for d in (1, 0):
        order = range(NT - 1, -1, -1) if d == 1 else range(NT)
        for it in order:
            for g in range(8):
                units.append((d, it, g, tcount % 2))
            tcount += 1
    NU = len(units)

    def ph_a(i):
        d, it, g, ysl = units[i]
        gd = d * 8 + g
        s = i % 2
        ts = slice(it * 512, (it + 1) * 512)
        P.op("pe", lambda e: e.matmul(pA[s][:], lhsT=BA[:, gd, :], rhs=uT[:, ts], start=True, stop=True),
             reads=[(tag, "Btab"), (tag, "uT")], writes=[(tag, "pA", s)])
        P.op("pe", lambda e: e.matmul(pB[s][:], lhsT=BB[:, gd, :], rhs=uT[:, ts], start=True, stop=True),
             reads=[(tag, "Btab"), (tag, "uT")], writes=[(tag, "pB", s)])
        P.op("act", lambda e: e.copy(out=bA[s][:], in_=pA[s][:]), writes=[(tag, "bA", s), (tag, "pA", s)])
        P.op("act", lambda e: e.copy(out=bB[s][:], in_=pB[s][:]), writes=[(tag, "bB", s), (tag, "pB", s)])

    def tabs(i):
        d, it, g, ysl = units[i]
        gd = d * 8 + g
        return gd, cosT[:, gd, :], sinT[:, gd, :], (tag, "cosT", gd), (tag, "sinT", gd)

    def ph_c(i):
        gd, ct, st_, kct, kst = tabs(i)
        s, w = i % 2, i % 3
        P.op("pool", lambda e: e.tensor_tensor(out=w3[w][:], in0=bB[s][:], in1=ct, op=ALU.mult), reads=[(tag, "bB", s), kct], writes=[(tag, "w3", w)])
        P.op("pool", lambda e: e.tensor_tensor(out=w4[w][:], in0=bA[s][:], in1=st_, op=ALU.mult), reads=[(tag, "bA", s), kst], writes=[(tag, "w4", w)])
        P.op("pool", lambda e: e.tensor_tensor(out=w3[w][:], in0=w3[w][:], in1=w4[w][:], op=ALU.subtract), reads=[(tag, "w3", w), (tag, "w4", w)], writes=[(tag, "w3", w)])

    def ph_df(i_scan, i_chain):
        ops_scan, ops_chain = [], []
        if i_scan is not None:
            d, it, g, ysl = units[i_scan]
            gd = d * 8 + g
            w = i_scan % 3
            rb = sm["e"][:, gd:gd + 1].to_broadcast([128, 512])
            if d == 0:
                oA, iA, oB, iB, lastc = w2[w][:], w1[w][:], w4[w][:], w3[w][:], slice(511, 512)
            else:
                oA, iA, oB, iB, lastc = w2[w][:, ::-1], w1[w][:, ::-1], w4[w][:, ::-1], w3[w][:, ::-1], slice(0, 1)
            c5, s5 = sm["c512"][:, gd:gd + 1], sm["s512"][:, gd:gd + 1]
            ops_scan = [
                (lambda e: e.tensor_tensor_scan(out=oA, data0=rb, data1=iA, initial=cA[:, gd:gd + 1], op0=ALU.mult, op1=ALU.add),
                 [(tag, "w1", w), k("e"), (tag, "cA", gd)], [(tag, "w2", w)]),
                (lambda e: e.tensor_tensor_scan(out=oB, data0=rb, data1=iB, initial=cB[:, gd:gd + 1], op0=ALU.mult, op1=ALU.add),
                 [(tag, "w3", w), k("e"), (tag, "cB", gd)], [(tag, "w4", w)]),
                (lambda e: e.tensor_tensor(out=ctmp[:, 0:1], in0=w4[w][:, lastc], in1=s5, op=ALU.mult), [(tag, "w4", w), k("s512")], [(tag, "ct0")]),
                (lambda e: e.tensor_tensor(out=ctmp[:, 1:2], in0=w2[w][:, lastc], in1=s5, op=ALU.mult), [(tag, "w2", w), k("s512")], [(tag, "ct1")]),
                (lambda e: e.scalar_tensor_tensor(out=cA[:, gd:gd + 1], in0=w2[w][:, lastc], scalar=c5, in1=ctmp[:, 0:1], op0=ALU.mult, op1=ALU.subtract),
                 [(tag, "w2", w), k("c512"), (tag, "ct0")], [(tag, "cA", gd)]),
                (lambda e: e.scalar_tensor_tensor(out=cB[:, gd:gd + 1], in0=w4[w][:, lastc], scalar=c5, in1=ctmp[:, 1:2], op0=ALU.mult, op1=ALU.add),
                 [(tag, "w4", w), k("c512"), (tag, "ct1")], [(tag, "cB", gd)]),
            ]
        if i_chain is not None:
            gd2, ct, st_, kct, kst = tabs(i_chain)
            s, w_ = i_chain % 2, i_chain % 3
            ops_chain = [
                (lambda e: e.tensor_tensor(out=w1[w_][:], in0=bA[s][:], in1=ct, op=ALU.mult), [(tag, "bA", s), kct], [(tag, "w1", w_)]),
                (lambda e: e.tensor_tensor(out=w2[w_][:], in0=bB[s][:], in1=st_, op=ALU.mult), [(tag, "bB", s), kst], [(tag, "w2", w_)]),
                (lambda e: e.tensor_tensor(out=w1[w_][:], in0=w1[w_][:], in1=w2[w_][:], op=ALU.add), [(tag, "w1", w_), (tag, "w2", w_)], [(tag, "w1", w_)]),
            ]
        order = []
        sq, ch = list(ops_scan), list(ops_chain)
        pattern = ["s", "c", "s", "c", "s", "s", "c", "s", "s"]
        for p_ in pattern:
            if p_ == "s" and sq:
                order.append(sq.pop(0))
            elif p_ == "c" and ch:
                order.append(ch.pop(0))
        order += sq + ch
        for fn, rd, wr in order:
            P.op("dve", fn, reads=rd, writes=wr)

    def ph_e(i):
        gd, ct, st_, kct, kst = tabs(i)
        w = i % 3
        P.op("pool", lambda e: e.tensor_tensor(out=w1[w][:], in0=w2[w][:], in1=ct, op=ALU.mult), reads=[(tag, "w2", w), kct], writes=[(tag, "w1", w)])
        P.op("pool", lambda e: e.tensor_tensor(out=w3[w][:], in0=w4[w][:], in1=st_, op=ALU.mult), reads=[(tag, "w4", w), kst], writes=[(tag, "w3", w)])

    def ph_b(i):
        d, it, g, ysl = units[i]
        gd = d * 8 + g
        w, s = i % 3, i % 2
        ts = slice(it * 512, (it + 1) * 512)
        P.op("dve", lambda e: e.tensor_tensor(out=xbf[s][:], in0=w1[w][:], in1=w3[w][:], op=ALU.subtract), reads=[(tag, "w1", w), (tag, "w3", w)], writes=[(tag, "xbf", s)])
        P.op("pe", lambda e: e.matmul(yps[ysl][:], lhsT=CP[:, gd, :], rhs=xbf[s][:], start=(g == 0), stop=(g == 7 and d == 1)),
             reads=[(tag, "CP"), (tag, "xbf", s)], writes=[(tag, "yps", ysl)])
        if g != 7:
            return
        if d == 1:
            P.op("act", lambda e: e.copy(out=yb[:, ts], in_=yps[ysl][:]), writes=[(tag, "yb", it), (tag, "yps", ysl)])
        else:
            P.op("pe", lambda e: e.matmul(yps[ysl][:], lhsT=Dg[:], rhs=uT[:, ts], start=False, stop=True),
                 reads=[(tag, "Dg"), (tag, "uT")], writes=[(tag, "yps", ysl)])
            P.op("dve", lambda e: e.tensor_tensor(out=yo[ysl][:], in0=yps[ysl][:], in1=yb[:, ts], op=ALU.add),
                 reads=[(tag, "yb", it)], writes=[(tag, "yo", ysl), (tag, "yps", ysl)])
            emit_gelu(P, (tag, "g", ysl), yo[ysl][:], (tag, "yo", ysl), gt[ysl][:], yob[ysl][:], (tag, "yob", ysl))
            P.dma("sp", ys5T_d[:, ts], yob[ysl][:], chan=(tag, "yo", ysl), reads=[(tag, "yob", ysl)], writes=[(tag, "ys5T", it)])

    ph_a(0)
    for idx in range(NU + 2):
        if idx + 1 < NU:
            ph_a(idx + 1)
        if 0 <= idx - 2 < NU:
            ph_b(idx - 2)
        if idx < NU:
            ph_c(idx)
        ph_df(idx - 1 if 0 <= idx - 1 < NU else None, idx if idx < NU else None)
        if 0 <= idx - 1 < NU:
            ph_e(idx - 1)


def build_A4(L, scan_pool=True):
    nc = bass.Bass("TRN2", target_bir_lowering=False)
    with ExitStack() as es:
        C = Ctx(nc, es)
        fmT_d = C.din("fmT", [7, 128, L], BF16)
        s5p_d = C.din("s5p", [128, 49], F32)
        bre_d = C.din("s5bre", [128, 16, 16], F32)
        bim_d = C.din("s5bim", [128, 16, 16], F32)
        cc_d = C.din("s5cc", [128, 2, 128], F32)
        sc_d = C.din("s5c", [128, S5C_W], F32)
        idf_d = C.din("identf", [128, 128], F32)
        ys5T_d = C.dout("ys5T", [128, L], BF16)
        stage_A4(C, fmT_d, s5p_d[:, :], bre_d[:, :, :], bim_d[:, :, :], cc_d[:, :, :], sc_d[:, :], idf_d[:, :], ys5T_d, L, scan_pool=scan_pool)
        C.P.emit(es)
    return nc


W0_SPECS = (("w_out", 1024, 1024), ("ple_gate_w", 1024, 1024), ("ple_w", 256, 1024), ("s5_glu_w", 256, 256),
            ("ffn_w_up", 1024, 5632), ("ffn_w_down", 2816, 1024))


def stage_W0(C, pairs, tag="w0"):
    P = C.P
    stg = [C.sb([128, 2816], F32, "w0s") for _ in range(2)]
    obf = [C.sb([128, 2816], BF16, "w0o") for _ in range(2)]
    i = 0
    engs = ("dve", "pool", "act")
    for src_d, dst_d, R, N in pairs:
        for r0 in range(0, R, 128):
            for n0 in range(0, N, 2816):
                n1 = min(N, n0 + 2816)
                w = n1 - n0
                s = i % 2
                P.dma("sp", stg[s][:, 0:w], src_d[r0:r0 + 128, n0:n1], chan=(tag, "in", s), writes=[(tag, "stg", s)])
                en = engs[i % 3]
                if en == "act":
                    P.op("act", lambda e, s=s, w=w: e.copy(out=obf[s][:, 0:w], in_=stg[s][:, 0:w]), reads=[(tag, "stg", s)], writes=[(tag, "obf", s)])
                else:
                    P.op(en, lambda e, s=s, w=w: e.tensor_copy(out=obf[s][:, 0:w], in_=stg[s][:, 0:w]), reads=[(tag, "stg", s)], writes=[(tag, "obf", s)])
                P.dma("pool", dst_d[r0:r0 + 128, n0:n1], obf[s][:, 0:w], chan=(tag, "out", s), reads=[(tag, "obf", s)], writes=[(tag, "dst", i)])
                i += 1


def build_W0():
    nc = bass.Bass("TRN2", target_bir_lowering=False)
    with ExitStack() as es:
        C = Ctx(nc, es)
        pairs = []
        for nm, R, N in W0_SPECS:
            pairs.append((C.din(nm, [R, N], F32), C.dout(nm + "_bf", [R, N], BF16), R, N))
        stage_W0(C, pairs)
        C.P.emit(es)
    return nc


def emit_ln(P, tag, h, hkey, st, mv, rstd, epsb, gtab, btab, gkeys, out, okey, eng2="pool"):
    ks, km, kr = (tag, "st"), (tag, "mv"), (tag, "rstd")
    for j in range(2):
        P.op("dve", lambda e, j=j: e.bn_stats(out=st[:, j, :], in_=h[:, j * 512:(j + 1) * 512]), reads=[hkey], writes=[ks])
    P.op("dve", lambda e: e.bn_aggr(out=mv[:], in_=st[:]), reads=[ks], writes=[km])
    P.op("act", lambda e: e.activation(out=rstd[:], in_=mv[:, 1:2], func=AF.Sqrt, bias=epsb[:, 0:1], scale=1.0), reads=[km] + gkeys, writes=[kr])
    P.op("dve", lambda e: e.reciprocal(out=rstd[:], in_=rstd[:]), reads=[kr], writes=[kr])
    P.op("dve", lambda e: e.tensor_scalar(out=h, in0=h, scalar1=mv[:, 0:1], scalar2=rstd[:, 0:1], op0=ALU.subtract, op1=ALU.mult),
         reads=[hkey, km, kr], writes=[hkey])
    P.op(eng2, lambda e: e.tensor_tensor(out=h, in0=h, in1=gtab, op=ALU.mult), reads=[hkey] + gkeys, writes=[hkey])
    P.op(eng2, lambda e: e.tensor_tensor(out=out, in0=h, in1=btab, op=ALU.add), reads=[hkey] + gkeys, writes=[okey])


def emit_to_featmajor(P, C_tag, src32, skey, xbf, pst, dstT, dkey_fn, ident, j):
    tag = C_tag
    P.op("act", lambda e: e.copy(out=xbf, in_=src32), reads=[skey], writes=[(tag, "xbf")])
    for c in range(8):
        P.op("pe", lambda e, c=c: e.transpose(out=pst[:, c, :], in_=xbf[:, c * 128:(c + 1) * 128], identity=ident[:]),
             reads=[(tag, "xbf"), "ident"], writes=[(tag, "pst")])
    P.op("dve", lambda e: e.tensor_copy(out=dstT[:, :, j * 128:(j + 1) * 128], in_=pst[:]), writes=[dkey_fn(j), (tag, "pst")])


def stage_P1(C, ycT_d, x_d, wout_d, glw_d, glb_d, g1_d, b1_d, x1_d, x1T_d, ident, NTOK, tag="p1"):
    P = C.P
    wout = C.sb([128, 8, 1024], BF16, "wout")
    glw = C.sb([128, 2, 256], BF16, "glw")
    glb = C.sb([128, 2], F32, "glb")
    gtab = C.sb([128, 1024], F32, "gtab")
    btab = C.sb([128, 1024], F32, "btab")
    epsb = C.sb([128, 1], F32, "epsb")
    P.dma("sp", wout[:], wout_d.rearrange("(c p) n -> p c n", p=128), chan=(tag, "c0"), writes=[(tag, "wout")])
    P.dma("sp", glw[:], glw_d.rearrange("(c p) n -> p c n", p=128), chan=(tag, "c0"), writes=[(tag, "glw")])
    for oc in range(2):
        P.dma("sp", glb[:, oc:oc + 1], glb_d[oc * 128:(oc + 1) * 128].rearrange("(p o) -> p o", o=1), chan=(tag, "c0"), writes=[(tag, "glb")])
    P.dma("sp", gtab[:], g1_d.partition_broadcast(128), chan=(tag, "c0"), writes=[(tag, "gtab")])
    P.dma("sp", btab[:], b1_d.partition_broadcast(128), chan=(tag, "c0"), writes=[(tag, "btab")])
    P.op("dve", lambda e: e.memset(epsb[:], LN_EPS), writes=[(tag, "epsb")])
    gk = [(tag, "gtab"), (tag, "btab"), (tag, "epsb")]
    yc = [C.sb([128, 8, 512], BF16, "yc") for _ in range(2)]
    ysg = [C.sb([128, 2, 512], BF16, "ysg") for _ in range(2)]
    gsig = C.sb([128, 512], F32, "gsig")
    xin = [C.sb([128, 1024], F32, "xin") for _ in range(2)]
    hh = [C.sb([128, 1024], F32, "hh") for _ in range(2)]
    x1o = [C.sb([128, 1024], F32, "x1o") for _ in range(2)]
    xbf = C.sb([128, 1024], BF16, "xbf")
    x1T = [C.sb([128, 8, 512], BF16, "x1T") for _ in range(2)]
    st = C.sb([128, 2, 6], F32, "st")
    mv = C.sb([128, 2], F32, "mv")
    rstd = C.sb([128, 1], F32, "rstd")
    pgl = [C.ps([128, 512], F32, "pgl") for _ in range(2)]
    pmx = [C.ps([128, 512], F32, "pmx") for _ in range(4)]
    pst = C.ps([128, 8, 128], BF16, "pst")
    ycT_v = ycT_d.rearrange("(c p) t -> p c t", p=128)
    x1T_v = x1T_d.rearrange("(c p) t -> p c t", p=128)
    NTT = NTOK // 512
    kk = 0
    for tt in range(NTT):
        s = tt % 2
        t0 = tt * 512
        P.dma("sp", yc[s][:], ycT_v[:, :, t0:t0 + 512], chan=(tag, "yc", s), writes=[(tag, "yc", s)])
        for oc in range(2):
            for kc in range(2):
                P.op("pe", lambda e, s=s, oc=oc, kc=kc: e.matmul(pgl[oc][:], lhsT=glw[:, kc, oc * 128:(oc + 1) * 128], rhs=yc[s][:, 6 + kc, :],
                                                                 start=(kc == 0), stop=(kc == 1)),
                     reads=[(tag, "glw"), (tag, "yc", s)], writes=[(tag, "pgl", oc)])
            P.op("act", lambda e, oc=oc: e.activation(out=gsig[:], in_=pgl[oc][:], func=AF.Sigmoid, bias=glb[:, oc:oc + 1], scale=1.0),
                 reads=[(tag, "glb")], writes=[(tag, "gsig"), (tag, "pgl", oc)])
            P.op("dve", lambda e, s=s, oc=oc: e.tensor_tensor(out=ysg[s][:, oc, :], in0=gsig[:], in1=yc[s][:, 6 + oc, :], op=ALU.mult),
                 reads=[(tag, "gsig"), (tag, "yc", s)], writes=[(tag, "ysg", s, oc)])
        for j in range(4):
            b = kk % 2
            kk += 1
            r0 = t0 + j * 128
            P.dma("sp", xin[b][:], x_d[r0:r0 + 128, :], chan=(tag, "xin", b), writes=[(tag, "xin", b)])
            for nb in range(2):
                pb = (2 * b + nb)
                for c in range(8):
                    def lhs(c=c, s=s, j=j):
                        return yc[s][:, c, j * 128:(j + 1) * 128] if c < 6 else ysg[s][:, c - 6, j * 128:(j + 1) * 128]
                    P.op("pe", lambda e, pb=pb, c=c, nb=nb, lhs=lhs: e.matmul(pmx[pb][:], lhsT=lhs(), rhs=wout[:, c, nb * 512:(nb + 1) * 512],
                                                                              start=(c == 0), stop=(c == 7)),
                         reads=[(tag, "wout"), (tag, "yc", s), (tag, "ysg", s, 0), (tag, "ysg", s, 1)], writes=[(tag, "pmx", pb)])
                P.op("dve", lambda e, b=b, pb=pb, nb=nb: e.scalar_tensor_tensor(out=hh[b][:, nb * 512:(nb + 1) * 512], in0=xin[b][:, nb * 512:(nb + 1) * 512],
                                                                               scalar=ALPHA, in1=pmx[pb][:], op0=ALU.mult, op1=ALU.add),
                     reads=[(tag, "xin", b)], writes=[(tag, "hh", b), (tag, "pmx", pb)])
            emit_ln(P, (tag, "ln"), hh[b][:], (tag, "hh", b), st, mv, rstd, epsb, gtab[:], btab[:], gk, x1o[b][:], (tag, "x1o", b))
            P.dma("pool", x1_d[r0:r0 + 128, :], x1o[b][:], chan=(tag, "x1o", b), reads=[(tag, "x1o", b)], writes=[(tag, "x1d", tt, j)])
            emit_to_featmajor(P, (tag, "fm"), x1o[b][:], (tag, "x1o", b), xbf[:], pst, x1T[s], lambda jj, s=s: (tag, "x1T", s, jj), ident, j)
        P.dma("pool", x1T_v[:, :, t0:t0 + 512], x1T[s][:], chan=(tag, "x1T", s), reads=[(tag, "x1T", s, jj) for jj in range(4)],
              writes=[(tag, "x1Td", tt)])


def build_P1(NTOK):
    nc = bass.Bass("TRN2", target_bir_lowering=False)
    with ExitStack() as es:
        C = Ctx(nc, es)
        ycT_d = C.din("ycT", [1024, NTOK], BF16)
        x_d = C.din("x", [NTOK, 1024], F32)
        wout_d = C.din("w_out_bf", [1024, 1024], BF16)
        glw_d = C.din("s5_glu_w_bf", [256, 256], BF16)
        glb_d = C.din("glb", [256], F32)
        g1_d = C.din("ln_g", [1024], F32)
        b1_d = C.din("ln_b", [1024], F32)
        id_d = C.din("ident", [128, 128], BF16)
        x1_d = C.dout("x1", [NTOK, 1024], F32)
        x1T_d = C.dout("x1T", [1024, NTOK], BF16)
        ident = C.sb([128, 128], BF16, "ident")
        C.P.dma("sp", ident[:], id_d[:, :], chan="ident", writes=["ident"])
        stage_P1(C, ycT_d, x_d, wout_d, glw_d, glb_d, g1_d, b1_d, x1_d, x1T_d, ident, NTOK)
        C.P.emit(es)
    return nc


NFT = D_FF // 128


def conv_layout(conv_w, conv_b):
    a = np.concatenate([np.asarray(conv_w), np.asarray(conv_b)[None, :]], axis=0)
    return np.ascontiguousarray(a.reshape(4, NFT, 128).transpose(2, 1, 0)).astype(np.float32)


def stage_P2(C, x1T_d, x1_d, p_d, wup_d, wdn_d, plew_d, plegw_d, cwb_d, g2_d, b2_d, x2_d, x2T_d, ident, NTOK, tag="p2"):
    P = C.P
    plegw = C.sb([128, 8, 1024], BF16, "plegw")
    plew = C.sb([128, 2, 1024], BF16, "plew")
    wdn = C.sb([128, NFT, 1024], BF16, "wdn")
    cwb = C.sb([128, NFT, 4], F32, "cwb")
    gtab = C.sb([128, 1024], F32, "gtab")
    btab = C.sb([128, 1024], F32, "btab")
    epsb = C.sb([128, 1], F32, "epsb")
    P.dma("sp", plegw[:], plegw_d.rearrange("(c p) n -> p c n", p=128), chan=(tag, "c0"), writes=[(tag, "plegw")])
    P.dma("sp", plew[:], plew_d.rearrange("(c p) n -> p c n", p=128), chan=(tag, "c0"), writes=[(tag, "plew")])
    P.dma("sp", wdn[:], wdn_d.rearrange("(c p) n -> p c n", p=128), chan=(tag, "c0"), writes=[(tag, "wdn")])
    P.dma("sp", cwb[:], cwb_d, chan=(tag, "c0"), writes=[(tag, "cwb")])
    P.dma("sp", gtab[:], g2_d.partition_broadcast(128), chan=(tag, "c0"), writes=[(tag, "gtab")])
    P.dma("sp", btab[:], b2_d.partition_broadcast(128), chan=(tag, "c0"), writes=[(tag, "btab")])
    P.op("dve", lambda e: e.memset(epsb[:], LN_EPS), writes=[(tag, "epsb")])
    gk = [(tag, "gtab"), (tag, "btab"), (tag, "epsb")]

    xt = [C.sb([128, 8, 514], BF16, "xt") for _ in range(2)]
    wg = [C.sb([128, 8, 128], BF16, "wg") for _ in range(3)]
    wv = [C.sb([128, 8, 128], BF16, "wv") for _ in range(3)]
    hm = C.sb([128, NFT, 512], BF16, "hm")
    racc = [C.sb([128, 1024], F32, "racc") for _ in range(4)]
    x1t = [C.sb([128, 1024], F32, "x1t") for _ in range(2)]
    pin = [C.sb([128, 256], F32, "pin") for _ in range(2)]
    pbf = [C.sb([128, 256], BF16, "pbf") for _ in range(2)]
    pT = C.sb([128, 2, 512], BF16, "pT")
    sg = [C.sb([128, 512], F32, "sg") for _ in range(2)]
    gext = [C.sb([128, 514], F32, "gext") for _ in range(3)]
    cv = [C.sb([128, 512], F32, "cv") for _ in range(3)]
    tmp = [C.sb([128, 512], F32, "tmp") for _ in range(3)]
    oneb = C.sb([128, 1], F32, "oneb")
    P.op("dve", lambda e: e.memset(oneb[:], 1.0), writes=[(tag, "oneb")])
    xbf = C.sb([128, 1024], BF16, "xbf")
    x2T = C.sb([128, 8, 512], BF16, "x2T")
    st = C.sb([128, 2, 6], F32, "st")
    mv = C.sb([128, 2], F32, "mv")
    rstd = C.sb([128, 1], F32, "rstd")
    pgate = [C.ps([128, 512], F32, "pgate") for _ in range(2)]
    pval = [C.ps([128, 512], F32, "pval") for _ in range(2)]
    pd = [C.ps([128, 512], F32, "pd") for _ in range(2)]
    phalo = C.ps([128, 2], F32, "phalo")
    pst = C.ps([128, 8, 128], BF16, "pst")
    x1T_v = x1T_d.rearrange("(c p) t -> p c t", p=128)
    x2T_v = x2T_d.rearrange("(c p) t -> p c t", p=128)
    wup_v = wup_d.rearrange("(c p) n -> p c n", p=128)
    NTT = NTOK // 512
    cnt = {"w": 0, "u": 0, "d": 0, "x": 0, "p": 0, "q": 0}

    def tile(tt):
        s = tt % 2
        t0 = tt * 512
        P.dma("sp", xt[s][:], x1T_v[:, :, t0:t0 + 514], chan=(tag, "xt", s), writes=[(tag, "xt", s)])
        for j in range(4):
            b = cnt["p"] % 2
            cnt["p"] += 1
            r0 = t0 + j * 128
            P.dma("sp", pin[b][:], p_d[r0:r0 + 128, :], chan=(tag, "pin", b), writes=[(tag, "pin", b)])
            P.op("act", lambda e, b=b: e.copy(out=pbf[b][:], in_=pin[b][:]), reads=[(tag, "pin", b)], writes=[(tag, "pbf", b)])
            for c in range(2):
                P.op("pe", lambda e, b=b, c=c: e.transpose(out=pst[:, c, :], in_=pbf[b][:, c * 128:(c + 1) * 128], identity=ident[:]),
                     reads=[(tag, "pbf", b), "ident"], writes=[(tag, "pst")])
            P.op("dve", lambda e, j=j: e.tensor_copy(out=pT[:, :, j * 128:(j + 1) * 128], in_=pst[:, 0:2, :]), writes=[(tag, "pT", j), (tag, "pst")])
        for j in range(4):
            b = cnt["x"] % 2
            cnt["x"] += 1
            r0 = t0 + j * 128
            P.dma("sp", x1t[b][:], x1_d[r0:r0 + 128, :], chan=(tag, "x1t", b), writes=[(tag, "x1t", b)])
            for nb in range(2):
                q = cnt["q"] % 2
                cnt["q"] += 1
                ns = slice(nb * 512, (nb + 1) * 512)
                for c in range(2):
                    P.op("pe", lambda e, c=c, j=j, ns=ns: e.matmul(pd[0][:], lhsT=pT[:, c, j * 128:(j + 1) * 128], rhs=plew[:, c, ns], start=(c == 0), stop=(c == 1)),
                         reads=[(tag, "pT", j), (tag, "plew")], writes=[(tag, "pd", 0)])
                for c in range(8):
                    P.op("pe", lambda e, c=c, j=j, ns=ns, s=s: e.matmul(pd[1][:], lhsT=xt[s][:, c, 1 + j * 128:1 + (j + 1) * 128], rhs=plegw[:, c, ns],
                                                                       start=(c == 0), stop=(c == 7)),
                         reads=[(tag, "xt", s), (tag, "plegw")], writes=[(tag, "pd", 1)])
                P.op("act", lambda e, q=q: e.activation(out=sg[q][:], in_=pd[1][:], func=AF.Sigmoid), writes=[(tag, "sg", q), (tag, "pd", 1)])
                P.op("dve", lambda e, q=q: e.tensor_tensor(out=sg[q][:], in0=pd[0][:], in1=sg[q][:], op=ALU.mult),
                     reads=[(tag, "sg", q)], writes=[(tag, "sg", q), (tag, "pd", 0)])
                P.op("dve", lambda e, q=q, b=b, j=j, ns=ns: e.scalar_tensor_tensor(out=racc[j][:, ns], in0=x1t[b][:, ns], scalar=ALPHA, in1=sg[q][:],
                                                                                  op0=ALU.mult, op1=ALU.add),
                     reads=[(tag, "sg", q), (tag, "x1t", b)], writes=[(tag, "racc", j)])
        def wload(f):
            w = (cnt["w"] + f) % 3
            P.dma("sp", wg[w][:], wup_v[:, :, f * 128:(f + 1) * 128], chan=(tag, "wg", w), writes=[(tag, "wg", w)])
            P.dma("sp", wv[w][:], wup_v[:, :, D_FF + f * 128:D_FF + (f + 1) * 128], chan=(tag, "wv", w), writes=[(tag, "wv", w)])
        wload(0)
        wload(1)
        for f in range(NFT):
            w = (cnt["w"] + f) % 3
            u = cnt["u"] % 2
            g3 = cnt["u"] % 3
            cnt["u"] += 1
            if f + 2 < NFT:
                wload(f + 2)
            for c in range(8):
                P.op("pe", lambda e, c=c, w=w, u=u, s=s: e.matmul(pgate[u][:], lhsT=wg[w][:, c, :], rhs=xt[s][:, c, 1:513], start=(c == 0), stop=(c == 7)),
                     reads=[(tag, "wg", w), (tag, "xt", s)], writes=[(tag, "pgate", u)])
            for c in range(8):
                P.op("pe", lambda e, c=c, w=w, s=s: e.matmul(phalo[:], lhsT=wg[w][:, c, :], rhs=xt[s][:, c, 0:514:513], start=(c == 0), stop=(c == 7)),
                     reads=[(tag, "wg", w), (tag, "xt", s)], writes=[(tag, "phalo")])
            for c in range(8):
                P.op("pe", lambda e, c=c, w=w, u=u, s=s: e.matmul(pval[u][:], lhsT=wv[w][:, c, :], rhs=xt[s][:, c, 1:513], start=(c == 0), stop=(c == 7)),
                     reads=[(tag, "wv", w), (tag, "xt", s)], writes=[(tag, "pval", u)])
            kc, kt = (tag, "cv", g3), (tag, "tmp", g3)
            P.op("act", lambda e, u=u, g3=g3: e.copy(out=gext[g3][:, 1:513], in_=pgate[u][:]), writes=[(tag, "gext", g3), (tag, "pgate", u)])
            P.op("act", lambda e, u=u, g3=g3, f=f: e.activation(out=cv[g3][:], in_=pgate[u][:], func=AF.Identity, scale=cwb[:, f, 1:2], bias=cwb[:, f, 3:4]),
                 reads=[(tag, "cwb")], writes=[kc, (tag, "pgate", u)])
            P.op("act", lambda e, g3=g3: e.copy(out=gext[g3][:, 0:514:513], in_=phalo[:]), writes=[(tag, "gexth", g3), (tag, "phalo")])
            gkeys = [(tag, "gext", g3), (tag, "gexth", g3), (tag, "cwb")]
            P.op("dve", lambda e, g3=g3, f=f: e.scalar_tensor_tensor(out=cv[g3][:], in0=gext[g3][:, 0:512], scalar=cwb[:, f, 0:1], in1=cv[g3][:],
                                                                     op0=ALU.mult, op1=ALU.add), reads=gkeys + [kc], writes=[kc])
            P.op("dve", lambda e, g3=g3, f=f: e.scalar_tensor_tensor(out=cv[g3][:], in0=gext[g3][:, 2:514], scalar=cwb[:, f, 2:3], in1=cv[g3][:],
                                                                     op0=ALU.mult, op1=ALU.add), reads=gkeys + [kc], writes=[kc])
            P.op("act", lambda e, g3=g3: e.activation(out=tmp[g3][:], in_=cv[g3][:], func=AF.Square), reads=[kc], writes=[kt])
            P.op("act", lambda e, g3=g3: e.activation(out=tmp[g3][:], in_=tmp[g3][:], func=AF.Identity, scale=0.044715, bias=oneb[:, 0:1]),
                 reads=[kt, (tag, "oneb")], writes=[kt])
            P.op("pool", lambda e, g3=g3: e.tensor_tensor(out=tmp[g3][:], in0=tmp[g3][:], in1=cv[g3][:], op=ALU.mult), reads=[kt, kc], writes=[kt])
            P.op("act", lambda e, g3=g3: e.activation(out=tmp[g3][:], in_=tmp[g3][:], func=AF.Sigmoid, scale=GELU_K), reads=[kt], writes=[kt])
            P.op("pool", lambda e, g3=g3: e.tensor_tensor(out=tmp[g3][:], in0=tmp[g3][:], in1=cv[g3][:], op=ALU.mult), reads=[kt, kc], writes=[kt])
            P.op("dve", lambda e, u=u, g3=g3, f=f: e.tensor_tensor(out=hm[:, f, :], in0=pval[u][:], in1=tmp[g3][:], op=ALU.mult),
                 reads=[kt], writes=[(tag, "hm", f), (tag, "pval", u)])
        cnt["w"] += NFT
        for j in range(4):
            for nb in range(2):
                d = cnt["d"] % 2
                cnt["d"] += 1
                ns = slice(nb * 512, (nb + 1) * 512)
                for f in range(NFT):
                    P.op("pe", lambda e, f=f, j=j, ns=ns, d=d: e.matmul(pd[d][:], lhsT=hm[:, f, j * 128:(j + 1) * 128], rhs=wdn[:, f, ns],
                                                                       start=(f == 0), stop=(f == NFT - 1)),
                         reads=[(tag, "hm", f), (tag, "wdn")], writes=[(tag, "pd", d)])
                P.op("dve", lambda e, j=j, ns=ns, d=d: e.tensor_tensor(out=racc[j][:, ns], in0=pd[d][:], in1=racc[j][:, ns], op=ALU.add),
                     reads=[(tag, "racc", j)], writes=[(tag, "racc", j), (tag, "pd", d)])
            r0 = t0 + j * 128
            emit_ln(P, (tag, "ln"), racc[j][:], (tag, "racc", j), st, mv, rstd, epsb, gtab[:], btab[:], gk, racc[j][:], (tag, "racc", j))
            P.dma("sp", x2_d[r0:r0 + 128, :], racc[j][:], chan=(tag, "x2o", j), reads=[(tag, "racc", j)], writes=[(tag, "x2d", tt, j)])
            emit_to_featmajor(P, (tag, "fm"), racc[j][:], (tag, "racc", j), xbf[:], pst, x2T, lambda jj: (tag, "x2T", jj), ident, j)
        P.dma("sp", x2T_v[:, :, t0:t0 + 512], x2T[:], chan=(tag, "x2T"), reads=[(tag, "x2T", jj) for jj in range(4)], writes=[(tag, "x2Td", tt)])

    for tt in range(NTT):
        tile(tt)


def build_P2(NTOK):
    nc = bass.Bass("TRN2", target_bir_lowering=False)
    with ExitStack() as es:
        C = Ctx(nc, es)
        x1T_d = C.din("x1Te", [1024, NTOK + 2], BF16)
        x1_d = C.din("x1", [NTOK, 1024], F32)
        p_d = C.din("p", [NTOK, 256], F32)
        wup_d = C.din("ffn_w_up_bf", [1024, 2 * D_FF], BF16)
        wdn_d = C.din("ffn_w_down_bf", [D_FF, 1024], BF16)
        plew_d = C.din("ple_w_bf", [256, 1024], BF16)
        plegw_d = C.din("ple_gate_w_bf", [1024, 1024], BF16)
        cwb_d = C.din("cwb", [128, NFT, 4], F32)
        g2_d = C.din("ln_g", [1024], F32)
        b2_d = C.din("ln_b", [1024], F32)
        id_d = C.din("ident", [128, 128], BF16)
        x2_d = C.dout("x2", [NTOK, 1024], F32)
        x2T_d = C.dout("x2T", [1024, NTOK], BF16)
        ident = C.sb([128, 128], BF16, "ident")
        C.P.dma("sp", ident[:], id_d[:, :], chan="ident", writes=["ident"])
        stage_P2(C, x1T_d, x1_d, p_d, wup_d, wdn_d, plew_d, plegw_d, cwb_d[:, :, :], g2_d, b2_d, x2_d, x2T_d, ident, NTOK)
        C.P.emit(es)
    return nc


def build_W0_split():
    nc = bass.Bass("TRN2", target_bir_lowering=False)
    with ExitStack() as es:
        C = Ctx(nc, es)
        pairs = []
        for i in range(DEPTH):
            for nm, R, N in W0_SPECS:
                r = R // NCORES
                pairs.append((C.din("%s_%d" % (nm, i), [r, N], F32), C.dout("%s_%d_bf" % (nm, i), [r, N], BF16), r, N))
        stage_W0v(C, pairs)
        C.P.emit(es)
    return nc


def stage_W0v(C, pairs, tag="w0"):
    P = C.P
    stg = [C.sb([128, 2816], F32, "w0s") for _ in range(2)]
    obf = [C.sb([128, 2816], BF16, "w0o") for _ in range(2)]
    i = 0
    engs = ("dve", "pool", "act")
    for src_d, dst_d, R, N in pairs:
        for r0 in range(0, R, 128):
            pr = min(128, R - r0)
            for n0 in range(0, N, 2816):
                n1 = min(N, n0 + 2816)
                w = n1 - n0
                s = i % 2
                P.dma("sp", stg[s][0:pr, 0:w], src_d[r0:r0 + pr, n0:n1], chan=(tag, "in", s), writes=[(tag, "stg", s)])
                en = engs[i % 3]
                if en == "act":
                    P.op("act", lambda e, s=s, w=w, pr=pr: e.copy(out=obf[s][0:pr, 0:w], in_=stg[s][0:pr, 0:w]), reads=[(tag, "stg", s)], writes=[(tag, "obf", s)])
                else:
                    P.op(en, lambda e, s=s, w=w, pr=pr: e.tensor_copy(out=obf[s][0:pr, 0:w], in_=stg[s][0:pr, 0:w]), reads=[(tag, "stg", s)], writes=[(tag, "obf", s)])
                P.dma("pool", dst_d[r0:r0 + pr, n0:n1], obf[s][0:pr, 0:w], chan=(tag, "out", s), reads=[(tag, "obf", s)], writes=[(tag, "dst", i)])
                i += 1


def build_fused(L):
    nc = bass.Bass("TRN2", target_bir_lowering=False)

    def din(name, shape, dt):
        return nc.dram_tensor(name, list(shape), dt, kind="ExternalInput").ap()

    def dint(name, shape, dt):
        return nc.dram_tensor(name, list(shape), dt, kind="Internal").ap()

    x_d = din("x", [L, 1024], F32)
    p_d = [din("p%d" % l, [L, 256], F32) for l in range(DEPTH)]
    pos_d = din("pos", [L], I32)
    rc_d = din("rc", [128, 4], F32)
    idb_d = din("ident", [128, 128], BF16)
    idf_d = din("identf", [128, 128], F32)
    s5c_d = din("s5c", [128, S5C_W], F32)
    zero_d = din("zeros", [128, 8], BF16)
    wsrc = {}
    wbf = {}
    for l in range(DEPTH):
        for nm, R, N in W0_SPECS:
            wsrc[(nm, l)] = din("%s_%d" % (nm, l), [R, N], F32)
            wbf[(nm, l)] = dint("%s_%d_bf" % (nm, l), [R, N], BF16)
    wfm_d = {(l, hp): din("wfm_%d_%d" % (l, hp), [1024, NFM * 128], F32) for l in range(DEPTH) for hp in range(2)}
    wtm_d = {(l, hp): din("wtm_%d_%d" % (l, hp), [1024, 512], F32) for l in range(DEPTH) for hp in range(2)}
    lamp_d = [din("lamp%d" % l, [4, 64], F32) for l in range(DEPTH)]
    subg_d = [din("subg%d" % l, [128], F32) for l in range(DEPTH)]
    lc_d = [din("lc%d" % l, [128, 2], F32) for l in range(DEPTH)]
    rt_d = [din("rt%d" % hp, [128, RT_W], F32) for hp in range(2)]
    gng_d = [din("gng%d" % l, [64], F32) for l in range(DEPTH)]
    gnb_d = [din("gnb%d" % l, [64], F32) for l in range(DEPTH)]
    s5_d = {(l, hp): (din("s5p_%d_%d" % (l, hp), [128, 49], F32), din("s5bre_%d_%d" % (l, hp), [128, 16, 16], F32),
                      din("s5bim_%d_%d" % (l, hp), [128, 16, 16], F32), din("s5cc_%d_%d" % (l, hp), [128, 2, 128], F32))
            for l in range(DEPTH) for hp in range(2)}
    glb_d = [din("glb%d" % l, [256], F32) for l in range(DEPTH)]
    ln_d = [[din("ln%d_%s%d" % (k, gb, l), [1024], F32) for gb in ("g", "b")] for l in range(DEPTH) for k in (1, 2)]
    cwb_d = [din("cwb%d" % l, [128, NFT, 4], F32) for l in range(DEPTH)]
    out_d = nc.dram_tensor("out", [L, 1024], F32, kind="ExternalOutput").ap()

    xT = dint("xT_s", [1024, L], BF16)
    fmT = dint("fmT_s", [7, 128, L], BF16)
    vr = dint("vr_s", [L, 384], BF16)
    sgd = dint("sg_s", [L, 128], F32)
    ycT = dint("ycT_s", [1024, L], BF16)
    x1 = dint("x1_s", [L, 1024], F32)
    x1Te = dint("x1Te_s", [1024, L + 2], BF16)
    xmid = dint("xmid_s", [L, 1024], F32)
    ycT_r = ycT.rearrange("(r p) t -> r p t", p=128)

    def scope(fn, with_ident=False):
        with ExitStack() as es:
            C = Ctx(nc, es)
            ident = None
            if with_ident:
                ident = C.sb([128, 128], BF16, "ident")
                C.P.dma("sp", ident[:], idb_d[:, :], chan="ident", writes=["ident"])
            fn(C, ident)
            C.P.emit(es, own_sems=True)
        nc.all_engine_barrier()
        nc.clear_and_free_semaphores(C.P.sem_handles)
        nc.all_engine_barrier()

    def zero_halo(C, ident):
        z = C.sb([128, 8], BF16, "z")
        C.P.dma("sp", z[:], zero_d[:, :], chan="z", writes=["z"])
        v = x1Te.rearrange("(c p) t -> p c t", p=128)
        C.P.dma("sp", v[:, :, 0:1], z[:].unsqueeze(2), chan="z2", reads=["z"], slow=True)
        C.P.dma("sp", v[:, :, L + 1:L + 2], z[:].unsqueeze(2), chan="z2", reads=["z"], slow=True)

    scope(lambda C, i: stage_W0v(C, [(wsrc[(nm, l)], wbf[(nm, l)], R, N) for l in range(DEPTH) for nm, R, N in W0_SPECS]))
    scope(zero_halo)
    scope(lambda C, i: stage_T0(C, x_d, xT, i, L), with_ident=True)
    for l in range(DEPTH):
        xin_d = x_d if l == 0 else xmid
        xout_d = xmid if l == 0 else out_d
        for hp in range(2):
            scope(lambda C, i, l=l, hp=hp: stage_A1(C, xT, wfm_d[(l, hp)], wtm_d[(l, hp)], pos_d, rc_d, fmT, vr, sgd, L))
            scope(lambda C, i, l=l, hp=hp: stage_A2(C, fmT, vr, lamp_d[l], subg_d[l], lc_d[l], ycT_r[2 * hp:2 * hp + 2, :, :], i, L), with_ident=True)
            scope(lambda C, i, l=l, hp=hp: stage_A3(C, fmT, vr, sgd, rt_d[hp], gng_d[l], gnb_d[l], ycT_r[4 + hp, :, :], i, L), with_ident=True)
            scope(lambda C, i, l=l, hp=hp: stage_A4(C, fmT, s5_d[(l, hp)][0][:, :], s5_d[(l, hp)][1][:, :, :], s5_d[(l, hp)][2][:, :, :],
                                                   s5_d[(l, hp)][3][:, :, :], s5c_d[:, :], idf_d[:, :], ycT_r[6 + hp, :, :], L))
        scope(lambda C, i, l=l, xin_d=xin_d: stage_P1(C, ycT, xin_d, wbf[("w_out", l)], wbf[("s5_glu_w", l)], glb_d[l], ln_d[2 * l][0], ln_d[2 * l][1],
                                                     x1, x1Te[:, 1:L + 1], i, L), with_ident=True)
        scope(lambda C, i, l=l, xout_d=xout_d: stage_P2(C, x1Te, x1, p_d[l], wbf[("ffn_w_up", l)], wbf[("ffn_w_down", l)], wbf[("ple_w", l)],
                                                       wbf[("ple_gate_w", l)], cwb_d[l][:, :, :], ln_d[2 * l + 1][0], ln_d[2 * l + 1][1], xout_d, xT, i, L),
              with_ident=True)
    return nc


def fused_inputs(inp, b):
    m = {"x": np.ascontiguousarray(inp["x"][b], dtype=np.float32), "pos": np.ascontiguousarray(inp["positions"][b], dtype=np.int32),
         "rc": rc_const(), "ident": np.eye(128, dtype=np.float32).astype(ml_dtypes.bfloat16), "identf": np.eye(128, dtype=np.float32),
         "s5c": s5_consts(), "zeros": np.zeros((128, 8), ml_dtypes.bfloat16)}
    for l in range(DEPTH):
        lam_init = 0.8 - 0.6 * math.exp(-0.3 * l)
        m["p%d" % l] = np.ascontiguousarray(inp["p"][l][b], dtype=np.float32)
        for nm, R, N in W0_SPECS:
            m["%s_%d" % (nm, l)] = np.ascontiguousarray(inp[nm][l], dtype=np.float32)
        for hp in range(2):
            fm, tm = a1_columns(hp)
            m["wfm_%d_%d" % (l, hp)] = np.ascontiguousarray(inp["w_in"][l][:, fm], dtype=np.float32)
            m["wtm_%d_%d" % (l, hp)] = np.ascontiguousarray(inp["w_in"][l][:, tm], dtype=np.float32)
            s5 = s5_layout(inp, l, hp)
            for k in ("s5p", "s5bre", "s5bim", "s5cc"):
                m["%s_%d_%d" % (k, l, hp)] = s5[k]
        m["lamp%d" % l] = np.stack([inp["da_lambda_q1"][l], inp["da_lambda_k1"][l], inp["da_lambda_q2"][l], inp["da_lambda_k2"][l]]).astype(np.float32)
        m["subg%d" % l] = np.ascontiguousarray(inp["da_subln_g"][l], dtype=np.float32)
        m["lc%d" % l] = np.tile(np.array([[lam_init, 1.0 - lam_init]], np.float32), (128, 1))
        m["gng%d" % l] = np.ascontiguousarray(inp["ret_gn_g"][l], dtype=np.float32)
        m["gnb%d" % l] = np.ascontiguousarray(inp["ret_gn_b"][l], dtype=np.float32)
        m["glb%d" % l] = np.ascontiguousarray(inp["s5_glu_b"][l], dtype=np.float32)
        for k in (1, 2):
            m["ln%d_g%d" % (k, l)] = np.ascontiguousarray(inp["ln%d_g" % k][l], dtype=np.float32)
            m["ln%d_b%d" % (k, l)] = np.ascontiguousarray(inp["ln%d_b" % k][l], dtype=np.float32)
        m["cwb%d" % l] = conv_layout(inp["ffn_conv_w"][l], inp["ffn_conv_b"][l])
    for hp in range(2):
        m["rt%d" % hp] = ret_consts(hp)
    return m


def kernel_fused(**inputs):
    inp = {k: np.asarray(v) for k, v in inputs.items()}
    B, L, _ = inp["x"].shape
    NTOK = L // 2
    nc = _prog("fused", build_fused, L)
    bmaps = [fused_inputs(inp, b) for b in range(B)]
    res = _run(nc, [bmaps[c // 2] for c in range(NCORES)])
    out = np.empty((B, L, D_MODEL), np.float32)
    for c in range(NCORES):
        b, h = c // 2, c % 2
        out[b, h * NTOK:(h + 1) * NTOK] = res[c]["out"][h * NTOK:(h + 1) * NTOK]
    return out


_PROGS = {}


def _prog(name, fn, *args):
    key = (name,) + args
    if key not in _PROGS:
        _PROGS[key] = fn(*args)
    return _PROGS[key]


def _run(nc, maps):
    return run_bass_kernel_spmd(nc, maps, core_ids=list(range(NCORES))).results


def kernel_unfused(**inputs):
    inp = {k: np.asarray(v) for k, v in inputs.items()}
    x = np.ascontiguousarray(inp["x"], dtype=np.float32)
    B, L, _ = x.shape
    assert B * 2 == NCORES
    NTOK = L // 2
    ident_bf = np.eye(128, dtype=np.float32).astype(ml_dtypes.bfloat16)
    identf = np.eye(128, dtype=np.float32)
    rc = rc_const()
    s5c = s5_consts()
    cores = [(c // 2, c % 2) for c in range(NCORES)]

    maps = []
    for c in range(NCORES):
        m = {}
        for i in range(DEPTH):
            for nm, R, N in W0_SPECS:
                r = R // NCORES
                m["%s_%d" % (nm, i)] = np.ascontiguousarray(inp[nm][i][c * r:(c + 1) * r], dtype=np.float32)
        maps.append(m)
    res = _run(_prog("W0", build_W0_split), maps)
    wbf = [{nm: np.concatenate([res[c]["%s_%d_bf" % (nm, i)] for c in range(NCORES)], axis=0) for nm, R, N in W0_SPECS}
           for i in range(DEPTH)]

    res = _run(_prog("T0", build_T0, NTOK), [{"x": np.ascontiguousarray(x[b, h * NTOK:(h + 1) * NTOK]), "ident": ident_bf} for b, h in cores])
    xT = [np.concatenate([res[2 * b]["xT"], res[2 * b + 1]["xT"]], axis=1) for b in range(B)]
    xcur = [np.ascontiguousarray(x[b, h * NTOK:(h + 1) * NTOK]) for b, h in cores]

    for i in range(DEPTH):
        lam_init = 0.8 - 0.6 * math.exp(-0.3 * i)
        w_in = inp["w_in"][i]
        maps = []
        for b, h in cores:
            fm, tm = a1_columns(h)
            maps.append({"xT": xT[b], "wfm": np.ascontiguousarray(w_in[:, fm], dtype=np.float32),
                         "wtm": np.ascontiguousarray(w_in[:, tm], dtype=np.float32),
                         "pos": np.ascontiguousarray(inp["positions"][b], dtype=np.int32), "rc": rc})
        a1 = _run(_prog("A1", build_A1, L), maps)
        lamp = np.stack([inp["da_lambda_q1"][i], inp["da_lambda_k1"][i], inp["da_lambda_q2"][i], inp["da_lambda_k2"][i]]).astype(np.float32)
        lc = np.tile(np.array([[lam_init, 1.0 - lam_init]], np.float32), (128, 1))
        a2 = _run(_prog("A2", build_A2, L), [{"fmT": a1[c]["fmT"], "vr": a1[c]["vr"], "lamp": lamp,
                                              "subg": np.ascontiguousarray(inp["da_subln_g"][i], dtype=np.float32), "lc": lc, "ident": ident_bf}
                                             for c in range(NCORES)])
        a3 = _run(_prog("A3", build_A3, L), [{"fmT": a1[c]["fmT"], "vr": a1[c]["vr"], "sg": a1[c]["sg"], "rt": ret_consts(cores[c][1]),
                                              "gng": np.ascontiguousarray(inp["ret_gn_g"][i], dtype=np.float32),
                                              "gnb": np.ascontiguousarray(inp["ret_gn_b"][i], dtype=np.float32), "ident": ident_bf}
                                             for c in range(NCORES)])
        maps = []
        for c, (b, h) in enumerate(cores):
            m = {"fmT": a1[c]["fmT"], "s5c": s5c, "identf": identf}
            m.update(s5_layout(inp, i, h))
            maps.append(m)
        a4 = _run(_prog("A4", build_A4, L), maps)
        maps = []
        for c, (b, h) in enumerate(cores):
            ts = slice(h * NTOK, (h + 1) * NTOK)
            rows = []
            for hp in range(2):
                rows += [a2[2 * b + hp]["ydaT"][0][:, ts], a2[2 * b + hp]["ydaT"][1][:, ts]]
            rows += [a3[2 * b + hp]["yretT"][:, ts] for hp in range(2)]
            rows += [a4[2 * b + hp]["ys5T"][:, ts] for hp in range(2)]
            maps.append({"ycT": np.ascontiguousarray(np.concatenate(rows, axis=0)), "x": xcur[c], "w_out_bf": wbf[i]["w_out"],
                         "s5_glu_w_bf": wbf[i]["s5_glu_w"], "glb": np.ascontiguousarray(inp["s5_glu_b"][i], dtype=np.float32),
                         "ln_g": np.ascontiguousarray(inp["ln1_g"][i], dtype=np.float32),
                         "ln_b": np.ascontiguousarray(inp["ln1_b"][i], dtype=np.float32), "ident": ident_bf})
        p1 = _run(_prog("P1", build_P1, NTOK), maps)
        cwb = conv_layout(inp["ffn_conv_w"][i], inp["ffn_conv_b"][i])
        maps = []
        for c, (b, h) in enumerate(cores):
            ts = slice(h * NTOK, (h + 1) * NTOK)
            ext = np.zeros((1024, NTOK + 2), dtype=ml_dtypes.bfloat16)
            ext[:, 1:NTOK + 1] = p1[c]["x1T"]
            if h == 1:
                ext[:, 0] = p1[c - 1]["x1T"][:, NTOK - 1]
            else:
                ext[:, NTOK + 1] = p1[c + 1]["x1T"][:, 0]
            maps.append({"x1Te": ext, "x1": p1[c]["x1"], "p": np.ascontiguousarray(inp["p"][i][b][ts], dtype=np.float32),
                         "ffn_w_up_bf": wbf[i]["ffn_w_up"], "ffn_w_down_bf": wbf[i]["ffn_w_down"], "ple_w_bf": wbf[i]["ple_w"],
                         "ple_gate_w_bf": wbf[i]["ple_gate_w"], "cwb": cwb,
                         "ln_g": np.ascontiguousarray(inp["ln2_g"][i], dtype=np.float32),
                         "ln_b": np.ascontiguousarray(inp["ln2_b"][i], dtype=np.float32), "ident": ident_bf})
        p2 = _run(_prog("P2", build_P2, NTOK), maps)
        xcur = [p2[c]["x2"] for c in range(NCORES)]
        xT = [np.concatenate([p2[2 * b]["x2T"], p2[2 * b + 1]["x2T"]], axis=1) for b in range(B)]

    out = np.empty((B, L, D_MODEL), np.float32)
    for c, (b, h) in enumerate(cores):
        out[b, h * NTOK:(h + 1) * NTOK] = xcur[c]
    return out


def kernel(**inputs):
    return kernel_fused(**inputs)
```
